# Optimizing a Trainium2 kernel written in Bass

```python
import math
import jax
import jax.numpy as jnp
from jax import lax
import numpy as np

D_MODEL = 1024
BATCH = 1
SEQ = 16384
DEPTH = 2
DEC_BATCH = 32
DEC_SEQ = 4
PAST_LEN = 16384
PAGE_SIZE = 128

N_MAMBA_LAYERS = (DEPTH + 1) // 2
N_ATT_LAYERS = DEPTH // 2
SSD_HEAD_DIM = 64
SSD_WIDTH = D_MODEL
SSD_HEADS = SSD_WIDTH // SSD_HEAD_DIM
SSD_GROUPS = 4
SSD_STATE = 128
SSD_CONV = 4
SSD_CHUNK = 256
SSD_CONV_DIM = SSD_WIDTH + 2 * SSD_GROUPS * SSD_STATE
CONF_WIDTH = D_MODEL
CONF_CONV_WIDTH = 31
IN0_COLS = SSD_WIDTH + SSD_CONV_DIM + SSD_HEADS + 2 * CONF_WIDTH + CONF_WIDTH
MIX0_WIDTH = SSD_WIDTH + CONF_WIDTH
ATT_HEADS = 16
ATT_KV_HEADS = 4
ATT_GROUP = ATT_HEADS // ATT_KV_HEADS
HEAD_DIM = 64
ATT_WIDTH = ATT_HEADS * HEAD_DIM
KV_WIDTH = ATT_KV_HEADS * HEAD_DIM
IN1_COLS = 2 * ATT_WIDTH + 2 * KV_WIDTH
MOBA_BLOCK = 256
MOBA_TOP_K = 3
Q_CHUNK = 64
SCALE = HEAD_DIM ** -0.5
NORM_EPS = 1e-6

kernel_name = 'ssd_conformer_moba_hybrid_step'


def rms_norm(x, g):
    xf = x.astype(jnp.float32)
    y = xf * lax.rsqrt(jnp.mean(xf * xf, axis=-1, keepdims=True) + NORM_EPS)
    return (y * g.astype(jnp.float32)).astype(x.dtype)


def layer_norm(x, g, b):
    xf = x.astype(jnp.float32)
    mu = jnp.mean(xf, axis=-1, keepdims=True)
    xc = xf - mu
    var = jnp.mean(xc * xc, axis=-1, keepdims=True)
    return (xc * lax.rsqrt(var + NORM_EPS) * g.astype(jnp.float32) + b.astype(jnp.float32)).astype(x.dtype)


def causal_depthwise_conv(x_full, w, b):
    y = lax.conv_general_dilated(x_full, w[:, None, :].astype(x_full.dtype), (1,), 'VALID',
                                 dimension_numbers=('NWC', 'WIO', 'NWC'), feature_group_count=w.shape[-1])
    return y + b.astype(y.dtype)


def ssd_chunked_scan(x, dt, a_head, bm, cm, h0, chunk):
    f32 = jnp.float32
    b, t, h, p = x.shape
    g, n = bm.shape[-2:]
    r = h // g
    nc = -(-t // chunk)
    pad = nc * chunk - t
    xdt = x.astype(f32) * dt[..., None]
    a = dt * a_head
    bf = bm.astype(f32)
    cf = cm.astype(f32)
    if pad:
        padt = lambda arr: jnp.pad(arr, [(0, 0), (0, pad)] + [(0, 0)] * (arr.ndim - 2))
        xdt, a, bf, cf = padt(xdt), padt(a), padt(bf), padt(cf)
    xc = xdt.reshape(b, nc, chunk, g, r, p)
    ac = a.reshape(b, nc, chunk, g, r)
    bc = bf.reshape(b, nc, chunk, g, n)
    cc = cf.reshape(b, nc, chunk, g, n)
    acs = jnp.cumsum(ac, axis=2)
    causal = jnp.tril(jnp.ones((chunk, chunk), bool))
    seg = acs[:, :, :, None] - acs[:, :, None, :]
    decay = jnp.exp(jnp.where(causal[None, None, :, :, None, None], seg, -jnp.inf))
    cb = jnp.einsum('bclgn,bcsgn->bclsg', cc, bc)
    y_diag = jnp.einsum('bclsgr,bcsgrp->bclgrp', decay * cb[..., None], xc)
    decay_end = jnp.exp(acs[:, :, -1:] - acs)
    chunk_states = jnp.einsum('bclgn,bclgr,bclgrp->bcgrpn', bc, decay_end, xc)
    chunk_decay = jnp.exp(acs[:, :, -1])

    def step(hc, inp):
        s_c, d_c = inp
        return hc * d_c[..., None, None] + s_c, hc

    h_t, h_in = lax.scan(step, h0.astype(f32).reshape(b, g, r, p, n),
                         (jnp.moveaxis(chunk_states, 1, 0), jnp.moveaxis(chunk_decay, 1, 0)))
    h_in = jnp.moveaxis(h_in, 0, 1)
    y_off = jnp.einsum('bclgn,bcgrpn->bclgrp', cc, h_in) * jnp.exp(acs)[..., None]
    y = (y_diag + y_off).reshape(b, nc * chunk, h, p)[:, :t]
    return y, h_t.reshape(b, h, p, n)


def hybrid_layer(x, h0, ssd_hist, conf_hist, norm_g, w_in, conv_w, conv_b, dt_bias, a_log, d_skip,
                 ssd_norm_g, cconv_w, cconv_b, cln_g, cln_b, w_out):
    f32 = jnp.float32
    b, t, _ = x.shape
    u = rms_norm(x, norm_g) @ w_in
    o1 = SSD_WIDTH
    o2 = o1 + SSD_CONV_DIM
    o3 = o2 + SSD_HEADS
    o4 = o3 + 2 * CONF_WIDTH
    z, xbc, dt_raw, glu_in, c_gate = jnp.split(u, [o1, o2, o3, o4], axis=-1)
    xbc_full = jnp.concatenate([ssd_hist.astype(u.dtype), xbc], axis=1)
    new_ssd_hist = xbc_full[:, xbc_full.shape[1] - (SSD_CONV - 1):]
    xbc = jax.nn.silu(causal_depthwise_conv(xbc_full, conv_w, conv_b))
    xs, bm, cm = jnp.split(xbc, [SSD_WIDTH, SSD_WIDTH + SSD_GROUPS * SSD_STATE], axis=-1)
    xs = xs.reshape(b, t, SSD_HEADS, SSD_HEAD_DIM)
    bm = bm.reshape(b, t, SSD_GROUPS, SSD_STATE)
    cm = cm.reshape(b, t, SSD_GROUPS, SSD_STATE)
    dt = jax.nn.softplus(dt_raw.astype(f32) + dt_bias.astype(f32))
    a_head = -jnp.exp(a_log.astype(f32))
    y, h_t = ssd_chunked_scan(xs, dt, a_head, bm, cm, h0, min(SSD_CHUNK, t))
    y = y + d_skip.astype(f32)[:, None] * xs.astype(f32)
    y = y.reshape(b, t, SSD_WIDTH) * jax.nn.silu(z.astype(f32))
    yg = y.reshape(b, t, SSD_GROUPS, SSD_WIDTH // SSD_GROUPS)
    yg = yg * lax.rsqrt(jnp.mean(yg * yg, axis=-1, keepdims=True) + NORM_EPS)
    y = (yg.reshape(b, t, SSD_WIDTH) * ssd_norm_g.astype(f32)).astype(x.dtype)
    ga, gb = jnp.split(glu_in, 2, axis=-1)
    gl = ga * jax.nn.sigmoid(gb)
    g_full = jnp.concatenate([conf_hist.astype(gl.dtype), gl], axis=1)
    new_conf_hist = g_full[:, g_full.shape[1] - (CONF_CONV_WIDTH - 1):]
    c = layer_norm(causal_depthwise_conv(g_full, cconv_w, cconv_b), cln_g, cln_b)
    c = jax.nn.silu(c) * jax.nn.silu(c_gate)
    out = jnp.concatenate([y, c.astype(x.dtype)], axis=-1) @ w_out
    return out, h_t, new_ssd_hist, new_conf_hist


def att_project(x, norm_g, w_in, qn_g, kn_g):
    b, t, _ = x.shape
    u = rms_norm(x, norm_g) @ w_in
    q, k, v, gate = jnp.split(u, [ATT_WIDTH, ATT_WIDTH + KV_WIDTH, ATT_WIDTH + 2 * KV_WIDTH], axis=-1)
    q = rms_norm(q.reshape(b, t, ATT_KV_HEADS, ATT_GROUP, HEAD_DIM), qn_g)
    k = rms_norm(k.reshape(b, t, ATT_KV_HEADS, HEAD_DIM), kn_g)
    v = v.reshape(b, t, ATT_KV_HEADS, HEAD_DIM)
    return q, k, v, gate


def att_output(o, gate, w_out):
    return (o * jax.nn.silu(gate)) @ w_out


def moba_attend(q, k_own, v_own, own_mask, k_sel, v_sel, sel_mask):
    s_own = jnp.einsum('bthgd,bjhd->bthgj', q, k_own).astype(jnp.float32) * SCALE
    s_own = jnp.where(own_mask, s_own, -jnp.inf)
    if k_sel is None:
        p = jax.nn.softmax(s_own, axis=-1).astype(v_own.dtype)
        return jnp.einsum('bthgj,bjhd->bthgd', p, v_own)
    s_sel = jnp.einsum('bthgd,bthgkjd->bthgkj', q, k_sel).astype(jnp.float32) * SCALE
    s_sel = jnp.where(sel_mask, s_sel, -jnp.inf)
    b, t, hk, g, kk, j = s_sel.shape
    s = jnp.concatenate([s_sel.reshape(b, t, hk, g, kk * j), s_own], axis=-1)
    p = jax.nn.softmax(s, axis=-1).astype(v_own.dtype)
    p_sel = p[..., :kk * j].reshape(b, t, hk, g, kk, j)
    p_own = p[..., kk * j:]
    return (jnp.einsum('bthgkj,bthgkjd->bthgd', p_sel, v_sel)
            + jnp.einsum('bthgj,bjhd->bthgd', p_own, v_own))


def moba_prompt(q, k, v):
    f32 = jnp.float32
    b, s = q.shape[:2]
    nb = -(-s // MOBA_BLOCK)
    pad = ((0, 0), (0, nb * MOBA_BLOCK - s), (0, 0), (0, 0))
    kp = jnp.pad(k, pad)
    vp = jnp.pad(v, pad)
    ksel = min(MOBA_TOP_K, nb - 1)
    nq = s // Q_CHUNK

    def chunked(arr):
        return jnp.moveaxis(arr.reshape((b, nq, Q_CHUNK) + arr.shape[2:]), 1, 0)

    xs = (jnp.arange(nq), chunked(q))
    if ksel > 0:
        kb = kp.reshape(b, nb, MOBA_BLOCK, ATT_KV_HEADS, HEAD_DIM)
        vb = vp.reshape(b, nb, MOBA_BLOCK, ATT_KV_HEADS, HEAD_DIM)
        own = jnp.arange(s) // MOBA_BLOCK
        kmean = jnp.mean(kb.astype(f32), axis=2)
        gate = jnp.einsum('bshgd,bnhd->bshgn', q.astype(f32), kmean)
        is_past = jnp.arange(nb)[None, :] < own[:, None]
        gate = jnp.where(is_past[None, :, None, None, :], gate, -jnp.inf)
        idx = lax.top_k(gate, ksel)[1]
        valid = idx < own[None, :, None, None, None]
        xs = xs + (chunked(idx), chunked(valid))
        kbh = jnp.moveaxis(kb, 3, 1)
        vbh = jnp.moveaxis(vb, 3, 1)
        bi = jnp.arange(b)[:, None, None, None, None]
        hi = jnp.arange(ATT_KV_HEADS)[None, None, :, None, None]

    def one_chunk(args):
        ci, qi = args[0], args[1]
        start = ci * Q_CHUNK
        blk_start = (start // MOBA_BLOCK) * MOBA_BLOCK
        k_own = lax.dynamic_slice_in_dim(kp, blk_start, MOBA_BLOCK, axis=1)
        v_own = lax.dynamic_slice_in_dim(vp, blk_start, MOBA_BLOCK, axis=1)
        qpos = start + jnp.arange(Q_CHUNK)
        kpos = blk_start + jnp.arange(MOBA_BLOCK)
        own_mask = (kpos[None, :] <= qpos[:, None])[None, :, None, None, :]
        if ksel == 0:
            return moba_attend(qi, k_own, v_own, own_mask, None, None, None)
        ii, vi = args[2], args[3]
        return moba_attend(qi, k_own, v_own, own_mask, kbh[bi, hi, ii], vbh[bi, hi, ii], vi[..., None])

    o = lax.map(one_chunk, xs)
    return jnp.moveaxis(o, 0, 1).reshape(b, s, ATT_WIDTH)


def moba_sample(q, k_new, v_new, k_pool, v_pool, page_table):
    f32 = jnp.float32
    db, t = q.shape[:2]
    n_pages = page_table.shape[1]
    ppb = MOBA_BLOCK // PAGE_SIZE
    ob = (n_pages * PAGE_SIZE) // MOBA_BLOCK
    n_own_pages = n_pages - ob * ppb
    own_pages = page_table[:, ob * ppb:ob * ppb + n_own_pages]
    n_c = n_own_pages * PAGE_SIZE
    k_own = jnp.concatenate([k_pool[own_pages].reshape(db, n_c, ATT_KV_HEADS, HEAD_DIM).astype(k_new.dtype), k_new], axis=1)
    v_own = jnp.concatenate([v_pool[own_pages].reshape(db, n_c, ATT_KV_HEADS, HEAD_DIM).astype(v_new.dtype), v_new], axis=1)
    own_mask = jnp.concatenate([jnp.ones((t, n_c), bool), jnp.tril(jnp.ones((t, t), bool))], axis=1)
    own_mask = own_mask[None, :, None, None, :]
    ksel = min(MOBA_TOP_K, ob)
    if ksel == 0:
        o = moba_attend(q, k_own, v_own, own_mask, None, None, None)
        return o.reshape(db, t, ATT_WIDTH)
    past_pages = page_table[:, :ob * ppb]
    kmean = jnp.mean(k_pool[past_pages].astype(f32).reshape(db, ob, MOBA_BLOCK, ATT_KV_HEADS, HEAD_DIM), axis=2)
    gate = jnp.einsum('bthgd,bnhd->bthgn', q.astype(f32), kmean)
    idx = lax.top_k(gate, ksel)[1]
    bi = jnp.arange(db)[:, None, None, None, None, None]
    phys = page_table[bi, idx[..., None] * ppb + jnp.arange(ppb)]
    hi = jnp.arange(ATT_KV_HEADS)[None, None, :, None, None, None, None]
    rows = jnp.arange(PAGE_SIZE)
    sel_shape = (db, t, ATT_KV_HEADS, ATT_GROUP, ksel, MOBA_BLOCK, HEAD_DIM)
    k_sel = k_pool[phys[..., None], rows, hi].reshape(sel_shape).astype(k_new.dtype)
    v_sel = v_pool[phys[..., None], rows, hi].reshape(sel_shape).astype(v_new.dtype)
    o = moba_attend(q, k_own, v_own, own_mask, k_sel, v_sel, True)
    return o.reshape(db, t, ATT_WIDTH)


def setup_inputs(seed: int = 0) -> dict:
    key = jax.random.key(seed)
    ks = iter(jax.random.split(key, 40))
    f32 = jnp.float32
    nrm = lambda shape, s: s * jax.random.normal(next(ks), shape, f32)
    nm, na = N_MAMBA_LAYERS, N_ATT_LAYERS
    n_pages = PAST_LEN // PAGE_SIZE
    n_used = DEC_BATCH * n_pages
    n_pool = n_used + (n_used + 3) // 4
    page_table = jax.random.permutation(next(ks), n_pool)[:n_used].reshape(DEC_BATCH, n_pages).astype(jnp.int32)
    dt0 = jnp.exp(jax.random.uniform(next(ks), (nm, SSD_HEADS), f32, math.log(1e-3), math.log(1e-1)))
    dt_bias = dt0 + jnp.log(-jnp.expm1(-dt0))
    a_log = jnp.log(jax.random.uniform(next(ks), (nm, SSD_HEADS), f32, 1.0, 16.0))
    return {
        'x_prompt': nrm((BATCH, SEQ, D_MODEL), 1.0),
        'x_sample': nrm((DEC_BATCH, DEC_SEQ, D_MODEL), 1.0),
        'state_ssm': nrm((nm, DEC_BATCH, SSD_HEADS, SSD_HEAD_DIM, SSD_STATE), 0.1),
        'state_ssd_conv': nrm((nm, DEC_BATCH, SSD_CONV - 1, SSD_CONV_DIM), 1.0),
        'state_conf_conv': nrm((nm, DEC_BATCH, CONF_CONV_WIDTH - 1, CONF_WIDTH), 0.5),
        'cache_k': nrm((na, n_pool, PAGE_SIZE, ATT_KV_HEADS, HEAD_DIM), 1.0),
        'cache_v': nrm((na, n_pool, PAGE_SIZE, ATT_KV_HEADS, HEAD_DIM), 1.0),
        'page_table': page_table,
        'norm0_g': 1.0 + nrm((nm, D_MODEL), 0.05),
        'w_in0': nrm((nm, D_MODEL, IN0_COLS), D_MODEL ** -0.5),
        'ssd_conv_w': nrm((nm, SSD_CONV, SSD_CONV_DIM), SSD_CONV ** -0.5),
        'ssd_conv_b': nrm((nm, SSD_CONV_DIM), 0.02),
        'ssd_dt_bias': dt_bias,
        'ssd_a_log': a_log,
        'ssd_d': 1.0 + nrm((nm, SSD_HEADS), 0.1),
        'ssd_norm_g': 1.0 + nrm((nm, SSD_WIDTH), 0.05),
        'conf_conv_w': nrm((nm, CONF_CONV_WIDTH, CONF_WIDTH), CONF_CONV_WIDTH ** -0.5),
        'conf_conv_b': nrm((nm, CONF_WIDTH), 0.02),
        'conf_ln_g': 1.0 + nrm((nm, CONF_WIDTH), 0.05),
        'conf_ln_b': nrm((nm, CONF_WIDTH), 0.02),
        'w_out0': nrm((nm, MIX0_WIDTH, D_MODEL), 0.5 * MIX0_WIDTH ** -0.5),
        'norm1_g': 1.0 + nrm((na, D_MODEL), 0.05),
        'w_in1': nrm((na, D_MODEL, IN1_COLS), D_MODEL ** -0.5),
        'q_norm_g': 1.0 + nrm((na, HEAD_DIM), 0.05),
        'k_norm_g': 1.0 + nrm((na, HEAD_DIM), 0.05),
        'w_out1': nrm((na, ATT_WIDTH, D_MODEL), 0.5 * ATT_WIDTH ** -0.5),
    }


def reference(x_prompt, x_sample, state_ssm, state_ssd_conv, state_conf_conv, cache_k, cache_v, page_table,
              norm0_g, w_in0, ssd_conv_w, ssd_conv_b, ssd_dt_bias, ssd_a_log, ssd_d, ssd_norm_g,
              conf_conv_w, conf_conv_b, conf_ln_g, conf_ln_b, w_out0,
              norm1_g, w_in1, q_norm_g, k_norm_g, w_out1):
    yp, ys = x_prompt, x_sample
    ssm_p, ssm_s, sconv_p, sconv_s, cconv_p, cconv_s = [], [], [], [], [], []
    k_p, v_p, k_s, v_s = [], [], [], []
    for layer in range(DEPTH):
        i = layer // 2
        if layer % 2 == 0:
            w = (norm0_g[i], w_in0[i], ssd_conv_w[i], ssd_conv_b[i], ssd_dt_bias[i], ssd_a_log[i], ssd_d[i],
                 ssd_norm_g[i], conf_conv_w[i], conf_conv_b[i], conf_ln_g[i], conf_ln_b[i], w_out0[i])
            bp = yp.shape[0]
            out_p, h_pn, sc_pn, cc_pn = hybrid_layer(
                yp, jnp.zeros((bp, SSD_HEADS, SSD_HEAD_DIM, SSD_STATE), jnp.float32),
                jnp.zeros((bp, SSD_CONV - 1, SSD_CONV_DIM), yp.dtype),
                jnp.zeros((bp, CONF_CONV_WIDTH - 1, CONF_WIDTH), yp.dtype), *w)
            out_s, h_sn, sc_sn, cc_sn = hybrid_layer(ys, state_ssm[i], state_ssd_conv[i], state_conf_conv[i], *w)
            yp = yp + out_p
            ys = ys + out_s
            ssm_p.append(h_pn)
            ssm_s.append(h_sn)
            sconv_p.append(sc_pn)
            sconv_s.append(sc_sn)
            cconv_p.append(cc_pn)
            cconv_s.append(cc_sn)
        else:
            qp, kpn, vpn, gp = att_project(yp, norm1_g[i], w_in1[i], q_norm_g[i], k_norm_g[i])
            yp = yp + att_output(moba_prompt(qp, kpn, vpn), gp, w_out1[i])
            qs, ksn, vsn, gs = att_project(ys, norm1_g[i], w_in1[i], q_norm_g[i], k_norm_g[i])
            ys = ys + att_output(moba_sample(qs, ksn, vsn, cache_k[i], cache_v[i], page_table), gs, w_out1[i])
            k_p.append(kpn)
            v_p.append(vpn)
            k_s.append(ksn)
            v_s.append(vsn)
    return (yp, ys, jnp.stack(ssm_p), jnp.stack(ssm_s), jnp.stack(sconv_p), jnp.stack(sconv_s),
            jnp.stack(cconv_p), jnp.stack(cconv_s), jnp.stack(k_p), jnp.stack(v_p), jnp.stack(k_s), jnp.stack(v_s))
```

```python
import numpy as np
from contextlib import ExitStack
import concourse.bass as bass
import concourse.mybir as mybir
from concourse.bass_utils import run_bass_kernel_spmd

F32 = mybir.dt.float32
BF16 = mybir.dt.bfloat16
I32 = mybir.dt.int32
ALU = mybir.AluOpType
AF = mybir.ActivationFunctionType
AX = mybir.AxisListType

NCORE = 8
D = 1024
HP = 32
EPS = 1e-6
NEG = -30000.0


class Sched:
    def __init__(self, nc, stack, n_dma_sems=24):
        self.nc = nc
        self.engs = {"pe": nc.tensor, "act": nc.scalar, "dve": nc.vector, "pool": nc.gpsimd, "sp": nc.sync}
        self.sem = {k: stack.enter_context(nc.semaphore("s_" + k)) for k in ("pe", "act", "dve", "pool")}
        self.cnt = {k: 0 for k in self.sem}
        self.dsem = [stack.enter_context(nc.semaphore("d%d" % i)) for i in range(n_dma_sems)]
        self.dval = [0] * n_dma_sems
        self.dnext = 0
        self.waited = {}
        self.lastw = {}
        self.reads = {}
        self.n_instr = 0

    def _semobj(self, key):
        return self.sem[key] if isinstance(key, str) else self.dsem[key[1]]

    def _wait(self, eng, key, val):
        if self.waited.get((eng, key), 0) >= val:
            return
        self.waited[(eng, key)] = val
        self.engs[eng].wait_ge(self._semobj(key), val)

    def _deps(self, eng, reads, writes):
        for b in reads:
            t = self.lastw.get(b)
            if t is not None:
                self._wait(eng, t[0], t[1])
        for b in writes:
            t = self.lastw.get(b)
            if t is not None:
                self._wait(eng, t[0], t[1])
            for k, v in self.reads.get(b, {}).items():
                if k != eng:
                    self._wait(eng, k, v)

    def _record(self, key, val, reads, writes):
        for b in reads:
            d = self.reads.setdefault(b, {})
            if d.get(key, 0) < val:
                d[key] = val
        for b in writes:
            self.lastw[b] = (key, val)
            self.reads[b] = {}

    def op(self, eng, fn, reads=(), writes=()):
        self._deps(eng, reads, writes)
        ins = fn(self.engs[eng])
        self.cnt[eng] += 1
        ins.then_inc(self.sem[eng], 1)
        self._record(eng, self.cnt[eng], reads, writes)
        self.n_instr += 1

    def mm(self, fns, reads=(), writes=()):
        self._deps("pe", reads, writes)
        ins = None
        for fn in fns:
            ins = fn(self.nc.tensor)
            self.n_instr += 1
        self.cnt["pe"] += 1
        ins.then_inc(self.sem["pe"], 1)
        self._record("pe", self.cnt["pe"], reads, writes)

    def dma(self, out, in_, reads=(), writes=(), q="sp", **kw):
        i = self.dnext
        self.dnext = (self.dnext + 1) % len(self.dsem)
        key = ("d", i)
        if self.dval[i]:
            self._wait(q, key, self.dval[i])
        self._deps(q, reads, writes)
        self.dval[i] += 16
        ins = self.engs[q].dma_start(out=out, in_=in_, **kw)
        ins.then_inc(self.dsem[i], 16)
        self._record(key, self.dval[i], reads, writes)
        self.n_instr += 1

    def idma(self, out, in_, idx_ap, reads=(), writes=()):
        q = "pool"
        i = self.dnext
        self.dnext = (self.dnext + 1) % len(self.dsem)
        key = ("d", i)
        if self.dval[i]:
            self._wait(q, key, self.dval[i])
        self._deps(q, reads, writes)
        self.dval[i] += 16
        ins = self.nc.gpsimd.indirect_dma_start(out=out, out_offset=None, in_=in_,
                                                in_offset=bass.IndirectOffsetOnAxis(ap=idx_ap, axis=0))
        ins.then_inc(self.dsem[i], 16)
        self._record(key, self.dval[i], reads, writes)
        self.n_instr += 1

    def barrier(self):
        for e in ("pe", "act", "dve", "pool", "sp"):
            for i, v in enumerate(self.dval):
                if v:
                    self._wait(e, ("d", i), v)
            for k, v in self.cnt.items():
                if v and k != e:
                    self._wait(e, k, v)

    def finish(self, eng="sp"):
        for i, v in enumerate(self.dval):
            if v:
                self._wait(eng, ("d", i), v)
        for k, v in self.cnt.items():
            if v:
                self._wait(eng, k, v)


class Cfg:
    def __init__(self, seq, dec_batch, past_len):
        self.seq = seq
        self.tpc = seq // NCORE
        self.nch = self.tpc // 128
        self.dbt = dec_batch
        self.db = dec_batch // NCORE
        self.npg = past_len // 128
        n_used = dec_batch * self.npg
        self.npool = n_used + (n_used + 3) // 4
        self.nblk_p = seq // 256
        self.bpc = self.tpc // 256


def build(cfg, debug_l0=False):
    nc = bass.Bass("TRN2", target_bir_lowering=False)
    TPC, NCH, DB, NPG = cfg.tpc, cfg.nch, cfg.db, cfg.npg
    SEQ = cfg.seq
    NCHA = SEQ // 128

    def din(name, shape, dt=F32):
        return nc.dram_tensor(name, list(shape), dt, kind="ExternalInput").ap()

    def dout(name, shape, dt=F32):
        return nc.dram_tensor(name, list(shape), dt, kind="ExternalOutput").ap()

    xp = din("xp", [SEQ, D])
    xsm = din("xsm", [DB * 4, D])
    st_ssm = din("st_ssm", [DB, 1024, 128])
    st_sc = din("st_sc", [DB, 3, 2048])
    st_cc = din("st_cc", [DB, 30, 1024])
    w_in0 = din("w_in0", [128, 8, 6160])
    w_out0 = din("w_out0", [128, 16, 1024])
    g0_d = din("g0", [128, 8])
    cw_d = din("cw", [128, 16, 4])
    cb_d = din("cb", [128, 16])
    ccw_d = din("ccw", [128, 8, 31])
    ccb_d = din("ccb", [128, 8])
    lng_d = din("lng", [128, 8])
    lnb_d = din("lnb", [128, 8])
    dtb_d = din("dtb", [128, 16])
    alog_d = din("alog", [128, 16])
    dsk_d = din("dsk", [128, 16])
    sng_d = din("sng", [128, 8])
    cmask_d = din("cmask", [128, 8])
    NSLOT_ = TPC // 128
    wq_d = din("wq", [128, 8, 1024])
    wkv_d = din("wkv", [128, 8, 512])
    wg_d = din("wg", [128, 8, 1024])
    wo_d = din("wo", [64, 16, 1024])
    g1_d = din("g1", [128, 8])
    qg_d = din("qg", [128, 1])
    kgb_d = din("kgb", [128, 256])
    pm_d = din("pm", [128, NSLOT_ + 1, 64])
    pneg_d = din("pneg", [128, NSLOT_ + 1, 64])
    om_d = din("om", [128, NSLOT_ + 1, 64])
    maskM_d = din("maskM", [128, 8, 128])
    maskS_d = din("maskS", [4, 16])
    oidx_d = din("oidx", [128, NSLOT_], I32)
    pidx_d = din("pidx", [128, 1])
    ptb_d = din("ptb", [128, DB, NPG], I32)
    ck_d = din("ck", [cfg.npool * 128, 256])
    cv_d = din("cv", [cfg.npool * 128, 256])
    k_p = dout("k_p", [SEQ, 256])
    v_p = dout("v_p", [SEQ, 256])
    k_s = dout("k_s", [DB * 4, 256])
    v_s = dout("v_s", [DB * 4, 256])

    y_p = dout("y_p", [TPC, D])
    y_s = dout("y_s", [DB * 4, D])
    ssm_p = dout("ssm_p", [1024, 128])
    ssm_s = dout("ssm_s", [DB, 1024, 128])
    sc_p = dout("sc_p", [3, 2048])
    sc_s = dout("sc_s", [DB, 3, 2048])
    cc_p = dout("cc_p", [30, 1024])
    cc_s = dout("cc_s", [DB, 30, 1024])

    x1_d = nc.dram_tensor("x1_d", [SEQ + DB * 4, D], F32, kind="Internal").ap()
    KT_d = nc.dram_tensor("KT_d", [2, 128, SEQ + DB * 4], BF16, kind="Internal").ap()
    V_d = nc.dram_tensor("V_d", [SEQ + DB * 4, 260], BF16, kind="Internal").ap()
    KTs_d = nc.dram_tensor("KTs_d", [DB, 2, 128, NPG * 128], BF16, kind="Internal").ap()
    Vs_d = nc.dram_tensor("Vs_d", [DB, NPG * 128, 260], BF16, kind="Internal").ap()

    st = ExitStack()
    with st:
        S = Sched(nc, st)
        A = lambda eng, fn, r=(), w=(): S.op(eng, fn, reads=r, writes=w)

        cur = [st]

        def T(name, shape, dt=F32):
            return cur[0].enter_context(nc.sbuf_tensor(name, list(shape), dt))

        pst = [st.enter_context(nc.psum_tensor("ps%d" % i, [128, 512], F32)) for i in range(8)]
        psi = [0]

        psn = [8]

        def nextps(n=None):
            n = n or psn[0]
            i = psi[0] % n
            psi[0] = (i + 1) % n
            return pst[i], "ps%d" % i

        identf = T("identf", [128, 128])
        triU = T("triU", [128, 128])
        SU = T("SU", [128, 128])
        onesf = T("onesf", [128, 128])
        epsT = T("epsT", [128, 1])
        oneT = T("oneT", [128, 1])
        for t_, cmp_, sgn in ((identf, ALU.is_equal, 1), (triU, ALU.is_ge, -1)):
            A("pool", lambda e, t_=t_: e.memset(t_[:], 1.0), w=[t_.name])
            A("pool", lambda e, t_=t_, cmp_=cmp_, sgn=sgn: e.affine_select(out=t_[:], in_=t_[:], pattern=[[-sgn, 128]], compare_op=cmp_,
                                                          fill=0.0, base=0, channel_multiplier=sgn), r=[t_.name], w=[t_.name])
        A("dve", lambda e: e.tensor_scalar(out=SU[:], in0=triU[:], scalar1=-1.0, scalar2=1.0, op0=ALU.mult, op1=ALU.add), r=["triU"], w=["SU"])
        A("pool", lambda e: e.memset(onesf[:], 1.0), w=["onesf"])
        A("pool", lambda e: e.memset(epsT[:], EPS), w=["epsT"])
        A("pool", lambda e: e.memset(oneT[:], 1.0), w=["oneT"])

        BD = T("BD", [128, 128])
        A("pool", lambda e: e.memset(BD[:], 0.0), w=["BD"])
        A("pool", lambda e: e.memset(BD[0:64, 0:64], 1.0), r=["BD"], w=["BD"])
        A("pool", lambda e: e.memset(BD[64:128, 64:128], 1.0), r=["BD"], w=["BD"])
        st0 = ExitStack()
        cur[0] = st0
        def ld(name, src, shape):
            t = T(name, shape)
            S.dma(t[:], src, writes=[name])
            return t
        g0 = ld("g0t", g0_d[:, :], [128, 8])
        cw = ld("cwt", cw_d[:, :, :], [128, 16, 4])
        cb = ld("cbt", cb_d[:, :], [128, 16])
        ccw = ld("ccwt", ccw_d[:, :, :], [128, 8, 31])
        ccb = ld("ccbt", ccb_d[:, :], [128, 8])
        lng = ld("lngt", lng_d[:, :], [128, 8])
        lnb = ld("lnbt", lnb_d[:, :], [128, 8])
        dtb = ld("dtbt", dtb_d[:, :], [128, 16])
        Ab = ld("Abt", alog_d[:, :], [128, 16])
        dsk = ld("dskt", dsk_d[:, :], [128, 16])
        sng = ld("sngt", sng_d[:, :], [128, 8])
        cmask = ld("cmaskt", cmask_d[:, :], [128, 8])
        A("act", lambda e: e.activation(out=Ab[:], in_=Ab[:], func=AF.Exp), r=["Abt"], w=["Abt"])
        A("dve", lambda e: e.tensor_scalar(out=Ab[:], in0=Ab[:], scalar1=-1.0, scalar2=None, op0=ALU.mult), r=["Abt"], w=["Abt"])

        Win = T("Win", [128, 8, 6160], BF16)
        Wout = T("Wout", [128, 16, 1024], BF16)
        xbc_c = T("xbc_c", [128, 16, 128])
        xbc_flat = xbc_c[:, :, :].rearrange("p a b -> p (a b)")
        XBCC = ["xbcc%d" % t for t in range(16)]
        stg = [xbc_flat[:, 0:770], xbc_flat[:, 1024:1024 + 770]]
        stgn = [XBCC[:8], XBCC[8:]]
        si = 0
        cast_engs = ["dve", "pool"]
        for kt in range(8):
            for q8 in range(8):
                sb = stg[si % 2]
                S.dma(sb, w_in0[:, kt, q8 * 770:(q8 + 1) * 770], writes=stgn[si % 2])
                A(cast_engs[si % 2], lambda e, sb=sb, kt=kt, q8=q8: e.tensor_scalar(
                    out=Win[:, kt, q8 * 770:(q8 + 1) * 770], in0=sb, scalar1=g0[:, kt:kt + 1], scalar2=None, op0=ALU.mult),
                    r=stgn[si % 2] + ["g0t"], w=["Win"])
                si += 1
        for t_ in range(16):
            for hf in range(2):
                sb = stg[si % 2]
                S.dma(sb[:, :512], w_out0[:, t_, hf * 512:(hf + 1) * 512], writes=stgn[si % 2])
                A(cast_engs[si % 2], lambda e, sb=sb, t_=t_, hf=hf: e.tensor_copy(out=Wout[:, t_, hf * 512:(hf + 1) * 512], in_=sb[:, :512]), r=stgn[si % 2], w=["Wout"])
                si += 1

        xt = T("xt", [128, D])
        xn = T("xn", [128, D])
        ss = T("ss", [128, 8])
        xnT = T("xnT", [128, 8, 128], BF16)
        xbc_f = T("xbc_f", [128, 16, HP + 128])
        gl_f = T("gl_f", [128, 8, HP + 128])
        scg = T("scg", [128, 8, 128], BF16)
        c_f = T("c_f", [128, 8, 128])
        cat_f = T("cat_f", [128, 16, 128], BF16)
        CTb = T("CTb", [128, 4, 128], BF16)
        BTb = T("BTb", [128, 4, 128], BF16)
        Btm = T("Btm", [128, 512], BF16)
        dtt = T("dtt", [128, 8, 16])
        aSU4 = [T("aSU0", [128, 4, 128])] * 2
        dec4 = [T("dec0", [128, 4, 128])] * 2
        cbm = T("cbm", [128, 4, 128])
        MT = T("MT", [128, 16, 128], BF16)
        xdt = T("xdt", [128, 1024], BF16)
        xdte = T("xdte", [128, 1024], BF16)
        yacc = T("yacc", [128, 1024])
        ytmp = T("ytmp", [128, 1024])
        H = T("H", [128, 1024])
        Hb = T("Hb", [128, 1024], BF16)
        ptmp = T("ptmp", [128, 128])
        cdb = T("cdb", [128, 16])
        Atot = T("Atot", [128, 16])
        hist_tm = xbc_flat[:32, :]
        hout = xbc_flat[:32, :]
        sz = xn
        csq = ytmp[:, :].rearrange("p (a b) -> p a b", b=128)
        sig = aSU4[0]
        lnst = dec4[0]

        def fm_inproj(L, col0s, evac):
            ps, pn = nextps()
            fns = []
            for j, c0 in enumerate(col0s):
                for kt in range(8):
                    fns.append(lambda e, j=j, c0=c0, kt=kt: e.matmul(out=ps[:, j * 128:j * 128 + L], lhsT=Win[:, kt, c0:c0 + 128],
                                                                   rhs=xnT[:, kt, :L], start=(kt == 0), stop=(kt == 7)))
            S.mm(fns, reads=["Win", "xnT"], writes=[pn])
            v = ps[:, :].rearrange("p (a b) -> p a b", b=128)[:, :len(col0s), :L]
            evac(v, pn)

        def transposes_to_tm(L, srcs, src_names, nm):
            ps, pn = nextps()
            S.mm([lambda e, j=j, s=s: e.transpose(out=ps[:L, j * 128:(j + 1) * 128], in_=s, identity=identf[:, :])
                  for j, s in enumerate(srcs)], reads=list(src_names) + ["identf"], writes=[pn])
            return ps[:L, :len(srcs) * 128], pn

        def conv_taps(eng, out_ap, src_tile, t, ntap, w_t, b_t, L, rnames, wname):
            o0 = HP - (ntap - 1)
            A(eng, lambda e: e.tensor_scalar(out=out_ap, in0=src_tile[:, t, o0:o0 + L], scalar1=w_t[:, t, 0:1], scalar2=b_t[:, t:t + 1],
                                             op0=ALU.mult, op1=ALU.add), r=rnames, w=[wname])
            for j in range(1, ntap):
                if eng == "dve":
                    A(eng, lambda e, j=j: e.scalar_tensor_tensor(out=out_ap, in0=src_tile[:, t, o0 + j:o0 + j + L], scalar=w_t[:, t, j:j + 1],
                                                                in1=out_ap, op0=ALU.mult, op1=ALU.add), r=rnames + [wname], w=[wname])
                else:
                    A(eng, lambda e, j=j: e.tensor_scalar(out=ptmp[:, :L], in0=src_tile[:, t, o0 + j:o0 + j + L], scalar1=w_t[:, t, j:j + 1], scalar2=None,
                                                         op0=ALU.mult), r=rnames, w=["ptmp"])
                    A(eng, lambda e: e.tensor_tensor(out=out_ap, in0=out_ap, in1=ptmp[:, :L], op=ALU.add), r=["ptmp", wname], w=[wname])

        def l0_chunk(src_ap, L, mode, dst_x1=None):
            full = mode == "full"
            nx = 16 if (full or mode == "halo2") else 12
            S.dma(xt[:L, :], src_ap, writes=["xt"])
            A("act", lambda e: e.activation(out=xn[:L, :], in_=xt[:L, :], func=AF.Square, scale=1.0 / 32, accum_out=ss[:L, 0:1]),
              r=["xt"], w=["xn", "ss"])
            A("act", lambda e: e.activation(out=ss[:L, 1:2], in_=ss[:L, 0:1], func=AF.Ln, bias=epsT[:L, 0:1]), r=["ss", "epsT"], w=["ss"])
            A("act", lambda e: e.activation(out=ss[:L, 2:3], in_=ss[:L, 1:2], func=AF.Exp, scale=-0.5), r=["ss"], w=["ss"])
            A("dve", lambda e: e.tensor_scalar(out=xn[:L, :], in0=xt[:L, :], scalar1=ss[:L, 2:3], scalar2=None, op0=ALU.mult),
              r=["xt", "ss"], w=["xn"])
            for half in range(2):
                ps, pn = nextps()
                S.mm([lambda e, j=j: e.transpose(out=ps[:, j * 128:j * 128 + L], in_=xn[:L, (half * 4 + j) * 128:(half * 4 + j + 1) * 128],
                                                 identity=identf[:L, :L]) for j in range(4)], reads=["xn", "identf"], writes=[pn])
                v = ps[:, :].rearrange("p (a b) -> p a b", b=128)[:, :, :L]
                A("act", lambda e, v=v, half=half: e.activation(out=xnT[:, half * 4:half * 4 + 4, :L], in_=v, func=AF.Copy), r=[pn], w=["xnT"])
            for g in range(nx // 4):
                def ev(v, pn, g=g):
                    A("act", lambda e: e.activation(out=xbc_f[:, 4 * g:4 * g + 4, HP:HP + L], in_=v, func=AF.Copy), r=[pn],
                      w=["xbcf%d" % t for t in range(4 * g, 4 * g + 4)])
                fm_inproj(L, [1024 + 128 * t for t in range(4 * g, 4 * g + 4)], ev)
            if full or mode == "halo2":
                for g in range(2):
                    def evb(v, pn):
                        A("act", lambda e: e.activation(out=sig[:, :, :L], in_=v, func=AF.Sigmoid), r=[pn], w=["aSU0"])
                    fm_inproj(L, [4112 + 128 * t for t in range(4 * g, 4 * g + 4)], evb)

                    def eva(v, pn, g=g):
                        A("dve", lambda e: e.tensor_tensor(out=gl_f[:, 4 * g:4 * g + 4, HP:HP + L], in0=v, in1=sig[:, :, :L], op=ALU.mult),
                          r=[pn, "aSU0"], w=["glf%d" % t for t in range(4 * g, 4 * g + 4)])
                    fm_inproj(L, [3088 + 128 * t for t in range(4 * g, 4 * g + 4)], eva)
            if mode in ("halo1", "halo2"):
                for t in range(nx):
                    A("pool", lambda e, t=t: e.tensor_copy(out=xbc_f[:, t, 0:HP], in_=xbc_f[:, t, HP:2 * HP]), r=["xbcf%d" % t], w=["xbcf%d" % t])
                if mode == "halo2":
                    for t in range(8):
                        A("pool", lambda e, t=t: e.tensor_copy(out=gl_f[:, t, 0:HP], in_=gl_f[:, t, HP:2 * HP]), r=["glf%d" % t], w=["glf%d" % t])
                return
            if full:
                for g in range(2):
                    def evc(v, pn, g=g):
                        A("act", lambda e: e.activation(out=scg[:, 4 * g:4 * g + 4, :L], in_=v, func=AF.Silu), r=[pn], w=["scg"])
                    fm_inproj(L, [5136 + 128 * t for t in range(4 * g, 4 * g + 4)], evc)
            for t in range(nx):
                conv_taps("dve", xbc_c[:, t, :L], xbc_f, t, 4, cw, cb, L, ["xbcf%d" % t, "cwt", "cbt"], "xbcc%d" % t)
            for g in range(nx // 4):
                A("act", lambda e, g=g: e.activation(out=xbc_c[:, 4 * g:4 * g + 4, :L], in_=xbc_c[:, 4 * g:4 * g + 4, :L], func=AF.Silu),
                  r=["xbcc%d" % t for t in range(4 * g, 4 * g + 4)], w=["xbcc%d" % t for t in range(4 * g, 4 * g + 4)])
            if L >= HP:
                for t in range(nx):
                    A("pool", lambda e, t=t: e.tensor_copy(out=xbc_f[:, t, 0:HP], in_=xbc_f[:, t, L:L + HP]), r=["xbcf%d" % t], w=["xbcf%d" % t])
            ps, pn = nextps()
            S.mm([lambda e, kt=kt: e.matmul(out=ps[:L, 0:16], lhsT=xnT[:, kt, :L], rhs=Win[:, kt, 3072:3088], start=(kt == 0), stop=(kt == 7))
                  for kt in range(8)], reads=["xnT", "Win"], writes=[pn])
            dtr, dta, dte, dtl, dtv, av, acs, eacs = [dtt[:L, i, :] for i in range(8)]
            A("dve", lambda e: e.tensor_tensor(out=dtr, in0=ps[:L, 0:16], in1=dtb[:L, :], op=ALU.add), r=[pn, "dtbt"], w=["dtt"])
            A("dve", lambda e: e.scalar_tensor_tensor(out=dta, in0=dtr, scalar=-1.0, in1=dtr, op0=ALU.mult, op1=ALU.min), r=["dtt"], w=["dtt"])
            A("act", lambda e: e.activation(out=dte, in_=dta, func=AF.Exp), r=["dtt"], w=["dtt"])
            A("act", lambda e: e.activation(out=dtl, in_=dte, func=AF.Ln, bias=oneT[:L, 0:1]), r=["dtt", "oneT"], w=["dtt"])
            A("dve", lambda e: e.scalar_tensor_tensor(out=dtv, in0=dtr, scalar=0.0, in1=dtl, op0=ALU.max, op1=ALU.add), r=["dtt"], w=["dtt"])
            A("dve", lambda e: e.tensor_tensor(out=av, in0=dtv, in1=Ab[:L, :], op=ALU.mult), r=["dtt", "Abt"], w=["dtt"])
            ps2, pn2 = nextps()
            S.mm([lambda e: e.matmul(out=ps2[:L, 0:16], lhsT=triU[:L, :L], rhs=av, start=True, stop=True),
                  lambda e: e.matmul(out=ps2[:, 16:32], lhsT=onesf[:L, :], rhs=av, start=True, stop=True)],
                 reads=["triU", "onesf", "dtt"], writes=[pn2])
            A("dve", lambda e: e.tensor_copy(out=acs, in_=ps2[:L, 0:16]), r=[pn2], w=["dtt"])
            A("act", lambda e: e.activation(out=cdb[:, :], in_=ps2[:, 16:32], func=AF.Exp), r=[pn2], w=["cdb"])
            A("dve", lambda e: e.tensor_tensor(out=Atot[:, :], in0=Atot[:, :], in1=ps2[:, 16:32], op=ALU.add), r=[pn2, "Atot"], w=["Atot"])
            A("dve", lambda e: e.tensor_tensor(out=dta, in0=ps2[:L, 16:32], in1=acs, op=ALU.subtract), r=[pn2, "dtt"], w=["dtt"])
            A("act", lambda e: e.activation(out=dta, in_=dta, func=AF.Exp), r=["dtt"], w=["dtt"])
            A("act", lambda e: e.activation(out=eacs, in_=acs, func=AF.Exp), r=["dtt"], w=["dtt"])
            for g in range(2):
                v, pn = transposes_to_tm(L, [xbc_c[:, 4 * g + j, :L] for j in range(4)], ["xbcc%d" % (4 * g + j) for j in range(4)], "xs")
                v3 = v.rearrange("p (h q) -> p h q", q=64)
                sl = slice(512 * g, 512 * (g + 1))
                A("dve", lambda e, v3=v3, sl=sl, g=g: e.tensor_tensor(out=xdt[:L, sl].rearrange("p (h q) -> p h q", q=64), in0=v3,
                                                                      in1=dtv[:, 8 * g:8 * g + 8].unsqueeze(2).broadcast_to([L, 8, 64]), op=ALU.mult),
                  r=[pn, "dtt"], w=["xdt"])
                if full:
                    A("dve", lambda e, v=v, sl=sl, g=g: e.tensor_tensor(out=yacc[:L, sl].rearrange("p (h q) -> p h q", q=64), in0=v.rearrange("p (h q) -> p h q", q=64), in1=dsk[:L, 8 * g:8 * g + 8].unsqueeze(2).broadcast_to([L, 8, 64]), op=ALU.mult),
                      r=[pn, "dskt"], w=["yacc"])
            v, pn = transposes_to_tm(L, [xbc_c[:, 8 + j, :L] for j in range(4)], ["xbcc%d" % (8 + j) for j in range(4)], "B")
            A("act", lambda e: e.activation(out=Btm[:L, :], in_=v, func=AF.Copy), r=[pn], w=["Btm"])
            A("dve", lambda e: e.tensor_tensor(out=xdte[:L, :].rearrange("p (h q) -> p h q", q=64), in0=xdt[:L, :].rearrange("p (h q) -> p h q", q=64),
                                               in1=dta.unsqueeze(2).broadcast_to([L, 16, 64]), op=ALU.mult), r=["xdt", "dtt"], w=["xdte"])
            if full:
                A("pool", lambda e: e.tensor_copy(out=BTb[:, :, :L], in_=xbc_c[:, 8:12, :L]), r=["xbcc%d" % t for t in range(8, 12)], w=["BTb"])
                A("pool", lambda e: e.tensor_copy(out=CTb[:, :, :L], in_=xbc_c[:, 12:16, :L]), r=["xbcc%d" % t for t in range(12, 16)], w=["CTb"])
                psc, pnc = nextps()
                S.mm([lambda e, g=g: e.matmul(out=psc[:L, g * 128:g * 128 + L], lhsT=BTb[:, g, :L], rhs=CTb[:, g, :L], start=True, stop=True)
                      for g in range(4)], reads=["BTb", "CTb"], writes=[pnc])
                A("dve", lambda e: e.tensor_tensor(out=cbm[:L, :, :L], in0=psc[:L, :].rearrange("p (a b) -> p a b", b=128)[:, :, :L],
                                                   in1=triU[:L, :L].unsqueeze(1).broadcast_to([L, 4, L]), op=ALU.mult), r=[pnc, "triU"], w=["cbm"])
                for q4 in range(4):
                    aS, dc = aSU4[q4 % 2], dec4[q4 % 2]
                    A("pool", lambda e, aS=aS, q4=q4: e.tensor_tensor(out=aS[:L, :, :L], in0=SU[:L, :L].unsqueeze(1).broadcast_to([L, 4, L]),
                                                                     in1=av[:, 4 * q4:4 * q4 + 4].unsqueeze(2).broadcast_to([L, 4, L]), op=ALU.mult),
                      r=["SU", "dtt"], w=[aS.name])
                    ps, pn = nextps()
                    S.mm([lambda e, j=j, aS=aS: e.matmul(out=ps[:L, j * 128:j * 128 + L], lhsT=aS[:L, j, :L], rhs=triU[:L, :L], start=True, stop=True)
                          for j in range(4)], reads=[aS.name, "triU"], writes=[pn])
                    A("act", lambda e, ps=ps, dc=dc: e.activation(out=dc[:L, :, :L], in_=ps[:L, :].rearrange("p (a b) -> p a b", b=128)[:, :, :L],
                                                                  func=AF.Exp), r=[pn], w=[dc.name])
                    A("dve", lambda e, q4=q4, dc=dc: e.tensor_tensor(out=MT[:L, 4 * q4:4 * q4 + 4, :L], in0=dc[:L, :, :L],
                                                                     in1=cbm[:L, q4, :L].unsqueeze(1).broadcast_to([L, 4, L]), op=ALU.mult), r=[dc.name, "cbm"], w=["MT"])
                for half in range(2):
                    psd, pnd = nextps()
                    S.mm([lambda e, h=h: e.matmul(out=psd[:L, (h % 8) * 64:(h % 8) * 64 + 64], lhsT=MT[:L, h, :L], rhs=xdt[:L, h * 64:(h + 1) * 64],
                                                  start=True, stop=True) for h in range(8 * half, 8 * half + 8)], reads=["MT", "xdt"], writes=[pnd])
                    pso, pno = nextps()
                    S.mm([lambda e, g=g: e.matmul(out=pso[:L, (g % 2) * 256:(g % 2) * 256 + 256], lhsT=CTb[:, g, :L], rhs=Hb[:, g * 256:(g + 1) * 256],
                                                  start=True, stop=True) for g in range(2 * half, 2 * half + 2)], reads=["CTb", "Hb"], writes=[pno])
                    sl = slice(512 * half, 512 * half + 512)
                    A("dve", lambda e, pso=pso, sl=sl, half=half: e.tensor_tensor(
                        out=ytmp[:L, sl].rearrange("p (h q) -> p h q", q=64), in0=pso[:L, :].rearrange("p (h q) -> p h q", q=64),
                        in1=eacs[:, 8 * half:8 * half + 8].unsqueeze(2).broadcast_to([L, 8, 64]), op=ALU.mult), r=[pno, "dtt"], w=["ytmp"])
                    A("pool", lambda e, sl=sl: e.tensor_tensor(out=yacc[:L, sl], in0=yacc[:L, sl], in1=ytmp[:L, sl], op=ALU.add), r=["yacc", "ytmp"], w=["yacc"])
                    A("dve", lambda e, psd=psd, sl=sl: e.tensor_tensor(out=yacc[:L, sl], in0=yacc[:L, sl], in1=psd[:L, :], op=ALU.add), r=["yacc", pnd], w=["yacc"])
            for half in range(2):
                pss, pns = nextps()
                S.mm([lambda e, g=g: e.matmul(out=pss[:, (g % 2) * 256:(g % 2) * 256 + 256], lhsT=Btm[:L, g * 128:(g + 1) * 128],
                                              rhs=xdte[:L, g * 256:(g + 1) * 256], start=True, stop=True) for g in range(2 * half, 2 * half + 2)],
                     reads=["Btm", "xdte"], writes=[pns])
                sl = slice(512 * half, 512 * half + 512)
                A("dve", lambda e, sl=sl, half=half: e.tensor_tensor(out=H[:, sl].rearrange("p (h q) -> p h q", q=64), in0=H[:, sl].rearrange("p (h q) -> p h q", q=64),
                                                                     in1=cdb[:, 8 * half:8 * half + 8].unsqueeze(2).broadcast_to([128, 8, 64]), op=ALU.mult),
                  r=["H", "cdb", "Hb"], w=["H"])
                A("dve", lambda e, sl=sl, pss=pss: e.tensor_tensor(out=H[:, sl], in0=H[:, sl], in1=pss[:, :], op=ALU.add), r=["H", pns], w=["H"])
            if not full:
                return
            A("act", lambda e: e.activation(out=Hb[:, :], in_=H[:, :], func=AF.Copy), r=["H"], w=["Hb"])
            for half in range(2):
                ps, pn = nextps()
                S.mm([lambda e, kt=kt, half=half: e.matmul(out=ps[:L, :], lhsT=xnT[:, kt, :L], rhs=Win[:, kt, 512 * half:512 * half + 512],
                                                          start=(kt == 0), stop=(kt == 7)) for kt in range(8)], reads=["xnT", "Win"], writes=[pn])
                sl = slice(512 * half, 512 * half + 512)
                A("act", lambda e, ps=ps, sl=sl: e.activation(out=sz[:L, sl], in_=ps[:L, :], func=AF.Silu), r=[pn], w=["xn"])
            A("dve", lambda e: e.tensor_tensor(out=yacc[:L, :], in0=yacc[:L, :], in1=sz[:L, :], op=ALU.mult), r=["yacc", "xn"], w=["yacc"])
            for g in range(4):
                A("act", lambda e, g=g: e.activation(out=ytmp[:L, 256 * g:256 * g + 256], in_=yacc[:L, 256 * g:256 * g + 256], func=AF.Square, scale=1.0 / 16,
                                                     accum_out=ss[:L, 4 + g:5 + g]), r=["yacc"], w=["ytmp", "ss"])
            A("act", lambda e: e.activation(out=ss[:L, 4:8], in_=ss[:L, 4:8], func=AF.Ln, bias=epsT[:L, 0:1]), r=["ss", "epsT"], w=["ss"])
            A("act", lambda e: e.activation(out=ss[:L, 4:8], in_=ss[:L, 4:8], func=AF.Exp, scale=-0.5), r=["ss"], w=["ss"])
            A("dve", lambda e: e.tensor_tensor(out=yacc[:L, :].rearrange("p (g q) -> p g q", q=256), in0=yacc[:L, :].rearrange("p (g q) -> p g q", q=256),
                                               in1=ss[:L, 4:8].unsqueeze(2).broadcast_to([L, 4, 256]), op=ALU.mult), r=["yacc", "ss"], w=["yacc"])
            for half in range(2):
                ps, pn = nextps()
                S.mm([lambda e, j=j, half=half: e.transpose(out=ps[:, j * 128:j * 128 + L], in_=yacc[:L, (half * 4 + j) * 128:(half * 4 + j + 1) * 128],
                                                           identity=identf[:L, :L]) for j in range(4)], reads=["yacc", "identf"], writes=[pn])
                v = ps[:, :].rearrange("p (a b) -> p a b", b=128)[:, :, :L]
                for j in range(4):
                    A("act", lambda e, v=v, half=half, j=j: e.activation(out=cat_f[:, half * 4 + j, :L], in_=v[:, j, :], func=AF.Copy,
                                                                         scale=sng[:, half * 4 + j:half * 4 + j + 1]), r=[pn, "sngt"], w=["cat_f"])
            for t in range(8):
                conv_taps("dve" if t % 2 == 0 else "pool", c_f[:, t, :L], gl_f, t, 31, ccw, ccb, L, ["glf%d" % t, "ccwt", "ccbt"], "cf%d" % t)
            if L >= HP:
                for t in range(8):
                    A("pool", lambda e, t=t: e.tensor_copy(out=gl_f[:, t, 0:HP], in_=gl_f[:, t, L:L + HP]), r=["glf%d" % t], w=["glf%d" % t])
            cfn = ["cf%d" % t for t in range(8)]
            A("act", lambda e: e.activation(out=csq[:, :, :L], in_=c_f[:, :, :L], func=AF.Square), r=cfn, w=["ytmp"])
            ps, pn = nextps()
            S.mm([lambda e, t=t: e.matmul(out=ps[:, 0:L], lhsT=onesf[:, :], rhs=c_f[:, t, :L], start=(t == 0), stop=(t == 7)) for t in range(8)] +
                 [lambda e, t=t: e.matmul(out=ps[:, 128:128 + L], lhsT=onesf[:, :], rhs=csq[:, t, :L], start=(t == 0), stop=(t == 7)) for t in range(8)],
                 reads=cfn + ["ytmp", "onesf"], writes=[pn])
            mean, ex2, var, rstd = [lnst[:, i, :L] for i in range(4)]
            A("dve", lambda e: e.tensor_scalar(out=mean, in0=ps[:, 0:L], scalar1=1.0 / 1024, scalar2=None, op0=ALU.mult), r=[pn], w=["dec0"])
            A("dve", lambda e: e.tensor_scalar(out=ex2, in0=ps[:, 128:128 + L], scalar1=1.0 / 1024, scalar2=None, op0=ALU.mult), r=[pn], w=["dec0"])
            A("dve", lambda e: e.tensor_tensor(out=var, in0=mean, in1=mean, op=ALU.mult), r=["dec0"], w=["dec0"])
            A("dve", lambda e: e.tensor_tensor(out=var, in0=ex2, in1=var, op=ALU.subtract), r=["dec0"], w=["dec0"])
            A("act", lambda e: e.activation(out=rstd, in_=var, func=AF.Ln, bias=epsT[:, 0:1]), r=["dec0", "epsT"], w=["dec0"])
            A("act", lambda e: e.activation(out=rstd, in_=rstd, func=AF.Exp, scale=-0.5), r=["dec0"], w=["dec0"])
            A("dve", lambda e: e.tensor_tensor(out=c_f[:, :, :L], in0=c_f[:, :, :L], in1=mean.unsqueeze(1).broadcast_to([128, 8, L]), op=ALU.subtract),
              r=cfn + ["dec0"], w=cfn)
            A("dve", lambda e: e.tensor_tensor(out=c_f[:, :, :L], in0=c_f[:, :, :L], in1=rstd.unsqueeze(1).broadcast_to([128, 8, L]), op=ALU.mult),
              r=cfn + ["dec0"], w=cfn)
            A("pool", lambda e: e.tensor_tensor(out=c_f[:, :, :L], in0=c_f[:, :, :L], in1=lng[:, :].unsqueeze(2).broadcast_to([128, 8, L]), op=ALU.mult),
              r=cfn + ["lngt"], w=cfn)
            A("pool", lambda e: e.tensor_tensor(out=c_f[:, :, :L], in0=c_f[:, :, :L], in1=lnb[:, :].unsqueeze(2).broadcast_to([128, 8, L]), op=ALU.add),
              r=cfn + ["lnbt"], w=cfn)
            A("act", lambda e: e.activation(out=c_f[:, :, :L], in_=c_f[:, :, :L], func=AF.Silu), r=cfn, w=cfn)
            A("dve", lambda e: e.tensor_tensor(out=cat_f[:, 8:16, :L], in0=c_f[:, :, :L], in1=scg[:, :, :L], op=ALU.mult), r=cfn + ["scg"], w=["cat_f"])
            for half in range(2):
                ps, pn = nextps()
                S.mm([lambda e, t=t, half=half: e.matmul(out=ps[:L, :], lhsT=cat_f[:, t, :L], rhs=Wout[:, t, 512 * half:512 * half + 512],
                                                        start=(t == 0), stop=(t == 15)) for t in range(16)], reads=["cat_f", "Wout"], writes=[pn])
                sl = slice(512 * half, 512 * half + 512)
                A("dve", lambda e, ps=ps, sl=sl: e.tensor_tensor(out=xn[:L, sl], in0=xt[:L, sl], in1=ps[:L, :], op=ALU.add), r=["xt", pn], w=["xn"])
            S.dma(dst_x1, xn[:L, :], reads=["xn"], writes=["x1_d"])

        def load_hist_from_state(b):
            S.dma(hist_tm[:3, :], st_sc[b, :, :], writes=XBCC)
            for g in range(4):
                ps, pn = nextps()
                S.mm([lambda e, j=j, g=g: e.transpose(out=ps[:, j * 128:j * 128 + 3], in_=hist_tm[:3, (4 * g + j) * 128:(4 * g + j + 1) * 128],
                                                      identity=identf[:3, :3]) for j in range(4)], reads=XBCC + ["identf"], writes=[pn])
                A("act", lambda e, ps=ps, g=g: e.activation(out=xbc_f[:, 4 * g:4 * g + 4, HP - 3:HP], in_=ps[:, :].rearrange("p (a b) -> p a b", b=128)[:, :, :3],
                                                            func=AF.Copy), r=[pn], w=["xbcf%d" % t for t in range(4 * g, 4 * g + 4)])
            S.dma(hist_tm[:30, :1024], st_cc[b, :, :], writes=XBCC)
            for g in range(2):
                ps, pn = nextps()
                S.mm([lambda e, j=j, g=g: e.transpose(out=ps[:, j * 128:j * 128 + 30], in_=hist_tm[:30, (4 * g + j) * 128:(4 * g + j + 1) * 128],
                                                      identity=identf[:30, :30]) for j in range(4)], reads=XBCC + ["identf"], writes=[pn])
                A("act", lambda e, ps=ps, g=g: e.activation(out=gl_f[:, 4 * g:4 * g + 4, HP - 30:HP], in_=ps[:, :].rearrange("p (a b) -> p a b", b=128)[:, :, :30],
                                                            func=AF.Copy), r=[pn], w=["glf%d" % t for t in range(4 * g, 4 * g + 4)])
            for g in range(2):
                S.dma(ytmp[:, 512 * g:512 * g + 512].rearrange("p (a n) -> p a n", n=128),
                      st_ssm[b, 512 * g:512 * g + 512, :].rearrange("(a p) n -> p a n", p=128), writes=["ytmp"])
                ps, pn = nextps()
                S.mm([lambda e, j=j, g=g: e.transpose(out=ps[:, j * 128:(j + 1) * 128], in_=ytmp[:, 512 * g + j * 128:512 * g + (j + 1) * 128],
                                                      identity=identf[:, :]) for j in range(4)], reads=["ytmp", "identf"], writes=[pn])
                A("dve", lambda e, ps=ps, g=g: e.tensor_copy(out=H[:, 512 * g:512 * g + 512], in_=ps[:, :]), r=[pn, "Hb"], w=["H"])
            A("act", lambda e: e.activation(out=Hb[:, :], in_=H[:, :], func=AF.Copy), r=["H"], w=["Hb"])

        def store_state_outputs(L, sc_dst, cc_dst, ssm_dst):
            for g in range(4):
                ps, pn = nextps()
                S.mm([lambda e, j=j, g=g: e.transpose(out=ps[:3, j * 128:(j + 1) * 128], in_=xbc_f[:, 4 * g + j, HP + L - 3:HP + L], identity=identf[:, :])
                      for j in range(4)], reads=["xbcf%d" % t for t in range(4 * g, 4 * g + 4)] + ["identf"], writes=[pn])
                A("act", lambda e, ps=ps, g=g: e.activation(out=hout[:3, 512 * g:512 * g + 512], in_=ps[:3, :], func=AF.Copy), r=[pn], w=XBCC)
            S.dma(sc_dst, hout[:3, :], reads=XBCC)
            for g in range(2):
                ps, pn = nextps()
                S.mm([lambda e, j=j, g=g: e.transpose(out=ps[:30, j * 128:(j + 1) * 128], in_=gl_f[:, 4 * g + j, HP + L - 30:HP + L], identity=identf[:, :])
                      for j in range(4)], reads=["glf%d" % t for t in range(4 * g, 4 * g + 4)] + ["identf"], writes=[pn])
                A("act", lambda e, ps=ps, g=g: e.activation(out=hout[:30, 512 * g:512 * g + 512], in_=ps[:30, :], func=AF.Copy), r=[pn], w=XBCC)
            S.dma(cc_dst, hout[:30, :1024], reads=XBCC)
            for g in range(2):
                ps, pn = nextps()
                S.mm([lambda e, j=j, g=g: e.transpose(out=ps[:, j * 128:(j + 1) * 128], in_=H[:, 512 * g + j * 128:512 * g + (j + 1) * 128], identity=identf[:, :])
                      for j in range(4)], reads=["H", "identf"], writes=[pn])
                A("dve", lambda e, ps=ps, g=g: e.tensor_copy(out=ytmp[:, 512 * g:512 * g + 512], in_=ps[:, :]), r=[pn], w=["ytmp"])
                S.dma(ssm_dst[512 * g:512 * g + 512, :].rearrange("(a p) n -> p a n", p=128),
                      ytmp[:, 512 * g:512 * g + 512].rearrange("p (a n) -> p a n", n=128), reads=["ytmp"])

        def zero_state():
            A("pool", lambda e: e.memset(H[:, :], 0.0), r=["Hb"], w=["H"])
            A("pool", lambda e: e.memset(Hb[:, :], 0.0), w=["Hb"])
            A("pool", lambda e: e.memset(Atot[:, :], 0.0), w=["Atot"])

        zero_state()
        for t in range(16):
            A("pool", lambda e, t=t: e.memset(xbc_f[:, t, 0:HP], 0.0), w=["xbcf%d" % t])
        for t in range(8):
            A("pool", lambda e, t=t: e.memset(gl_f[:, t, 0:HP], 0.0), w=["glf%d" % t])
        for c in range(NCHA):
            l0_chunk(xp[c * 128:(c + 1) * 128, :], 128, "full", dst_x1=x1_d[c * 128:(c + 1) * 128, :])
        store_state_outputs(128, sc_p[:, :], cc_p[:, :], ssm_p)
        for b in range(DB):
            load_hist_from_state(b)
            l0_chunk(xsm[b * 4:(b + 1) * 4, :], 4, "full", dst_x1=x1_d[SEQ + b * 4:SEQ + (b + 1) * 4, :])
            store_state_outputs(4, sc_s[b, :, :], cc_s[b, :, :], ssm_s[b])


        S.barrier()
        st0.close()
        if debug_l0:
            st1 = ExitStack(); cur[0] = st1
            xt = T("xt_dbg", [128, D])
            for c in range(NCH):
                S.dma(xt[:, :], x1_d[c * 128:(c + 1) * 128, :], reads=["x1_d"], writes=["xt"])
                S.dma(y_p[c * 128:(c + 1) * 128, :], xt[:, :], reads=["xt"])
            S.dma(xt[:DB * 4, :], x1_d[SEQ:SEQ + DB * 4, :], reads=["x1_d"], writes=["xt"])
            S.dma(y_s[:, :], xt[:DB * 4, :], reads=["xt"])
            S.finish("sp")
            st1.close()
            return nc
        st1 = ExitStack(); cur[0] = st1
        NSLOT = TPC // 128
        NB = 64
        psn[0] = 6
        Wq = T("Wq", [128, 8, 1024], BF16)
        Wg = T("Wg", [128, 8, 1024], BF16)
        Wo = T("Wo", [64, 16, 1024], BF16)
        g1 = T("g1t", [128, 8]); S.dma(g1[:], g1_d[:, :], writes=["g1t"])
        qg = T("qgt", [128, 1]); S.dma(qg[:], qg_d[:, :], writes=["qgt"])
        kgb = T("kgbt", [128, 256]); S.dma(kgb[:], kgb_d[:, :], writes=["kgbt"])
        pm = T("pmt", [128, 64])
        pneg = T("pnegt", [128, 64])
        om = T("omt", [128, 64])
        maskM = T("maskMt", [128, 8, 128]); S.dma(maskM[:], maskM_d[:, :, :], writes=["maskMt"])
        maskS = T("maskSt", [4, 16]); S.dma(maskS[:], maskS_d[:, :], writes=["maskSt"])
        oidx = T("oidxt", [128, NSLOT], I32); S.dma(oidx[:], oidx_d[:, :], writes=["oidx"])
        pidx = T("pidxt", [128, 1]); S.dma(pidx[:], pidx_d[:, :], writes=["pidx"])
        Eoh = T("Eoh", [64, 64, 128], BF16)
        A("pool", lambda e: e.memset(Eoh[:], 1.0), w=["Eoh"])
        A("pool", lambda e: e.affine_select(out=Eoh[:], in_=Eoh[:], pattern=[[-1, 64], [0, 128]], compare_op=ALU.is_equal, fill=0.0,
                                            base=0, channel_multiplier=1), r=["Eoh"], w=["Eoh"])
        xt1 = T("xt1", [128, D])
        xn1 = T("xn1", [128, D])
        ss1 = T("ss1", [128, 8])
        xT1 = T("xT1", [128, 8, 128], BF16)
        kms = T("kms", [128, 2, max(NCHA, NPG)])
        kmT = T("kmT", [128, 2, 64])
        A("pool", lambda e: e.memset(kmT[:], 0.0), w=["kmT"])
        stA = ExitStack(); cur[0] = stA
        Wkv = T("Wkv", [128, 8, 512], BF16)
        stg1 = [T("stg1a", [128, 1024]), T("stg1b", [128, 1024])]
        si = 0
        for (wd, wt, ncol, wname) in ((wq_d, Wq, 1024, "Wq"), (wkv_d, Wkv, 512, "Wkv"), (wg_d, Wg, 1024, "Wg")):
            for kt in range(8):
                sb = stg1[si % 2]
                S.dma(sb[:, :ncol], wd[:, kt, :], writes=[sb.name])
                A("dve" if si % 2 == 0 else "pool", lambda e, sb=sb, wt=wt, kt=kt, ncol=ncol: e.tensor_scalar(
                    out=wt[:, kt, :], in0=sb[:, :ncol], scalar1=g1[:, kt:kt + 1], scalar2=None, op0=ALU.mult), r=[sb.name, "g1t"], w=[wname])
                si += 1
        for h in range(16):
            sb = stg1[si % 2]
            S.dma(sb[:64, :], wo_d[:, h, :], writes=[sb.name])
            A("dve" if si % 2 == 0 else "pool", lambda e, sb=sb, h=h: e.tensor_copy(out=Wo[:, h, :], in_=sb[:64, :]), r=[sb.name], w=["Wo"])
            si += 1

        KV = T("KV", [128, 512])
        KTst = T("KTst", [128, 2, 128], BF16)
        Vb = T("Vb", [128, 4, 65], BF16)
        A("pool", lambda e: e.memset(Vb[:], 1.0), w=["Vb"])

        def l1_norm_T(L):
            A("act", lambda e: e.activation(out=xn1[:L, :], in_=xt1[:L, :], func=AF.Square, scale=1.0 / 32, accum_out=ss1[:L, 0:1]), r=["xt1"], w=["xn1", "ss1"])
            A("act", lambda e: e.activation(out=ss1[:L, 1:2], in_=ss1[:L, 0:1], func=AF.Ln, bias=epsT[:L, 0:1]), r=["ss1", "epsT"], w=["ss1"])
            A("act", lambda e: e.activation(out=ss1[:L, 2:3], in_=ss1[:L, 1:2], func=AF.Exp, scale=-0.5), r=["ss1"], w=["ss1"])
            A("dve", lambda e: e.tensor_scalar(out=xn1[:L, :], in0=xt1[:L, :], scalar1=ss1[:L, 2:3], scalar2=None, op0=ALU.mult), r=["xt1", "ss1"], w=["xn1"])
            for half in range(2):
                ps, pn = nextps()
                S.mm([lambda e, j=j, half=half: e.transpose(out=ps[:, j * 128:j * 128 + L], in_=xn1[:L, (half * 4 + j) * 128:(half * 4 + j + 1) * 128],
                                                           identity=identf[:L, :L]) for j in range(4)], reads=["xn1", "identf"], writes=[pn])
                v = ps[:, :].rearrange("p (a b) -> p a b", b=128)[:, :, :L]
                A("act", lambda e, v=v, half=half: e.activation(out=xT1[:, half * 4:half * 4 + 4, :L], in_=v, func=AF.Copy), r=[pn], w=["xT1"])

        def l1_kv(src, L, kdst, vdst, col0, chunk_idx):
            S.dma(xt1[:L, :], src, reads=["x1_d"], writes=["xt1"])
            l1_norm_T(L)
            ps, pn = nextps()
            S.mm([lambda e, kt=kt: e.matmul(out=ps[:L, :], lhsT=xT1[:, kt, :L], rhs=Wkv[:, kt, :], start=(kt == 0), stop=(kt == 7)) for kt in range(8)],
                 reads=["xT1", "Wkv"], writes=[pn])
            for h in range(4):
                A("act", lambda e, h=h: e.activation(out=xn1[:L, h * 64:(h + 1) * 64], in_=ps[:L, h * 64:(h + 1) * 64], func=AF.Square, scale=0.125,
                                                     accum_out=ss1[:L, 4 + h:5 + h]), r=[pn], w=["xn1", "ss1"])
            A("act", lambda e: e.activation(out=ss1[:L, 4:8], in_=ss1[:L, 4:8], func=AF.Ln, bias=epsT[:L, 0:1]), r=["ss1", "epsT"], w=["ss1"])
            A("act", lambda e: e.activation(out=ss1[:L, 4:8], in_=ss1[:L, 4:8], func=AF.Exp, scale=-0.5), r=["ss1"], w=["ss1"])
            A("dve", lambda e: e.tensor_tensor(out=KV[:L, 0:256].rearrange("p (h d) -> p h d", d=64), in0=ps[:L, 0:256].rearrange("p (h d) -> p h d", d=64),
                                               in1=ss1[:L, 4:8].unsqueeze(2).broadcast_to([L, 4, 64]), op=ALU.mult), r=[pn, "ss1"], w=["KV"])
            A("pool", lambda e: e.tensor_tensor(out=KV[:L, 0:256], in0=KV[:L, 0:256], in1=kgb[:L, :], op=ALU.mult), r=["KV", "kgbt"], w=["KV"])
            A("act", lambda e: e.activation(out=KV[:L, 256:512], in_=ps[:L, 256:512], func=AF.Copy), r=[pn], w=["KV"])
            S.dma(kdst, KV[:L, 0:256], reads=["KV"])
            S.dma(vdst, KV[:L, 256:512], reads=["KV"])
            ps2, pn2 = nextps()
            S.mm([lambda e, pr=pr: e.transpose(out=ps2[:, pr * 128:pr * 128 + L], in_=KV[:L, pr * 128:(pr + 1) * 128], identity=identf[:L, :L]) for pr in range(2)],
                 reads=["KV", "identf"], writes=[pn2])
            for pr in range(2):
                A("act", lambda e, pr=pr: e.activation(out=KTst[:, pr, :L], in_=ps2[:, pr * 128:pr * 128 + L], func=AF.Copy,
                                                       accum_out=kms[:, pr, chunk_idx:chunk_idx + 1]), r=[pn2], w=["KTst", "kms"])
                S.dma(KT_d[pr, :, col0:col0 + L], KTst[:, pr, :L], reads=["KTst"], writes=["KT_d"])
            A("pool", lambda e: e.tensor_copy(out=Vb[:L, :, 0:64], in_=KV[:L, 256:512].rearrange("p (h d) -> p h d", d=64)), r=["KV"], w=["Vb"])
            S.dma(V_d[col0:col0 + L, :], Vb[:L, :, :].rearrange("p h c -> p (h c)"), reads=["Vb"], writes=["V_d"])

        for t in range(NCHA):
            l1_kv(x1_d[t * 128:(t + 1) * 128, :], 128, k_p[t * 128:(t + 1) * 128, :], v_p[t * 128:(t + 1) * 128, :], t * 128, t)
        NBP = NCHA // 2
        kv2 = kms[:, :, 0:NCHA].rearrange("p a (n two) -> p a n two", two=2)
        A("dve", lambda e: e.tensor_tensor(out=kmT[:, :, 0:NBP], in0=kv2[:, :, :, 0], in1=kv2[:, :, :, 1], op=ALU.add), r=["kms"], w=["kmT"])
        A("dve", lambda e: e.tensor_scalar(out=kmT[:, :, 0:NBP], in0=kmT[:, :, 0:NBP], scalar1=1.0 / 256, scalar2=None, op0=ALU.mult), r=["kmT"], w=["kmT"])
        for b in range(DB):
            l1_kv(x1_d[SEQ + b * 4:SEQ + (b + 1) * 4, :], 4, k_s[b * 4:(b + 1) * 4, :], v_s[b * 4:(b + 1) * 4, :], SEQ + b * 4, NCHA - 1 if False else 0)

        S.barrier()
        stA.close()
        cur[0] = st1
        QF = T("QF", [128, 8, 128])
        sq1 = T("sq1", [128, 4, 128])
        QPf = T("QPf", [128, 16, 128])
        QPb = T("QPb", [128, 16, 128], BF16)
        SG = T("SG", [64, 16, 128], BF16)
        Gm = T("Gm", [128, 16, 64])
        m8 = T("m8", [128, 16, 8])
        bsel = Gm
        biasT = T("biasT", [64, 16, 128], BF16)
        ATg = T("ATg", [64, 16, 128], BF16)
        rdt = xn1
        bcs = T("bcs", [64, 512])
        atmp = T("atmp", [64, 512])
        PT = [T("PT0", [128, 512], BF16), T("PT1", [128, 512], BF16)]
        KTst2 = T("KTst2", [128, 2, 128], BF16)
        A("pool", lambda e: e.memset(QPf[:], 0.0), w=["QPf"])

        def l1_q(L):
            l1_norm_T(L)
            for half in range(2):
                ps, pn = nextps()
                S.mm([lambda e, r=r, kt=kt, half=half: e.matmul(out=ps[:, r * 128:r * 128 + L], lhsT=Wq[:, kt, (4 * half + r) * 128:(4 * half + r + 1) * 128],
                                                             rhs=xT1[:, kt, :L], start=(kt == 0), stop=(kt == 7)) for r in range(4) for kt in range(8)],
                     reads=["Wq", "xT1"], writes=[pn])
                v = ps[:, :].rearrange("p (a b) -> p a b", b=128)[:, :, :L]
                A("act", lambda e, v=v: e.activation(out=sq1[:, :, :L], in_=v, func=AF.Square, scale=0.125), r=[pn], w=["sq1"])
                pss, pns = nextps()
                S.mm([lambda e, r=r: e.matmul(out=pss[:, r * 128:r * 128 + L], lhsT=BD[:, :], rhs=sq1[:, r, :L], start=True, stop=True) for r in range(4)],
                     reads=["BD", "sq1"], writes=[pns])
                vs = pss[:, :].rearrange("p (a b) -> p a b", b=128)[:, :, :L]
                A("act", lambda e, vs=vs: e.activation(out=sq1[:, :, :L], in_=vs, func=AF.Ln, bias=epsT[:, 0:1]), r=[pns, "epsT"], w=["sq1"])
                A("act", lambda e: e.activation(out=sq1[:, :, :L], in_=sq1[:, :, :L], func=AF.Exp, scale=-0.5), r=["sq1"], w=["sq1"])
                A("dve", lambda e, v=v, half=half: e.tensor_tensor(out=QF[:, 4 * half:4 * half + 4, :L], in0=v, in1=sq1[:, :, :L], op=ALU.mult), r=[pn, "sq1", "QFk0", "QFk1", "QFv0", "QFv1"], w=["QF", "QFk0", "QFk1", "QFv0", "QFv1"])
            A("dve", lambda e: e.tensor_scalar(out=QF[:, :, :L], in0=QF[:, :, :L], scalar1=qg[:, 0:1], scalar2=None, op0=ALU.mult), r=["QF", "qgt"], w=["QF"])
            for hk in range(4):
                rows = slice(64 * (hk % 2), 64 * (hk % 2) + 64)
                t0 = (hk // 2) * 4
                A("pool", lambda e, hk=hk, rows=rows, t0=t0: e.tensor_copy(out=QPf[rows, 4 * hk:4 * hk + 4, :L], in_=QF[rows, t0:t0 + 4, :L]), r=["QF"], w=["QPf"])
            A("act", lambda e: e.activation(out=QPb[:, :, :L], in_=QPf[:, :, :L], func=AF.Copy), r=["QPf"], w=["QPb"])
            for bk in range(4):
                ps, pn = nextps()
                S.mm([lambda e, r=r, kt=kt, bk=bk: e.matmul(out=ps[0:64, r * 128:r * 128 + L], lhsT=Wg[:, kt, (4 * bk + r) * 64:(4 * bk + r + 1) * 64],
                                                           rhs=xT1[:, kt, :L], start=(kt == 0), stop=(kt == 7)) for r in range(4) for kt in range(8)],
                     reads=["Wg", "xT1"], writes=[pn])
                A("act", lambda e, ps=ps, bk=bk: e.activation(out=SG[:, 4 * bk:4 * bk + 4, :L], in_=ps[0:64, :].rearrange("p (a b) -> p a b", b=128)[:, :, :L],
                                                              func=AF.Silu), r=[pn], w=["SG"])

        def moba_select(L, kmt, kmname, slot):
            S.dma(pm[:, :], pm_d[:, slot, :], writes=["pmt"])
            S.dma(pneg[:, :], pneg_d[:, slot, :], writes=["pnegt"])
            S.dma(om[:, :], om_d[:, slot, :], writes=["omt"])
            for half in range(2):
                ps, pn = nextps()
                S.mm([lambda e, i=i, half=half: e.matmul(out=ps[:L, i * 64:(i + 1) * 64], lhsT=QPf[:, 8 * half + i, :L], rhs=kmt[:, (8 * half + i) // 8, :],
                                                        start=True, stop=True) for i in range(8)], reads=["QPf", kmname], writes=[pn])
                A("dve", lambda e, ps=ps, half=half: e.tensor_tensor(out=Gm[:L, 8 * half:8 * half + 8, :], in0=ps[:L, :].rearrange("p (a b) -> p a b", b=64),
                                                                     in1=pm[:L, :].unsqueeze(1).broadcast_to([L, 8, 64]), op=ALU.mult), r=[pn, "pmt"], w=["Gm"])
            A("dve", lambda e: e.tensor_tensor(out=Gm[:L, :, :], in0=Gm[:L, :, :], in1=pneg[:L, :].unsqueeze(1).broadcast_to([L, 16, 64]), op=ALU.add),
              r=["Gm", "pnegt"], w=["Gm"])
            for h in range(16):
                A("dve", lambda e, h=h: e.max(out=m8[:L, h, :], in_=Gm[:L, h, :]), r=["Gm"], w=["m8"])
            for h in range(16):
                A("dve", lambda e, h=h: e.tensor_scalar(out=bsel[:L, h, :], in0=Gm[:L, h, :], scalar1=m8[:L, h, 2:3], scalar2=None, op0=ALU.is_ge), r=["Gm", "m8"], w=["Gm"])
            A("dve", lambda e: e.tensor_tensor(out=bsel[:L, :, :], in0=bsel[:L, :, :], in1=pm[:L, :].unsqueeze(1).broadcast_to([L, 16, 64]), op=ALU.mult),
              r=["Gm", "pmt"], w=["Gm"])
            A("dve", lambda e: e.tensor_tensor(out=bsel[:L, :, :], in0=bsel[:L, :, :], in1=om[:L, :].unsqueeze(1).broadcast_to([L, 16, 64]), op=ALU.add),
              r=["Gm", "omt"], w=["Gm"])
            A("dve", lambda e: e.tensor_scalar(out=bsel[:L, :, :], in0=bsel[:L, :, :], scalar1=-NEG, scalar2=NEG, op0=ALU.mult, op1=ALU.add), r=["Gm"], w=["Gm"])
            for bk in range(4):
                ps, pn = nextps()
                S.mm([lambda e, r=r, bk=bk: e.transpose(out=ps[0:64, r * 128:r * 128 + L], in_=bsel[:L, 4 * bk + r, :], identity=identf[:L, :L]) for r in range(4)],
                     reads=["Gm", "identf"], writes=[pn])
                A("act", lambda e, ps=ps, bk=bk: e.activation(out=biasT[:, 4 * bk:4 * bk + 4, :L], in_=ps[0:64, :].rearrange("p (a b) -> p a b", b=128)[:, :, :L],
                                                              func=AF.Copy), r=[pn], w=["biasT"])

        def finish_group(L, hk, psO, pnO):
            W4 = 4 * L
            A("dve", lambda e: e.reciprocal(out=rdt[64:65, :W4], in_=psO[64:65, :W4]), r=[pnO], w=["xn1"])
            psb, pnb = nextps()
            S.mm([lambda e: e.matmul(out=psb[0:64, :W4], lhsT=onesf[64:65, 0:64], rhs=rdt[64:65, :W4], start=True, stop=True)], reads=["onesf", "xn1"], writes=[pnb])
            A("act", lambda e: e.activation(out=bcs[:, :W4], in_=psb[0:64, :W4], func=AF.Copy), r=[pnb], w=["bcs"])
            A("dve", lambda e: e.tensor_tensor(out=atmp[:, :W4], in0=psO[0:64, :W4], in1=bcs[:, :W4], op=ALU.mult), r=[pnO, "bcs"], w=["atmp"])
            A("pool", lambda e: e.tensor_tensor(out=ATg[:, 4 * hk:4 * hk + 4, :L], in0=atmp[:, :W4].rearrange("p (a b) -> p a b", b=L), in1=SG[:, 4 * hk:4 * hk + 4, :L],
                                                op=ALU.mult), r=["atmp", "SG"], w=["ATg"])

        def out_proj(L, dst):
            for half in range(2):
                ps, pn = nextps()
                S.mm([lambda e, h=h, half=half: e.matmul(out=ps[:L, :], lhsT=ATg[:, h, :L], rhs=Wo[:, h, 512 * half:512 * half + 512], start=(h == 0), stop=(h == 15))
                      for h in range(16)], reads=["ATg", "Wo"], writes=[pn])
                sl = slice(512 * half, 512 * half + 512)
                A("dve", lambda e, ps=ps, sl=sl: e.tensor_tensor(out=xn1[:L, sl], in0=xt1[:L, sl], in1=ps[:L, :], op=ALU.add), r=["xt1", pn], w=["xn1"])
            S.dma(dst, xn1[:L, :], reads=["xn1"])

        st2 = ExitStack(); cur[0] = st2
        NKT = NCHA
        NKB = max(NKT, NPG)
        KTp1 = T("KTp0", [128, NKB * 128], BF16)
        KTp = [KTp1, KTp1]
        Vp = [T("Vp%d" % i, [128, NKB, 65], BF16) for i in range(2)]
        bufi = 0
        for j in range(NSLOT):
            nkt = 8 * j + 8
            S.idma(xt1[:, :], x1_d[:, :], oidx[:, j:j + 1], reads=["x1_d", "oidx"], writes=["xt1"])
            l1_q(128)
            moba_select(128, kmT, "kmT", j)
            for hk in range(4):
                pr = hk // 2
                if hk % 2 == 0:
                    ktp = KTp[pr]
                    S.dma(ktp[:, :nkt * 128], KT_d[pr, :, 0:nkt * 128], reads=["KT_d"], writes=[ktp.name])
                vp = Vp[hk % 2]
                S.dma(vp[:, :nkt, :], V_d[0:nkt * 128, hk * 65:(hk + 1) * 65].rearrange("(k p) c -> p k c", p=128), reads=["V_d"], writes=[vp.name])
                psO, pnO = pst[7 - (hk % 2)], "ps%d" % (7 - (hk % 2))
                for kt in range(nkt):
                    ps, pn = nextps(6)
                    S.mm([lambda e, ps=ps, kt=kt, ktp=ktp, hk=hk: e.matmul(out=ps[:, :], lhsT=ktp[:, kt * 128:(kt + 1) * 128],
                                                                         rhs=QPb[:, 4 * hk:4 * hk + 4, :].rearrange("p a b -> p (a b)"), start=True, stop=False),
                          lambda e, ps=ps, kt=kt, hk=hk: e.matmul(out=ps[:, :], lhsT=Eoh[:, kt // 2, :], rhs=biasT[:, 4 * hk:4 * hk + 4, :].rearrange("p a b -> p (a b)"),
                                                                start=False, stop=True)], reads=[ktp.name, "QPb", "Eoh", "biasT"], writes=[pn])
                    pt = PT[kt % 2]
                    A("act", lambda e, ps=ps, pt=pt: e.activation(out=pt[:, :], in_=ps[:, :], func=AF.Exp, scale=0.125), r=[pn], w=[pt.name])
                    if kt >= 8 * j:
                        A("dve", lambda e, pt=pt, kt=kt, j=j: e.tensor_tensor(out=pt[:, :].rearrange("p (a b) -> p a b", b=128), in0=pt[:, :].rearrange("p (a b) -> p a b", b=128),
                                                                             in1=maskM[:, kt - 8 * j, :].unsqueeze(1).broadcast_to([128, 4, 128]), op=ALU.mult),
                          r=[pt.name, "maskMt"], w=[pt.name])
                    S.mm([lambda e, kt=kt, vp=vp, pt=pt, psO=psO, nkt=nkt: e.matmul(out=psO[0:65, :], lhsT=vp[:, kt, :], rhs=pt[:, :], start=(kt == 0), stop=(kt == nkt - 1))],
                         reads=[vp.name, pt.name], writes=[pnO])
                finish_group(128, hk, psO, pnO)
            out_proj(128, y_p[j * 128:(j + 1) * 128, :])

        PTs = Gm[:, :, :].rearrange("p a b -> p (a b)").bitcast(BF16)
        PTn = T("PTn", [4, 16], BF16)
        KTn = T("KTn", [128, 2, 4], BF16)
        Vn = T("Vn", [4, 4, 65], BF16)
        ptf = m8[:, :, :].rearrange("p a b -> p (a b)")[:, :NPG]
        pti = T("pti", [128, NPG], I32)
        kmTs = kmT
        VsA = T("VsA", [128, 4, 65], BF16)
        QFl = QF[:, :, :].rearrange("p a b -> p (a b)")
        Kpg = [QFl[:, 0:256], QFl[:, 256:512]]
        Vpg = [QFl[:, 512:768], QFl[:, 768:1024]]
        A("pool", lambda e: e.memset(VsA[:], 1.0), w=["VsA"])
        A("pool", lambda e: e.memset(kmTs[:], 0.0), w=["kmT"])
        NBS = NPG // 2
        for b in range(DB):
            S.dma(pti[:, :], ptb_d[:, b, :], writes=["pti"])
            A("dve", lambda e: e.tensor_copy(out=ptf[:, :], in_=pti[:, :]), r=["pti"], w=["m8"])
            A("dve", lambda e: e.tensor_scalar(out=ptf[:, :], in0=ptf[:, :], scalar1=128.0, scalar2=pidx[:, 0:1], op0=ALU.mult, op1=ALU.add), r=["m8", "pidx"], w=["m8"])
            A("dve", lambda e: e.tensor_copy(out=pti[:, :], in_=ptf[:, :]), r=["m8"], w=["pti"])
            for pg in range(NPG):
                kp, vq = Kpg[pg % 2], Vpg[pg % 2]
                kn_, vn_ = "QFk%d" % (pg % 2), "QFv%d" % (pg % 2)
                S.idma(kp, ck_d[:, :], pti[:, pg:pg + 1], reads=["pti", "QF"], writes=[kn_])
                S.idma(vq, cv_d[:, :], pti[:, pg:pg + 1], reads=["pti", "QF"], writes=[vn_])
                ps, pn = nextps()
                S.mm([lambda e, pr=pr, kp=kp, ps=ps: e.transpose(out=ps[:, pr * 128:(pr + 1) * 128], in_=kp[:, pr * 128:(pr + 1) * 128], identity=identf[:, :]) for pr in range(2)],
                     reads=[kn_, "identf"], writes=[pn])
                for pr in range(2):
                    A("act", lambda e, pr=pr, pg=pg, ps=ps: e.activation(out=KTst2[:, pr, :], in_=ps[:, pr * 128:(pr + 1) * 128], func=AF.Copy,
                                                                       accum_out=kms[:, pr, pg:pg + 1]), r=[pn], w=["KTst2", "kms"])
                    S.dma(KTs_d[b, pr, :, pg * 128:(pg + 1) * 128], KTst2[:, pr, :], reads=["KTst2"], writes=["KTs_d"])
                A("pool", lambda e, vq=vq: e.tensor_copy(out=VsA[:, :, 0:64], in_=vq.rearrange("p (h d) -> p h d", d=64)), r=[vn_], w=["VsA"])
                S.dma(Vs_d[b, pg * 128:(pg + 1) * 128, :], VsA[:, :, :].rearrange("p h c -> p (h c)"), reads=["VsA"], writes=["Vs_d"])
            kv3 = kms[:, :, 0:NPG].rearrange("p a (n two) -> p a n two", two=2)
            A("dve", lambda e: e.tensor_tensor(out=kmTs[:, :, 0:NBS], in0=kv3[:, :, :, 0], in1=kv3[:, :, :, 1], op=ALU.add), r=["kms"], w=["kmT"])
            A("dve", lambda e: e.tensor_scalar(out=kmTs[:, :, 0:NBS], in0=kmTs[:, :, 0:NBS], scalar1=1.0 / 256, scalar2=None, op0=ALU.mult), r=["kmT"], w=["kmT"])
            S.dma(KTn[:, :, :], KT_d[:, :, SEQ + b * 4:SEQ + (b + 1) * 4].rearrange("a p c -> p a c"), reads=["KT_d"], writes=["KTn"])
            S.dma(Vn[:, :, :].rearrange("p h c -> p (h c)"), V_d[SEQ + b * 4:SEQ + (b + 1) * 4, :], reads=["V_d"], writes=["Vn"])
            S.dma(xt1[:4, :], x1_d[SEQ + b * 4:SEQ + (b + 1) * 4, :], reads=["x1_d"], writes=["xt1"])
            l1_q(4)
            moba_select(4, kmTs, "kmT", NSLOT)
            for hk in range(4):
                pr = hk // 2
                if hk % 2 == 0:
                    S.dma(KTp1[:, :NPG * 128], KTs_d[b, pr, :, :], reads=["KTs_d"], writes=["KTp0"])
                vp = Vp[hk % 2]
                S.dma(vp[:, :NPG, :], Vs_d[b, :, hk * 65:(hk + 1) * 65].rearrange("(k p) c -> p k c", p=128), reads=["Vs_d"], writes=[vp.name])
                qrhs = QPb[:, 4 * hk:4 * hk + 4, :4]
                brhs = biasT[:, 4 * hk:4 * hk + 4, :4]
                PPB = 32
                for bk in range((NPG + PPB - 1) // PPB):
                    ps, pn = nextps()
                    pgs = list(range(bk * PPB, min(NPG, (bk + 1) * PPB)))
                    fns = []
                    for pg in pgs:
                        o = ps[:, (pg - bk * PPB) * 16:(pg - bk * PPB + 1) * 16].rearrange("p (a b) -> p a b", b=4)
                        fns.append(lambda e, o=o, pg=pg, qrhs=qrhs: e.matmul(out=o, lhsT=KTp1[:, pg * 128:(pg + 1) * 128], rhs=qrhs, start=True, stop=False))
                        fns.append(lambda e, o=o, pg=pg, brhs=brhs: e.matmul(out=o, lhsT=Eoh[:, pg // 2, :], rhs=brhs, start=False, stop=True))
                    S.mm(fns, reads=["KTp0", "QPb", "Eoh", "biasT"], writes=[pn])
                    ncol = len(pgs) * 16
                    A("act", lambda e, ps=ps, bk=bk, ncol=ncol: e.activation(out=PTs[:, bk * PPB * 16:bk * PPB * 16 + ncol], in_=ps[:, :ncol], func=AF.Exp, scale=0.125),
                      r=[pn], w=["Gm"])
                ps, pn = nextps()
                S.mm([lambda e, ps=ps, pr=pr, qrhs=qrhs: e.matmul(out=ps[0:4, 0:16].rearrange("p (a b) -> p a b", b=4), lhsT=KTn[:, pr, :], rhs=qrhs, start=True, stop=True)],
                     reads=["KTn", "QPb"], writes=[pn])
                A("act", lambda e, ps=ps: e.activation(out=PTn[:, :], in_=ps[0:4, 0:16], func=AF.Exp, scale=0.125), r=[pn], w=["PTn"])
                A("dve", lambda e: e.tensor_tensor(out=PTn[:, :], in0=PTn[:, :], in1=maskS[:, :], op=ALU.mult), r=["PTn", "maskSt"], w=["PTn"])
                psO, pnO = pst[7 - (hk % 2)], "ps%d" % (7 - (hk % 2))
                fns = [lambda e, pg=pg, vp=vp, psO=psO: e.matmul(out=psO[0:65, 0:16], lhsT=vp[:, pg, :], rhs=PTs[:, pg * 16:(pg + 1) * 16], start=(pg == 0), stop=False)
                       for pg in range(NPG)]
                fns.append(lambda e, hk=hk, psO=psO: e.matmul(out=psO[0:65, 0:16], lhsT=Vn[:, hk, :], rhs=PTn[:, :], start=False, stop=True))
                S.mm(fns, reads=[vp.name, "Gm", "Vn", "PTn"], writes=[pnO])
                finish_group(4, hk, psO, pnO)
            out_proj(4, y_s[b * 4:(b + 1) * 4, :])
        S.barrier()
        st2.close()
        st1.close()
        S.finish("sp")
        print("instructions:", S.n_instr, "sem counts", S.cnt)
    return nc


def prep_inputs(cfg, inp):
    f = lambda a: np.ascontiguousarray(np.asarray(a, dtype=np.float32))
    TPC, DB = cfg.tpc, cfg.db
    xp_full = f(inp["x_prompt"])[0]
    common = {
        "w_in0": f(f(inp["w_in0"])[0].reshape(8, 128, 6160).transpose(1, 0, 2)),
        "w_out0": f(f(inp["w_out0"])[0].reshape(16, 128, 1024).transpose(1, 0, 2)),
        "g0": f(f(inp["norm0_g"])[0].reshape(8, 128).T),
        "cw": f(f(inp["ssd_conv_w"])[0].reshape(4, 16, 128).transpose(2, 1, 0)),
        "cb": f(f(inp["ssd_conv_b"])[0].reshape(16, 128).T),
        "ccw": f(f(inp["conf_conv_w"])[0].reshape(31, 8, 128).transpose(2, 1, 0)),
        "ccb": f(f(inp["conf_conv_b"])[0].reshape(8, 128).T),
        "lng": f(f(inp["conf_ln_g"])[0].reshape(8, 128).T),
        "lnb": f(f(inp["conf_ln_b"])[0].reshape(8, 128).T),
        "dtb": f(np.broadcast_to(f(inp["ssd_dt_bias"])[0][None, :], (128, 16))),
        "alog": f(np.broadcast_to(f(inp["ssd_a_log"])[0][None, :], (128, 16))),
        "dsk": f(np.broadcast_to(f(inp["ssd_d"])[0][None, :], (128, 16))),
        "sng": f(f(inp["ssd_norm_g"])[0].reshape(8, 128).T),
    }
    perm = [0, 4, 1, 5, 2, 6, 3, 7, 8, 12, 9, 13, 10, 14, 11, 15]
    w1 = f(inp["w_in1"])[0]
    wq = w1[:, :1024].reshape(1024, 16, 64)[:, perm, :].reshape(1024, 1024)
    wkv = w1[:, 1024:1536]
    wg = w1[:, 1536:2560]
    kt_layout = lambda w: f(w.reshape(8, 128, w.shape[1]).transpose(1, 0, 2))
    NSLOT = TPC // 128
    NPG = cfg.npg
    npool = cfg.npool
    common.update({
        "wq": kt_layout(wq), "wkv": kt_layout(wkv), "wg": kt_layout(wg),
        "wo": f(f(inp["w_out1"])[0].reshape(16, 64, 1024).transpose(1, 0, 2)),
        "g1": f(f(inp["norm1_g"])[0].reshape(8, 128).T),
        "qg": f(np.tile(f(inp["q_norm_g"])[0], 2).reshape(128, 1)),
        "kgb": f(np.broadcast_to(np.tile(f(inp["k_norm_g"])[0], 4)[None, :], (128, 256))),
        "maskS": f(np.tile((np.arange(4)[:, None] <= np.arange(4)[None, :]).astype(np.float32), (1, 4))),
        "pidx": f(np.arange(128, dtype=np.float32).reshape(128, 1)),
        "ck": f(inp["cache_k"]).reshape(npool * 128, 256),
        "cv": f(inp["cache_v"]).reshape(npool * 128, 256),
    })
    pt_all = np.ascontiguousarray(np.asarray(inp["page_table"], dtype=np.int32))
    tri = (np.arange(128)[:, None] <= np.arange(128)[None, :]).astype(np.float32)
    maps = []
    for c in range(NCORE):
        m = dict(common)
        pm = np.zeros((NSLOT + 1, 64), np.float32)
        om = np.zeros((NSLOT + 1, 64), np.float32)
        for j in range(NSLOT):
            own = (8 * j + c) // 2
            pm[j, :own] = 1.0
            om[j, own] = 1.0
        pm[NSLOT, :NPG // 2] = 1.0
        bc = lambda a: f(np.broadcast_to(a[None], (128,) + a.shape))
        m["pm"] = bc(pm)
        m["pneg"] = bc((pm - 1.0) * np.float32(1e30))
        m["om"] = bc(om)
        mm_ = np.ones((128, 8, 128), np.float32)
        for dl in range(8):
            if dl // 2 == c // 2:
                if dl == c:
                    mm_[:, dl, :] = tri
                elif dl > c:
                    mm_[:, dl, :] = 0.0
        m["maskM"] = mm_
        m["oidx"] = np.ascontiguousarray(((8 * np.arange(NSLOT)[None, :] + c) * 128 + np.arange(128)[:, None]).astype(np.int32))
        m["ptb"] = np.ascontiguousarray(np.broadcast_to(pt_all[c * DB:(c + 1) * DB][None], (128, DB, NPG)).astype(np.int32))
        m["xp"] = xp_full
        m["xsm"] = f(f(inp["x_sample"])[c * DB:(c + 1) * DB].reshape(DB * 4, D))
        m["st_ssm"] = f(f(inp["state_ssm"])[0, c * DB:(c + 1) * DB].reshape(DB, 1024, 128))
        m["st_sc"] = f(f(inp["state_ssd_conv"])[0, c * DB:(c + 1) * DB])
        m["st_cc"] = f(f(inp["state_conf_conv"])[0, c * DB:(c + 1) * DB])
        cm = np.zeros((128, 8), np.float32)
        cm[:, :c] = 1.0
        m["cmask"] = cm
        maps.append(m)
    return maps


_NC_CACHE = {}


def run(cfg, inp, debug_l0=False):
    key = (cfg.seq, cfg.dbt, cfg.npg, debug_l0)
    if key not in _NC_CACHE:
        _NC_CACHE[key] = build(cfg, debug_l0)
    nc = _NC_CACHE[key]
    maps = prep_inputs(cfg, inp)
    res = run_bass_kernel_spmd(nc, maps, core_ids=list(range(NCORE)))
    R = res.results
    cat = lambda k: np.concatenate([r[k] for r in R], axis=0)
    TPC, DB = cfg.tpc, cfg.db
    y_p = cat("y_p")[None]
    y_s = cat("y_s").reshape(cfg.dbt, 4, D)
    ssm_p = R[0]["ssm_p"].reshape(1, 1, 16, 64, 128)
    ssm_s = cat("ssm_s").reshape(1, cfg.dbt, 16, 64, 128)
    sc_p = R[0]["sc_p"].reshape(1, 1, 3, 2048)
    sc_s = cat("sc_s").reshape(1, cfg.dbt, 3, 2048)
    cc_p = R[0]["cc_p"].reshape(1, 1, 30, 1024)
    cc_s = cat("cc_s").reshape(1, cfg.dbt, 30, 1024)
    if debug_l0:
        return (y_p, y_s, ssm_p, ssm_s, sc_p, sc_s, cc_p, cc_s)
    NSLOT = TPC // 128
    yp = np.empty((cfg.seq, D), np.float32)
    for c in range(NCORE):
        for j in range(NSLOT):
            t = 8 * j + c
            yp[t * 128:(t + 1) * 128] = R[c]["y_p"][j * 128:(j + 1) * 128]
    y_p = yp[None]
    k_p = R[0]["k_p"].reshape(1, 1, cfg.seq, 4, 64)
    v_p = R[0]["v_p"].reshape(1, 1, cfg.seq, 4, 64)
    k_s = cat("k_s").reshape(1, cfg.dbt, 4, 4, 64)
    v_s = cat("v_s").reshape(1, cfg.dbt, 4, 4, 64)
    return (y_p, y_s, ssm_p, ssm_s, sc_p, sc_s, cc_p, cc_s, k_p, v_p, k_s, v_s)


def kernel(**inputs):
    cfg = Cfg(inputs["x_prompt"].shape[1], inputs["x_sample"].shape[0], inputs["page_table"].shape[1] * 128)
    return run(cfg, inputs)
```

```python
import numpy as np
from contextlib import ExitStack
import concourse.bass as bass
import concourse.mybir as mybir
from concourse.bass_utils import run_bass_kernel_spmd

F32 = mybir.dt.float32
BF16 = mybir.dt.bfloat16
I32 = mybir.dt.int32
ALU = mybir.AluOpType
AF = mybir.ActivationFunctionType
AX = mybir.AxisListType

NCORE = 8
D = 1024
HP = 32
EPS = 1e-6
NEG = -30000.0


class Sched:
    def __init__(self, nc, stack, n_dma_sems=24):
        self.nc = nc
        self.engs = {"pe": nc.tensor, "act": nc.scalar, "dve": nc.vector, "pool": nc.gpsimd, "sp": nc.sync}
        self.sem = {k: stack.enter_context(nc.semaphore("s_" + k)) for k in ("pe", "act", "dve", "pool")}
        self.cnt = {k: 0 for k in self.sem}
        self.dsem = [stack.enter_context(nc.semaphore("d%d" % i)) for i in range(n_dma_sems)]
        self.dval = [0] * n_dma_sems
        self.dnext = 0
        self.waited = {}
        self.lastw = {}
        self.reads = {}
        self.n_instr = 0

    def _semobj(self, key):
        return self.sem[key] if isinstance(key, str) else self.dsem[key[1]]

    def _wait(self, eng, key, val):
        if self.waited.get((eng, key), 0) >= val:
            return
        self.waited[(eng, key)] = val
        self.engs[eng].wait_ge(self._semobj(key), val)

    def _deps(self, eng, reads, writes):
        for b in reads:
            t = self.lastw.get(b)
            if t is not None:
                self._wait(eng, t[0], t[1])
        for b in writes:
            t = self.lastw.get(b)
            if t is not None:
                self._wait(eng, t[0], t[1])
            for k, v in self.reads.get(b, {}).items():
                if k != eng:
                    self._wait(eng, k, v)

    def _record(self, key, val, reads, writes):
        for b in reads:
            d = self.reads.setdefault(b, {})
            if d.get(key, 0) < val:
                d[key] = val
        for b in writes:
            self.lastw[b] = (key, val)
            self.reads[b] = {}

    def op(self, eng, fn, reads=(), writes=()):
        self._deps(eng, reads, writes)
        ins = fn(self.engs[eng])
        self.cnt[eng] += 1
        ins.then_inc(self.sem[eng], 1)
        self._record(eng, self.cnt[eng], reads, writes)
        self.n_instr += 1

    def mm(self, fns, reads=(), writes=()):
        self._deps("pe", reads, writes)
        ins = None
        for fn in fns:
            ins = fn(self.nc.tensor)
            self.n_instr += 1
        self.cnt["pe"] += 1
        ins.then_inc(self.sem["pe"], 1)
        self._record("pe", self.cnt["pe"], reads, writes)

    def dma(self, out, in_, reads=(), writes=(), q="sp", **kw):
        i = self.dnext
        self.dnext = (self.dnext + 1) % len(self.dsem)
        key = ("d", i)
        if self.dval[i]:
            self._wait(q, key, self.dval[i])
        self._deps(q, reads, writes)
        self.dval[i] += 16
        ins = self.engs[q].dma_start(out=out, in_=in_, **kw)
        ins.then_inc(self.dsem[i], 16)
        self._record(key, self.dval[i], reads, writes)
        self.n_instr += 1

    def idma(self, out, in_, idx_ap, reads=(), writes=()):
        q = "pool"
        i = self.dnext
        self.dnext = (self.dnext + 1) % len(self.dsem)
        key = ("d", i)
        if self.dval[i]:
            self._wait(q, key, self.dval[i])
        self._deps(q, reads, writes)
        self.dval[i] += 16
        ins = self.nc.gpsimd.indirect_dma_start(out=out, out_offset=None, in_=in_,
                                                in_offset=bass.IndirectOffsetOnAxis(ap=idx_ap, axis=0))
        ins.then_inc(self.dsem[i], 16)
        self._record(key, self.dval[i], reads, writes)
        self.n_instr += 1

    def barrier(self):
        for e in ("pe", "act", "dve", "pool", "sp"):
            for i, v in enumerate(self.dval):
                if v:
                    self._wait(e, ("d", i), v)
            for k, v in self.cnt.items():
                if v and k != e:
                    self._wait(e, k, v)

    def finish(self, eng="sp"):
        for i, v in enumerate(self.dval):
            if v:
                self._wait(eng, ("d", i), v)
        for k, v in self.cnt.items():
            if v:
                self._wait(eng, k, v)


class Cfg:
    def __init__(self, seq, dec_batch, past_len):
        self.seq = seq
        self.tpc = seq // NCORE
        self.nch = self.tpc // 128
        self.dbt = dec_batch
        self.db = dec_batch // NCORE
        self.npg = past_len // 128
        n_used = dec_batch * self.npg
        self.npool = n_used + (n_used + 3) // 4
        self.nblk_p = seq // 256
        self.bpc = self.tpc // 256


def build(cfg, debug_l0=False):
    nc = bass.Bass("TRN2", target_bir_lowering=False)
    TPC, NCH, DB, NPG = cfg.tpc, cfg.nch, cfg.db, cfg.npg
    SEQ = cfg.seq
    NCHA = SEQ // 128

    def din(name, shape, dt=F32):
        return nc.dram_tensor(name, list(shape), dt, kind="ExternalInput").ap()

    def dout(name, shape, dt=F32):
        return nc.dram_tensor(name, list(shape), dt, kind="ExternalOutput").ap()

    xp = din("xp", [SEQ, D])
    xsm = din("xsm", [DB * 4, D])
    st_ssm = din("st_ssm", [DB, 1024, 128])
    st_sc = din("st_sc", [DB, 3, 2048])
    st_cc = din("st_cc", [DB, 30, 1024])
    w_in0 = din("w_in0", [128, 8, 6160])
    w_out0 = din("w_out0", [128, 16, 1024])
    g0_d = din("g0", [128, 8])
    cw_d = din("cw", [128, 16, 4])
    cb_d = din("cb", [128, 16])
    ccw_d = din("ccw", [128, 8, 31])
    ccb_d = din("ccb", [128, 8])
    lng_d = din("lng", [128, 8])
    lnb_d = din("lnb", [128, 8])
    dtb_d = din("dtb", [128, 16])
    alog_d = din("alog", [128, 16])
    dsk_d = din("dsk", [128, 16])
    sng_d = din("sng", [128, 8])
    cmask_d = din("cmask", [128, 8])
    NSLOT_ = TPC // 128
    wq_d = din("wq", [128, 8, 1024])
    wkv_d = din("wkv", [128, 8, 512])
    wg_d = din("wg", [128, 8, 1024])
    wo_d = din("wo", [64, 16, 1024])
    g1_d = din("g1", [128, 8])
    qg_d = din("qg", [128, 1])
    kgb_d = din("kgb", [128, 256])
    pm_d = din("pm", [128, NSLOT_ + 1, 64])
    pneg_d = din("pneg", [128, NSLOT_ + 1, 64])
    om_d = din("om", [128, NSLOT_ + 1, 64])
    maskM_d = din("maskM", [128, 8, 128])
    maskS_d = din("maskS", [4, 16])
    oidx_d = din("oidx", [128, NSLOT_], I32)
    pidx_d = din("pidx", [128, 1])
    ptb_d = din("ptb", [128, DB, NPG], I32)
    ck_d = din("ck", [cfg.npool * 128, 256])
    cv_d = din("cv", [cfg.npool * 128, 256])
    k_p = dout("k_p", [SEQ, 256])
    v_p = dout("v_p", [SEQ, 256])
    k_s = dout("k_s", [DB * 4, 256])
    v_s = dout("v_s", [DB * 4, 256])

    y_p = dout("y_p", [TPC, D])
    y_s = dout("y_s", [DB * 4, D])
    ssm_p = dout("ssm_p", [1024, 128])
    ssm_s = dout("ssm_s", [DB, 1024, 128])
    sc_p = dout("sc_p", [3, 2048])
    sc_s = dout("sc_s", [DB, 3, 2048])
    cc_p = dout("cc_p", [30, 1024])
    cc_s = dout("cc_s", [DB, 30, 1024])

    x1_d = nc.dram_tensor("x1_d", [SEQ + DB * 4, D], F32, kind="Internal").ap()
    KT_d = nc.dram_tensor("KT_d", [2, 128, SEQ + DB * 4], BF16, kind="Internal").ap()
    V_d = nc.dram_tensor("V_d", [SEQ + DB * 4, 260], BF16, kind="Internal").ap()
    KTs_d = nc.dram_tensor("KTs_d", [DB, 2, 128, NPG * 128], BF16, kind="Internal").ap()
    Vs_d = nc.dram_tensor("Vs_d", [DB, NPG * 128, 260], BF16, kind="Internal").ap()

    st = ExitStack()
    with st:
        S = Sched(nc, st)
        A = lambda eng, fn, r=(), w=(): S.op(eng, fn, reads=r, writes=w)

        cur = [st]

        def T(name, shape, dt=F32):
            return cur[0].enter_context(nc.sbuf_tensor(name, list(shape), dt))

        pst = [st.enter_context(nc.psum_tensor("ps%d" % i, [128, 512], F32)) for i in range(8)]
        psi = [0]

        psn = [8]

        def nextps(n=None):
            n = n or psn[0]
            i = psi[0] % n
            psi[0] = (i + 1) % n
            return pst[i], "ps%d" % i

        identf = T("identf", [128, 128])
        triU = T("triU", [128, 128])
        SU = T("SU", [128, 128])
        onesf = T("onesf", [128, 128])
        epsT = T("epsT", [128, 1])
        oneT = T("oneT", [128, 1])
        for t_, cmp_, sgn in ((identf, ALU.is_equal, 1), (triU, ALU.is_ge, -1)):
            A("pool", lambda e, t_=t_: e.memset(t_[:], 1.0), w=[t_.name])
            A("pool", lambda e, t_=t_, cmp_=cmp_, sgn=sgn: e.affine_select(out=t_[:], in_=t_[:], pattern=[[-sgn, 128]], compare_op=cmp_,
                                                          fill=0.0, base=0, channel_multiplier=sgn), r=[t_.name], w=[t_.name])
        A("dve", lambda e: e.tensor_scalar(out=SU[:], in0=triU[:], scalar1=-1.0, scalar2=1.0, op0=ALU.mult, op1=ALU.add), r=["triU"], w=["SU"])
        A("pool", lambda e: e.memset(onesf[:], 1.0), w=["onesf"])
        A("pool", lambda e: e.memset(epsT[:], EPS), w=["epsT"])
        A("pool", lambda e: e.memset(oneT[:], 1.0), w=["oneT"])

        BD = T("BD", [128, 128])
        A("pool", lambda e: e.memset(BD[:], 0.0), w=["BD"])
        A("pool", lambda e: e.memset(BD[0:64, 0:64], 1.0), r=["BD"], w=["BD"])
        A("pool", lambda e: e.memset(BD[64:128, 64:128], 1.0), r=["BD"], w=["BD"])
        st0 = ExitStack()
        cur[0] = st0
        def ld(name, src, shape):
            t = T(name, shape)
            S.dma(t[:], src, writes=[name])
            return t
        g0 = ld("g0t", g0_d[:, :], [128, 8])
        cw = ld("cwt", cw_d[:, :, :], [128, 16, 4])
        cb = ld("cbt", cb_d[:, :], [128, 16])
        ccw = ld("ccwt", ccw_d[:, :, :], [128, 8, 31])
        ccb = ld("ccbt", ccb_d[:, :], [128, 8])
        lng = ld("lngt", lng_d[:, :], [128, 8])
        lnb = ld("lnbt", lnb_d[:, :], [128, 8])
        dtb = ld("dtbt", dtb_d[:, :], [128, 16])
        Ab = ld("Abt", alog_d[:, :], [128, 16])
        dsk = ld("dskt", dsk_d[:, :], [128, 16])
        sng = ld("sngt", sng_d[:, :], [128, 8])
        cmask = ld("cmaskt", cmask_d[:, :], [128, 8])
        A("act", lambda e: e.activation(out=Ab[:], in_=Ab[:], func=AF.Exp), r=["Abt"], w=["Abt"])
        A("dve", lambda e: e.tensor_scalar(out=Ab[:], in0=Ab[:], scalar1=-1.0, scalar2=None, op0=ALU.mult), r=["Abt"], w=["Abt"])

        Win = T("Win", [128, 8, 6160], BF16)
        Wout = T("Wout", [128, 16, 1024], BF16)
        xbc_c = T("xbc_c", [128, 16, 128])
        xbc_flat = xbc_c[:, :, :].rearrange("p a b -> p (a b)")
        XBCC = ["xbcc%d" % t for t in range(16)]
        stg = [xbc_flat[:, 0:770], xbc_flat[:, 1024:1024 + 770]]
        stgn = [XBCC[:8], XBCC[8:]]
        si = 0
        cast_engs = ["dve", "pool"]
        for kt in range(8):
            for q8 in range(8):
                sb = stg[si % 2]
                S.dma(sb, w_in0[:, kt, q8 * 770:(q8 + 1) * 770], writes=stgn[si % 2])
                A(cast_engs[si % 2], lambda e, sb=sb, kt=kt, q8=q8: e.tensor_scalar(
                    out=Win[:, kt, q8 * 770:(q8 + 1) * 770], in0=sb, scalar1=g0[:, kt:kt + 1], scalar2=None, op0=ALU.mult),
                    r=stgn[si % 2] + ["g0t"], w=["Win"])
                si += 1
        for t_ in range(16):
            for hf in range(2):
                sb = stg[si % 2]
                S.dma(sb[:, :512], w_out0[:, t_, hf * 512:(hf + 1) * 512], writes=stgn[si % 2])
                A(cast_engs[si % 2], lambda e, sb=sb, t_=t_, hf=hf: e.tensor_copy(out=Wout[:, t_, hf * 512:(hf + 1) * 512], in_=sb[:, :512]), r=stgn[si % 2], w=["Wout"])
                si += 1

        xt = T("xt", [128, D])
        xn = T("xn", [128, D])
        ss = T("ss", [128, 8])
        xnT = T("xnT", [128, 8, 128], BF16)
        xbc_f = T("xbc_f", [128, 16, HP + 128])
        gl_f = T("gl_f", [128, 8, HP + 128])
        scg = T("scg", [128, 8, 128], BF16)
        c_f = T("c_f", [128, 8, 128])
        cat_f = T("cat_f", [128, 16, 128], BF16)
        CTb = T("CTb", [128, 4, 128], BF16)
        BTb = T("BTb", [128, 4, 128], BF16)
        Btm = T("Btm", [128, 512], BF16)
        dtt = T("dtt", [128, 8, 16])
        aSU4 = [T("aSU0", [128, 4, 128])] * 2
        dec4 = [T("dec0", [128, 4, 128])] * 2
        cbm = T("cbm", [128, 4, 128])
        MT = T("MT", [128, 16, 128], BF16)
        xdt = T("xdt", [128, 1024], BF16)
        xdte = T("xdte", [128, 1024], BF16)
        yacc = T("yacc", [128, 1024])
        ytmp = T("ytmp", [128, 1024])
        H = T("H", [128, 1024])
        Hb = T("Hb", [128, 1024], BF16)
        ptmp2 = [T("ptmpa", [128, 128])] * 2
        cdb = T("cdb", [128, 16])
        Atot = T("Atot", [128, 16])
        hist_tm = xbc_flat[:32, :]
        hout = xbc_flat[:32, :]
        sz = xn
        csq = ytmp[:, :].rearrange("p (a b) -> p a b", b=128)
        sig = aSU4[0]
        lnst = dec4[0]

        def fm_inproj(L, col0s, evac):
            ps, pn = nextps()
            fns = []
            for j, c0 in enumerate(col0s):
                for kt in range(8):
                    fns.append(lambda e, j=j, c0=c0, kt=kt: e.matmul(out=ps[:, j * 128:j * 128 + L], lhsT=Win[:, kt, c0:c0 + 128],
                                                                   rhs=xnT[:, kt, :L], start=(kt == 0), stop=(kt == 7)))
            S.mm(fns, reads=["Win", "xnT"], writes=[pn])
            v = ps[:, :].rearrange("p (a b) -> p a b", b=128)[:, :len(col0s), :L]
            evac(v, pn)

        def transposes_to_tm(L, srcs, src_names, nm):
            ps, pn = nextps()
            S.mm([lambda e, j=j, s=s: e.transpose(out=ps[:L, j * 128:(j + 1) * 128], in_=s, identity=identf[:, :])
                  for j, s in enumerate(srcs)], reads=list(src_names) + ["identf"], writes=[pn])
            return ps[:L, :len(srcs) * 128], pn

        def conv_all(tiles, engs, out_tile, src_tile, ntap, w_t, b_t, L, src_pref, w_names, out_pref):
            o0 = HP - (ntap - 1)
            for j in range(ntap):
                for t in tiles:
                    eng = engs[t]
                    out_ap = out_tile[:, t, :L]
                    rn = [src_pref % t] + w_names
                    wn = out_pref % t
                    src = src_tile[:, t, o0 + j:o0 + j + L]
                    if j == 0:
                        A(eng, lambda e, out_ap=out_ap, src=src, t=t: e.tensor_scalar(out=out_ap, in0=src, scalar1=w_t[:, t, 0:1], scalar2=b_t[:, t:t + 1],
                                                                                    op0=ALU.mult, op1=ALU.add), r=rn, w=[wn])
                    elif eng == "dve":
                        A(eng, lambda e, out_ap=out_ap, src=src, t=t, j=j: e.scalar_tensor_tensor(out=out_ap, in0=src, scalar=w_t[:, t, j:j + 1], in1=out_ap,
                                                                                              op0=ALU.mult, op1=ALU.add), r=rn + [wn], w=[wn])
                    else:
                        pt_ = ptmp2[t % 2]
                        A(eng, lambda e, src=src, t=t, j=j, pt_=pt_: e.tensor_tensor(out=pt_[:, :L], in0=src, in1=w_t[:, t, j:j + 1].broadcast_to([128, L]), op=ALU.mult),
                          r=rn, w=[pt_.name])
                        A(eng, lambda e, out_ap=out_ap, pt_=pt_: e.tensor_tensor(out=out_ap, in0=out_ap, in1=pt_[:, :L], op=ALU.add), r=[pt_.name, wn], w=[wn])

        def l0_chunk(src_ap, L, mode, dst_x1=None):
            full = mode == "full"
            nx = 16 if (full or mode == "halo2") else 12
            S.dma(xt[:L, :], src_ap, writes=["xt"])
            A("act", lambda e: e.activation(out=xn[:L, :], in_=xt[:L, :], func=AF.Square, scale=1.0 / 32, accum_out=ss[:L, 0:1]),
              r=["xt"], w=["xn", "ss"])
            A("act", lambda e: e.activation(out=ss[:L, 1:2], in_=ss[:L, 0:1], func=AF.Ln, bias=epsT[:L, 0:1]), r=["ss", "epsT"], w=["ss"])
            A("act", lambda e: e.activation(out=ss[:L, 2:3], in_=ss[:L, 1:2], func=AF.Exp, scale=-0.5), r=["ss"], w=["ss"])
            A("dve", lambda e: e.tensor_scalar(out=xn[:L, :], in0=xt[:L, :], scalar1=ss[:L, 2:3], scalar2=None, op0=ALU.mult),
              r=["xt", "ss"], w=["xn"])
            for half in range(2):
                ps, pn = nextps()
                S.mm([lambda e, j=j: e.transpose(out=ps[:, j * 128:j * 128 + L], in_=xn[:L, (half * 4 + j) * 128:(half * 4 + j + 1) * 128],
                                                 identity=identf[:L, :L]) for j in range(4)], reads=["xn", "identf"], writes=[pn])
                v = ps[:, :].rearrange("p (a b) -> p a b", b=128)[:, :, :L]
                A("act", lambda e, v=v, half=half: e.activation(out=xnT[:, half * 4:half * 4 + 4, :L], in_=v, func=AF.Copy), r=[pn], w=["xnT"])
            for g in range(nx // 4):
                def ev(v, pn, g=g):
                    A("act", lambda e: e.activation(out=xbc_f[:, 4 * g:4 * g + 4, HP:HP + L], in_=v, func=AF.Copy), r=[pn],
                      w=["xbcf%d" % t for t in range(4 * g, 4 * g + 4)])
                fm_inproj(L, [1024 + 128 * t for t in range(4 * g, 4 * g + 4)], ev)
            if full or mode == "halo2":
                for g in range(2):
                    def evb(v, pn):
                        A("act", lambda e: e.activation(out=sig[:, :, :L], in_=v, func=AF.Sigmoid), r=[pn], w=["aSU0"])
                    fm_inproj(L, [4112 + 128 * t for t in range(4 * g, 4 * g + 4)], evb)

                    def eva(v, pn, g=g):
                        A("dve", lambda e: e.tensor_tensor(out=gl_f[:, 4 * g:4 * g + 4, HP:HP + L], in0=v, in1=sig[:, :, :L], op=ALU.mult),
                          r=[pn, "aSU0"], w=["glf%d" % t for t in range(4 * g, 4 * g + 4)])
                    fm_inproj(L, [3088 + 128 * t for t in range(4 * g, 4 * g + 4)], eva)
            if mode in ("halo1", "halo2"):
                for t in range(nx):
                    A("pool", lambda e, t=t: e.tensor_copy(out=xbc_f[:, t, 0:HP], in_=xbc_f[:, t, HP:2 * HP]), r=["xbcf%d" % t], w=["xbcf%d" % t])
                if mode == "halo2":
                    for t in range(8):
                        A("pool", lambda e, t=t: e.tensor_copy(out=gl_f[:, t, 0:HP], in_=gl_f[:, t, HP:2 * HP]), r=["glf%d" % t], w=["glf%d" % t])
                return
            if full:
                for g in range(2):
                    def evc(v, pn, g=g):
                        A("act", lambda e: e.activation(out=scg[:, 4 * g:4 * g + 4, :L], in_=v, func=AF.Silu), r=[pn], w=["scg"])
                    fm_inproj(L, [5136 + 128 * t for t in range(4 * g, 4 * g + 4)], evc)
            conv_all(list(range(nx)), ["dve"] * 16, xbc_c, xbc_f, 4, cw, cb, L, "xbcf%d", ["cwt", "cbt"], "xbcc%d")
            for g in range(nx // 4):
                A("act", lambda e, g=g: e.activation(out=xbc_c[:, 4 * g:4 * g + 4, :L], in_=xbc_c[:, 4 * g:4 * g + 4, :L], func=AF.Silu),
                  r=["xbcc%d" % t for t in range(4 * g, 4 * g + 4)], w=["xbcc%d" % t for t in range(4 * g, 4 * g + 4)])
            if L >= HP:
                for t in range(nx):
                    A("pool", lambda e, t=t: e.tensor_copy(out=xbc_f[:, t, 0:HP], in_=xbc_f[:, t, L:L + HP]), r=["xbcf%d" % t], w=["xbcf%d" % t])
            ps, pn = nextps()
            S.mm([lambda e, kt=kt: e.matmul(out=ps[:L, 0:16], lhsT=xnT[:, kt, :L], rhs=Win[:, kt, 3072:3088], start=(kt == 0), stop=(kt == 7))
                  for kt in range(8)], reads=["xnT", "Win"], writes=[pn])
            dtr, dta, dte, dtl, dtv, av, acs, eacs = [dtt[:L, i, :] for i in range(8)]
            A("dve", lambda e: e.tensor_tensor(out=dtr, in0=ps[:L, 0:16], in1=dtb[:L, :], op=ALU.add), r=[pn, "dtbt"], w=["dtt"])
            A("dve", lambda e: e.scalar_tensor_tensor(out=dta, in0=dtr, scalar=-1.0, in1=dtr, op0=ALU.mult, op1=ALU.min), r=["dtt"], w=["dtt"])
            A("act", lambda e: e.activation(out=dte, in_=dta, func=AF.Exp), r=["dtt"], w=["dtt"])
            A("act", lambda e: e.activation(out=dtl, in_=dte, func=AF.Ln, bias=oneT[:L, 0:1]), r=["dtt", "oneT"], w=["dtt"])
            A("dve", lambda e: e.scalar_tensor_tensor(out=dtv, in0=dtr, scalar=0.0, in1=dtl, op0=ALU.max, op1=ALU.add), r=["dtt"], w=["dtt"])
            A("dve", lambda e: e.tensor_tensor(out=av, in0=dtv, in1=Ab[:L, :], op=ALU.mult), r=["dtt", "Abt"], w=["dtt"])
            ps2, pn2 = nextps()
            S.mm([lambda e: e.matmul(out=ps2[:L, 0:16], lhsT=triU[:L, :L], rhs=av, start=True, stop=True),
                  lambda e: e.matmul(out=ps2[:, 16:32], lhsT=onesf[:L, :], rhs=av, start=True, stop=True)],
                 reads=["triU", "onesf", "dtt"], writes=[pn2])
            A("dve", lambda e: e.tensor_copy(out=acs, in_=ps2[:L, 0:16]), r=[pn2], w=["dtt"])
            A("act", lambda e: e.activation(out=cdb[:, :], in_=ps2[:, 16:32], func=AF.Exp), r=[pn2], w=["cdb"])
            A("dve", lambda e: e.tensor_tensor(out=Atot[:, :], in0=Atot[:, :], in1=ps2[:, 16:32], op=ALU.add), r=[pn2, "Atot"], w=["Atot"])
            A("dve", lambda e: e.tensor_tensor(out=dta, in0=ps2[:L, 16:32], in1=acs, op=ALU.subtract), r=[pn2, "dtt"], w=["dtt"])
            A("act", lambda e: e.activation(out=dta, in_=dta, func=AF.Exp), r=["dtt"], w=["dtt"])
            A("act", lambda e: e.activation(out=eacs, in_=acs, func=AF.Exp), r=["dtt"], w=["dtt"])
            for g in range(2):
                v, pn = transposes_to_tm(L, [xbc_c[:, 4 * g + j, :L] for j in range(4)], ["xbcc%d" % (4 * g + j) for j in range(4)], "xs")
                v3 = v.rearrange("p (h q) -> p h q", q=64)
                sl = slice(512 * g, 512 * (g + 1))
                A("dve", lambda e, v3=v3, sl=sl, g=g: e.tensor_tensor(out=xdt[:L, sl].rearrange("p (h q) -> p h q", q=64), in0=v3,
                                                                      in1=dtv[:, 8 * g:8 * g + 8].unsqueeze(2).broadcast_to([L, 8, 64]), op=ALU.mult),
                  r=[pn, "dtt"], w=["xdt"])
                if full:
                    A("dve", lambda e, v=v, sl=sl, g=g: e.tensor_tensor(out=yacc[:L, sl].rearrange("p (h q) -> p h q", q=64), in0=v.rearrange("p (h q) -> p h q", q=64), in1=dsk[:L, 8 * g:8 * g + 8].unsqueeze(2).broadcast_to([L, 8, 64]), op=ALU.mult),
                      r=[pn, "dskt"], w=["yacc"])
            v, pn = transposes_to_tm(L, [xbc_c[:, 8 + j, :L] for j in range(4)], ["xbcc%d" % (8 + j) for j in range(4)], "B")
            A("act", lambda e: e.activation(out=Btm[:L, :], in_=v, func=AF.Copy), r=[pn], w=["Btm"])
            A("dve", lambda e: e.tensor_tensor(out=xdte[:L, :].rearrange("p (h q) -> p h q", q=64), in0=xdt[:L, :].rearrange("p (h q) -> p h q", q=64),
                                               in1=dta.unsqueeze(2).broadcast_to([L, 16, 64]), op=ALU.mult), r=["xdt", "dtt"], w=["xdte"])
            if full:
                A("pool", lambda e: e.tensor_copy(out=BTb[:, :, :L], in_=xbc_c[:, 8:12, :L]), r=["xbcc%d" % t for t in range(8, 12)], w=["BTb"])
                A("pool", lambda e: e.tensor_copy(out=CTb[:, :, :L], in_=xbc_c[:, 12:16, :L]), r=["xbcc%d" % t for t in range(12, 16)], w=["CTb"])
                psc, pnc = nextps()
                S.mm([lambda e, g=g: e.matmul(out=psc[:L, g * 128:g * 128 + L], lhsT=BTb[:, g, :L], rhs=CTb[:, g, :L], start=True, stop=True)
                      for g in range(4)], reads=["BTb", "CTb"], writes=[pnc])
                A("dve", lambda e: e.tensor_tensor(out=cbm[:L, :, :L], in0=psc[:L, :].rearrange("p (a b) -> p a b", b=128)[:, :, :L],
                                                   in1=triU[:L, :L].unsqueeze(1).broadcast_to([L, 4, L]), op=ALU.mult), r=[pnc, "triU"], w=["cbm"])
                for q4 in range(4):
                    aS, dc = aSU4[q4 % 2], dec4[q4 % 2]
                    A("pool", lambda e, aS=aS, q4=q4: e.tensor_tensor(out=aS[:L, :, :L], in0=SU[:L, :L].unsqueeze(1).broadcast_to([L, 4, L]),
                                                                     in1=av[:, 4 * q4:4 * q4 + 4].unsqueeze(2).broadcast_to([L, 4, L]), op=ALU.mult),
                      r=["SU", "dtt"], w=[aS.name])
                    ps, pn = nextps()
                    S.mm([lambda e, j=j, aS=aS: e.matmul(out=ps[:L, j * 128:j * 128 + L], lhsT=aS[:L, j, :L], rhs=triU[:L, :L], start=True, stop=True)
                          for j in range(4)], reads=[aS.name, "triU"], writes=[pn])
                    A("act", lambda e, ps=ps, dc=dc: e.activation(out=dc[:L, :, :L], in_=ps[:L, :].rearrange("p (a b) -> p a b", b=128)[:, :, :L],
                                                                  func=AF.Exp), r=[pn], w=[dc.name])
                    A("dve", lambda e, q4=q4, dc=dc: e.tensor_tensor(out=MT[:L, 4 * q4:4 * q4 + 4, :L], in0=dc[:L, :, :L],
                                                                     in1=cbm[:L, q4, :L].unsqueeze(1).broadcast_to([L, 4, L]), op=ALU.mult), r=[dc.name, "cbm"], w=["MT"])
                for half in range(2):
                    psd, pnd = nextps()
                    S.mm([lambda e, h=h: e.matmul(out=psd[:L, (h % 8) * 64:(h % 8) * 64 + 64], lhsT=MT[:L, h, :L], rhs=xdt[:L, h * 64:(h + 1) * 64],
                                                  start=True, stop=True) for h in range(8 * half, 8 * half + 8)], reads=["MT", "xdt"], writes=[pnd])
                    pso, pno = nextps()
                    S.mm([lambda e, g=g: e.matmul(out=pso[:L, (g % 2) * 256:(g % 2) * 256 + 256], lhsT=CTb[:, g, :L], rhs=Hb[:, g * 256:(g + 1) * 256],
                                                  start=True, stop=True) for g in range(2 * half, 2 * half + 2)], reads=["CTb", "Hb"], writes=[pno])
                    sl = slice(512 * half, 512 * half + 512)
                    A("dve", lambda e, pso=pso, sl=sl, half=half: e.tensor_tensor(
                        out=ytmp[:L, sl].rearrange("p (h q) -> p h q", q=64), in0=pso[:L, :].rearrange("p (h q) -> p h q", q=64),
                        in1=eacs[:, 8 * half:8 * half + 8].unsqueeze(2).broadcast_to([L, 8, 64]), op=ALU.mult), r=[pno, "dtt"], w=["ytmp"])
                    A("pool", lambda e, sl=sl: e.tensor_tensor(out=yacc[:L, sl], in0=yacc[:L, sl], in1=ytmp[:L, sl], op=ALU.add), r=["yacc", "ytmp"], w=["yacc"])
                    A("dve", lambda e, psd=psd, sl=sl: e.tensor_tensor(out=yacc[:L, sl], in0=yacc[:L, sl], in1=psd[:L, :], op=ALU.add), r=["yacc", pnd], w=["yacc"])
            for half in range(2):
                pss, pns = nextps()
                S.mm([lambda e, g=g: e.matmul(out=pss[:, (g % 2) * 256:(g % 2) * 256 + 256], lhsT=Btm[:L, g * 128:(g + 1) * 128],
                                              rhs=xdte[:L, g * 256:(g + 1) * 256], start=True, stop=True) for g in range(2 * half, 2 * half + 2)],
                     reads=["Btm", "xdte"], writes=[pns])
                sl = slice(512 * half, 512 * half + 512)
                A("dve", lambda e, sl=sl, half=half: e.tensor_tensor(out=H[:, sl].rearrange("p (h q) -> p h q", q=64), in0=H[:, sl].rearrange("p (h q) -> p h q", q=64),
                                                                     in1=cdb[:, 8 * half:8 * half + 8].unsqueeze(2).broadcast_to([128, 8, 64]), op=ALU.mult),
                  r=["H", "cdb", "Hb"], w=["H"])
                A("dve", lambda e, sl=sl, pss=pss: e.tensor_tensor(out=H[:, sl], in0=H[:, sl], in1=pss[:, :], op=ALU.add), r=["H", pns], w=["H"])
            if not full:
                return
            A("act", lambda e: e.activation(out=Hb[:, :], in_=H[:, :], func=AF.Copy), r=["H"], w=["Hb"])
            for half in range(2):
                ps, pn = nextps()
                S.mm([lambda e, kt=kt, half=half: e.matmul(out=ps[:L, :], lhsT=xnT[:, kt, :L], rhs=Win[:, kt, 512 * half:512 * half + 512],
                                                          start=(kt == 0), stop=(kt == 7)) for kt in range(8)], reads=["xnT", "Win"], writes=[pn])
                sl = slice(512 * half, 512 * half + 512)
                A("act", lambda e, ps=ps, sl=sl: e.activation(out=sz[:L, sl], in_=ps[:L, :], func=AF.Silu), r=[pn], w=["xn"])
            A("dve", lambda e: e.tensor_tensor(out=yacc[:L, :], in0=yacc[:L, :], in1=sz[:L, :], op=ALU.mult), r=["yacc", "xn"], w=["yacc"])
            for g in range(4):
                A("act", lambda e, g=g: e.activation(out=ytmp[:L, 256 * g:256 * g + 256], in_=yacc[:L, 256 * g:256 * g + 256], func=AF.Square, scale=1.0 / 16,
                                                     accum_out=ss[:L, 4 + g:5 + g]), r=["yacc"], w=["ytmp", "ss"])
            A("act", lambda e: e.activation(out=ss[:L, 4:8], in_=ss[:L, 4:8], func=AF.Ln, bias=epsT[:L, 0:1]), r=["ss", "epsT"], w=["ss"])
            A("act", lambda e: e.activation(out=ss[:L, 4:8], in_=ss[:L, 4:8], func=AF.Exp, scale=-0.5), r=["ss"], w=["ss"])
            A("dve", lambda e: e.tensor_tensor(out=yacc[:L, :].rearrange("p (g q) -> p g q", q=256), in0=yacc[:L, :].rearrange("p (g q) -> p g q", q=256),
                                               in1=ss[:L, 4:8].unsqueeze(2).broadcast_to([L, 4, 256]), op=ALU.mult), r=["yacc", "ss"], w=["yacc"])
            for half in range(2):
                ps, pn = nextps()
                S.mm([lambda e, j=j, half=half: e.transpose(out=ps[:, j * 128:j * 128 + L], in_=yacc[:L, (half * 4 + j) * 128:(half * 4 + j + 1) * 128],
                                                           identity=identf[:L, :L]) for j in range(4)], reads=["yacc", "identf"], writes=[pn])
                v = ps[:, :].rearrange("p (a b) -> p a b", b=128)[:, :, :L]
                for j in range(4):
                    A("act", lambda e, v=v, half=half, j=j: e.activation(out=cat_f[:, half * 4 + j, :L], in_=v[:, j, :], func=AF.Copy,
                                                                         scale=sng[:, half * 4 + j:half * 4 + j + 1]), r=[pn, "sngt"], w=["cat_f"])
            conv_all(list(range(8)), ["dve"] * 6 + ["pool"] * 2, c_f, gl_f, 31, ccw, ccb, L, "glf%d", ["ccwt", "ccbt"], "cf%d")
            if L >= HP:
                for t in range(8):
                    A("pool", lambda e, t=t: e.tensor_copy(out=gl_f[:, t, 0:HP], in_=gl_f[:, t, L:L + HP]), r=["glf%d" % t], w=["glf%d" % t])
            cfn = ["cf%d" % t for t in range(8)]
            A("act", lambda e: e.activation(out=csq[:, :, :L], in_=c_f[:, :, :L], func=AF.Square), r=cfn, w=["ytmp"])
            ps, pn = nextps()
            S.mm([lambda e, t=t: e.matmul(out=ps[:, 0:L], lhsT=onesf[:, :], rhs=c_f[:, t, :L], start=(t == 0), stop=(t == 7)) for t in range(8)] +
                 [lambda e, t=t: e.matmul(out=ps[:, 128:128 + L], lhsT=onesf[:, :], rhs=csq[:, t, :L], start=(t == 0), stop=(t == 7)) for t in range(8)],
                 reads=cfn + ["ytmp", "onesf"], writes=[pn])
            mean, ex2, var, rstd = [lnst[:, i, :L] for i in range(4)]
            A("dve", lambda e: e.tensor_scalar(out=mean, in0=ps[:, 0:L], scalar1=1.0 / 1024, scalar2=None, op0=ALU.mult), r=[pn], w=["dec0"])
            A("dve", lambda e: e.tensor_scalar(out=ex2, in0=ps[:, 128:128 + L], scalar1=1.0 / 1024, scalar2=None, op0=ALU.mult), r=[pn], w=["dec0"])
            A("dve", lambda e: e.tensor_tensor(out=var, in0=mean, in1=mean, op=ALU.mult), r=["dec0"], w=["dec0"])
            A("dve", lambda e: e.tensor_tensor(out=var, in0=ex2, in1=var, op=ALU.subtract), r=["dec0"], w=["dec0"])
            A("act", lambda e: e.activation(out=rstd, in_=var, func=AF.Ln, bias=epsT[:, 0:1]), r=["dec0", "epsT"], w=["dec0"])
            A("act", lambda e: e.activation(out=rstd, in_=rstd, func=AF.Exp, scale=-0.5), r=["dec0"], w=["dec0"])
            A("dve", lambda e: e.tensor_tensor(out=c_f[:, :, :L], in0=c_f[:, :, :L], in1=mean.unsqueeze(1).broadcast_to([128, 8, L]), op=ALU.subtract),
              r=cfn + ["dec0"], w=cfn)
            A("dve", lambda e: e.tensor_tensor(out=c_f[:, :, :L], in0=c_f[:, :, :L], in1=rstd.unsqueeze(1).broadcast_to([128, 8, L]), op=ALU.mult),
              r=cfn + ["dec0"], w=cfn)
            A("pool", lambda e: e.tensor_tensor(out=c_f[:, :, :L], in0=c_f[:, :, :L], in1=lng[:, :].unsqueeze(2).broadcast_to([128, 8, L]), op=ALU.mult),
              r=cfn + ["lngt"], w=cfn)
            A("pool", lambda e: e.tensor_tensor(out=c_f[:, :, :L], in0=c_f[:, :, :L], in1=lnb[:, :].unsqueeze(2).broadcast_to([128, 8, L]), op=ALU.add),
              r=cfn + ["lnbt"], w=cfn)
            A("act", lambda e: e.activation(out=c_f[:, :, :L], in_=c_f[:, :, :L], func=AF.Silu), r=cfn, w=cfn)
            A("dve", lambda e: e.tensor_tensor(out=cat_f[:, 8:16, :L], in0=c_f[:, :, :L], in1=scg[:, :, :L], op=ALU.mult), r=cfn + ["scg"], w=["cat_f"])
            for half in range(2):
                ps, pn = nextps()
                S.mm([lambda e, t=t, half=half: e.matmul(out=ps[:L, :], lhsT=cat_f[:, t, :L], rhs=Wout[:, t, 512 * half:512 * half + 512],
                                                        start=(t == 0), stop=(t == 15)) for t in range(16)], reads=["cat_f", "Wout"], writes=[pn])
                sl = slice(512 * half, 512 * half + 512)
                A("dve", lambda e, ps=ps, sl=sl: e.tensor_tensor(out=xn[:L, sl], in0=xt[:L, sl], in1=ps[:L, :], op=ALU.add), r=["xt", pn], w=["xn"])
            S.dma(dst_x1, xn[:L, :], reads=["xn"], writes=["x1_d"])

        def load_hist_from_state(b):
            S.dma(hist_tm[:3, :], st_sc[b, :, :], writes=XBCC)
            for g in range(4):
                ps, pn = nextps()
                S.mm([lambda e, j=j, g=g: e.transpose(out=ps[:, j * 128:j * 128 + 3], in_=hist_tm[:3, (4 * g + j) * 128:(4 * g + j + 1) * 128],
                                                      identity=identf[:3, :3]) for j in range(4)], reads=XBCC + ["identf"], writes=[pn])
                A("act", lambda e, ps=ps, g=g: e.activation(out=xbc_f[:, 4 * g:4 * g + 4, HP - 3:HP], in_=ps[:, :].rearrange("p (a b) -> p a b", b=128)[:, :, :3],
                                                            func=AF.Copy), r=[pn], w=["xbcf%d" % t for t in range(4 * g, 4 * g + 4)])
            S.dma(hist_tm[:30, :1024], st_cc[b, :, :], writes=XBCC)
            for g in range(2):
                ps, pn = nextps()
                S.mm([lambda e, j=j, g=g: e.transpose(out=ps[:, j * 128:j * 128 + 30], in_=hist_tm[:30, (4 * g + j) * 128:(4 * g + j + 1) * 128],
                                                      identity=identf[:30, :30]) for j in range(4)], reads=XBCC + ["identf"], writes=[pn])
                A("act", lambda e, ps=ps, g=g: e.activation(out=gl_f[:, 4 * g:4 * g + 4, HP - 30:HP], in_=ps[:, :].rearrange("p (a b) -> p a b", b=128)[:, :, :30],
                                                            func=AF.Copy), r=[pn], w=["glf%d" % t for t in range(4 * g, 4 * g + 4)])
            for g in range(2):
                S.dma(ytmp[:, 512 * g:512 * g + 512].rearrange("p (a n) -> p a n", n=128),
                      st_ssm[b, 512 * g:512 * g + 512, :].rearrange("(a p) n -> p a n", p=128), writes=["ytmp"])
                ps, pn = nextps()
                S.mm([lambda e, j=j, g=g: e.transpose(out=ps[:, j * 128:(j + 1) * 128], in_=ytmp[:, 512 * g + j * 128:512 * g + (j + 1) * 128],
                                                      identity=identf[:, :]) for j in range(4)], reads=["ytmp", "identf"], writes=[pn])
                A("dve", lambda e, ps=ps, g=g: e.tensor_copy(out=H[:, 512 * g:512 * g + 512], in_=ps[:, :]), r=[pn, "Hb"], w=["H"])
            A("act", lambda e: e.activation(out=Hb[:, :], in_=H[:, :], func=AF.Copy), r=["H"], w=["Hb"])

        def store_state_outputs(L, sc_dst, cc_dst, ssm_dst):
            for g in range(4):
                ps, pn = nextps()
                S.mm([lambda e, j=j, g=g: e.transpose(out=ps[:3, j * 128:(j + 1) * 128], in_=xbc_f[:, 4 * g + j, HP + L - 3:HP + L], identity=identf[:, :])
                      for j in range(4)], reads=["xbcf%d" % t for t in range(4 * g, 4 * g + 4)] + ["identf"], writes=[pn])
                A("act", lambda e, ps=ps, g=g: e.activation(out=hout[:3, 512 * g:512 * g + 512], in_=ps[:3, :], func=AF.Copy), r=[pn], w=XBCC)
            S.dma(sc_dst, hout[:3, :], reads=XBCC)
            for g in range(2):
                ps, pn = nextps()
                S.mm([lambda e, j=j, g=g: e.transpose(out=ps[:30, j * 128:(j + 1) * 128], in_=gl_f[:, 4 * g + j, HP + L - 30:HP + L], identity=identf[:, :])
                      for j in range(4)], reads=["glf%d" % t for t in range(4 * g, 4 * g + 4)] + ["identf"], writes=[pn])
                A("act", lambda e, ps=ps, g=g: e.activation(out=hout[:30, 512 * g:512 * g + 512], in_=ps[:30, :], func=AF.Copy), r=[pn], w=XBCC)
            S.dma(cc_dst, hout[:30, :1024], reads=XBCC)
            for g in range(2):
                ps, pn = nextps()
                S.mm([lambda e, j=j, g=g: e.transpose(out=ps[:, j * 128:(j + 1) * 128], in_=H[:, 512 * g + j * 128:512 * g + (j + 1) * 128], identity=identf[:, :])
                      for j in range(4)], reads=["H", "identf"], writes=[pn])
                A("dve", lambda e, ps=ps, g=g: e.tensor_copy(out=ytmp[:, 512 * g:512 * g + 512], in_=ps[:, :]), r=[pn], w=["ytmp"])
                S.dma(ssm_dst[512 * g:512 * g + 512, :].rearrange("(a p) n -> p a n", p=128),
                      ytmp[:, 512 * g:512 * g + 512].rearrange("p (a n) -> p a n", n=128), reads=["ytmp"])

        def zero_state():
            A("pool", lambda e: e.memset(H[:, :], 0.0), r=["Hb"], w=["H"])
            A("pool", lambda e: e.memset(Hb[:, :], 0.0), w=["Hb"])
            A("pool", lambda e: e.memset(Atot[:, :], 0.0), w=["Atot"])

        zero_state()
        for t in range(16):
            A("pool", lambda e, t=t: e.memset(xbc_f[:, t, 0:HP], 0.0), w=["xbcf%d" % t])
        for t in range(8):
            A("pool", lambda e, t=t: e.memset(gl_f[:, t, 0:HP], 0.0), w=["glf%d" % t])
        for c in range(NCHA):
            l0_chunk(xp[c * 128:(c + 1) * 128, :], 128, "full", dst_x1=x1_d[c * 128:(c + 1) * 128, :])
        store_state_outputs(128, sc_p[:, :], cc_p[:, :], ssm_p)
        for b in range(DB):
            load_hist_from_state(b)
            l0_chunk(xsm[b * 4:(b + 1) * 4, :], 4, "full", dst_x1=x1_d[SEQ + b * 4:SEQ + (b + 1) * 4, :])
            store_state_outputs(4, sc_s[b, :, :], cc_s[b, :, :], ssm_s[b])


        S.barrier()
        st0.close()
        if debug_l0:
            st1 = ExitStack(); cur[0] = st1
            xt = T("xt_dbg", [128, D])
            for c in range(NCH):
                S.dma(xt[:, :], x1_d[c * 128:(c + 1) * 128, :], reads=["x1_d"], writes=["xt"])
                S.dma(y_p[c * 128:(c + 1) * 128, :], xt[:, :], reads=["xt"])
            S.dma(xt[:DB * 4, :], x1_d[SEQ:SEQ + DB * 4, :], reads=["x1_d"], writes=["xt"])
            S.dma(y_s[:, :], xt[:DB * 4, :], reads=["xt"])
            S.finish("sp")
            st1.close()
            return nc
        st1 = ExitStack(); cur[0] = st1
        NSLOT = TPC // 128
        NB = 64
        psn[0] = 6
        Wq = T("Wq", [128, 8, 1024], BF16)
        Wg = T("Wg", [128, 8, 1024], BF16)
        Wo = T("Wo", [64, 16, 1024], BF16)
        g1 = T("g1t", [128, 8]); S.dma(g1[:], g1_d[:, :], writes=["g1t"])
        qg = T("qgt", [128, 1]); S.dma(qg[:], qg_d[:, :], writes=["qgt"])
        kgb = T("kgbt", [128, 256]); S.dma(kgb[:], kgb_d[:, :], writes=["kgbt"])
        pm = T("pmt", [128, 64])
        pneg = T("pnegt", [128, 64])
        om = T("omt", [128, 64])
        maskM = T("maskMt", [128, 8, 128]); S.dma(maskM[:], maskM_d[:, :, :], writes=["maskMt"])
        maskS = T("maskSt", [4, 16]); S.dma(maskS[:], maskS_d[:, :], writes=["maskSt"])
        oidx = T("oidxt", [128, NSLOT], I32); S.dma(oidx[:], oidx_d[:, :], writes=["oidx"])
        pidx = T("pidxt", [128, 1]); S.dma(pidx[:], pidx_d[:, :], writes=["pidx"])
        Eoh = T("Eoh", [64, 64, 128], BF16)
        A("pool", lambda e: e.memset(Eoh[:], 1.0), w=["Eoh"])
        A("pool", lambda e: e.affine_select(out=Eoh[:], in_=Eoh[:], pattern=[[-1, 64], [0, 128]], compare_op=ALU.is_equal, fill=0.0,
                                            base=0, channel_multiplier=1), r=["Eoh"], w=["Eoh"])
        xt1 = T("xt1", [128, D])
        xn1 = T("xn1", [128, D])
        ss1 = T("ss1", [128, 8])
        xT1 = T("xT1", [128, 8, 128], BF16)
        kms = T("kms", [128, 2, max(NCHA, NPG)])
        kmT = T("kmT", [128, 2, 64])
        A("pool", lambda e: e.memset(kmT[:], 0.0), w=["kmT"])
        stA = ExitStack(); cur[0] = stA
        Wkv = T("Wkv", [128, 8, 512], BF16)
        stg1 = [T("stg1a", [128, 1024]), T("stg1b", [128, 1024])]
        si = 0
        for (wd, wt, ncol, wname) in ((wq_d, Wq, 1024, "Wq"), (wkv_d, Wkv, 512, "Wkv"), (wg_d, Wg, 1024, "Wg")):
            for kt in range(8):
                sb = stg1[si % 2]
                S.dma(sb[:, :ncol], wd[:, kt, :], writes=[sb.name])
                A("dve" if si % 2 == 0 else "pool", lambda e, sb=sb, wt=wt, kt=kt, ncol=ncol: e.tensor_scalar(
                    out=wt[:, kt, :], in0=sb[:, :ncol], scalar1=g1[:, kt:kt + 1], scalar2=None, op0=ALU.mult), r=[sb.name, "g1t"], w=[wname])
                si += 1
        for h in range(16):
            sb = stg1[si % 2]
            S.dma(sb[:64, :], wo_d[:, h, :], writes=[sb.name])
            A("dve" if si % 2 == 0 else "pool", lambda e, sb=sb, h=h: e.tensor_copy(out=Wo[:, h, :], in_=sb[:64, :]), r=[sb.name], w=["Wo"])
            si += 1

        KV = T("KV", [128, 512])
        KTst = T("KTst", [128, 2, 128], BF16)
        Vb = T("Vb", [128, 4, 65], BF16)
        A("pool", lambda e: e.memset(Vb[:], 1.0), w=["Vb"])

        def l1_norm_T(L):
            A("act", lambda e: e.activation(out=xn1[:L, :], in_=xt1[:L, :], func=AF.Square, scale=1.0 / 32, accum_out=ss1[:L, 0:1]), r=["xt1"], w=["xn1", "ss1"])
            A("act", lambda e: e.activation(out=ss1[:L, 1:2], in_=ss1[:L, 0:1], func=AF.Ln, bias=epsT[:L, 0:1]), r=["ss1", "epsT"], w=["ss1"])
            A("act", lambda e: e.activation(out=ss1[:L, 2:3], in_=ss1[:L, 1:2], func=AF.Exp, scale=-0.5), r=["ss1"], w=["ss1"])
            A("dve", lambda e: e.tensor_scalar(out=xn1[:L, :], in0=xt1[:L, :], scalar1=ss1[:L, 2:3], scalar2=None, op0=ALU.mult), r=["xt1", "ss1"], w=["xn1"])
            for half in range(2):
                ps, pn = nextps()
                S.mm([lambda e, j=j, half=half: e.transpose(out=ps[:, j * 128:j * 128 + L], in_=xn1[:L, (half * 4 + j) * 128:(half * 4 + j + 1) * 128],
                                                           identity=identf[:L, :L]) for j in range(4)], reads=["xn1", "identf"], writes=[pn])
                v = ps[:, :].rearrange("p (a b) -> p a b", b=128)[:, :, :L]
                A("act", lambda e, v=v, half=half: e.activation(out=xT1[:, half * 4:half * 4 + 4, :L], in_=v, func=AF.Copy), r=[pn], w=["xT1"])

        def l1_kv(src, L, kdst, vdst, col0, chunk_idx):
            S.dma(xt1[:L, :], src, reads=["x1_d"], writes=["xt1"])
            l1_norm_T(L)
            ps, pn = nextps()
            S.mm([lambda e, kt=kt: e.matmul(out=ps[:L, :], lhsT=xT1[:, kt, :L], rhs=Wkv[:, kt, :], start=(kt == 0), stop=(kt == 7)) for kt in range(8)],
                 reads=["xT1", "Wkv"], writes=[pn])
            for h in range(4):
                A("act", lambda e, h=h: e.activation(out=xn1[:L, h * 64:(h + 1) * 64], in_=ps[:L, h * 64:(h + 1) * 64], func=AF.Square, scale=0.125,
                                                     accum_out=ss1[:L, 4 + h:5 + h]), r=[pn], w=["xn1", "ss1"])
            A("act", lambda e: e.activation(out=ss1[:L, 4:8], in_=ss1[:L, 4:8], func=AF.Ln, bias=epsT[:L, 0:1]), r=["ss1", "epsT"], w=["ss1"])
            A("act", lambda e: e.activation(out=ss1[:L, 4:8], in_=ss1[:L, 4:8], func=AF.Exp, scale=-0.5), r=["ss1"], w=["ss1"])
            A("dve", lambda e: e.tensor_tensor(out=KV[:L, 0:256].rearrange("p (h d) -> p h d", d=64), in0=ps[:L, 0:256].rearrange("p (h d) -> p h d", d=64),
                                               in1=ss1[:L, 4:8].unsqueeze(2).broadcast_to([L, 4, 64]), op=ALU.mult), r=[pn, "ss1"], w=["KV"])
            A("pool", lambda e: e.tensor_tensor(out=KV[:L, 0:256], in0=KV[:L, 0:256], in1=kgb[:L, :], op=ALU.mult), r=["KV", "kgbt"], w=["KV"])
            A("act", lambda e: e.activation(out=KV[:L, 256:512], in_=ps[:L, 256:512], func=AF.Copy), r=[pn], w=["KV"])
            S.dma(kdst, KV[:L, 0:256], reads=["KV"])
            S.dma(vdst, KV[:L, 256:512], reads=["KV"])
            ps2, pn2 = nextps()
            S.mm([lambda e, pr=pr: e.transpose(out=ps2[:, pr * 128:pr * 128 + L], in_=KV[:L, pr * 128:(pr + 1) * 128], identity=identf[:L, :L]) for pr in range(2)],
                 reads=["KV", "identf"], writes=[pn2])
            for pr in range(2):
                A("act", lambda e, pr=pr: e.activation(out=KTst[:, pr, :L], in_=ps2[:, pr * 128:pr * 128 + L], func=AF.Copy,
                                                       accum_out=kms[:, pr, chunk_idx:chunk_idx + 1]), r=[pn2], w=["KTst", "kms"])
                S.dma(KT_d[pr, :, col0:col0 + L], KTst[:, pr, :L], reads=["KTst"], writes=["KT_d"])
            A("pool", lambda e: e.tensor_copy(out=Vb[:L, :, 0:64], in_=KV[:L, 256:512].rearrange("p (h d) -> p h d", d=64)), r=["KV"], w=["Vb"])
            S.dma(V_d[col0:col0 + L, :], Vb[:L, :, :].rearrange("p h c -> p (h c)"), reads=["Vb"], writes=["V_d"])

        for t in range(NCHA):
            l1_kv(x1_d[t * 128:(t + 1) * 128, :], 128, k_p[t * 128:(t + 1) * 128, :], v_p[t * 128:(t + 1) * 128, :], t * 128, t)
        NBP = NCHA // 2
        kv2 = kms[:, :, 0:NCHA].rearrange("p a (n two) -> p a n two", two=2)
        A("dve", lambda e: e.tensor_tensor(out=kmT[:, :, 0:NBP], in0=kv2[:, :, :, 0], in1=kv2[:, :, :, 1], op=ALU.add), r=["kms"], w=["kmT"])
        A("dve", lambda e: e.tensor_scalar(out=kmT[:, :, 0:NBP], in0=kmT[:, :, 0:NBP], scalar1=1.0 / 256, scalar2=None, op0=ALU.mult), r=["kmT"], w=["kmT"])
        for b in range(DB):
            l1_kv(x1_d[SEQ + b * 4:SEQ + (b + 1) * 4, :], 4, k_s[b * 4:(b + 1) * 4, :], v_s[b * 4:(b + 1) * 4, :], SEQ + b * 4, NCHA - 1 if False else 0)

        S.barrier()
        stA.close()
        cur[0] = st1
        QF = T("QF", [128, 8, 128])
        sq1 = T("sq1", [128, 4, 128])
        QPf = T("QPf", [128, 16, 128])
        QPb = T("QPb", [128, 16, 128], BF16)
        SG = T("SG", [64, 16, 128], BF16)
        Gm = T("Gm", [128, 16, 64])
        m8 = T("m8", [128, 16, 8])
        bsel = Gm
        biasT = T("biasT", [64, 16, 128], BF16)
        ATg = T("ATg", [64, 16, 128], BF16)
        rdt = xn1
        bcs = T("bcs", [64, 512])
        atmp = T("atmp", [64, 512])
        PT = [T("PT0", [128, 512], BF16), T("PT1", [128, 512], BF16)]
        KTst2 = T("KTst2", [128, 2, 128], BF16)
        A("pool", lambda e: e.memset(QPf[:], 0.0), w=["QPf"])

        def l1_q(L):
            l1_norm_T(L)
            for half in range(2):
                ps, pn = nextps()
                S.mm([lambda e, r=r, kt=kt, half=half: e.matmul(out=ps[:, r * 128:r * 128 + L], lhsT=Wq[:, kt, (4 * half + r) * 128:(4 * half + r + 1) * 128],
                                                             rhs=xT1[:, kt, :L], start=(kt == 0), stop=(kt == 7)) for r in range(4) for kt in range(8)],
                     reads=["Wq", "xT1"], writes=[pn])
                v = ps[:, :].rearrange("p (a b) -> p a b", b=128)[:, :, :L]
                A("act", lambda e, v=v: e.activation(out=sq1[:, :, :L], in_=v, func=AF.Square, scale=0.125), r=[pn], w=["sq1"])
                pss, pns = nextps()
                S.mm([lambda e, r=r: e.matmul(out=pss[:, r * 128:r * 128 + L], lhsT=BD[:, :], rhs=sq1[:, r, :L], start=True, stop=True) for r in range(4)],
                     reads=["BD", "sq1"], writes=[pns])
                vs = pss[:, :].rearrange("p (a b) -> p a b", b=128)[:, :, :L]
                A("act", lambda e, vs=vs: e.activation(out=sq1[:, :, :L], in_=vs, func=AF.Ln, bias=epsT[:, 0:1]), r=[pns, "epsT"], w=["sq1"])
                A("act", lambda e: e.activation(out=sq1[:, :, :L], in_=sq1[:, :, :L], func=AF.Exp, scale=-0.5), r=["sq1"], w=["sq1"])
                A("dve", lambda e, v=v, half=half: e.tensor_tensor(out=QF[:, 4 * half:4 * half + 4, :L], in0=v, in1=sq1[:, :, :L], op=ALU.mult), r=[pn, "sq1", "QFk0", "QFk1", "QFv0", "QFv1"], w=["QF", "QFk0", "QFk1", "QFv0", "QFv1"])
            A("dve", lambda e: e.tensor_scalar(out=QF[:, :, :L], in0=QF[:, :, :L], scalar1=qg[:, 0:1], scalar2=None, op0=ALU.mult), r=["QF", "qgt"], w=["QF"])
            for hk in range(4):
                rows = slice(64 * (hk % 2), 64 * (hk % 2) + 64)
                t0 = (hk // 2) * 4
                A("pool", lambda e, hk=hk, rows=rows, t0=t0: e.tensor_copy(out=QPf[rows, 4 * hk:4 * hk + 4, :L], in_=QF[rows, t0:t0 + 4, :L]), r=["QF"], w=["QPf"])
            A("act", lambda e: e.activation(out=QPb[:, :, :L], in_=QPf[:, :, :L], func=AF.Copy), r=["QPf"], w=["QPb"])
            for bk in range(4):
                ps, pn = nextps()
                S.mm([lambda e, r=r, kt=kt, bk=bk: e.matmul(out=ps[0:64, r * 128:r * 128 + L], lhsT=Wg[:, kt, (4 * bk + r) * 64:(4 * bk + r + 1) * 64],
                                                           rhs=xT1[:, kt, :L], start=(kt == 0), stop=(kt == 7)) for r in range(4) for kt in range(8)],
                     reads=["Wg", "xT1"], writes=[pn])
                A("act", lambda e, ps=ps, bk=bk: e.activation(out=SG[:, 4 * bk:4 * bk + 4, :L], in_=ps[0:64, :].rearrange("p (a b) -> p a b", b=128)[:, :, :L],
                                                              func=AF.Silu), r=[pn], w=["SG"])

        def moba_select(L, kmt, kmname, slot):
            S.dma(pm[:, :], pm_d[:, slot, :], writes=["pmt"])
            S.dma(pneg[:, :], pneg_d[:, slot, :], writes=["pnegt"])
            S.dma(om[:, :], om_d[:, slot, :], writes=["omt"])
            for half in range(2):
                ps, pn = nextps()
                S.mm([lambda e, i=i, half=half: e.matmul(out=ps[:L, i * 64:(i + 1) * 64], lhsT=QPf[:, 8 * half + i, :L], rhs=kmt[:, (8 * half + i) // 8, :],
                                                        start=True, stop=True) for i in range(8)], reads=["QPf", kmname], writes=[pn])
                A("dve", lambda e, ps=ps, half=half: e.tensor_tensor(out=Gm[:L, 8 * half:8 * half + 8, :], in0=ps[:L, :].rearrange("p (a b) -> p a b", b=64),
                                                                     in1=pm[:L, :].unsqueeze(1).broadcast_to([L, 8, 64]), op=ALU.mult), r=[pn, "pmt"], w=["Gm"])
            A("dve", lambda e: e.tensor_tensor(out=Gm[:L, :, :], in0=Gm[:L, :, :], in1=pneg[:L, :].unsqueeze(1).broadcast_to([L, 16, 64]), op=ALU.add),
              r=["Gm", "pnegt"], w=["Gm"])
            for h in range(16):
                A("dve", lambda e, h=h: e.max(out=m8[:L, h, :], in_=Gm[:L, h, :]), r=["Gm"], w=["m8"])
            for h in range(16):
                A("dve", lambda e, h=h: e.tensor_scalar(out=bsel[:L, h, :], in0=Gm[:L, h, :], scalar1=m8[:L, h, 2:3], scalar2=None, op0=ALU.is_ge), r=["Gm", "m8"], w=["Gm"])
            A("dve", lambda e: e.tensor_tensor(out=bsel[:L, :, :], in0=bsel[:L, :, :], in1=pm[:L, :].unsqueeze(1).broadcast_to([L, 16, 64]), op=ALU.mult),
              r=["Gm", "pmt"], w=["Gm"])
            A("dve", lambda e: e.tensor_tensor(out=bsel[:L, :, :], in0=bsel[:L, :, :], in1=om[:L, :].unsqueeze(1).broadcast_to([L, 16, 64]), op=ALU.add),
              r=["Gm", "omt"], w=["Gm"])
            A("dve", lambda e: e.tensor_scalar(out=bsel[:L, :, :], in0=bsel[:L, :, :], scalar1=-NEG, scalar2=NEG, op0=ALU.mult, op1=ALU.add), r=["Gm"], w=["Gm"])
            for bk in range(4):
                ps, pn = nextps()
                S.mm([lambda e, r=r, bk=bk: e.transpose(out=ps[0:64, r * 128:r * 128 + L], in_=bsel[:L, 4 * bk + r, :], identity=identf[:L, :L]) for r in range(4)],
                     reads=["Gm", "identf"], writes=[pn])
                A("act", lambda e, ps=ps, bk=bk: e.activation(out=biasT[:, 4 * bk:4 * bk + 4, :L], in_=ps[0:64, :].rearrange("p (a b) -> p a b", b=128)[:, :, :L],
                                                              func=AF.Copy), r=[pn], w=["biasT"])

        def finish_group(L, hk, psO, pnO):
            W4 = 4 * L
            A("dve", lambda e: e.reciprocal(out=rdt[64:65, :W4], in_=psO[64:65, :W4]), r=[pnO], w=["xn1"])
            psb, pnb = nextps()
            S.mm([lambda e: e.matmul(out=psb[0:64, :W4], lhsT=onesf[64:65, 0:64], rhs=rdt[64:65, :W4], start=True, stop=True)], reads=["onesf", "xn1"], writes=[pnb])
            A("act", lambda e: e.activation(out=bcs[:, :W4], in_=psb[0:64, :W4], func=AF.Copy), r=[pnb], w=["bcs"])
            A("dve", lambda e: e.tensor_tensor(out=atmp[:, :W4], in0=psO[0:64, :W4], in1=bcs[:, :W4], op=ALU.mult), r=[pnO, "bcs"], w=["atmp"])
            A("pool", lambda e: e.tensor_tensor(out=ATg[:, 4 * hk:4 * hk + 4, :L], in0=atmp[:, :W4].rearrange("p (a b) -> p a b", b=L), in1=SG[:, 4 * hk:4 * hk + 4, :L],
                                                op=ALU.mult), r=["atmp", "SG"], w=["ATg"])

        def out_proj(L, dst):
            for half in range(2):
                ps, pn = nextps()
                S.mm([lambda e, h=h, half=half: e.matmul(out=ps[:L, :], lhsT=ATg[:, h, :L], rhs=Wo[:, h, 512 * half:512 * half + 512], start=(h == 0), stop=(h == 15))
                      for h in range(16)], reads=["ATg", "Wo"], writes=[pn])
                sl = slice(512 * half, 512 * half + 512)
                A("dve", lambda e, ps=ps, sl=sl: e.tensor_tensor(out=xn1[:L, sl], in0=xt1[:L, sl], in1=ps[:L, :], op=ALU.add), r=["xt1", pn], w=["xn1"])
            S.dma(dst, xn1[:L, :], reads=["xn1"])

        st2 = ExitStack(); cur[0] = st2
        NKT = NCHA
        NKB = max(NKT, NPG)
        KTp1 = T("KTp0", [128, NKB * 128], BF16)
        KTp = [KTp1, KTp1]
        Vp = [T("Vp%d" % i, [128, NKB, 65], BF16) for i in range(2)]
        bufi = 0
        for j in range(NSLOT):
            nkt = 8 * j + 8
            S.idma(xt1[:, :], x1_d[:, :], oidx[:, j:j + 1], reads=["x1_d", "oidx"], writes=["xt1"])
            l1_q(128)
            moba_select(128, kmT, "kmT", j)
            for hk in range(4):
                pr = hk // 2
                if hk % 2 == 0:
                    ktp = KTp[pr]
                    S.dma(ktp[:, :nkt * 128], KT_d[pr, :, 0:nkt * 128], reads=["KT_d"], writes=[ktp.name])
                vp = Vp[hk % 2]
                S.dma(vp[:, :nkt, :], V_d[0:nkt * 128, hk * 65:(hk + 1) * 65].rearrange("(k p) c -> p k c", p=128), reads=["V_d"], writes=[vp.name])
                psO, pnO = pst[7 - (hk % 2)], "ps%d" % (7 - (hk % 2))
                for kt in range(nkt):
                    ps, pn = nextps(6)
                    S.mm([lambda e, ps=ps, kt=kt, ktp=ktp, hk=hk: e.matmul(out=ps[:, :], lhsT=ktp[:, kt * 128:(kt + 1) * 128],
                                                                         rhs=QPb[:, 4 * hk:4 * hk + 4, :].rearrange("p a b -> p (a b)"), start=True, stop=False),
                          lambda e, ps=ps, kt=kt, hk=hk: e.matmul(out=ps[:, :], lhsT=Eoh[:, kt // 2, :], rhs=biasT[:, 4 * hk:4 * hk + 4, :].rearrange("p a b -> p (a b)"),
                                                                start=False, stop=True)], reads=[ktp.name, "QPb", "Eoh", "biasT"], writes=[pn])
                    pt = PT[kt % 2]
                    A("act", lambda e, ps=ps, pt=pt: e.activation(out=pt[:, :], in_=ps[:, :], func=AF.Exp, scale=0.125), r=[pn], w=[pt.name])
                    if kt >= 8 * j:
                        A("dve", lambda e, pt=pt, kt=kt, j=j: e.tensor_tensor(out=pt[:, :].rearrange("p (a b) -> p a b", b=128), in0=pt[:, :].rearrange("p (a b) -> p a b", b=128),
                                                                             in1=maskM[:, kt - 8 * j, :].unsqueeze(1).broadcast_to([128, 4, 128]), op=ALU.mult),
                          r=[pt.name, "maskMt"], w=[pt.name])
                    S.mm([lambda e, kt=kt, vp=vp, pt=pt, psO=psO, nkt=nkt: e.matmul(out=psO[0:65, :], lhsT=vp[:, kt, :], rhs=pt[:, :], start=(kt == 0), stop=(kt == nkt - 1))],
                         reads=[vp.name, pt.name], writes=[pnO])
                finish_group(128, hk, psO, pnO)
            out_proj(128, y_p[j * 128:(j + 1) * 128, :])

        PTs = Gm[:, :, :].rearrange("p a b -> p (a b)").bitcast(BF16)
        PTn = T("PTn", [4, 16], BF16)
        KTn = T("KTn", [128, 2, 4], BF16)
        Vn = T("Vn", [4, 4, 65], BF16)
        ptf = m8[:, :, :].rearrange("p a b -> p (a b)")[:, :NPG]
        pti = T("pti", [128, NPG], I32)
        kmTs = kmT
        VsA = T("VsA", [128, 4, 65], BF16)
        QFl = QF[:, :, :].rearrange("p a b -> p (a b)")
        Kpg = [QFl[:, 0:256], QFl[:, 256:512]]
        Vpg = [QFl[:, 512:768], QFl[:, 768:1024]]
        A("pool", lambda e: e.memset(VsA[:], 1.0), w=["VsA"])
        A("pool", lambda e: e.memset(kmTs[:], 0.0), w=["kmT"])
        NBS = NPG // 2
        for b in range(DB):
            S.dma(pti[:, :], ptb_d[:, b, :], writes=["pti"])
            A("dve", lambda e: e.tensor_copy(out=ptf[:, :], in_=pti[:, :]), r=["pti"], w=["m8"])
            A("dve", lambda e: e.tensor_scalar(out=ptf[:, :], in0=ptf[:, :], scalar1=128.0, scalar2=pidx[:, 0:1], op0=ALU.mult, op1=ALU.add), r=["m8", "pidx"], w=["m8"])
            A("dve", lambda e: e.tensor_copy(out=pti[:, :], in_=ptf[:, :]), r=["m8"], w=["pti"])
            for pg in range(NPG):
                kp, vq = Kpg[pg % 2], Vpg[pg % 2]
                kn_, vn_ = "QFk%d" % (pg % 2), "QFv%d" % (pg % 2)
                S.idma(kp, ck_d[:, :], pti[:, pg:pg + 1], reads=["pti", "QF"], writes=[kn_])
                S.idma(vq, cv_d[:, :], pti[:, pg:pg + 1], reads=["pti", "QF"], writes=[vn_])
                ps, pn = nextps()
                S.mm([lambda e, pr=pr, kp=kp, ps=ps: e.transpose(out=ps[:, pr * 128:(pr + 1) * 128], in_=kp[:, pr * 128:(pr + 1) * 128], identity=identf[:, :]) for pr in range(2)],
                     reads=[kn_, "identf"], writes=[pn])
                for pr in range(2):
                    A("act", lambda e, pr=pr, pg=pg, ps=ps: e.activation(out=KTst2[:, pr, :], in_=ps[:, pr * 128:(pr + 1) * 128], func=AF.Copy,
                                                                       accum_out=kms[:, pr, pg:pg + 1]), r=[pn], w=["KTst2", "kms"])
                    S.dma(KTs_d[b, pr, :, pg * 128:(pg + 1) * 128], KTst2[:, pr, :], reads=["KTst2"], writes=["KTs_d"])
                A("pool", lambda e, vq=vq: e.tensor_copy(out=VsA[:, :, 0:64], in_=vq.rearrange("p (h d) -> p h d", d=64)), r=[vn_], w=["VsA"])
                S.dma(Vs_d[b, pg * 128:(pg + 1) * 128, :], VsA[:, :, :].rearrange("p h c -> p (h c)"), reads=["VsA"], writes=["Vs_d"])
            kv3 = kms[:, :, 0:NPG].rearrange("p a (n two) -> p a n two", two=2)
            A("dve", lambda e: e.tensor_tensor(out=kmTs[:, :, 0:NBS], in0=kv3[:, :, :, 0], in1=kv3[:, :, :, 1], op=ALU.add), r=["kms"], w=["kmT"])
            A("dve", lambda e: e.tensor_scalar(out=kmTs[:, :, 0:NBS], in0=kmTs[:, :, 0:NBS], scalar1=1.0 / 256, scalar2=None, op0=ALU.mult), r=["kmT"], w=["kmT"])
            S.dma(KTn[:, :, :], KT_d[:, :, SEQ + b * 4:SEQ + (b + 1) * 4].rearrange("a p c -> p a c"), reads=["KT_d"], writes=["KTn"])
            S.dma(Vn[:, :, :].rearrange("p h c -> p (h c)"), V_d[SEQ + b * 4:SEQ + (b + 1) * 4, :], reads=["V_d"], writes=["Vn"])
            S.dma(xt1[:4, :], x1_d[SEQ + b * 4:SEQ + (b + 1) * 4, :], reads=["x1_d"], writes=["xt1"])
            l1_q(4)
            moba_select(4, kmTs, "kmT", NSLOT)
            for hk in range(4):
                pr = hk // 2
                if hk % 2 == 0:
                    S.dma(KTp1[:, :NPG * 128], KTs_d[b, pr, :, :], reads=["KTs_d"], writes=["KTp0"])
                vp = Vp[hk % 2]
                S.dma(vp[:, :NPG, :], Vs_d[b, :, hk * 65:(hk + 1) * 65].rearrange("(k p) c -> p k c", p=128), reads=["Vs_d"], writes=[vp.name])
                qrhs = QPb[:, 4 * hk:4 * hk + 4, :4]
                brhs = biasT[:, 4 * hk:4 * hk + 4, :4]
                PPB = 32
                for bk in range((NPG + PPB - 1) // PPB):
                    ps, pn = nextps()
                    pgs = list(range(bk * PPB, min(NPG, (bk + 1) * PPB)))
                    fns = []
                    for pg in pgs:
                        o = ps[:, (pg - bk * PPB) * 16:(pg - bk * PPB + 1) * 16].rearrange("p (a b) -> p a b", b=4)
                        fns.append(lambda e, o=o, pg=pg, qrhs=qrhs: e.matmul(out=o, lhsT=KTp1[:, pg * 128:(pg + 1) * 128], rhs=qrhs, start=True, stop=False))
                        fns.append(lambda e, o=o, pg=pg, brhs=brhs: e.matmul(out=o, lhsT=Eoh[:, pg // 2, :], rhs=brhs, start=False, stop=True))
                    S.mm(fns, reads=["KTp0", "QPb", "Eoh", "biasT"], writes=[pn])
                    ncol = len(pgs) * 16
                    A("act", lambda e, ps=ps, bk=bk, ncol=ncol: e.activation(out=PTs[:, bk * PPB * 16:bk * PPB * 16 + ncol], in_=ps[:, :ncol], func=AF.Exp, scale=0.125),
                      r=[pn], w=["Gm"])
                ps, pn = nextps()
                S.mm([lambda e, ps=ps, pr=pr, qrhs=qrhs: e.matmul(out=ps[0:4, 0:16].rearrange("p (a b) -> p a b", b=4), lhsT=KTn[:, pr, :], rhs=qrhs, start=True, stop=True)],
                     reads=["KTn", "QPb"], writes=[pn])
                A("act", lambda e, ps=ps: e.activation(out=PTn[:, :], in_=ps[0:4, 0:16], func=AF.Exp, scale=0.125), r=[pn], w=["PTn"])
                A("dve", lambda e: e.tensor_tensor(out=PTn[:, :], in0=PTn[:, :], in1=maskS[:, :], op=ALU.mult), r=["PTn", "maskSt"], w=["PTn"])
                psO, pnO = pst[7 - (hk % 2)], "ps%d" % (7 - (hk % 2))
                fns = [lambda e, pg=pg, vp=vp, psO=psO: e.matmul(out=psO[0:65, 0:16], lhsT=vp[:, pg, :], rhs=PTs[:, pg * 16:(pg + 1) * 16], start=(pg == 0), stop=False)
                       for pg in range(NPG)]
                fns.append(lambda e, hk=hk, psO=psO: e.matmul(out=psO[0:65, 0:16], lhsT=Vn[:, hk, :], rhs=PTn[:, :], start=False, stop=True))
                S.mm(fns, reads=[vp.name, "Gm", "Vn", "PTn"], writes=[pnO])
                finish_group(4, hk, psO, pnO)
            out_proj(4, y_s[b * 4:(b + 1) * 4, :])
        S.barrier()
        st2.close()
        st1.close()
        S.finish("sp")
        print("instructions:", S.n_instr, "sem counts", S.cnt)
    return nc


def prep_inputs(cfg, inp):
    f = lambda a: np.ascontiguousarray(np.asarray(a, dtype=np.float32))
    TPC, DB = cfg.tpc, cfg.db
    xp_full = f(inp["x_prompt"])[0]
    common = {
        "w_in0": f(f(inp["w_in0"])[0].reshape(8, 128, 6160).transpose(1, 0, 2)),
        "w_out0": f(f(inp["w_out0"])[0].reshape(16, 128, 1024).transpose(1, 0, 2)),
        "g0": f(f(inp["norm0_g"])[0].reshape(8, 128).T),
        "cw": f(f(inp["ssd_conv_w"])[0].reshape(4, 16, 128).transpose(2, 1, 0)),
        "cb": f(f(inp["ssd_conv_b"])[0].reshape(16, 128).T),
        "ccw": f(f(inp["conf_conv_w"])[0].reshape(31, 8, 128).transpose(2, 1, 0)),
        "ccb": f(f(inp["conf_conv_b"])[0].reshape(8, 128).T),
        "lng": f(f(inp["conf_ln_g"])[0].reshape(8, 128).T),
        "lnb": f(f(inp["conf_ln_b"])[0].reshape(8, 128).T),
        "dtb": f(np.broadcast_to(f(inp["ssd_dt_bias"])[0][None, :], (128, 16))),
        "alog": f(np.broadcast_to(f(inp["ssd_a_log"])[0][None, :], (128, 16))),
        "dsk": f(np.broadcast_to(f(inp["ssd_d"])[0][None, :], (128, 16))),
        "sng": f(f(inp["ssd_norm_g"])[0].reshape(8, 128).T),
    }
    perm = [0, 4, 1, 5, 2, 6, 3, 7, 8, 12, 9, 13, 10, 14, 11, 15]
    w1 = f(inp["w_in1"])[0]
    wq = w1[:, :1024].reshape(1024, 16, 64)[:, perm, :].reshape(1024, 1024)
    wkv = w1[:, 1024:1536]
    wg = w1[:, 1536:2560]
    kt_layout = lambda w: f(w.reshape(8, 128, w.shape[1]).transpose(1, 0, 2))
    NSLOT = TPC // 128
    NPG = cfg.npg
    npool = cfg.npool
    common.update({
        "wq": kt_layout(wq), "wkv": kt_layout(wkv), "wg": kt_layout(wg),
        "wo": f(f(inp["w_out1"])[0].reshape(16, 64, 1024).transpose(1, 0, 2)),
        "g1": f(f(inp["norm1_g"])[0].reshape(8, 128).T),
        "qg": f(np.tile(f(inp["q_norm_g"])[0], 2).reshape(128, 1)),
        "kgb": f(np.broadcast_to(np.tile(f(inp["k_norm_g"])[0], 4)[None, :], (128, 256))),
        "maskS": f(np.tile((np.arange(4)[:, None] <= np.arange(4)[None, :]).astype(np.float32), (1, 4))),
        "pidx": f(np.arange(128, dtype=np.float32).reshape(128, 1)),
        "ck": f(inp["cache_k"]).reshape(npool * 128, 256),
        "cv": f(inp["cache_v"]).reshape(npool * 128, 256),
    })
    pt_all = np.ascontiguousarray(np.asarray(inp["page_table"], dtype=np.int32))
    tri = (np.arange(128)[:, None] <= np.arange(128)[None, :]).astype(np.float32)
    maps = []
    for c in range(NCORE):
        m = dict(common)
        pm = np.zeros((NSLOT + 1, 64), np.float32)
        om = np.zeros((NSLOT + 1, 64), np.float32)
        for j in range(NSLOT):
            own = (8 * j + c) // 2
            pm[j, :own] = 1.0
            om[j, own] = 1.0
        pm[NSLOT, :NPG // 2] = 1.0
        bc = lambda a: f(np.broadcast_to(a[None], (128,) + a.shape))
        m["pm"] = bc(pm)
        m["pneg"] = bc((pm - 1.0) * np.float32(1e30))
        m["om"] = bc(om)
        mm_ = np.ones((128, 8, 128), np.float32)
        for dl in range(8):
            if dl // 2 == c // 2:
                if dl == c:
                    mm_[:, dl, :] = tri
                elif dl > c:
                    mm_[:, dl, :] = 0.0
        m["maskM"] = mm_
        m["oidx"] = np.ascontiguousarray(((8 * np.arange(NSLOT)[None, :] + c) * 128 + np.arange(128)[:, None]).astype(np.int32))
        m["ptb"] = np.ascontiguousarray(np.broadcast_to(pt_all[c * DB:(c + 1) * DB][None], (128, DB, NPG)).astype(np.int32))
        m["xp"] = xp_full
        m["xsm"] = f(f(inp["x_sample"])[c * DB:(c + 1) * DB].reshape(DB * 4, D))
        m["st_ssm"] = f(f(inp["state_ssm"])[0, c * DB:(c + 1) * DB].reshape(DB, 1024, 128))
        m["st_sc"] = f(f(inp["state_ssd_conv"])[0, c * DB:(c + 1) * DB])
        m["st_cc"] = f(f(inp["state_conf_conv"])[0, c * DB:(c + 1) * DB])
        cm = np.zeros((128, 8), np.float32)
        cm[:, :c] = 1.0
        m["cmask"] = cm
        maps.append(m)
    return maps


_NC_CACHE = {}


def run(cfg, inp, debug_l0=False):
    key = (cfg.seq, cfg.dbt, cfg.npg, debug_l0)
    if key not in _NC_CACHE:
        _NC_CACHE[key] = build(cfg, debug_l0)
    nc = _NC_CACHE[key]
    maps = prep_inputs(cfg, inp)
    res = run_bass_kernel_spmd(nc, maps, core_ids=list(range(NCORE)))
    R = res.results
    cat = lambda k: np.concatenate([r[k] for r in R], axis=0)
    TPC, DB = cfg.tpc, cfg.db
    y_p = cat("y_p")[None]
    y_s = cat("y_s").reshape(cfg.dbt, 4, D)
    ssm_p = R[0]["ssm_p"].reshape(1, 1, 16, 64, 128)
    ssm_s = cat("ssm_s").reshape(1, cfg.dbt, 16, 64, 128)
    sc_p = R[0]["sc_p"].reshape(1, 1, 3, 2048)
    sc_s = cat("sc_s").reshape(1, cfg.dbt, 3, 2048)
    cc_p = R[0]["cc_p"].reshape(1, 1, 30, 1024)
    cc_s = cat("cc_s").reshape(1, cfg.dbt, 30, 1024)
    if debug_l0:
        return (y_p, y_s, ssm_p, ssm_s, sc_p, sc_s, cc_p, cc_s)
    NSLOT = TPC // 128
    yp = np.empty((cfg.seq, D), np.float32)
    for c in range(NCORE):
        for j in range(NSLOT):
            t = 8 * j + c
            yp[t * 128:(t + 1) * 128] = R[c]["y_p"][j * 128:(j + 1) * 128]
    y_p = yp[None]
    k_p = R[0]["k_p"].reshape(1, 1, cfg.seq, 4, 64)
    v_p = R[0]["v_p"].reshape(1, 1, cfg.seq, 4, 64)
    k_s = cat("k_s").reshape(1, cfg.dbt, 4, 4, 64)
    v_s = cat("v_s").reshape(1, cfg.dbt, 4, 4, 64)
    return (y_p, y_s, ssm_p, ssm_s, sc_p, sc_s, cc_p, cc_s, k_p, v_p, k_s, v_s)


def kernel(**inputs):
    cfg = Cfg(inputs["x_prompt"].shape[1], inputs["x_sample"].shape[0], inputs["page_table"].shape[1] * 128)
    return run(cfg, inputs)
```

```python
import numpy as np
from contextlib import ExitStack
import concourse.bass as bass
import concourse.mybir as mybir
from concourse.bass_utils import run_bass_kernel_spmd

F32 = mybir.dt.float32
BF16 = mybir.dt.bfloat16
I32 = mybir.dt.int32
ALU = mybir.AluOpType
AF = mybir.ActivationFunctionType
AX = mybir.AxisListType

NCORE = 8
D = 1024
HP = 32
EPS = 1e-6
NEG = -30000.0


class Sched:
    def __init__(self, nc, stack, n_dma_sems=24):
        self.nc = nc
        self.engs = {"pe": nc.tensor, "act": nc.scalar, "dve": nc.vector, "pool": nc.gpsimd, "sp": nc.sync}
        self.sem = {k: stack.enter_context(nc.semaphore("s_" + k)) for k in ("pe", "act", "dve", "pool")}
        self.cnt = {k: 0 for k in self.sem}
        self.dsem = [stack.enter_context(nc.semaphore("d%d" % i)) for i in range(n_dma_sems)]
        self.dval = [0] * n_dma_sems
        self.dnext = 0
        self.waited = {}
        self.lastw = {}
        self.reads = {}
        self.n_instr = 0

    def _semobj(self, key):
        return self.sem[key] if isinstance(key, str) else self.dsem[key[1]]

    def _wait(self, eng, key, val):
        if self.waited.get((eng, key), 0) >= val:
            return
        self.waited[(eng, key)] = val
        self.engs[eng].wait_ge(self._semobj(key), val)

    def _deps(self, eng, reads, writes):
        for b in reads:
            t = self.lastw.get(b)
            if t is not None:
                self._wait(eng, t[0], t[1])
        for b in writes:
            t = self.lastw.get(b)
            if t is not None:
                self._wait(eng, t[0], t[1])
            for k, v in self.reads.get(b, {}).items():
                if k != eng:
                    self._wait(eng, k, v)

    def _record(self, key, val, reads, writes):
        for b in reads:
            d = self.reads.setdefault(b, {})
            if d.get(key, 0) < val:
                d[key] = val
        for b in writes:
            self.lastw[b] = (key, val)
            self.reads[b] = {}

    def op(self, eng, fn, reads=(), writes=()):
        self._deps(eng, reads, writes)
        ins = fn(self.engs[eng])
        self.cnt[eng] += 1
        ins.then_inc(self.sem[eng], 1)
        self._record(eng, self.cnt[eng], reads, writes)
        self.n_instr += 1

    def mm(self, fns, reads=(), writes=()):
        self._deps("pe", reads, writes)
        ins = None
        for fn in fns:
            ins = fn(self.nc.tensor)
            self.n_instr += 1
        self.cnt["pe"] += 1
        ins.then_inc(self.sem["pe"], 1)
        self._record("pe", self.cnt["pe"], reads, writes)

    def dma(self, out, in_, reads=(), writes=(), q="sp", **kw):
        i = self.dnext
        self.dnext = (self.dnext + 1) % len(self.dsem)
        key = ("d", i)
        if self.dval[i]:
            self._wait(q, key, self.dval[i])
        self._deps(q, reads, writes)
        self.dval[i] += 16
        ins = self.engs[q].dma_start(out=out, in_=in_, **kw)
        ins.then_inc(self.dsem[i], 16)
        self._record(key, self.dval[i], reads, writes)
        self.n_instr += 1

    def idma(self, out, in_, idx_ap, reads=(), writes=()):
        q = "pool"
        i = self.dnext
        self.dnext = (self.dnext + 1) % len(self.dsem)
        key = ("d", i)
        if self.dval[i]:
            self._wait(q, key, self.dval[i])
        self._deps(q, reads, writes)
        self.dval[i] += 16
        ins = self.nc.gpsimd.indirect_dma_start(out=out, out_offset=None, in_=in_,
                                                in_offset=bass.IndirectOffsetOnAxis(ap=idx_ap, axis=0))
        ins.then_inc(self.dsem[i], 16)
        self._record(key, self.dval[i], reads, writes)
        self.n_instr += 1

    def barrier(self):
        for e in ("pe", "act", "dve", "pool", "sp"):
            for i, v in enumerate(self.dval):
                if v:
                    self._wait(e, ("d", i), v)
            for k, v in self.cnt.items():
                if v and k != e:
                    self._wait(e, k, v)

    def finish(self, eng="sp"):
        for i, v in enumerate(self.dval):
            if v:
                self._wait(eng, ("d", i), v)
        for k, v in self.cnt.items():
            if v:
                self._wait(eng, k, v)


class Cfg:
    def __init__(self, seq, dec_batch, past_len):
        self.seq = seq
        self.tpc = seq // NCORE
        self.nch = self.tpc // 128
        self.dbt = dec_batch
        self.db = dec_batch // NCORE
        self.npg = past_len // 128
        n_used = dec_batch * self.npg
        self.npool = n_used + (n_used + 3) // 4
        self.nblk_p = seq // 256
        self.bpc = self.tpc // 256


def build(cfg, debug_l0=False):
    nc = bass.Bass("TRN2", target_bir_lowering=False)
    TPC, NCH, DB, NPG = cfg.tpc, cfg.nch, cfg.db, cfg.npg
    SEQ = cfg.seq
    NCHA = SEQ // 128

    def din(name, shape, dt=F32):
        return nc.dram_tensor(name, list(shape), dt, kind="ExternalInput").ap()

    def dout(name, shape, dt=F32):
        return nc.dram_tensor(name, list(shape), dt, kind="ExternalOutput").ap()

    xp = din("xp", [SEQ, D])
    xsm = din("xsm", [DB * 4, D])
    st_ssm = din("st_ssm", [DB, 1024, 128])
    st_sc = din("st_sc", [DB, 3, 2048])
    st_cc = din("st_cc", [DB, 30, 1024])
    w_in0 = din("w_in0", [128, 8, 6160])
    w_out0 = din("w_out0", [128, 16, 1024])
    g0_d = din("g0", [128, 8])
    cw_d = din("cw", [128, 16, 4])
    cb_d = din("cb", [128, 16])
    ccw_d = din("ccw", [128, 8, 31])
    ccb_d = din("ccb", [128, 8])
    lng_d = din("lng", [128, 8])
    lnb_d = din("lnb", [128, 8])
    dtb_d = din("dtb", [128, 16])
    alog_d = din("alog", [128, 16])
    dsk_d = din("dsk", [128, 16])
    sng_d = din("sng", [128, 8])
    cmask_d = din("cmask", [128, 8])
    NSLOT_ = TPC // 128
    wq_d = din("wq", [128, 8, 1024])
    wkv_d = din("wkv", [128, 8, 512])
    wg_d = din("wg", [128, 8, 1024])
    wo_d = din("wo", [64, 16, 1024])
    g1_d = din("g1", [128, 8])
    qg_d = din("qg", [128, 1])
    kgb_d = din("kgb", [128, 256])
    pm_d = din("pm", [128, NSLOT_ + 1, 64])
    pneg_d = din("pneg", [128, NSLOT_ + 1, 64])
    om_d = din("om", [128, NSLOT_ + 1, 64])
    maskM_d = din("maskM", [128, 8, 128])
    maskS_d = din("maskS", [4, 16])
    oidx_d = din("oidx", [128, NSLOT_], I32)
    pidx_d = din("pidx", [128, 1])
    ptb_d = din("ptb", [128, DB, NPG], I32)
    ck_d = din("ck", [cfg.npool * 128, 256])
    cv_d = din("cv", [cfg.npool * 128, 256])
    k_p = dout("k_p", [SEQ, 256])
    v_p = dout("v_p", [SEQ, 256])
    k_s = dout("k_s", [DB * 4, 256])
    v_s = dout("v_s", [DB * 4, 256])

    y_p = dout("y_p", [TPC, D])
    y_s = dout("y_s", [DB * 4, D])
    ssm_p = dout("ssm_p", [1024, 128])
    ssm_s = dout("ssm_s", [DB, 1024, 128])
    sc_p = dout("sc_p", [3, 2048])
    sc_s = dout("sc_s", [DB, 3, 2048])
    cc_p = dout("cc_p", [30, 1024])
    cc_s = dout("cc_s", [DB, 30, 1024])

    x1_d = nc.dram_tensor("x1_d", [SEQ + DB * 4, D], F32, kind="Internal").ap()
    KT_d = nc.dram_tensor("KT_d", [2, 128, SEQ + DB * 4], BF16, kind="Internal").ap()
    V_d = nc.dram_tensor("V_d", [SEQ + DB * 4, 260], BF16, kind="Internal").ap()
    KTs_d = nc.dram_tensor("KTs_d", [DB, 2, 128, NPG * 128], BF16, kind="Internal").ap()
    Vs_d = nc.dram_tensor("Vs_d", [DB, NPG * 128, 260], BF16, kind="Internal").ap()

    st = ExitStack()
    with st:
        S = Sched(nc, st)
        A = lambda eng, fn, r=(), w=(): S.op(eng, fn, reads=r, writes=w)

        cur = [st]

        def T(name, shape, dt=F32):
            return cur[0].enter_context(nc.sbuf_tensor(name, list(shape), dt))

        pst = [st.enter_context(nc.psum_tensor("ps%d" % i, [128, 512], F32)) for i in range(8)]
        psi = [0]

        psn = [8]

        def nextps(n=None):
            n = n or psn[0]
            i = psi[0] % n
            psi[0] = (i + 1) % n
            return pst[i], "ps%d" % i

        identf = T("identf", [128, 128])
        triU = T("triU", [128, 128])
        SU = T("SU", [128, 128])
        onesf = T("onesf", [128, 128])
        epsT = T("epsT", [128, 1])
        oneT = T("oneT", [128, 1])
        for t_, cmp_, sgn in ((identf, ALU.is_equal, 1), (triU, ALU.is_ge, -1)):
            A("pool", lambda e, t_=t_: e.memset(t_[:], 1.0), w=[t_.name])
            A("pool", lambda e, t_=t_, cmp_=cmp_, sgn=sgn: e.affine_select(out=t_[:], in_=t_[:], pattern=[[-sgn, 128]], compare_op=cmp_,
                                                          fill=0.0, base=0, channel_multiplier=sgn), r=[t_.name], w=[t_.name])
        A("dve", lambda e: e.tensor_scalar(out=SU[:], in0=triU[:], scalar1=-1.0, scalar2=1.0, op0=ALU.mult, op1=ALU.add), r=["triU"], w=["SU"])
        A("pool", lambda e: e.memset(onesf[:], 1.0), w=["onesf"])
        A("pool", lambda e: e.memset(epsT[:], EPS), w=["epsT"])
        A("pool", lambda e: e.memset(oneT[:], 1.0), w=["oneT"])

        BD = T("BD", [128, 128])
        A("pool", lambda e: e.memset(BD[:], 0.0), w=["BD"])
        A("pool", lambda e: e.memset(BD[0:64, 0:64], 1.0), r=["BD"], w=["BD"])
        A("pool", lambda e: e.memset(BD[64:128, 64:128], 1.0), r=["BD"], w=["BD"])
        st0 = ExitStack()
        cur[0] = st0
        def ld(name, src, shape):
            t = T(name, shape)
            S.dma(t[:], src, writes=[name])
            return t
        g0 = ld("g0t", g0_d[:, :], [128, 8])
        cw = ld("cwt", cw_d[:, :, :], [128, 16, 4])
        cb = ld("cbt", cb_d[:, :], [128, 16])
        ccw = ld("ccwt", ccw_d[:, :, :], [128, 8, 31])
        ccb = ld("ccbt", ccb_d[:, :], [128, 8])
        lng = ld("lngt", lng_d[:, :], [128, 8])
        lnb = ld("lnbt", lnb_d[:, :], [128, 8])
        dtb = ld("dtbt", dtb_d[:, :], [128, 16])
        Ab = ld("Abt", alog_d[:, :], [128, 16])
        dsk = ld("dskt", dsk_d[:, :], [128, 16])
        sng = ld("sngt", sng_d[:, :], [128, 8])
        cmask = ld("cmaskt", cmask_d[:, :], [128, 8])
        A("act", lambda e: e.activation(out=Ab[:], in_=Ab[:], func=AF.Exp), r=["Abt"], w=["Abt"])
        A("dve", lambda e: e.tensor_scalar(out=Ab[:], in0=Ab[:], scalar1=-1.0, scalar2=None, op0=ALU.mult), r=["Abt"], w=["Abt"])

        Win = T("Win", [128, 8, 6160], BF16)
        Wout = T("Wout", [128, 16, 1024], BF16)
        xbc_c = T("xbc_c", [128, 16, 128])
        xbc_flat = xbc_c[:, :, :].rearrange("p a b -> p (a b)")
        XBCC = ["xbcc%d" % t for t in range(16)]
        stg = [xbc_flat[:, 0:770], xbc_flat[:, 1024:1024 + 770]]
        stgn = [XBCC[:8], XBCC[8:]]
        si = 0
        cast_engs = ["dve", "pool"]
        for kt in range(8):
            for q8 in range(8):
                sb = stg[si % 2]
                S.dma(sb, w_in0[:, kt, q8 * 770:(q8 + 1) * 770], writes=stgn[si % 2])
                A(cast_engs[si % 2], lambda e, sb=sb, kt=kt, q8=q8: e.tensor_scalar(
                    out=Win[:, kt, q8 * 770:(q8 + 1) * 770], in0=sb, scalar1=g0[:, kt:kt + 1], scalar2=None, op0=ALU.mult),
                    r=stgn[si % 2] + ["g0t"], w=["Win"])
                si += 1
        for t_ in range(16):
            for hf in range(2):
                sb = stg[si % 2]
                S.dma(sb[:, :512], w_out0[:, t_, hf * 512:(hf + 1) * 512], writes=stgn[si % 2])
                A(cast_engs[si % 2], lambda e, sb=sb, t_=t_, hf=hf: e.tensor_copy(out=Wout[:, t_, hf * 512:(hf + 1) * 512], in_=sb[:, :512]), r=stgn[si % 2], w=["Wout"])
                si += 1

        xt = T("xt", [128, D])
        xn = T("xn", [128, D])
        ss = T("ss", [128, 8])
        xnT = T("xnT", [128, 8, 128], BF16)
        xbc_f = T("xbc_f", [128, 16, HP + 128])
        gl_f = T("gl_f", [128, 8, HP + 128])
        scg = T("scg", [128, 8, 128], BF16)
        c_f = T("c_f", [128, 8, 128])
        cat_f = T("cat_f", [128, 16, 128], BF16)
        CTb = T("CTb", [128, 4, 128], BF16)
        BTb = T("BTb", [128, 4, 128], BF16)
        Btm = T("Btm", [128, 512], BF16)
        dtt = T("dtt", [128, 8, 16])
        aSU4 = [T("aSU0", [128, 4, 128])] * 2
        dec4 = [T("dec0", [128, 4, 128])] * 2
        cbm = T("cbm", [128, 4, 128])
        MT = T("MT", [128, 16, 128], BF16)
        xdt = T("xdt", [128, 1024], BF16)
        xdte = T("xdte", [128, 1024], BF16)
        yacc = T("yacc", [128, 1024])
        ytmp = T("ytmp", [128, 1024])
        H = T("H", [128, 1024])
        Hb = T("Hb", [128, 1024], BF16)
        ptmp2 = [T("ptmpa", [128, 128])] * 2
        cdb = T("cdb", [128, 16])
        Atot = T("Atot", [128, 16])
        hist_tm = xbc_flat[:32, :]
        hout = xbc_flat[:32, :]
        sz = xn
        csq = ytmp[:, :].rearrange("p (a b) -> p a b", b=128)
        sig = aSU4[0]
        lnst = dec4[0]

        def fm_inproj(L, col0s, evac):
            ps, pn = nextps()
            fns = []
            for j, c0 in enumerate(col0s):
                for kt in range(8):
                    fns.append(lambda e, j=j, c0=c0, kt=kt: e.matmul(out=ps[:, j * 128:j * 128 + L], lhsT=Win[:, kt, c0:c0 + 128],
                                                                   rhs=xnT[:, kt, :L], start=(kt == 0), stop=(kt == 7)))
            S.mm(fns, reads=["Win", "xnT"], writes=[pn])
            v = ps[:, :].rearrange("p (a b) -> p a b", b=128)[:, :len(col0s), :L]
            evac(v, pn)

        def transposes_to_tm(L, srcs, src_names, nm):
            ps, pn = nextps()
            S.mm([lambda e, j=j, s=s: e.transpose(out=ps[:L, j * 128:(j + 1) * 128], in_=s, identity=identf[:, :])
                  for j, s in enumerate(srcs)], reads=list(src_names) + ["identf"], writes=[pn])
            return ps[:L, :len(srcs) * 128], pn

        def conv_all(tiles, engs, out_tile, src_tile, ntap, w_t, b_t, L, src_pref, w_names, out_pref):
            o0 = HP - (ntap - 1)
            for j in range(ntap):
                for t in tiles:
                    eng = engs[t]
                    out_ap = out_tile[:, t, :L]
                    rn = [src_pref % t] + w_names
                    wn = out_pref % t
                    src = src_tile[:, t, o0 + j:o0 + j + L]
                    if j == 0:
                        A(eng, lambda e, out_ap=out_ap, src=src, t=t: e.tensor_scalar(out=out_ap, in0=src, scalar1=w_t[:, t, 0:1], scalar2=b_t[:, t:t + 1],
                                                                                    op0=ALU.mult, op1=ALU.add), r=rn, w=[wn])
                    elif eng == "dve":
                        A(eng, lambda e, out_ap=out_ap, src=src, t=t, j=j: e.scalar_tensor_tensor(out=out_ap, in0=src, scalar=w_t[:, t, j:j + 1], in1=out_ap,
                                                                                              op0=ALU.mult, op1=ALU.add), r=rn + [wn], w=[wn])
                    else:
                        pt_ = ptmp2[t % 2]
                        A(eng, lambda e, src=src, t=t, j=j, pt_=pt_: e.tensor_tensor(out=pt_[:, :L], in0=src, in1=w_t[:, t, j:j + 1].broadcast_to([128, L]), op=ALU.mult),
                          r=rn, w=[pt_.name])
                        A(eng, lambda e, out_ap=out_ap, pt_=pt_: e.tensor_tensor(out=out_ap, in0=out_ap, in1=pt_[:, :L], op=ALU.add), r=[pt_.name, wn], w=[wn])

        def l0_chunk(src_ap, L, mode, dst_x1=None):
            full = mode == "full"
            nx = 16 if (full or mode == "halo2") else 12
            S.dma(xt[:L, :], src_ap, writes=["xt"])
            A("act", lambda e: e.activation(out=xn[:L, :], in_=xt[:L, :], func=AF.Square, scale=1.0 / 32, accum_out=ss[:L, 0:1]),
              r=["xt"], w=["xn", "ss"])
            A("act", lambda e: e.activation(out=ss[:L, 1:2], in_=ss[:L, 0:1], func=AF.Ln, bias=epsT[:L, 0:1]), r=["ss", "epsT"], w=["ss"])
            A("act", lambda e: e.activation(out=ss[:L, 2:3], in_=ss[:L, 1:2], func=AF.Exp, scale=-0.5), r=["ss"], w=["ss"])
            A("dve", lambda e: e.tensor_scalar(out=xn[:L, :], in0=xt[:L, :], scalar1=ss[:L, 2:3], scalar2=None, op0=ALU.mult),
              r=["xt", "ss"], w=["xn"])
            for half in range(2):
                ps, pn = nextps()
                S.mm([lambda e, j=j: e.transpose(out=ps[:, j * 128:j * 128 + L], in_=xn[:L, (half * 4 + j) * 128:(half * 4 + j + 1) * 128],
                                                 identity=identf[:L, :L]) for j in range(4)], reads=["xn", "identf"], writes=[pn])
                v = ps[:, :].rearrange("p (a b) -> p a b", b=128)[:, :, :L]
                A("act", lambda e, v=v, half=half: e.activation(out=xnT[:, half * 4:half * 4 + 4, :L], in_=v, func=AF.Copy), r=[pn], w=["xnT"])
            ps, pn = nextps()
            S.mm([lambda e, kt=kt: e.matmul(out=ps[:L, 0:16], lhsT=xnT[:, kt, :L], rhs=Win[:, kt, 3072:3088], start=(kt == 0), stop=(kt == 7))
                  for kt in range(8)], reads=["xnT", "Win"], writes=[pn])
            dtr, dta, dte, dtl, dtv, av, acs, eacs = [dtt[:L, i, :] for i in range(8)]
            A("dve", lambda e: e.tensor_tensor(out=dtr, in0=ps[:L, 0:16], in1=dtb[:L, :], op=ALU.add), r=[pn, "dtbt"], w=["dtt"])
            A("dve", lambda e: e.scalar_tensor_tensor(out=dta, in0=dtr, scalar=-1.0, in1=dtr, op0=ALU.mult, op1=ALU.min), r=["dtt"], w=["dtt"])
            A("act", lambda e: e.activation(out=dte, in_=dta, func=AF.Exp), r=["dtt"], w=["dtt"])
            A("act", lambda e: e.activation(out=dtl, in_=dte, func=AF.Ln, bias=oneT[:L, 0:1]), r=["dtt", "oneT"], w=["dtt"])
            A("dve", lambda e: e.scalar_tensor_tensor(out=dtv, in0=dtr, scalar=0.0, in1=dtl, op0=ALU.max, op1=ALU.add), r=["dtt"], w=["dtt"])
            A("dve", lambda e: e.tensor_tensor(out=av, in0=dtv, in1=Ab[:L, :], op=ALU.mult), r=["dtt", "Abt"], w=["dtt"])
            ps2, pn2 = nextps()
            S.mm([lambda e: e.matmul(out=ps2[:L, 0:16], lhsT=triU[:L, :L], rhs=av, start=True, stop=True),
                  lambda e: e.matmul(out=ps2[:, 16:32], lhsT=onesf[:L, :], rhs=av, start=True, stop=True)],
                 reads=["triU", "onesf", "dtt"], writes=[pn2])
            A("dve", lambda e: e.tensor_copy(out=acs, in_=ps2[:L, 0:16]), r=[pn2], w=["dtt"])
            A("act", lambda e: e.activation(out=cdb[:, :], in_=ps2[:, 16:32], func=AF.Exp), r=[pn2], w=["cdb"])
            A("dve", lambda e: e.tensor_tensor(out=Atot[:, :], in0=Atot[:, :], in1=ps2[:, 16:32], op=ALU.add), r=[pn2, "Atot"], w=["Atot"])
            A("dve", lambda e: e.tensor_tensor(out=dta, in0=ps2[:L, 16:32], in1=acs, op=ALU.subtract), r=[pn2, "dtt"], w=["dtt"])
            A("act", lambda e: e.activation(out=dta, in_=dta, func=AF.Exp), r=["dtt"], w=["dtt"])
            A("act", lambda e: e.activation(out=eacs, in_=acs, func=AF.Exp), r=["dtt"], w=["dtt"])
            if full:
                for half in range(2):
                    ps, pn = nextps()
                    S.mm([lambda e, kt=kt, half=half: e.matmul(out=ps[:L, :], lhsT=xnT[:, kt, :L], rhs=Win[:, kt, 512 * half:512 * half + 512],
                                                              start=(kt == 0), stop=(kt == 7)) for kt in range(8)], reads=["xnT", "Win"], writes=[pn])
                    sl = slice(512 * half, 512 * half + 512)
                    A("act", lambda e, ps=ps, sl=sl: e.activation(out=sz[:L, sl], in_=ps[:L, :], func=AF.Silu), r=[pn], w=["xn"])

            for g in range(nx // 4):
                def ev(v, pn, g=g):
                    A("act", lambda e: e.activation(out=xbc_f[:, 4 * g:4 * g + 4, HP:HP + L], in_=v, func=AF.Copy), r=[pn],
                      w=["xbcf%d" % t for t in range(4 * g, 4 * g + 4)])
                fm_inproj(L, [1024 + 128 * t for t in range(4 * g, 4 * g + 4)], ev)
            if full or mode == "halo2":
                for g in range(2):
                    def evb(v, pn):
                        A("act", lambda e: e.activation(out=sig[:, :, :L], in_=v, func=AF.Sigmoid), r=[pn], w=["aSU0"])
                    fm_inproj(L, [4112 + 128 * t for t in range(4 * g, 4 * g + 4)], evb)

                    def eva(v, pn, g=g):
                        A("dve", lambda e: e.tensor_tensor(out=gl_f[:, 4 * g:4 * g + 4, HP:HP + L], in0=v, in1=sig[:, :, :L], op=ALU.mult),
                          r=[pn, "aSU0"], w=["glf%d" % t for t in range(4 * g, 4 * g + 4)])
                    fm_inproj(L, [3088 + 128 * t for t in range(4 * g, 4 * g + 4)], eva)
            if mode in ("halo1", "halo2"):
                for t in range(nx):
                    A("pool", lambda e, t=t: e.tensor_copy(out=xbc_f[:, t, 0:HP], in_=xbc_f[:, t, HP:2 * HP]), r=["xbcf%d" % t], w=["xbcf%d" % t])
                if mode == "halo2":
                    for t in range(8):
                        A("pool", lambda e, t=t: e.tensor_copy(out=gl_f[:, t, 0:HP], in_=gl_f[:, t, HP:2 * HP]), r=["glf%d" % t], w=["glf%d" % t])
                return
            if full:
                for g in range(2):
                    def evc(v, pn, g=g):
                        A("act", lambda e: e.activation(out=scg[:, 4 * g:4 * g + 4, :L], in_=v, func=AF.Silu), r=[pn], w=["scg"])
                    fm_inproj(L, [5136 + 128 * t for t in range(4 * g, 4 * g + 4)], evc)
            conv_all(list(range(nx)), ["dve"] * 16, xbc_c, xbc_f, 4, cw, cb, L, "xbcf%d", ["cwt", "cbt"], "xbcc%d")
            for g in range(nx // 4):
                A("act", lambda e, g=g: e.activation(out=xbc_c[:, 4 * g:4 * g + 4, :L], in_=xbc_c[:, 4 * g:4 * g + 4, :L], func=AF.Silu),
                  r=["xbcc%d" % t for t in range(4 * g, 4 * g + 4)], w=["xbcc%d" % t for t in range(4 * g, 4 * g + 4)])
            if L >= HP:
                for t in range(nx):
                    A("pool", lambda e, t=t: e.tensor_copy(out=xbc_f[:, t, 0:HP], in_=xbc_f[:, t, L:L + HP]), r=["xbcf%d" % t], w=["xbcf%d" % t])
            for g in range(2):
                v, pn = transposes_to_tm(L, [xbc_c[:, 4 * g + j, :L] for j in range(4)], ["xbcc%d" % (4 * g + j) for j in range(4)], "xs")
                v3 = v.rearrange("p (h q) -> p h q", q=64)
                sl = slice(512 * g, 512 * (g + 1))
                A("dve", lambda e, v3=v3, sl=sl, g=g: e.tensor_tensor(out=xdt[:L, sl].rearrange("p (h q) -> p h q", q=64), in0=v3,
                                                                      in1=dtv[:, 8 * g:8 * g + 8].unsqueeze(2).broadcast_to([L, 8, 64]), op=ALU.mult),
                  r=[pn, "dtt"], w=["xdt"])
                if full:
                    A("dve", lambda e, v=v, sl=sl, g=g: e.tensor_tensor(out=yacc[:L, sl].rearrange("p (h q) -> p h q", q=64), in0=v.rearrange("p (h q) -> p h q", q=64), in1=dsk[:L, 8 * g:8 * g + 8].unsqueeze(2).broadcast_to([L, 8, 64]), op=ALU.mult),
                      r=[pn, "dskt"], w=["yacc"])
            v, pn = transposes_to_tm(L, [xbc_c[:, 8 + j, :L] for j in range(4)], ["xbcc%d" % (8 + j) for j in range(4)], "B")
            A("act", lambda e: e.activation(out=Btm[:L, :], in_=v, func=AF.Copy), r=[pn], w=["Btm"])
            A("dve", lambda e: e.tensor_tensor(out=xdte[:L, :].rearrange("p (h q) -> p h q", q=64), in0=xdt[:L, :].rearrange("p (h q) -> p h q", q=64),
                                               in1=dta.unsqueeze(2).broadcast_to([L, 16, 64]), op=ALU.mult), r=["xdt", "dtt"], w=["xdte"])
            if full:
                A("pool", lambda e: e.tensor_copy(out=BTb[:, :, :L], in_=xbc_c[:, 8:12, :L]), r=["xbcc%d" % t for t in range(8, 12)], w=["BTb"])
                A("pool", lambda e: e.tensor_copy(out=CTb[:, :, :L], in_=xbc_c[:, 12:16, :L]), r=["xbcc%d" % t for t in range(12, 16)], w=["CTb"])
                psc, pnc = nextps()
                S.mm([lambda e, g=g: e.matmul(out=psc[:L, g * 128:g * 128 + L], lhsT=BTb[:, g, :L], rhs=CTb[:, g, :L], start=True, stop=True)
                      for g in range(4)], reads=["BTb", "CTb"], writes=[pnc])
                A("dve", lambda e: e.tensor_tensor(out=cbm[:L, :, :L], in0=psc[:L, :].rearrange("p (a b) -> p a b", b=128)[:, :, :L],
                                                   in1=triU[:L, :L].unsqueeze(1).broadcast_to([L, 4, L]), op=ALU.mult), r=[pnc, "triU"], w=["cbm"])
                for q4 in range(4):
                    aS, dc = aSU4[q4 % 2], dec4[q4 % 2]
                    A("pool", lambda e, aS=aS, q4=q4: e.tensor_tensor(out=aS[:L, :, :L], in0=SU[:L, :L].unsqueeze(1).broadcast_to([L, 4, L]),
                                                                     in1=av[:, 4 * q4:4 * q4 + 4].unsqueeze(2).broadcast_to([L, 4, L]), op=ALU.mult),
                      r=["SU", "dtt"], w=[aS.name])
                    ps, pn = nextps()
                    S.mm([lambda e, j=j, aS=aS: e.matmul(out=ps[:L, j * 128:j * 128 + L], lhsT=aS[:L, j, :L], rhs=triU[:L, :L], start=True, stop=True)
                          for j in range(4)], reads=[aS.name, "triU"], writes=[pn])
                    A("act", lambda e, ps=ps, dc=dc: e.activation(out=dc[:L, :, :L], in_=ps[:L, :].rearrange("p (a b) -> p a b", b=128)[:, :, :L],
                                                                  func=AF.Exp), r=[pn], w=[dc.name])
                    A("dve", lambda e, q4=q4, dc=dc: e.tensor_tensor(out=MT[:L, 4 * q4:4 * q4 + 4, :L], in0=dc[:L, :, :L],
                                                                     in1=cbm[:L, q4, :L].unsqueeze(1).broadcast_to([L, 4, L]), op=ALU.mult), r=[dc.name, "cbm"], w=["MT"])
                for half in range(2):
                    psd, pnd = nextps()
                    S.mm([lambda e, h=h: e.matmul(out=psd[:L, (h % 8) * 64:(h % 8) * 64 + 64], lhsT=MT[:L, h, :L], rhs=xdt[:L, h * 64:(h + 1) * 64],
                                                  start=True, stop=True) for h in range(8 * half, 8 * half + 8)], reads=["MT", "xdt"], writes=[pnd])
                    pso, pno = nextps()
                    S.mm([lambda e, g=g: e.matmul(out=pso[:L, (g % 2) * 256:(g % 2) * 256 + 256], lhsT=CTb[:, g, :L], rhs=Hb[:, g * 256:(g + 1) * 256],
                                                  start=True, stop=True) for g in range(2 * half, 2 * half + 2)], reads=["CTb", "Hb"], writes=[pno])
                    sl = slice(512 * half, 512 * half + 512)
                    A("dve", lambda e, pso=pso, sl=sl, half=half: e.tensor_tensor(
                        out=ytmp[:L, sl].rearrange("p (h q) -> p h q", q=64), in0=pso[:L, :].rearrange("p (h q) -> p h q", q=64),
                        in1=eacs[:, 8 * half:8 * half + 8].unsqueeze(2).broadcast_to([L, 8, 64]), op=ALU.mult), r=[pno, "dtt"], w=["ytmp"])
                    A("pool", lambda e, sl=sl: e.tensor_tensor(out=yacc[:L, sl], in0=yacc[:L, sl], in1=ytmp[:L, sl], op=ALU.add), r=["yacc", "ytmp"], w=["yacc"])
                    A("dve", lambda e, psd=psd, sl=sl: e.tensor_tensor(out=yacc[:L, sl], in0=yacc[:L, sl], in1=psd[:L, :], op=ALU.add), r=["yacc", pnd], w=["yacc"])
            for half in range(2):
                pss, pns = nextps()
                S.mm([lambda e, g=g: e.matmul(out=pss[:, (g % 2) * 256:(g % 2) * 256 + 256], lhsT=Btm[:L, g * 128:(g + 1) * 128],
                                              rhs=xdte[:L, g * 256:(g + 1) * 256], start=True, stop=True) for g in range(2 * half, 2 * half + 2)],
                     reads=["Btm", "xdte"], writes=[pns])
                sl = slice(512 * half, 512 * half + 512)
                A("dve", lambda e, sl=sl, half=half: e.tensor_tensor(out=H[:, sl].rearrange("p (h q) -> p h q", q=64), in0=H[:, sl].rearrange("p (h q) -> p h q", q=64),
                                                                     in1=cdb[:, 8 * half:8 * half + 8].unsqueeze(2).broadcast_to([128, 8, 64]), op=ALU.mult),
                  r=["H", "cdb", "Hb"], w=["H"])
                A("dve", lambda e, sl=sl, pss=pss: e.tensor_tensor(out=H[:, sl], in0=H[:, sl], in1=pss[:, :], op=ALU.add), r=["H", pns], w=["H"])
            if not full:
                return
            A("act", lambda e: e.activation(out=Hb[:, :], in_=H[:, :], func=AF.Copy), r=["H"], w=["Hb"])
            A("dve", lambda e: e.tensor_tensor(out=yacc[:L, :], in0=yacc[:L, :], in1=sz[:L, :], op=ALU.mult), r=["yacc", "xn"], w=["yacc"])
            for g in range(4):
                A("act", lambda e, g=g: e.activation(out=ytmp[:L, 256 * g:256 * g + 256], in_=yacc[:L, 256 * g:256 * g + 256], func=AF.Square, scale=1.0 / 16,
                                                     accum_out=ss[:L, 4 + g:5 + g]), r=["yacc"], w=["ytmp", "ss"])
            A("act", lambda e: e.activation(out=ss[:L, 4:8], in_=ss[:L, 4:8], func=AF.Ln, bias=epsT[:L, 0:1]), r=["ss", "epsT"], w=["ss"])
            A("act", lambda e: e.activation(out=ss[:L, 4:8], in_=ss[:L, 4:8], func=AF.Exp, scale=-0.5), r=["ss"], w=["ss"])
            A("dve", lambda e: e.tensor_tensor(out=yacc[:L, :].rearrange("p (g q) -> p g q", q=256), in0=yacc[:L, :].rearrange("p (g q) -> p g q", q=256),
                                               in1=ss[:L, 4:8].unsqueeze(2).broadcast_to([L, 4, 256]), op=ALU.mult), r=["yacc", "ss"], w=["yacc"])
            for half in range(2):
                ps, pn = nextps()
                S.mm([lambda e, j=j, half=half: e.transpose(out=ps[:, j * 128:j * 128 + L], in_=yacc[:L, (half * 4 + j) * 128:(half * 4 + j + 1) * 128],
                                                           identity=identf[:L, :L]) for j in range(4)], reads=["yacc", "identf"], writes=[pn])
                v = ps[:, :].rearrange("p (a b) -> p a b", b=128)[:, :, :L]
                for j in range(4):
                    A("act", lambda e, v=v, half=half, j=j: e.activation(out=cat_f[:, half * 4 + j, :L], in_=v[:, j, :], func=AF.Copy,
                                                                         scale=sng[:, half * 4 + j:half * 4 + j + 1]), r=[pn, "sngt"], w=["cat_f"])
            conv_all(list(range(8)), ["dve"] * 7 + ["pool"] * 1, c_f, gl_f, 31, ccw, ccb, L, "glf%d", ["ccwt", "ccbt"], "cf%d")
            if L >= HP:
                for t in range(8):
                    A("pool", lambda e, t=t: e.tensor_copy(out=gl_f[:, t, 0:HP], in_=gl_f[:, t, L:L + HP]), r=["glf%d" % t], w=["glf%d" % t])
            cfn = ["cf%d" % t for t in range(8)]
            A("act", lambda e: e.activation(out=csq[:, :, :L], in_=c_f[:, :, :L], func=AF.Square), r=cfn, w=["ytmp"])
            ps, pn = nextps()
            S.mm([lambda e, t=t: e.matmul(out=ps[:, 0:L], lhsT=onesf[:, :], rhs=c_f[:, t, :L], start=(t == 0), stop=(t == 7)) for t in range(8)] +
                 [lambda e, t=t: e.matmul(out=ps[:, 128:128 + L], lhsT=onesf[:, :], rhs=csq[:, t, :L], start=(t == 0), stop=(t == 7)) for t in range(8)],
                 reads=cfn + ["ytmp", "onesf"], writes=[pn])
            mean, ex2, var, rstd = [lnst[:, i, :L] for i in range(4)]
            A("dve", lambda e: e.tensor_scalar(out=mean, in0=ps[:, 0:L], scalar1=1.0 / 1024, scalar2=None, op0=ALU.mult), r=[pn], w=["dec0"])
            A("dve", lambda e: e.tensor_scalar(out=ex2, in0=ps[:, 128:128 + L], scalar1=1.0 / 1024, scalar2=None, op0=ALU.mult), r=[pn], w=["dec0"])
            A("dve", lambda e: e.tensor_tensor(out=var, in0=mean, in1=mean, op=ALU.mult), r=["dec0"], w=["dec0"])
            A("dve", lambda e: e.tensor_tensor(out=var, in0=ex2, in1=var, op=ALU.subtract), r=["dec0"], w=["dec0"])
            A("act", lambda e: e.activation(out=rstd, in_=var, func=AF.Ln, bias=epsT[:, 0:1]), r=["dec0", "epsT"], w=["dec0"])
            A("act", lambda e: e.activation(out=rstd, in_=rstd, func=AF.Exp, scale=-0.5), r=["dec0"], w=["dec0"])
            A("dve", lambda e: e.tensor_tensor(out=c_f[:, :, :L], in0=c_f[:, :, :L], in1=mean.unsqueeze(1).broadcast_to([128, 8, L]), op=ALU.subtract),
              r=cfn + ["dec0"], w=cfn)
            A("dve", lambda e: e.tensor_tensor(out=c_f[:, :, :L], in0=c_f[:, :, :L], in1=rstd.unsqueeze(1).broadcast_to([128, 8, L]), op=ALU.mult),
              r=cfn + ["dec0"], w=cfn)
            A("pool", lambda e: e.tensor_tensor(out=c_f[:, :, :L], in0=c_f[:, :, :L], in1=lng[:, :].unsqueeze(2).broadcast_to([128, 8, L]), op=ALU.mult),
              r=cfn + ["lngt"], w=cfn)
            A("pool", lambda e: e.tensor_tensor(out=c_f[:, :, :L], in0=c_f[:, :, :L], in1=lnb[:, :].unsqueeze(2).broadcast_to([128, 8, L]), op=ALU.add),
              r=cfn + ["lnbt"], w=cfn)
            A("act", lambda e: e.activation(out=c_f[:, :, :L], in_=c_f[:, :, :L], func=AF.Silu), r=cfn, w=cfn)
            A("dve", lambda e: e.tensor_tensor(out=cat_f[:, 8:16, :L], in0=c_f[:, :, :L], in1=scg[:, :, :L], op=ALU.mult), r=cfn + ["scg"], w=["cat_f"])
            for half in range(2):
                ps, pn = nextps()
                S.mm([lambda e, t=t, half=half: e.matmul(out=ps[:L, :], lhsT=cat_f[:, t, :L], rhs=Wout[:, t, 512 * half:512 * half + 512],
                                                        start=(t == 0), stop=(t == 15)) for t in range(16)], reads=["cat_f", "Wout"], writes=[pn])
                sl = slice(512 * half, 512 * half + 512)
                A("dve", lambda e, ps=ps, sl=sl: e.tensor_tensor(out=xn[:L, sl], in0=xt[:L, sl], in1=ps[:L, :], op=ALU.add), r=["xt", pn], w=["xn"])
            S.dma(dst_x1, xn[:L, :], reads=["xn"], writes=["x1_d"])

        def load_hist_from_state(b):
            S.dma(hist_tm[:3, :], st_sc[b, :, :], writes=XBCC)
            for g in range(4):
                ps, pn = nextps()
                S.mm([lambda e, j=j, g=g: e.transpose(out=ps[:, j * 128:j * 128 + 3], in_=hist_tm[:3, (4 * g + j) * 128:(4 * g + j + 1) * 128],
                                                      identity=identf[:3, :3]) for j in range(4)], reads=XBCC + ["identf"], writes=[pn])
                A("act", lambda e, ps=ps, g=g: e.activation(out=xbc_f[:, 4 * g:4 * g + 4, HP - 3:HP], in_=ps[:, :].rearrange("p (a b) -> p a b", b=128)[:, :, :3],
                                                            func=AF.Copy), r=[pn], w=["xbcf%d" % t for t in range(4 * g, 4 * g + 4)])
            S.dma(hist_tm[:30, :1024], st_cc[b, :, :], writes=XBCC)
            for g in range(2):
                ps, pn = nextps()
                S.mm([lambda e, j=j, g=g: e.transpose(out=ps[:, j * 128:j * 128 + 30], in_=hist_tm[:30, (4 * g + j) * 128:(4 * g + j + 1) * 128],
                                                      identity=identf[:30, :30]) for j in range(4)], reads=XBCC + ["identf"], writes=[pn])
                A("act", lambda e, ps=ps, g=g: e.activation(out=gl_f[:, 4 * g:4 * g + 4, HP - 30:HP], in_=ps[:, :].rearrange("p (a b) -> p a b", b=128)[:, :, :30],
                                                            func=AF.Copy), r=[pn], w=["glf%d" % t for t in range(4 * g, 4 * g + 4)])
            for g in range(2):
                S.dma(ytmp[:, 512 * g:512 * g + 512].rearrange("p (a n) -> p a n", n=128),
                      st_ssm[b, 512 * g:512 * g + 512, :].rearrange("(a p) n -> p a n", p=128), writes=["ytmp"])
                ps, pn = nextps()
                S.mm([lambda e, j=j, g=g: e.transpose(out=ps[:, j * 128:(j + 1) * 128], in_=ytmp[:, 512 * g + j * 128:512 * g + (j + 1) * 128],
                                                      identity=identf[:, :]) for j in range(4)], reads=["ytmp", "identf"], writes=[pn])
                A("dve", lambda e, ps=ps, g=g: e.tensor_copy(out=H[:, 512 * g:512 * g + 512], in_=ps[:, :]), r=[pn, "Hb"], w=["H"])
            A("act", lambda e: e.activation(out=Hb[:, :], in_=H[:, :], func=AF.Copy), r=["H"], w=["Hb"])

        def store_state_outputs(L, sc_dst, cc_dst, ssm_dst):
            for g in range(4):
                ps, pn = nextps()
                S.mm([lambda e, j=j, g=g: e.transpose(out=ps[:3, j * 128:(j + 1) * 128], in_=xbc_f[:, 4 * g + j, HP + L - 3:HP + L], identity=identf[:, :])
                      for j in range(4)], reads=["xbcf%d" % t for t in range(4 * g, 4 * g + 4)] + ["identf"], writes=[pn])
                A("act", lambda e, ps=ps, g=g: e.activation(out=hout[:3, 512 * g:512 * g + 512], in_=ps[:3, :], func=AF.Copy), r=[pn], w=XBCC)
            S.dma(sc_dst, hout[:3, :], reads=XBCC)
            for g in range(2):
                ps, pn = nextps()
                S.mm([lambda e, j=j, g=g: e.transpose(out=ps[:30, j * 128:(j + 1) * 128], in_=gl_f[:, 4 * g + j, HP + L - 30:HP + L], identity=identf[:, :])
                      for j in range(4)], reads=["glf%d" % t for t in range(4 * g, 4 * g + 4)] + ["identf"], writes=[pn])
                A("act", lambda e, ps=ps, g=g: e.activation(out=hout[:30, 512 * g:512 * g + 512], in_=ps[:30, :], func=AF.Copy), r=[pn], w=XBCC)
            S.dma(cc_dst, hout[:30, :1024], reads=XBCC)
            for g in range(2):
                ps, pn = nextps()
                S.mm([lambda e, j=j, g=g: e.transpose(out=ps[:, j * 128:(j + 1) * 128], in_=H[:, 512 * g + j * 128:512 * g + (j + 1) * 128], identity=identf[:, :])
                      for j in range(4)], reads=["H", "identf"], writes=[pn])
                A("dve", lambda e, ps=ps, g=g: e.tensor_copy(out=ytmp[:, 512 * g:512 * g + 512], in_=ps[:, :]), r=[pn], w=["ytmp"])
                S.dma(ssm_dst[512 * g:512 * g + 512, :].rearrange("(a p) n -> p a n", p=128),
                      ytmp[:, 512 * g:512 * g + 512].rearrange("p (a n) -> p a n", n=128), reads=["ytmp"])

        def zero_state():
            A("pool", lambda e: e.memset(H[:, :], 0.0), r=["Hb"], w=["H"])
            A("pool", lambda e: e.memset(Hb[:, :], 0.0), w=["Hb"])
            A("pool", lambda e: e.memset(Atot[:, :], 0.0), w=["Atot"])

        zero_state()
        for t in range(16):
            A("pool", lambda e, t=t: e.memset(xbc_f[:, t, 0:HP], 0.0), w=["xbcf%d" % t])
        for t in range(8):
            A("pool", lambda e, t=t: e.memset(gl_f[:, t, 0:HP], 0.0), w=["glf%d" % t])
        for c in range(NCHA):
            l0_chunk(xp[c * 128:(c + 1) * 128, :], 128, "full", dst_x1=x1_d[c * 128:(c + 1) * 128, :])
        store_state_outputs(128, sc_p[:, :], cc_p[:, :], ssm_p)
        for b in range(DB):
            load_hist_from_state(b)
            l0_chunk(xsm[b * 4:(b + 1) * 4, :], 4, "full", dst_x1=x1_d[SEQ + b * 4:SEQ + (b + 1) * 4, :])
            store_state_outputs(4, sc_s[b, :, :], cc_s[b, :, :], ssm_s[b])


        S.barrier()
        st0.close()
        if debug_l0:
            st1 = ExitStack(); cur[0] = st1
            xt = T("xt_dbg", [128, D])
            for c in range(NCH):
                S.dma(xt[:, :], x1_d[c * 128:(c + 1) * 128, :], reads=["x1_d"], writes=["xt"])
                S.dma(y_p[c * 128:(c + 1) * 128, :], xt[:, :], reads=["xt"])
            S.dma(xt[:DB * 4, :], x1_d[SEQ:SEQ + DB * 4, :], reads=["x1_d"], writes=["xt"])
            S.dma(y_s[:, :], xt[:DB * 4, :], reads=["xt"])
            S.finish("sp")
            st1.close()
            return nc
        st1 = ExitStack(); cur[0] = st1
        NSLOT = TPC // 128
        NB = 64
        psn[0] = 6
        Wq = T("Wq", [128, 8, 1024], BF16)
        Wg = T("Wg", [128, 8, 1024], BF16)
        Wo = T("Wo", [64, 16, 1024], BF16)
        g1 = T("g1t", [128, 8]); S.dma(g1[:], g1_d[:, :], writes=["g1t"])
        qg = T("qgt", [128, 1]); S.dma(qg[:], qg_d[:, :], writes=["qgt"])
        kgb = T("kgbt", [128, 256]); S.dma(kgb[:], kgb_d[:, :], writes=["kgbt"])
        pm = T("pmt", [128, 64])
        pneg = T("pnegt", [128, 64])
        om = T("omt", [128, 64])
        maskM = T("maskMt", [128, 8, 128]); S.dma(maskM[:], maskM_d[:, :, :], writes=["maskMt"])
        maskS = T("maskSt", [4, 16]); S.dma(maskS[:], maskS_d[:, :], writes=["maskSt"])
        oidx = T("oidxt", [128, NSLOT], I32); S.dma(oidx[:], oidx_d[:, :], writes=["oidx"])
        pidx = T("pidxt", [128, 1]); S.dma(pidx[:], pidx_d[:, :], writes=["pidx"])
        Eoh = T("Eoh", [64, 64, 128], BF16)
        A("pool", lambda e: e.memset(Eoh[:], 1.0), w=["Eoh"])
        A("pool", lambda e: e.affine_select(out=Eoh[:], in_=Eoh[:], pattern=[[-1, 64], [0, 128]], compare_op=ALU.is_equal, fill=0.0,
                                            base=0, channel_multiplier=1), r=["Eoh"], w=["Eoh"])
        xt1 = T("xt1", [128, D])
        xn1 = T("xn1", [128, D])
        ss1 = T("ss1", [128, 8])
        xT1 = T("xT1", [128, 8, 128], BF16)
        kms = T("kms", [128, 2, max(NCHA, NPG)])
        kmT = T("kmT", [128, 2, 64])
        A("pool", lambda e: e.memset(kmT[:], 0.0), w=["kmT"])
        stA = ExitStack(); cur[0] = stA
        Wkv = T("Wkv", [128, 8, 512], BF16)
        stg1 = [T("stg1a", [128, 1024]), T("stg1b", [128, 1024])]
        si = 0
        for (wd, wt, ncol, wname) in ((wq_d, Wq, 1024, "Wq"), (wkv_d, Wkv, 512, "Wkv"), (wg_d, Wg, 1024, "Wg")):
            for kt in range(8):
                sb = stg1[si % 2]
                S.dma(sb[:, :ncol], wd[:, kt, :], writes=[sb.name])
                A("dve" if si % 2 == 0 else "pool", lambda e, sb=sb, wt=wt, kt=kt, ncol=ncol: e.tensor_scalar(
                    out=wt[:, kt, :], in0=sb[:, :ncol], scalar1=g1[:, kt:kt + 1], scalar2=None, op0=ALU.mult), r=[sb.name, "g1t"], w=[wname])
                si += 1
        for h in range(16):
            sb = stg1[si % 2]
            S.dma(sb[:64, :], wo_d[:, h, :], writes=[sb.name])
            A("dve" if si % 2 == 0 else "pool", lambda e, sb=sb, h=h: e.tensor_copy(out=Wo[:, h, :], in_=sb[:64, :]), r=[sb.name], w=["Wo"])
            si += 1

        KV = T("KV", [128, 512])
        KTst = T("KTst", [128, 2, 128], BF16)
        Vb = T("Vb", [128, 4, 65], BF16)
        A("pool", lambda e: e.memset(Vb[:], 1.0), w=["Vb"])

        def l1_norm_T(L):
            A("act", lambda e: e.activation(out=xn1[:L, :], in_=xt1[:L, :], func=AF.Square, scale=1.0 / 32, accum_out=ss1[:L, 0:1]), r=["xt1"], w=["xn1", "ss1"])
            A("act", lambda e: e.activation(out=ss1[:L, 1:2], in_=ss1[:L, 0:1], func=AF.Ln, bias=epsT[:L, 0:1]), r=["ss1", "epsT"], w=["ss1"])
            A("act", lambda e: e.activation(out=ss1[:L, 2:3], in_=ss1[:L, 1:2], func=AF.Exp, scale=-0.5), r=["ss1"], w=["ss1"])
            A("dve", lambda e: e.tensor_scalar(out=xn1[:L, :], in0=xt1[:L, :], scalar1=ss1[:L, 2:3], scalar2=None, op0=ALU.mult), r=["xt1", "ss1"], w=["xn1"])
            for half in range(2):
                ps, pn = nextps()
                S.mm([lambda e, j=j, half=half: e.transpose(out=ps[:, j * 128:j * 128 + L], in_=xn1[:L, (half * 4 + j) * 128:(half * 4 + j + 1) * 128],
                                                           identity=identf[:L, :L]) for j in range(4)], reads=["xn1", "identf"], writes=[pn])
                v = ps[:, :].rearrange("p (a b) -> p a b", b=128)[:, :, :L]
                A("act", lambda e, v=v, half=half: e.activation(out=xT1[:, half * 4:half * 4 + 4, :L], in_=v, func=AF.Copy), r=[pn], w=["xT1"])

        def l1_kv(src, L, kdst, vdst, col0, chunk_idx):
            S.dma(xt1[:L, :], src, reads=["x1_d"], writes=["xt1"])
            l1_norm_T(L)
            ps, pn = nextps()
            S.mm([lambda e, kt=kt: e.matmul(out=ps[:L, :], lhsT=xT1[:, kt, :L], rhs=Wkv[:, kt, :], start=(kt == 0), stop=(kt == 7)) for kt in range(8)],
                 reads=["xT1", "Wkv"], writes=[pn])
            for h in range(4):
                A("act", lambda e, h=h: e.activation(out=xn1[:L, h * 64:(h + 1) * 64], in_=ps[:L, h * 64:(h + 1) * 64], func=AF.Square, scale=0.125,
                                                     accum_out=ss1[:L, 4 + h:5 + h]), r=[pn], w=["xn1", "ss1"])
            A("act", lambda e: e.activation(out=ss1[:L, 4:8], in_=ss1[:L, 4:8], func=AF.Ln, bias=epsT[:L, 0:1]), r=["ss1", "epsT"], w=["ss1"])
            A("act", lambda e: e.activation(out=ss1[:L, 4:8], in_=ss1[:L, 4:8], func=AF.Exp, scale=-0.5), r=["ss1"], w=["ss1"])
            A("dve", lambda e: e.tensor_tensor(out=KV[:L, 0:256].rearrange("p (h d) -> p h d", d=64), in0=ps[:L, 0:256].rearrange("p (h d) -> p h d", d=64),
                                               in1=ss1[:L, 4:8].unsqueeze(2).broadcast_to([L, 4, 64]), op=ALU.mult), r=[pn, "ss1"], w=["KV"])
            A("pool", lambda e: e.tensor_tensor(out=KV[:L, 0:256], in0=KV[:L, 0:256], in1=kgb[:L, :], op=ALU.mult), r=["KV", "kgbt"], w=["KV"])
            A("act", lambda e: e.activation(out=KV[:L, 256:512], in_=ps[:L, 256:512], func=AF.Copy), r=[pn], w=["KV"])
            S.dma(kdst, KV[:L, 0:256], reads=["KV"])
            S.dma(vdst, KV[:L, 256:512], reads=["KV"])
            ps2, pn2 = nextps()
            S.mm([lambda e, pr=pr: e.transpose(out=ps2[:, pr * 128:pr * 128 + L], in_=KV[:L, pr * 128:(pr + 1) * 128], identity=identf[:L, :L]) for pr in range(2)],
                 reads=["KV", "identf"], writes=[pn2])
            for pr in range(2):
                A("act", lambda e, pr=pr: e.activation(out=KTst[:, pr, :L], in_=ps2[:, pr * 128:pr * 128 + L], func=AF.Copy,
                                                       accum_out=kms[:, pr, chunk_idx:chunk_idx + 1]), r=[pn2], w=["KTst", "kms"])
                S.dma(KT_d[pr, :, col0:col0 + L], KTst[:, pr, :L], reads=["KTst"], writes=["KT_d"])
            A("pool", lambda e: e.tensor_copy(out=Vb[:L, :, 0:64], in_=KV[:L, 256:512].rearrange("p (h d) -> p h d", d=64)), r=["KV"], w=["Vb"])
            S.dma(V_d[col0:col0 + L, :], Vb[:L, :, :].rearrange("p h c -> p (h c)"), reads=["Vb"], writes=["V_d"])

        for t in range(NCHA):
            l1_kv(x1_d[t * 128:(t + 1) * 128, :], 128, k_p[t * 128:(t + 1) * 128, :], v_p[t * 128:(t + 1) * 128, :], t * 128, t)
        NBP = NCHA // 2
        kv2 = kms[:, :, 0:NCHA].rearrange("p a (n two) -> p a n two", two=2)
        A("dve", lambda e: e.tensor_tensor(out=kmT[:, :, 0:NBP], in0=kv2[:, :, :, 0], in1=kv2[:, :, :, 1], op=ALU.add), r=["kms"], w=["kmT"])
        A("dve", lambda e: e.tensor_scalar(out=kmT[:, :, 0:NBP], in0=kmT[:, :, 0:NBP], scalar1=1.0 / 256, scalar2=None, op0=ALU.mult), r=["kmT"], w=["kmT"])
        for b in range(DB):
            l1_kv(x1_d[SEQ + b * 4:SEQ + (b + 1) * 4, :], 4, k_s[b * 4:(b + 1) * 4, :], v_s[b * 4:(b + 1) * 4, :], SEQ + b * 4, NCHA - 1 if False else 0)

        S.barrier()
        stA.close()
        cur[0] = st1
        QF = T("QF", [128, 8, 128])
        sq1 = T("sq1", [128, 4, 128])
        QPf = T("QPf", [128, 16, 128])
        QPb = T("QPb", [128, 16, 128], BF16)
        SG = T("SG", [64, 16, 128], BF16)
        Gm = T("Gm", [128, 16, 64])
        m8 = T("m8", [128, 16, 8])
        bsel = Gm
        biasT = T("biasT", [64, 16, 128], BF16)
        ATg = T("ATg", [64, 16, 128], BF16)
        rdt = xn1
        bcs = T("bcs", [64, 512])
        atmp = T("atmp", [64, 512])
        PT = [T("PT0", [128, 512], BF16), T("PT1", [128, 512], BF16)]
        KTst2 = T("KTst2", [128, 2, 128], BF16)
        A("pool", lambda e: e.memset(QPf[:], 0.0), w=["QPf"])

        def l1_q(L):
            l1_norm_T(L)
            for half in range(2):
                ps, pn = nextps()
                S.mm([lambda e, r=r, kt=kt, half=half: e.matmul(out=ps[:, r * 128:r * 128 + L], lhsT=Wq[:, kt, (4 * half + r) * 128:(4 * half + r + 1) * 128],
                                                             rhs=xT1[:, kt, :L], start=(kt == 0), stop=(kt == 7)) for r in range(4) for kt in range(8)],
                     reads=["Wq", "xT1"], writes=[pn])
                v = ps[:, :].rearrange("p (a b) -> p a b", b=128)[:, :, :L]
                A("act", lambda e, v=v: e.activation(out=sq1[:, :, :L], in_=v, func=AF.Square, scale=0.125), r=[pn], w=["sq1"])
                pss, pns = nextps()
                S.mm([lambda e, r=r: e.matmul(out=pss[:, r * 128:r * 128 + L], lhsT=BD[:, :], rhs=sq1[:, r, :L], start=True, stop=True) for r in range(4)],
                     reads=["BD", "sq1"], writes=[pns])
                vs = pss[:, :].rearrange("p (a b) -> p a b", b=128)[:, :, :L]
                A("act", lambda e, vs=vs: e.activation(out=sq1[:, :, :L], in_=vs, func=AF.Ln, bias=epsT[:, 0:1]), r=[pns, "epsT"], w=["sq1"])
                A("act", lambda e: e.activation(out=sq1[:, :, :L], in_=sq1[:, :, :L], func=AF.Exp, scale=-0.5), r=["sq1"], w=["sq1"])
                A("dve", lambda e, v=v, half=half: e.tensor_tensor(out=QF[:, 4 * half:4 * half + 4, :L], in0=v, in1=sq1[:, :, :L], op=ALU.mult), r=[pn, "sq1", "QFk0", "QFk1", "QFv0", "QFv1"], w=["QF", "QFk0", "QFk1", "QFv0", "QFv1"])
            A("dve", lambda e: e.tensor_scalar(out=QF[:, :, :L], in0=QF[:, :, :L], scalar1=qg[:, 0:1], scalar2=None, op0=ALU.mult), r=["QF", "qgt"], w=["QF"])
            for hk in range(4):
                rows = slice(64 * (hk % 2), 64 * (hk % 2) + 64)
                t0 = (hk // 2) * 4
                A("pool", lambda e, hk=hk, rows=rows, t0=t0: e.tensor_copy(out=QPf[rows, 4 * hk:4 * hk + 4, :L], in_=QF[rows, t0:t0 + 4, :L]), r=["QF"], w=["QPf"])
            A("act", lambda e: e.activation(out=QPb[:, :, :L], in_=QPf[:, :, :L], func=AF.Copy), r=["QPf"], w=["QPb"])
            for bk in range(4):
                ps, pn = nextps()
                S.mm([lambda e, r=r, kt=kt, bk=bk: e.matmul(out=ps[0:64, r * 128:r * 128 + L], lhsT=Wg[:, kt, (4 * bk + r) * 64:(4 * bk + r + 1) * 64],
                                                           rhs=xT1[:, kt, :L], start=(kt == 0), stop=(kt == 7)) for r in range(4) for kt in range(8)],
                     reads=["Wg", "xT1"], writes=[pn])
                A("act", lambda e, ps=ps, bk=bk: e.activation(out=SG[:, 4 * bk:4 * bk + 4, :L], in_=ps[0:64, :].rearrange("p (a b) -> p a b", b=128)[:, :, :L],
                                                              func=AF.Silu), r=[pn], w=["SG"])

        def moba_select(L, kmt, kmname, slot):
            S.dma(pm[:, :], pm_d[:, slot, :], writes=["pmt"])
            S.dma(pneg[:, :], pneg_d[:, slot, :], writes=["pnegt"])
            S.dma(om[:, :], om_d[:, slot, :], writes=["omt"])
            for half in range(2):
                ps, pn = nextps()
                S.mm([lambda e, i=i, half=half: e.matmul(out=ps[:L, i * 64:(i + 1) * 64], lhsT=QPf[:, 8 * half + i, :L], rhs=kmt[:, (8 * half + i) // 8, :],
                                                        start=True, stop=True) for i in range(8)], reads=["QPf", kmname], writes=[pn])
                A("dve", lambda e, ps=ps, half=half: e.tensor_tensor(out=Gm[:L, 8 * half:8 * half + 8, :], in0=ps[:L, :].rearrange("p (a b) -> p a b", b=64),
                                                                     in1=pm[:L, :].unsqueeze(1).broadcast_to([L, 8, 64]), op=ALU.mult), r=[pn, "pmt"], w=["Gm"])
            A("dve", lambda e: e.tensor_tensor(out=Gm[:L, :, :], in0=Gm[:L, :, :], in1=pneg[:L, :].unsqueeze(1).broadcast_to([L, 16, 64]), op=ALU.add),
              r=["Gm", "pnegt"], w=["Gm"])
            for h in range(16):
                A("dve", lambda e, h=h: e.max(out=m8[:L, h, :], in_=Gm[:L, h, :]), r=["Gm"], w=["m8"])
            for h in range(16):
                A("dve", lambda e, h=h: e.tensor_scalar(out=bsel[:L, h, :], in0=Gm[:L, h, :], scalar1=m8[:L, h, 2:3], scalar2=None, op0=ALU.is_ge), r=["Gm", "m8"], w=["Gm"])
            A("dve", lambda e: e.tensor_tensor(out=bsel[:L, :, :], in0=bsel[:L, :, :], in1=pm[:L, :].unsqueeze(1).broadcast_to([L, 16, 64]), op=ALU.mult),
              r=["Gm", "pmt"], w=["Gm"])
            A("dve", lambda e: e.tensor_tensor(out=bsel[:L, :, :], in0=bsel[:L, :, :], in1=om[:L, :].unsqueeze(1).broadcast_to([L, 16, 64]), op=ALU.add),
              r=["Gm", "omt"], w=["Gm"])
            A("dve", lambda e: e.tensor_scalar(out=bsel[:L, :, :], in0=bsel[:L, :, :], scalar1=-NEG, scalar2=NEG, op0=ALU.mult, op1=ALU.add), r=["Gm"], w=["Gm"])
            for bk in range(4):
                ps, pn = nextps()
                S.mm([lambda e, r=r, bk=bk: e.transpose(out=ps[0:64, r * 128:r * 128 + L], in_=bsel[:L, 4 * bk + r, :], identity=identf[:L, :L]) for r in range(4)],
                     reads=["Gm", "identf"], writes=[pn])
                A("act", lambda e, ps=ps, bk=bk: e.activation(out=biasT[:, 4 * bk:4 * bk + 4, :L], in_=ps[0:64, :].rearrange("p (a b) -> p a b", b=128)[:, :, :L],
                                                              func=AF.Copy), r=[pn], w=["biasT"])

        def finish_group(L, hk, psO, pnO):
            W4 = 4 * L
            A("dve", lambda e: e.reciprocal(out=rdt[64:65, :W4], in_=psO[64:65, :W4]), r=[pnO], w=["xn1"])
            psb, pnb = nextps()
            S.mm([lambda e: e.matmul(out=psb[0:64, :W4], lhsT=onesf[64:65, 0:64], rhs=rdt[64:65, :W4], start=True, stop=True)], reads=["onesf", "xn1"], writes=[pnb])
            A("act", lambda e: e.activation(out=bcs[:, :W4], in_=psb[0:64, :W4], func=AF.Copy), r=[pnb], w=["bcs"])
            A("dve", lambda e: e.tensor_tensor(out=atmp[:, :W4], in0=psO[0:64, :W4], in1=bcs[:, :W4], op=ALU.mult), r=[pnO, "bcs"], w=["atmp"])
            A("pool", lambda e: e.tensor_tensor(out=ATg[:, 4 * hk:4 * hk + 4, :L], in0=atmp[:, :W4].rearrange("p (a b) -> p a b", b=L), in1=SG[:, 4 * hk:4 * hk + 4, :L],
                                                op=ALU.mult), r=["atmp", "SG"], w=["ATg"])

        def out_proj(L, dst):
            for half in range(2):
                ps, pn = nextps()
                S.mm([lambda e, h=h, half=half: e.matmul(out=ps[:L, :], lhsT=ATg[:, h, :L], rhs=Wo[:, h, 512 * half:512 * half + 512], start=(h == 0), stop=(h == 15))
                      for h in range(16)], reads=["ATg", "Wo"], writes=[pn])
                sl = slice(512 * half, 512 * half + 512)
                A("dve", lambda e, ps=ps, sl=sl: e.tensor_tensor(out=xn1[:L, sl], in0=xt1[:L, sl], in1=ps[:L, :], op=ALU.add), r=["xt1", pn], w=["xn1"])
            S.dma(dst, xn1[:L, :], reads=["xn1"])

        st2 = ExitStack(); cur[0] = st2
        NKT = NCHA
        NKB = max(NKT, NPG)
        KTp1 = T("KTp0", [128, NKB * 128], BF16)
        KTp = [KTp1, KTp1]
        Vp = [T("Vp%d" % i, [128, NKB, 65], BF16) for i in range(2)]
        bufi = 0
        for j in range(NSLOT):
            nkt = 8 * j + 8
            S.idma(xt1[:, :], x1_d[:, :], oidx[:, j:j + 1], reads=["x1_d", "oidx"], writes=["xt1"])
            l1_q(128)
            moba_select(128, kmT, "kmT", j)
            for hk in range(4):
                pr = hk // 2
                if hk % 2 == 0:
                    ktp = KTp[pr]
                    S.dma(ktp[:, :nkt * 128], KT_d[pr, :, 0:nkt * 128], reads=["KT_d"], writes=[ktp.name])
                vp = Vp[hk % 2]
                S.dma(vp[:, :nkt, :], V_d[0:nkt * 128, hk * 65:(hk + 1) * 65].rearrange("(k p) c -> p k c", p=128), reads=["V_d"], writes=[vp.name])
                psO, pnO = pst[7 - (hk % 2)], "ps%d" % (7 - (hk % 2))
                def emit_st(kt, ktp=ktp, hk=hk):
                    ps, pn = nextps(6)
                    S.mm([lambda e: e.matmul(out=ps[:, :], lhsT=ktp[:, kt * 128:(kt + 1) * 128],
                                             rhs=QPb[:, 4 * hk:4 * hk + 4, :].rearrange("p a b -> p (a b)"), start=True, stop=False),
                          lambda e: e.matmul(out=ps[:, :], lhsT=Eoh[:, kt // 2, :], rhs=biasT[:, 4 * hk:4 * hk + 4, :].rearrange("p a b -> p (a b)"),
                                             start=False, stop=True)], reads=[ktp.name, "QPb", "Eoh", "biasT"], writes=[pn])
                    return ps, pn
                pend = [emit_st(k_) for k_ in range(min(2, nkt))]
                for kt in range(nkt):
                    ps, pn = pend.pop(0)
                    if kt + 2 < nkt:
                        pend.append(emit_st(kt + 2))
                    pt = PT[kt % 2]
                    A("act", lambda e, ps=ps, pt=pt: e.activation(out=pt[:, :], in_=ps[:, :], func=AF.Exp, scale=0.125), r=[pn], w=[pt.name])
                    if kt >= 8 * j:
                        A("dve", lambda e, pt=pt, kt=kt, j=j: e.tensor_tensor(out=pt[:, :].rearrange("p (a b) -> p a b", b=128), in0=pt[:, :].rearrange("p (a b) -> p a b", b=128),
                                                                             in1=maskM[:, kt - 8 * j, :].unsqueeze(1).broadcast_to([128, 4, 128]), op=ALU.mult),
                          r=[pt.name, "maskMt"], w=[pt.name])
                    S.mm([lambda e, kt=kt, vp=vp, pt=pt, psO=psO, nkt=nkt: e.matmul(out=psO[0:65, :], lhsT=vp[:, kt, :], rhs=pt[:, :], start=(kt == 0), stop=(kt == nkt - 1))],
                         reads=[vp.name, pt.name], writes=[pnO])
                finish_group(128, hk, psO, pnO)
            out_proj(128, y_p[j * 128:(j + 1) * 128, :])

        PTs = Gm[:, :, :].rearrange("p a b -> p (a b)").bitcast(BF16)
        PTn = T("PTn", [4, 16], BF16)
        KTn = T("KTn", [128, 2, 4], BF16)
        Vn = T("Vn", [4, 4, 65], BF16)
        ptf = m8[:, :, :].rearrange("p a b -> p (a b)")[:, :NPG]
        pti = T("pti", [128, NPG], I32)
        kmTs = kmT
        VsA = T("VsA", [128, 4, 65], BF16)
        QFl = QF[:, :, :].rearrange("p a b -> p (a b)")
        Kpg = [QFl[:, 0:256], QFl[:, 256:512]]
        Vpg = [QFl[:, 512:768], QFl[:, 768:1024]]
        A("pool", lambda e: e.memset(VsA[:], 1.0), w=["VsA"])
        A("pool", lambda e: e.memset(kmTs[:], 0.0), w=["kmT"])
        NBS = NPG // 2
        for b in range(DB):
            S.dma(pti[:, :], ptb_d[:, b, :], writes=["pti"])
            A("dve", lambda e: e.tensor_copy(out=ptf[:, :], in_=pti[:, :]), r=["pti"], w=["m8"])
            A("dve", lambda e: e.tensor_scalar(out=ptf[:, :], in0=ptf[:, :], scalar1=128.0, scalar2=pidx[:, 0:1], op0=ALU.mult, op1=ALU.add), r=["m8", "pidx"], w=["m8"])
            A("dve", lambda e: e.tensor_copy(out=pti[:, :], in_=ptf[:, :]), r=["m8"], w=["pti"])
            for pg in range(NPG):
                kp, vq = Kpg[pg % 2], Vpg[pg % 2]
                kn_, vn_ = "QFk%d" % (pg % 2), "QFv%d" % (pg % 2)
                S.idma(kp, ck_d[:, :], pti[:, pg:pg + 1], reads=["pti", "QF"], writes=[kn_])
                S.idma(vq, cv_d[:, :], pti[:, pg:pg + 1], reads=["pti", "QF"], writes=[vn_])
                ps, pn = nextps()
                S.mm([lambda e, pr=pr, kp=kp, ps=ps: e.transpose(out=ps[:, pr * 128:(pr + 1) * 128], in_=kp[:, pr * 128:(pr + 1) * 128], identity=identf[:, :]) for pr in range(2)],
                     reads=[kn_, "identf"], writes=[pn])
                for pr in range(2):
                    A("act", lambda e, pr=pr, pg=pg, ps=ps: e.activation(out=KTst2[:, pr, :], in_=ps[:, pr * 128:(pr + 1) * 128], func=AF.Copy,
                                                                       accum_out=kms[:, pr, pg:pg + 1]), r=[pn], w=["KTst2", "kms"])
                    S.dma(KTs_d[b, pr, :, pg * 128:(pg + 1) * 128], KTst2[:, pr, :], reads=["KTst2"], writes=["KTs_d"])
                A("pool", lambda e, vq=vq: e.tensor_copy(out=VsA[:, :, 0:64], in_=vq.rearrange("p (h d) -> p h d", d=64)), r=[vn_], w=["VsA"])
                S.dma(Vs_d[b, pg * 128:(pg + 1) * 128, :], VsA[:, :, :].rearrange("p h c -> p (h c)"), reads=["VsA"], writes=["Vs_d"])
            kv3 = kms[:, :, 0:NPG].rearrange("p a (n two) -> p a n two", two=2)
            A("dve", lambda e: e.tensor_tensor(out=kmTs[:, :, 0:NBS], in0=kv3[:, :, :, 0], in1=kv3[:, :, :, 1], op=ALU.add), r=["kms"], w=["kmT"])
            A("dve", lambda e: e.tensor_scalar(out=kmTs[:, :, 0:NBS], in0=kmTs[:, :, 0:NBS], scalar1=1.0 / 256, scalar2=None, op0=ALU.mult), r=["kmT"], w=["kmT"])
            S.dma(KTn[:, :, :], KT_d[:, :, SEQ + b * 4:SEQ + (b + 1) * 4].rearrange("a p c -> p a c"), reads=["KT_d"], writes=["KTn"])
            S.dma(Vn[:, :, :].rearrange("p h c -> p (h c)"), V_d[SEQ + b * 4:SEQ + (b + 1) * 4, :], reads=["V_d"], writes=["Vn"])
            S.dma(xt1[:4, :], x1_d[SEQ + b * 4:SEQ + (b + 1) * 4, :], reads=["x1_d"], writes=["xt1"])
            l1_q(4)
            moba_select(4, kmTs, "kmT", NSLOT)
            for hk in range(4):
                pr = hk // 2
                if hk % 2 == 0:
                    S.dma(KTp1[:, :NPG * 128], KTs_d[b, pr, :, :], reads=["KTs_d"], writes=["KTp0"])
                vp = Vp[hk % 2]
                S.dma(vp[:, :NPG, :], Vs_d[b, :, hk * 65:(hk + 1) * 65].rearrange("(k p) c -> p k c", p=128), reads=["Vs_d"], writes=[vp.name])
                qrhs = QPb[:, 4 * hk:4 * hk + 4, :4]
                brhs = biasT[:, 4 * hk:4 * hk + 4, :4]
                PPB = 32
                for bk in range((NPG + PPB - 1) // PPB):
                    ps, pn = nextps()
                    pgs = list(range(bk * PPB, min(NPG, (bk + 1) * PPB)))
                    fns = []
                    for pg in pgs:
                        o = ps[:, (pg - bk * PPB) * 16:(pg - bk * PPB + 1) * 16].rearrange("p (a b) -> p a b", b=4)
                        fns.append(lambda e, o=o, pg=pg, qrhs=qrhs: e.matmul(out=o, lhsT=KTp1[:, pg * 128:(pg + 1) * 128], rhs=qrhs, start=True, stop=False))
                        fns.append(lambda e, o=o, pg=pg, brhs=brhs: e.matmul(out=o, lhsT=Eoh[:, pg // 2, :], rhs=brhs, start=False, stop=True))
                    S.mm(fns, reads=["KTp0", "QPb", "Eoh", "biasT"], writes=[pn])
                    ncol = len(pgs) * 16
                    A("act", lambda e, ps=ps, bk=bk, ncol=ncol: e.activation(out=PTs[:, bk * PPB * 16:bk * PPB * 16 + ncol], in_=ps[:, :ncol], func=AF.Exp, scale=0.125),
                      r=[pn], w=["Gm"])
                ps, pn = nextps()
                S.mm([lambda e, ps=ps, pr=pr, qrhs=qrhs: e.matmul(out=ps[0:4, 0:16].rearrange("p (a b) -> p a b", b=4), lhsT=KTn[:, pr, :], rhs=qrhs, start=True, stop=True)],
                     reads=["KTn", "QPb"], writes=[pn])
                A("act", lambda e, ps=ps: e.activation(out=PTn[:, :], in_=ps[0:4, 0:16], func=AF.Exp, scale=0.125), r=[pn], w=["PTn"])
                A("dve", lambda e: e.tensor_tensor(out=PTn[:, :], in0=PTn[:, :], in1=maskS[:, :], op=ALU.mult), r=["PTn", "maskSt"], w=["PTn"])
                psO, pnO = pst[7 - (hk % 2)], "ps%d" % (7 - (hk % 2))
                fns = [lambda e, pg=pg, vp=vp, psO=psO: e.matmul(out=psO[0:65, 0:16], lhsT=vp[:, pg, :], rhs=PTs[:, pg * 16:(pg + 1) * 16], start=(pg == 0), stop=False)
                       for pg in range(NPG)]
                fns.append(lambda e, hk=hk, psO=psO: e.matmul(out=psO[0:65, 0:16], lhsT=Vn[:, hk, :], rhs=PTn[:, :], start=False, stop=True))
                S.mm(fns, reads=[vp.name, "Gm", "Vn", "PTn"], writes=[pnO])
                finish_group(4, hk, psO, pnO)
            out_proj(4, y_s[b * 4:(b + 1) * 4, :])
        S.barrier()
        st2.close()
        st1.close()
        S.finish("sp")
        print("instructions:", S.n_instr, "sem counts", S.cnt)
    return nc


def prep_inputs(cfg, inp):
    f = lambda a: np.ascontiguousarray(np.asarray(a, dtype=np.float32))
    TPC, DB = cfg.tpc, cfg.db
    xp_full = f(inp["x_prompt"])[0]
    common = {
        "w_in0": f(f(inp["w_in0"])[0].reshape(8, 128, 6160).transpose(1, 0, 2)),
        "w_out0": f(f(inp["w_out0"])[0].reshape(16, 128, 1024).transpose(1, 0, 2)),
        "g0": f(f(inp["norm0_g"])[0].reshape(8, 128).T),
        "cw": f(f(inp["ssd_conv_w"])[0].reshape(4, 16, 128).transpose(2, 1, 0)),
        "cb": f(f(inp["ssd_conv_b"])[0].reshape(16, 128).T),
        "ccw": f(f(inp["conf_conv_w"])[0].reshape(31, 8, 128).transpose(2, 1, 0)),
        "ccb": f(f(inp["conf_conv_b"])[0].reshape(8, 128).T),
        "lng": f(f(inp["conf_ln_g"])[0].reshape(8, 128).T),
        "lnb": f(f(inp["conf_ln_b"])[0].reshape(8, 128).T),
        "dtb": f(np.broadcast_to(f(inp["ssd_dt_bias"])[0][None, :], (128, 16))),
        "alog": f(np.broadcast_to(f(inp["ssd_a_log"])[0][None, :], (128, 16))),
        "dsk": f(np.broadcast_to(f(inp["ssd_d"])[0][None, :], (128, 16))),
        "sng": f(f(inp["ssd_norm_g"])[0].reshape(8, 128).T),
    }
    perm = [0, 4, 1, 5, 2, 6, 3, 7, 8, 12, 9, 13, 10, 14, 11, 15]
    w1 = f(inp["w_in1"])[0]
    wq = w1[:, :1024].reshape(1024, 16, 64)[:, perm, :].reshape(1024, 1024)
    wkv = w1[:, 1024:1536]
    wg = w1[:, 1536:2560]
    kt_layout = lambda w: f(w.reshape(8, 128, w.shape[1]).transpose(1, 0, 2))
    NSLOT = TPC // 128
    NPG = cfg.npg
    npool = cfg.npool
    common.update({
        "wq": kt_layout(wq), "wkv": kt_layout(wkv), "wg": kt_layout(wg),
        "wo": f(f(inp["w_out1"])[0].reshape(16, 64, 1024).transpose(1, 0, 2)),
        "g1": f(f(inp["norm1_g"])[0].reshape(8, 128).T),
        "qg": f(np.tile(f(inp["q_norm_g"])[0], 2).reshape(128, 1)),
        "kgb": f(np.broadcast_to(np.tile(f(inp["k_norm_g"])[0], 4)[None, :], (128, 256))),
        "maskS": f(np.tile((np.arange(4)[:, None] <= np.arange(4)[None, :]).astype(np.float32), (1, 4))),
        "pidx": f(np.arange(128, dtype=np.float32).reshape(128, 1)),
        "ck": f(inp["cache_k"]).reshape(npool * 128, 256),
        "cv": f(inp["cache_v"]).reshape(npool * 128, 256),
    })
    pt_all = np.ascontiguousarray(np.asarray(inp["page_table"], dtype=np.int32))
    tri = (np.arange(128)[:, None] <= np.arange(128)[None, :]).astype(np.float32)
    maps = []
    for c in range(NCORE):
        m = dict(common)
        pm = np.zeros((NSLOT + 1, 64), np.float32)
        om = np.zeros((NSLOT + 1, 64), np.float32)
        for j in range(NSLOT):
            own = (8 * j + c) // 2
            pm[j, :own] = 1.0
            om[j, own] = 1.0
        pm[NSLOT, :NPG // 2] = 1.0
        bc = lambda a: f(np.broadcast_to(a[None], (128,) + a.shape))
        m["pm"] = bc(pm)
        m["pneg"] = bc((pm - 1.0) * np.float32(1e30))
        m["om"] = bc(om)
        mm_ = np.ones((128, 8, 128), np.float32)
        for dl in range(8):
            if dl // 2 == c // 2:
                if dl == c:
                    mm_[:, dl, :] = tri
                elif dl > c:
                    mm_[:, dl, :] = 0.0
        m["maskM"] = mm_
        m["oidx"] = np.ascontiguousarray(((8 * np.arange(NSLOT)[None, :] + c) * 128 + np.arange(128)[:, None]).astype(np.int32))
        m["ptb"] = np.ascontiguousarray(np.broadcast_to(pt_all[c * DB:(c + 1) * DB][None], (128, DB, NPG)).astype(np.int32))
        m["xp"] = xp_full
        m["xsm"] = f(f(inp["x_sample"])[c * DB:(c + 1) * DB].reshape(DB * 4, D))
        m["st_ssm"] = f(f(inp["state_ssm"])[0, c * DB:(c + 1) * DB].reshape(DB, 1024, 128))
        m["st_sc"] = f(f(inp["state_ssd_conv"])[0, c * DB:(c + 1) * DB])
        m["st_cc"] = f(f(inp["state_conf_conv"])[0, c * DB:(c + 1) * DB])
        cm = np.zeros((128, 8), np.float32)
        cm[:, :c] = 1.0
        m["cmask"] = cm
        maps.append(m)
    return maps


_NC_CACHE = {}


def run(cfg, inp, debug_l0=False):
    key = (cfg.seq, cfg.dbt, cfg.npg, debug_l0)
    if key not in _NC_CACHE:
        _NC_CACHE[key] = build(cfg, debug_l0)
    nc = _NC_CACHE[key]
    maps = prep_inputs(cfg, inp)
    res = run_bass_kernel_spmd(nc, maps, core_ids=list(range(NCORE)))
    R = res.results
    cat = lambda k: np.concatenate([r[k] for r in R], axis=0)
    TPC, DB = cfg.tpc, cfg.db
    y_p = cat("y_p")[None]
    y_s = cat("y_s").reshape(cfg.dbt, 4, D)
    ssm_p = R[0]["ssm_p"].reshape(1, 1, 16, 64, 128)
    ssm_s = cat("ssm_s").reshape(1, cfg.dbt, 16, 64, 128)
    sc_p = R[0]["sc_p"].reshape(1, 1, 3, 2048)
    sc_s = cat("sc_s").reshape(1, cfg.dbt, 3, 2048)
    cc_p = R[0]["cc_p"].reshape(1, 1, 30, 1024)
    cc_s = cat("cc_s").reshape(1, cfg.dbt, 30, 1024)
    if debug_l0:
        return (y_p, y_s, ssm_p, ssm_s, sc_p, sc_s, cc_p, cc_s)
    NSLOT = TPC // 128
    yp = np.empty((cfg.seq, D), np.float32)
    for c in range(NCORE):
        for j in range(NSLOT):
            t = 8 * j + c
            yp[t * 128:(t + 1) * 128] = R[c]["y_p"][j * 128:(j + 1) * 128]
    y_p = yp[None]
    k_p = R[0]["k_p"].reshape(1, 1, cfg.seq, 4, 64)
    v_p = R[0]["v_p"].reshape(1, 1, cfg.seq, 4, 64)
    k_s = cat("k_s").reshape(1, cfg.dbt, 4, 4, 64)
    v_s = cat("v_s").reshape(1, cfg.dbt, 4, 4, 64)
    return (y_p, y_s, ssm_p, ssm_s, sc_p, sc_s, cc_p, cc_s, k_p, v_p, k_s, v_s)


def kernel(**inputs):
    cfg = Cfg(inputs["x_prompt"].shape[1], inputs["x_sample"].shape[0], inputs["page_table"].shape[1] * 128)
    return run(cfg, inputs)
```

```python
import numpy as np
from contextlib import ExitStack
import concourse.bass as bass
import concourse.mybir as mybir
from concourse.bass_utils import run_bass_kernel_spmd

F32 = mybir.dt.float32
BF16 = mybir.dt.bfloat16
I32 = mybir.dt.int32
ALU = mybir.AluOpType
AF = mybir.ActivationFunctionType
AX = mybir.AxisListType

NCORE = 8
D = 1024
HP = 32
EPS = 1e-6
NEG = -30000.0


class Sched:
    def __init__(self, nc, stack, n_dma_sems=24):
        self.nc = nc
        self.engs = {"pe": nc.tensor, "act": nc.scalar, "dve": nc.vector, "pool": nc.gpsimd, "sp": nc.sync}
        self.sem = {k: stack.enter_context(nc.semaphore("s_" + k)) for k in ("pe", "act", "dve", "pool")}
        self.cnt = {k: 0 for k in self.sem}
        self.dsem = [stack.enter_context(nc.semaphore("d%d" % i)) for i in range(n_dma_sems)]
        self.dval = [0] * n_dma_sems
        self.dnext = 0
        self.waited = {}
        self.lastw = {}
        self.reads = {}
        self.n_instr = 0

    def _semobj(self, key):
        return self.sem[key] if isinstance(key, str) else self.dsem[key[1]]

    def _wait(self, eng, key, val):
        if self.waited.get((eng, key), 0) >= val:
            return
        self.waited[(eng, key)] = val
        self.engs[eng].wait_ge(self._semobj(key), val)

    def _deps(self, eng, reads, writes):
        for b in reads:
            t = self.lastw.get(b)
            if t is not None:
                self._wait(eng, t[0], t[1])
        for b in writes:
            t = self.lastw.get(b)
            if t is not None:
                self._wait(eng, t[0], t[1])
            for k, v in self.reads.get(b, {}).items():
                if k != eng:
                    self._wait(eng, k, v)

    def _record(self, key, val, reads, writes):
        for b in reads:
            d = self.reads.setdefault(b, {})
            if d.get(key, 0) < val:
                d[key] = val
        for b in writes:
            self.lastw[b] = (key, val)
            self.reads[b] = {}

    def op(self, eng, fn, reads=(), writes=()):
        self._deps(eng, reads, writes)
        ins = fn(self.engs[eng])
        self.cnt[eng] += 1
        ins.then_inc(self.sem[eng], 1)
        self._record(eng, self.cnt[eng], reads, writes)
        self.n_instr += 1

    def mm(self, fns, reads=(), writes=()):
        self._deps("pe", reads, writes)
        ins = None
        for fn in fns:
            ins = fn(self.nc.tensor)
            self.n_instr += 1
        self.cnt["pe"] += 1
        ins.then_inc(self.sem["pe"], 1)
        self._record("pe", self.cnt["pe"], reads, writes)

    def dma(self, out, in_, reads=(), writes=(), q="sp", **kw):
        i = self.dnext
        self.dnext = (self.dnext + 1) % len(self.dsem)
        key = ("d", i)
        if self.dval[i]:
            self._wait(q, key, self.dval[i])
        self._deps(q, reads, writes)
        self.dval[i] += 16
        ins = self.engs[q].dma_start(out=out, in_=in_, **kw)
        ins.then_inc(self.dsem[i], 16)
        self._record(key, self.dval[i], reads, writes)
        self.n_instr += 1

    def idma(self, out, in_, idx_ap, reads=(), writes=()):
        q = "pool"
        i = self.dnext
        self.dnext = (self.dnext + 1) % len(self.dsem)
        key = ("d", i)
        if self.dval[i]:
            self._wait(q, key, self.dval[i])
        self._deps(q, reads, writes)
        self.dval[i] += 16
        ins = self.nc.gpsimd.indirect_dma_start(out=out, out_offset=None, in_=in_,
                                                in_offset=bass.IndirectOffsetOnAxis(ap=idx_ap, axis=0))
        ins.then_inc(self.dsem[i], 16)
        self._record(key, self.dval[i], reads, writes)
        self.n_instr += 1

    def barrier(self):
        for e in ("pe", "act", "dve", "pool", "sp"):
            for i, v in enumerate(self.dval):
                if v:
                    self._wait(e, ("d", i), v)
            for k, v in self.cnt.items():
                if v and k != e:
                    self._wait(e, k, v)

    def finish(self, eng="sp"):
        for i, v in enumerate(self.dval):
            if v:
                self._wait(eng, ("d", i), v)
        for k, v in self.cnt.items():
            if v:
                self._wait(eng, k, v)


class Cfg:
    def __init__(self, seq, dec_batch, past_len):
        self.seq = seq
        self.tpc = seq // NCORE
        self.nch = self.tpc // 128
        self.dbt = dec_batch
        self.db = dec_batch // NCORE
        self.npg = past_len // 128
        n_used = dec_batch * self.npg
        self.npool = n_used + (n_used + 3) // 4
        self.nblk_p = seq // 256
        self.bpc = self.tpc // 256


def build(cfg, debug_l0=False):
    nc = bass.Bass("TRN2", target_bir_lowering=False)
    TPC, NCH, DB, NPG = cfg.tpc, cfg.nch, cfg.db, cfg.npg
    SEQ = cfg.seq
    NCHA = SEQ // 128

    def din(name, shape, dt=F32):
        return nc.dram_tensor(name, list(shape), dt, kind="ExternalInput").ap()

    def dout(name, shape, dt=F32):
        return nc.dram_tensor(name, list(shape), dt, kind="ExternalOutput").ap()

    xp = din("xp", [SEQ, D])
    xsm = din("xsm", [DB * 4, D])
    st_ssm = din("st_ssm", [DB, 1024, 128])
    st_sc = din("st_sc", [DB, 3, 2048])
    st_cc = din("st_cc", [DB, 30, 1024])
    w_in0 = din("w_in0", [128, 8, 6160])
    w_out0 = din("w_out0", [128, 16, 1024])
    g0_d = din("g0", [128, 8])
    cw_d = din("cw", [128, 16, 4])
    cb_d = din("cb", [128, 16])
    ccw_d = din("ccw", [128, 8, 31])
    ccb_d = din("ccb", [128, 8])
    lng_d = din("lng", [128, 8])
    lnb_d = din("lnb", [128, 8])
    dtb_d = din("dtb", [128, 16])
    alog_d = din("alog", [128, 16])
    dsk_d = din("dsk", [128, 16])
    sng_d = din("sng", [128, 8])
    cmask_d = din("cmask", [128, 8])
    NSLOT_ = TPC // 128
    wq_d = din("wq", [128, 8, 1024])
    wkv_d = din("wkv", [128, 8, 512])
    wg_d = din("wg", [128, 8, 1024])
    wo_d = din("wo", [64, 16, 1024])
    g1_d = din("g1", [128, 8])
    qg_d = din("qg", [128, 1])
    kgb_d = din("kgb", [128, 256])
    pm_d = din("pm", [128, NSLOT_ + 1, 64])
    pneg_d = din("pneg", [128, NSLOT_ + 1, 64])
    om_d = din("om", [128, NSLOT_ + 1, 64])
    maskM_d = din("maskM", [128, 8, 128])
    maskS_d = din("maskS", [4, 16])
    oidx_d = din("oidx", [128, NSLOT_], I32)
    pidx_d = din("pidx", [128, 1])
    ptb_d = din("ptb", [128, DB, NPG], I32)
    ck_d = din("ck", [cfg.npool * 128, 256])
    cv_d = din("cv", [cfg.npool * 128, 256])
    k_p = dout("k_p", [SEQ, 256])
    v_p = dout("v_p", [SEQ, 256])
    k_s = dout("k_s", [DB * 4, 256])
    v_s = dout("v_s", [DB * 4, 256])

    y_p = dout("y_p", [TPC, D])
    y_s = dout("y_s", [DB * 4, D])
    ssm_p = dout("ssm_p", [1024, 128])
    ssm_s = dout("ssm_s", [DB, 1024, 128])
    sc_p = dout("sc_p", [3, 2048])
    sc_s = dout("sc_s", [DB, 3, 2048])
    cc_p = dout("cc_p", [30, 1024])
    cc_s = dout("cc_s", [DB, 30, 1024])

    x1_d = nc.dram_tensor("x1_d", [SEQ + DB * 4, D], F32, kind="Internal").ap()
    KT_d = nc.dram_tensor("KT_d", [2, 128, SEQ + DB * 4], BF16, kind="Internal").ap()
    V_d = nc.dram_tensor("V_d", [SEQ + DB * 4, 260], BF16, kind="Internal").ap()
    KTs_d = nc.dram_tensor("KTs_d", [DB, 2, 128, NPG * 128], BF16, kind="Internal").ap()
    Vs_d = nc.dram_tensor("Vs_d", [DB, NPG * 128, 260], BF16, kind="Internal").ap()

    st = ExitStack()
    with st:
        S = Sched(nc, st)
        A = lambda eng, fn, r=(), w=(): S.op(eng, fn, reads=r, writes=w)

        cur = [st]

        def T(name, shape, dt=F32):
            return cur[0].enter_context(nc.sbuf_tensor(name, list(shape), dt))

        pst = [st.enter_context(nc.psum_tensor("ps%d" % i, [128, 512], F32)) for i in range(8)]
        psi = [0]

        psn = [8]

        def nextps(n=None):
            n = n or psn[0]
            i = psi[0] % n
            psi[0] = (i + 1) % n
            return pst[i], "ps%d" % i

        identf = T("identf", [128, 128])
        triU = T("triU", [128, 128])
        SU = T("SU", [128, 128])
        onesf = T("onesf", [128, 128])
        epsT = T("epsT", [128, 1])
        oneT = T("oneT", [128, 1])
        for t_, cmp_, sgn in ((identf, ALU.is_equal, 1), (triU, ALU.is_ge, -1)):
            A("pool", lambda e, t_=t_: e.memset(t_[:], 1.0), w=[t_.name])
            A("pool", lambda e, t_=t_, cmp_=cmp_, sgn=sgn: e.affine_select(out=t_[:], in_=t_[:], pattern=[[-sgn, 128]], compare_op=cmp_,
                                                          fill=0.0, base=0, channel_multiplier=sgn), r=[t_.name], w=[t_.name])
        A("dve", lambda e: e.tensor_scalar(out=SU[:], in0=triU[:], scalar1=-1.0, scalar2=1.0, op0=ALU.mult, op1=ALU.add), r=["triU"], w=["SU"])
        A("pool", lambda e: e.memset(onesf[:], 1.0), w=["onesf"])
        A("pool", lambda e: e.memset(epsT[:], EPS), w=["epsT"])
        A("pool", lambda e: e.memset(oneT[:], 1.0), w=["oneT"])

        BD = T("BD", [128, 128])
        A("pool", lambda e: e.memset(BD[:], 0.0), w=["BD"])
        A("pool", lambda e: e.memset(BD[0:64, 0:64], 1.0), r=["BD"], w=["BD"])
        A("pool", lambda e: e.memset(BD[64:128, 64:128], 1.0), r=["BD"], w=["BD"])
        st0 = ExitStack()
        cur[0] = st0
        def ld(name, src, shape):
            t = T(name, shape)
            S.dma(t[:], src, writes=[name])
            return t
        g0 = ld("g0t", g0_d[:, :], [128, 8])
        cw = ld("cwt", cw_d[:, :, :], [128, 16, 4])
        cb = ld("cbt", cb_d[:, :], [128, 16])
        ccw = ld("ccwt", ccw_d[:, :, :], [128, 8, 31])
        ccb = ld("ccbt", ccb_d[:, :], [128, 8])
        lng = ld("lngt", lng_d[:, :], [128, 8])
        lnb = ld("lnbt", lnb_d[:, :], [128, 8])
        dtb = ld("dtbt", dtb_d[:, :], [128, 16])
        Ab = ld("Abt", alog_d[:, :], [128, 16])
        dsk = ld("dskt", dsk_d[:, :], [128, 16])
        sng = ld("sngt", sng_d[:, :], [128, 8])
        cmask = ld("cmaskt", cmask_d[:, :], [128, 8])
        A("act", lambda e: e.activation(out=Ab[:], in_=Ab[:], func=AF.Exp), r=["Abt"], w=["Abt"])
        A("dve", lambda e: e.tensor_scalar(out=Ab[:], in0=Ab[:], scalar1=-1.0, scalar2=None, op0=ALU.mult), r=["Abt"], w=["Abt"])

        Win = T("Win", [128, 8, 6160], BF16)
        Wout = T("Wout", [128, 16, 1024], BF16)
        xbc_c = T("xbc_c", [128, 16, 128])
        xbc_flat = xbc_c[:, :, :].rearrange("p a b -> p (a b)")
        XBCC = ["xbcc%d" % t for t in range(16)]
        stg = [xbc_flat[:, 0:770], xbc_flat[:, 1024:1024 + 770]]
        stgn = [XBCC[:8], XBCC[8:]]
        si = 0
        cast_engs = ["dve", "pool"]
        for kt in range(8):
            for q8 in range(8):
                sb = stg[si % 2]
                S.dma(sb, w_in0[:, kt, q8 * 770:(q8 + 1) * 770], writes=stgn[si % 2])
                A(cast_engs[si % 2], lambda e, sb=sb, kt=kt, q8=q8: e.tensor_scalar(
                    out=Win[:, kt, q8 * 770:(q8 + 1) * 770], in0=sb, scalar1=g0[:, kt:kt + 1], scalar2=None, op0=ALU.mult),
                    r=stgn[si % 2] + ["g0t"], w=["Win"])
                si += 1
        for t_ in range(16):
            for hf in range(2):
                sb = stg[si % 2]
                S.dma(sb[:, :512], w_out0[:, t_, hf * 512:(hf + 1) * 512], writes=stgn[si % 2])
                A(cast_engs[si % 2], lambda e, sb=sb, t_=t_, hf=hf: e.tensor_copy(out=Wout[:, t_, hf * 512:(hf + 1) * 512], in_=sb[:, :512]), r=stgn[si % 2], w=["Wout"])
                si += 1

        xt = T("xt", [128, D])
        xn = T("xn", [128, D])
        ss = T("ss", [128, 8])
        xnT = T("xnT", [128, 8, 128], BF16)
        xbc_f = T("xbc_f", [128, 16, HP + 128])
        gl_f = T("gl_f", [128, 8, HP + 128])
        scg = T("scg", [128, 8, 128], BF16)
        c_f = T("c_f", [128, 8, 128])
        cat_f = T("cat_f", [128, 16, 128], BF16)
        CTb = T("CTb", [128, 4, 128], BF16)
        BTb = T("BTb", [128, 4, 128], BF16)
        Btm = T("Btm", [128, 512], BF16)
        dtt = T("dtt", [128, 8, 16])
        aSU4 = [T("aSU0", [128, 4, 128])] * 2
        dec4 = [T("dec0", [128, 4, 128])] * 2
        cbm = T("cbm", [128, 4, 128])
        MT = T("MT", [128, 16, 128], BF16)
        xdt = T("xdt", [128, 1024], BF16)
        xdte = T("xdte", [128, 1024], BF16)
        yacc = T("yacc", [128, 1024])
        ytmp = T("ytmp", [128, 1024])
        H = T("H", [128, 1024])
        Hb = T("Hb", [128, 1024], BF16)
        ptmp2 = [T("ptmpa", [128, 128])] * 2
        cdb = T("cdb", [128, 16])
        Atot = T("Atot", [128, 16])
        hist_tm = xbc_flat[:32, :]
        hout = xbc_flat[:32, :]
        sz = xn
        csq = ytmp[:, :].rearrange("p (a b) -> p a b", b=128)
        sig = aSU4[0]
        lnst = dec4[0]

        def fm_inproj(L, col0s, evac):
            ps, pn = nextps()
            fns = []
            for j, c0 in enumerate(col0s):
                for kt in range(8):
                    fns.append(lambda e, j=j, c0=c0, kt=kt: e.matmul(out=ps[:, j * 128:j * 128 + L], lhsT=Win[:, kt, c0:c0 + 128],
                                                                   rhs=xnT[:, kt, :L], start=(kt == 0), stop=(kt == 7)))
            S.mm(fns, reads=["Win", "xnT"], writes=[pn])
            v = ps[:, :].rearrange("p (a b) -> p a b", b=128)[:, :len(col0s), :L]
            evac(v, pn)

        def transposes_to_tm(L, srcs, src_names, nm):
            ps, pn = nextps()
            S.mm([lambda e, j=j, s=s: e.transpose(out=ps[:L, j * 128:(j + 1) * 128], in_=s, identity=identf[:, :])
                  for j, s in enumerate(srcs)], reads=list(src_names) + ["identf"], writes=[pn])
            return ps[:L, :len(srcs) * 128], pn

        def conv_all(tiles, engs, out_tile, src_tile, ntap, w_t, b_t, L, src_pref, w_names, out_pref):
            o0 = HP - (ntap - 1)
            for j in range(ntap):
                for t in tiles:
                    eng = engs[t]
                    out_ap = out_tile[:, t, :L]
                    rn = [src_pref % t] + w_names
                    wn = out_pref % t
                    src = src_tile[:, t, o0 + j:o0 + j + L]
                    if j == 0:
                        A(eng, lambda e, out_ap=out_ap, src=src, t=t: e.tensor_scalar(out=out_ap, in0=src, scalar1=w_t[:, t, 0:1], scalar2=b_t[:, t:t + 1],
                                                                                    op0=ALU.mult, op1=ALU.add), r=rn, w=[wn])
                    elif eng == "dve":
                        A(eng, lambda e, out_ap=out_ap, src=src, t=t, j=j: e.scalar_tensor_tensor(out=out_ap, in0=src, scalar=w_t[:, t, j:j + 1], in1=out_ap,
                                                                                              op0=ALU.mult, op1=ALU.add), r=rn + [wn], w=[wn])
                    else:
                        pt_ = ptmp2[t % 2]
                        A(eng, lambda e, src=src, t=t, j=j, pt_=pt_: e.tensor_tensor(out=pt_[:, :L], in0=src, in1=w_t[:, t, j:j + 1].broadcast_to([128, L]), op=ALU.mult),
                          r=rn, w=[pt_.name])
                        A(eng, lambda e, out_ap=out_ap, pt_=pt_: e.tensor_tensor(out=out_ap, in0=out_ap, in1=pt_[:, :L], op=ALU.add), r=[pt_.name, wn], w=[wn])

        def l0_chunk(src_ap, L, mode, dst_x1=None):
            full = mode == "full"
            nx = 16 if (full or mode == "halo2") else 12
            S.dma(xt[:L, :], src_ap, writes=["xt"])
            A("act", lambda e: e.activation(out=xn[:L, :], in_=xt[:L, :], func=AF.Square, scale=1.0 / 32, accum_out=ss[:L, 0:1]),
              r=["xt"], w=["xn", "ss"])
            A("act", lambda e: e.activation(out=ss[:L, 1:2], in_=ss[:L, 0:1], func=AF.Ln, bias=epsT[:L, 0:1]), r=["ss", "epsT"], w=["ss"])
            A("act", lambda e: e.activation(out=ss[:L, 2:3], in_=ss[:L, 1:2], func=AF.Exp, scale=-0.5), r=["ss"], w=["ss"])
            A("dve", lambda e: e.tensor_scalar(out=xn[:L, :], in0=xt[:L, :], scalar1=ss[:L, 2:3], scalar2=None, op0=ALU.mult),
              r=["xt", "ss"], w=["xn"])
            for half in range(2):
                ps, pn = nextps()
                S.mm([lambda e, j=j: e.transpose(out=ps[:, j * 128:j * 128 + L], in_=xn[:L, (half * 4 + j) * 128:(half * 4 + j + 1) * 128],
                                                 identity=identf[:L, :L]) for j in range(4)], reads=["xn", "identf"], writes=[pn])
                v = ps[:, :].rearrange("p (a b) -> p a b", b=128)[:, :, :L]
                A("act", lambda e, v=v, half=half: e.activation(out=xnT[:, half * 4:half * 4 + 4, :L], in_=v, func=AF.Copy), r=[pn], w=["xnT"])
            if full or mode == "halo2":
                for g in range(2):
                    def evb(v, pn):
                        A("act", lambda e: e.activation(out=sig[:, :, :L], in_=v, func=AF.Sigmoid), r=[pn], w=["aSU0"])
                    fm_inproj(L, [4112 + 128 * t for t in range(4 * g, 4 * g + 4)], evb)

                    def eva(v, pn, g=g):
                        A("dve", lambda e: e.tensor_tensor(out=gl_f[:, 4 * g:4 * g + 4, HP:HP + L], in0=v, in1=sig[:, :, :L], op=ALU.mult),
                          r=[pn, "aSU0"], w=["glf%d" % t for t in range(4 * g, 4 * g + 4)])
                    fm_inproj(L, [3088 + 128 * t for t in range(4 * g, 4 * g + 4)], eva)
            if full:
                conv_all(list(range(8)), ["dve"] * 7 + ["pool"] * 1, c_f, gl_f, 31, ccw, ccb, L, "glf%d", ["ccwt", "ccbt"], "cf%d")
                if L >= HP:
                    for t in range(8):
                        A("pool", lambda e, t=t: e.tensor_copy(out=gl_f[:, t, 0:HP], in_=gl_f[:, t, L:L + HP]), r=["glf%d" % t], w=["glf%d" % t])
            for g in range(nx // 4):
                def ev(v, pn, g=g):
                    A("act", lambda e: e.activation(out=xbc_f[:, 4 * g:4 * g + 4, HP:HP + L], in_=v, func=AF.Copy), r=[pn],
                      w=["xbcf%d" % t for t in range(4 * g, 4 * g + 4)])
                fm_inproj(L, [1024 + 128 * t for t in range(4 * g, 4 * g + 4)], ev)
            if mode in ("halo1", "halo2"):
                for t in range(nx):
                    A("pool", lambda e, t=t: e.tensor_copy(out=xbc_f[:, t, 0:HP], in_=xbc_f[:, t, HP:2 * HP]), r=["xbcf%d" % t], w=["xbcf%d" % t])
                if mode == "halo2":
                    for t in range(8):
                        A("pool", lambda e, t=t: e.tensor_copy(out=gl_f[:, t, 0:HP], in_=gl_f[:, t, HP:2 * HP]), r=["glf%d" % t], w=["glf%d" % t])
                return
            if full:
                for g in range(2):
                    def evc(v, pn, g=g):
                        A("act", lambda e: e.activation(out=scg[:, 4 * g:4 * g + 4, :L], in_=v, func=AF.Silu), r=[pn], w=["scg"])
                    fm_inproj(L, [5136 + 128 * t for t in range(4 * g, 4 * g + 4)], evc)
            if full:
                for half in range(2):
                    ps, pn = nextps()
                    S.mm([lambda e, kt=kt, half=half: e.matmul(out=ps[:L, :], lhsT=xnT[:, kt, :L], rhs=Win[:, kt, 512 * half:512 * half + 512],
                                                              start=(kt == 0), stop=(kt == 7)) for kt in range(8)], reads=["xnT", "Win"], writes=[pn])
                    sl = slice(512 * half, 512 * half + 512)
                    A("act", lambda e, ps=ps, sl=sl: e.activation(out=sz[:L, sl], in_=ps[:L, :], func=AF.Silu), r=[pn], w=["xn"])

            ps, pn = nextps()
            S.mm([lambda e, kt=kt: e.matmul(out=ps[:L, 0:16], lhsT=xnT[:, kt, :L], rhs=Win[:, kt, 3072:3088], start=(kt == 0), stop=(kt == 7))
                  for kt in range(8)], reads=["xnT", "Win"], writes=[pn])
            dtr, dta, dte, dtl, dtv, av, acs, eacs = [dtt[:L, i, :] for i in range(8)]
            A("dve", lambda e: e.tensor_tensor(out=dtr, in0=ps[:L, 0:16], in1=dtb[:L, :], op=ALU.add), r=[pn, "dtbt"], w=["dtt"])
            A("dve", lambda e: e.scalar_tensor_tensor(out=dta, in0=dtr, scalar=-1.0, in1=dtr, op0=ALU.mult, op1=ALU.min), r=["dtt"], w=["dtt"])
            A("act", lambda e: e.activation(out=dte, in_=dta, func=AF.Exp), r=["dtt"], w=["dtt"])
            A("act", lambda e: e.activation(out=dtl, in_=dte, func=AF.Ln, bias=oneT[:L, 0:1]), r=["dtt", "oneT"], w=["dtt"])
            A("dve", lambda e: e.scalar_tensor_tensor(out=dtv, in0=dtr, scalar=0.0, in1=dtl, op0=ALU.max, op1=ALU.add), r=["dtt"], w=["dtt"])
            A("dve", lambda e: e.tensor_tensor(out=av, in0=dtv, in1=Ab[:L, :], op=ALU.mult), r=["dtt", "Abt"], w=["dtt"])
            ps2, pn2 = nextps()
            S.mm([lambda e: e.matmul(out=ps2[:L, 0:16], lhsT=triU[:L, :L], rhs=av, start=True, stop=True),
                  lambda e: e.matmul(out=ps2[:, 16:32], lhsT=onesf[:L, :], rhs=av, start=True, stop=True)],
                 reads=["triU", "onesf", "dtt"], writes=[pn2])
            A("dve", lambda e: e.tensor_copy(out=acs, in_=ps2[:L, 0:16]), r=[pn2], w=["dtt"])
            A("act", lambda e: e.activation(out=cdb[:, :], in_=ps2[:, 16:32], func=AF.Exp), r=[pn2], w=["cdb"])
            A("dve", lambda e: e.tensor_tensor(out=Atot[:, :], in0=Atot[:, :], in1=ps2[:, 16:32], op=ALU.add), r=[pn2, "Atot"], w=["Atot"])
            A("dve", lambda e: e.tensor_tensor(out=dta, in0=ps2[:L, 16:32], in1=acs, op=ALU.subtract), r=[pn2, "dtt"], w=["dtt"])
            A("act", lambda e: e.activation(out=dta, in_=dta, func=AF.Exp), r=["dtt"], w=["dtt"])
            A("act", lambda e: e.activation(out=eacs, in_=acs, func=AF.Exp), r=["dtt"], w=["dtt"])
            conv_all(list(range(nx)), ["dve"] * 16, xbc_c, xbc_f, 4, cw, cb, L, "xbcf%d", ["cwt", "cbt"], "xbcc%d")
            for g in range(nx // 4):
                A("act", lambda e, g=g: e.activation(out=xbc_c[:, 4 * g:4 * g + 4, :L], in_=xbc_c[:, 4 * g:4 * g + 4, :L], func=AF.Silu),
                  r=["xbcc%d" % t for t in range(4 * g, 4 * g + 4)], w=["xbcc%d" % t for t in range(4 * g, 4 * g + 4)])
            if L >= HP:
                for t in range(nx):
                    A("pool", lambda e, t=t: e.tensor_copy(out=xbc_f[:, t, 0:HP], in_=xbc_f[:, t, L:L + HP]), r=["xbcf%d" % t], w=["xbcf%d" % t])
            for g in range(2):
                v, pn = transposes_to_tm(L, [xbc_c[:, 4 * g + j, :L] for j in range(4)], ["xbcc%d" % (4 * g + j) for j in range(4)], "xs")
                v3 = v.rearrange("p (h q) -> p h q", q=64)
                sl = slice(512 * g, 512 * (g + 1))
                A("dve", lambda e, v3=v3, sl=sl, g=g: e.tensor_tensor(out=xdt[:L, sl].rearrange("p (h q) -> p h q", q=64), in0=v3,
                                                                      in1=dtv[:, 8 * g:8 * g + 8].unsqueeze(2).broadcast_to([L, 8, 64]), op=ALU.mult),
                  r=[pn, "dtt"], w=["xdt"])
                if full:
                    A("dve", lambda e, v=v, sl=sl, g=g: e.tensor_tensor(out=yacc[:L, sl].rearrange("p (h q) -> p h q", q=64), in0=v.rearrange("p (h q) -> p h q", q=64), in1=dsk[:L, 8 * g:8 * g + 8].unsqueeze(2).broadcast_to([L, 8, 64]), op=ALU.mult),
                      r=[pn, "dskt"], w=["yacc"])
            v, pn = transposes_to_tm(L, [xbc_c[:, 8 + j, :L] for j in range(4)], ["xbcc%d" % (8 + j) for j in range(4)], "B")
            A("act", lambda e: e.activation(out=Btm[:L, :], in_=v, func=AF.Copy), r=[pn], w=["Btm"])
            A("dve", lambda e: e.tensor_tensor(out=xdte[:L, :].rearrange("p (h q) -> p h q", q=64), in0=xdt[:L, :].rearrange("p (h q) -> p h q", q=64),
                                               in1=dta.unsqueeze(2).broadcast_to([L, 16, 64]), op=ALU.mult), r=["xdt", "dtt"], w=["xdte"])
            if full:
                A("pool", lambda e: e.tensor_copy(out=BTb[:, :, :L], in_=xbc_c[:, 8:12, :L]), r=["xbcc%d" % t for t in range(8, 12)], w=["BTb"])
                A("pool", lambda e: e.tensor_copy(out=CTb[:, :, :L], in_=xbc_c[:, 12:16, :L]), r=["xbcc%d" % t for t in range(12, 16)], w=["CTb"])
                psc, pnc = nextps()
                S.mm([lambda e, g=g: e.matmul(out=psc[:L, g * 128:g * 128 + L], lhsT=BTb[:, g, :L], rhs=CTb[:, g, :L], start=True, stop=True)
                      for g in range(4)], reads=["BTb", "CTb"], writes=[pnc])
                A("dve", lambda e: e.tensor_tensor(out=cbm[:L, :, :L], in0=psc[:L, :].rearrange("p (a b) -> p a b", b=128)[:, :, :L],
                                                   in1=triU[:L, :L].unsqueeze(1).broadcast_to([L, 4, L]), op=ALU.mult), r=[pnc, "triU"], w=["cbm"])
                for q4 in range(4):
                    aS, dc = aSU4[q4 % 2], dec4[q4 % 2]
                    A("pool", lambda e, aS=aS, q4=q4: e.tensor_tensor(out=aS[:L, :, :L], in0=SU[:L, :L].unsqueeze(1).broadcast_to([L, 4, L]),
                                                                     in1=av[:, 4 * q4:4 * q4 + 4].unsqueeze(2).broadcast_to([L, 4, L]), op=ALU.mult),
                      r=["SU", "dtt"], w=[aS.name])
                    ps, pn = nextps()
                    S.mm([lambda e, j=j, aS=aS: e.matmul(out=ps[:L, j * 128:j * 128 + L], lhsT=aS[:L, j, :L], rhs=triU[:L, :L], start=True, stop=True)
                          for j in range(4)], reads=[aS.name, "triU"], writes=[pn])
                    A("act", lambda e, ps=ps, dc=dc: e.activation(out=dc[:L, :, :L], in_=ps[:L, :].rearrange("p (a b) -> p a b", b=128)[:, :, :L],
                                                                  func=AF.Exp), r=[pn], w=[dc.name])
                    A("dve", lambda e, q4=q4, dc=dc: e.tensor_tensor(out=MT[:L, 4 * q4:4 * q4 + 4, :L], in0=dc[:L, :, :L],
                                                                     in1=cbm[:L, q4, :L].unsqueeze(1).broadcast_to([L, 4, L]), op=ALU.mult), r=[dc.name, "cbm"], w=["MT"])
                for half in range(2):
                    psd, pnd = nextps()
                    S.mm([lambda e, h=h: e.matmul(out=psd[:L, (h % 8) * 64:(h % 8) * 64 + 64], lhsT=MT[:L, h, :L], rhs=xdt[:L, h * 64:(h + 1) * 64],
                                                  start=True, stop=True) for h in range(8 * half, 8 * half + 8)], reads=["MT", "xdt"], writes=[pnd])
                    pso, pno = nextps()
                    S.mm([lambda e, g=g: e.matmul(out=pso[:L, (g % 2) * 256:(g % 2) * 256 + 256], lhsT=CTb[:, g, :L], rhs=Hb[:, g * 256:(g + 1) * 256],
                                                  start=True, stop=True) for g in range(2 * half, 2 * half + 2)], reads=["CTb", "Hb"], writes=[pno])
                    sl = slice(512 * half, 512 * half + 512)
                    A("dve", lambda e, pso=pso, sl=sl, half=half: e.tensor_tensor(
                        out=ytmp[:L, sl].rearrange("p (h q) -> p h q", q=64), in0=pso[:L, :].rearrange("p (h q) -> p h q", q=64),
                        in1=eacs[:, 8 * half:8 * half + 8].unsqueeze(2).broadcast_to([L, 8, 64]), op=ALU.mult), r=[pno, "dtt"], w=["ytmp"])
                    A("pool", lambda e, sl=sl: e.tensor_tensor(out=yacc[:L, sl], in0=yacc[:L, sl], in1=ytmp[:L, sl], op=ALU.add), r=["yacc", "ytmp"], w=["yacc"])
                    A("dve", lambda e, psd=psd, sl=sl: e.tensor_tensor(out=yacc[:L, sl], in0=yacc[:L, sl], in1=psd[:L, :], op=ALU.add), r=["yacc", pnd], w=["yacc"])
            for half in range(2):
                pss, pns = nextps()
                S.mm([lambda e, g=g: e.matmul(out=pss[:, (g % 2) * 256:(g % 2) * 256 + 256], lhsT=Btm[:L, g * 128:(g + 1) * 128],
                                              rhs=xdte[:L, g * 256:(g + 1) * 256], start=True, stop=True) for g in range(2 * half, 2 * half + 2)],
                     reads=["Btm", "xdte"], writes=[pns])
                sl = slice(512 * half, 512 * half + 512)
                A("dve", lambda e, sl=sl, half=half: e.tensor_tensor(out=H[:, sl].rearrange("p (h q) -> p h q", q=64), in0=H[:, sl].rearrange("p (h q) -> p h q", q=64),
                                                                     in1=cdb[:, 8 * half:8 * half + 8].unsqueeze(2).broadcast_to([128, 8, 64]), op=ALU.mult),
                  r=["H", "cdb", "Hb"], w=["H"])
                A("dve", lambda e, sl=sl, pss=pss: e.tensor_tensor(out=H[:, sl], in0=H[:, sl], in1=pss[:, :], op=ALU.add), r=["H", pns], w=["H"])
            if not full:
                return
            A("act", lambda e: e.activation(out=Hb[:, :], in_=H[:, :], func=AF.Copy), r=["H"], w=["Hb"])
            A("dve", lambda e: e.tensor_tensor(out=yacc[:L, :], in0=yacc[:L, :], in1=sz[:L, :], op=ALU.mult), r=["yacc", "xn"], w=["yacc"])
            for g in range(4):
                A("act", lambda e, g=g: e.activation(out=ytmp[:L, 256 * g:256 * g + 256], in_=yacc[:L, 256 * g:256 * g + 256], func=AF.Square, scale=1.0 / 16,
                                                     accum_out=ss[:L, 4 + g:5 + g]), r=["yacc"], w=["ytmp", "ss"])
            A("act", lambda e: e.activation(out=ss[:L, 4:8], in_=ss[:L, 4:8], func=AF.Ln, bias=epsT[:L, 0:1]), r=["ss", "epsT"], w=["ss"])
            A("act", lambda e: e.activation(out=ss[:L, 4:8], in_=ss[:L, 4:8], func=AF.Exp, scale=-0.5), r=["ss"], w=["ss"])
            A("dve", lambda e: e.tensor_tensor(out=yacc[:L, :].rearrange("p (g q) -> p g q", q=256), in0=yacc[:L, :].rearrange("p (g q) -> p g q", q=256),
                                               in1=ss[:L, 4:8].unsqueeze(2).broadcast_to([L, 4, 256]), op=ALU.mult), r=["yacc", "ss"], w=["yacc"])
            for half in range(2):
                ps, pn = nextps()
                S.mm([lambda e, j=j, half=half: e.transpose(out=ps[:, j * 128:j * 128 + L], in_=yacc[:L, (half * 4 + j) * 128:(half * 4 + j + 1) * 128],
                                                           identity=identf[:L, :L]) for j in range(4)], reads=["yacc", "identf"], writes=[pn])
                v = ps[:, :].rearrange("p (a b) -> p a b", b=128)[:, :, :L]
                for j in range(4):
                    A("act", lambda e, v=v, half=half, j=j: e.activation(out=cat_f[:, half * 4 + j, :L], in_=v[:, j, :], func=AF.Copy,
                                                                         scale=sng[:, half * 4 + j:half * 4 + j + 1]), r=[pn, "sngt"], w=["cat_f"])
            cfn = ["cf%d" % t for t in range(8)]
            A("act", lambda e: e.activation(out=csq[:, :, :L], in_=c_f[:, :, :L], func=AF.Square), r=cfn, w=["ytmp"])
            ps, pn = nextps()
            S.mm([lambda e, t=t: e.matmul(out=ps[:, 0:L], lhsT=onesf[:, :], rhs=c_f[:, t, :L], start=(t == 0), stop=(t == 7)) for t in range(8)] +
                 [lambda e, t=t: e.matmul(out=ps[:, 128:128 + L], lhsT=onesf[:, :], rhs=csq[:, t, :L], start=(t == 0), stop=(t == 7)) for t in range(8)],
                 reads=cfn + ["ytmp", "onesf"], writes=[pn])
            mean, ex2, var, rstd = [lnst[:, i, :L] for i in range(4)]
            A("dve", lambda e: e.tensor_scalar(out=mean, in0=ps[:, 0:L], scalar1=1.0 / 1024, scalar2=None, op0=ALU.mult), r=[pn], w=["dec0"])
            A("dve", lambda e: e.tensor_scalar(out=ex2, in0=ps[:, 128:128 + L], scalar1=1.0 / 1024, scalar2=None, op0=ALU.mult), r=[pn], w=["dec0"])
            A("dve", lambda e: e.tensor_tensor(out=var, in0=mean, in1=mean, op=ALU.mult), r=["dec0"], w=["dec0"])
            A("dve", lambda e: e.tensor_tensor(out=var, in0=ex2, in1=var, op=ALU.subtract), r=["dec0"], w=["dec0"])
            A("act", lambda e: e.activation(out=rstd, in_=var, func=AF.Ln, bias=epsT[:, 0:1]), r=["dec0", "epsT"], w=["dec0"])
            A("act", lambda e: e.activation(out=rstd, in_=rstd, func=AF.Exp, scale=-0.5), r=["dec0"], w=["dec0"])
            A("dve", lambda e: e.tensor_tensor(out=c_f[:, :, :L], in0=c_f[:, :, :L], in1=mean.unsqueeze(1).broadcast_to([128, 8, L]), op=ALU.subtract),
              r=cfn + ["dec0"], w=cfn)
            A("dve", lambda e: e.tensor_tensor(out=c_f[:, :, :L], in0=c_f[:, :, :L], in1=rstd.unsqueeze(1).broadcast_to([128, 8, L]), op=ALU.mult),
              r=cfn + ["dec0"], w=cfn)
            A("pool", lambda e: e.tensor_tensor(out=c_f[:, :, :L], in0=c_f[:, :, :L], in1=lng[:, :].unsqueeze(2).broadcast_to([128, 8, L]), op=ALU.mult),
              r=cfn + ["lngt"], w=cfn)
            A("pool", lambda e: e.tensor_tensor(out=c_f[:, :, :L], in0=c_f[:, :, :L], in1=lnb[:, :].unsqueeze(2).broadcast_to([128, 8, L]), op=ALU.add),
              r=cfn + ["lnbt"], w=cfn)
            A("act", lambda e: e.activation(out=c_f[:, :, :L], in_=c_f[:, :, :L], func=AF.Silu), r=cfn, w=cfn)
            A("dve", lambda e: e.tensor_tensor(out=cat_f[:, 8:16, :L], in0=c_f[:, :, :L], in1=scg[:, :, :L], op=ALU.mult), r=cfn + ["scg"], w=["cat_f"])
            for half in range(2):
                ps, pn = nextps()
                S.mm([lambda e, t=t, half=half: e.matmul(out=ps[:L, :], lhsT=cat_f[:, t, :L], rhs=Wout[:, t, 512 * half:512 * half + 512],
                                                        start=(t == 0), stop=(t == 15)) for t in range(16)], reads=["cat_f", "Wout"], writes=[pn])
                sl = slice(512 * half, 512 * half + 512)
                A("dve", lambda e, ps=ps, sl=sl: e.tensor_tensor(out=xn[:L, sl], in0=xt[:L, sl], in1=ps[:L, :], op=ALU.add), r=["xt", pn], w=["xn"])
            S.dma(dst_x1, xn[:L, :], reads=["xn"], writes=["x1_d"])

        def load_hist_from_state(b):
            S.dma(hist_tm[:3, :], st_sc[b, :, :], writes=XBCC)
            for g in range(4):
                ps, pn = nextps()
                S.mm([lambda e, j=j, g=g: e.transpose(out=ps[:, j * 128:j * 128 + 3], in_=hist_tm[:3, (4 * g + j) * 128:(4 * g + j + 1) * 128],
                                                      identity=identf[:3, :3]) for j in range(4)], reads=XBCC + ["identf"], writes=[pn])
                A("act", lambda e, ps=ps, g=g: e.activation(out=xbc_f[:, 4 * g:4 * g + 4, HP - 3:HP], in_=ps[:, :].rearrange("p (a b) -> p a b", b=128)[:, :, :3],
                                                            func=AF.Copy), r=[pn], w=["xbcf%d" % t for t in range(4 * g, 4 * g + 4)])
            S.dma(hist_tm[:30, :1024], st_cc[b, :, :], writes=XBCC)
            for g in range(2):
                ps, pn = nextps()
                S.mm([lambda e, j=j, g=g: e.transpose(out=ps[:, j * 128:j * 128 + 30], in_=hist_tm[:30, (4 * g + j) * 128:(4 * g + j + 1) * 128],
                                                      identity=identf[:30, :30]) for j in range(4)], reads=XBCC + ["identf"], writes=[pn])
                A("act", lambda e, ps=ps, g=g: e.activation(out=gl_f[:, 4 * g:4 * g + 4, HP - 30:HP], in_=ps[:, :].rearrange("p (a b) -> p a b", b=128)[:, :, :30],
                                                            func=AF.Copy), r=[pn], w=["glf%d" % t for t in range(4 * g, 4 * g + 4)])
            for g in range(2):
                S.dma(ytmp[:, 512 * g:512 * g + 512].rearrange("p (a n) -> p a n", n=128),
                      st_ssm[b, 512 * g:512 * g + 512, :].rearrange("(a p) n -> p a n", p=128), writes=["ytmp"])
                ps, pn = nextps()
                S.mm([lambda e, j=j, g=g: e.transpose(out=ps[:, j * 128:(j + 1) * 128], in_=ytmp[:, 512 * g + j * 128:512 * g + (j + 1) * 128],
                                                      identity=identf[:, :]) for j in range(4)], reads=["ytmp", "identf"], writes=[pn])
                A("dve", lambda e, ps=ps, g=g: e.tensor_copy(out=H[:, 512 * g:512 * g + 512], in_=ps[:, :]), r=[pn, "Hb"], w=["H"])
            A("act", lambda e: e.activation(out=Hb[:, :], in_=H[:, :], func=AF.Copy), r=["H"], w=["Hb"])

        def store_state_outputs(L, sc_dst, cc_dst, ssm_dst):
            for g in range(4):
                ps, pn = nextps()
                S.mm([lambda e, j=j, g=g: e.transpose(out=ps[:3, j * 128:(j + 1) * 128], in_=xbc_f[:, 4 * g + j, HP + L - 3:HP + L], identity=identf[:, :])
                      for j in range(4)], reads=["xbcf%d" % t for t in range(4 * g, 4 * g + 4)] + ["identf"], writes=[pn])
                A("act", lambda e, ps=ps, g=g: e.activation(out=hout[:3, 512 * g:512 * g + 512], in_=ps[:3, :], func=AF.Copy), r=[pn], w=XBCC)
            S.dma(sc_dst, hout[:3, :], reads=XBCC)
            for g in range(2):
                ps, pn = nextps()
                S.mm([lambda e, j=j, g=g: e.transpose(out=ps[:30, j * 128:(j + 1) * 128], in_=gl_f[:, 4 * g + j, HP + L - 30:HP + L], identity=identf[:, :])
                      for j in range(4)], reads=["glf%d" % t for t in range(4 * g, 4 * g + 4)] + ["identf"], writes=[pn])
                A("act", lambda e, ps=ps, g=g: e.activation(out=hout[:30, 512 * g:512 * g + 512], in_=ps[:30, :], func=AF.Copy), r=[pn], w=XBCC)
            S.dma(cc_dst, hout[:30, :1024], reads=XBCC)
            for g in range(2):
                ps, pn = nextps()
                S.mm([lambda e, j=j, g=g: e.transpose(out=ps[:, j * 128:(j + 1) * 128], in_=H[:, 512 * g + j * 128:512 * g + (j + 1) * 128], identity=identf[:, :])
                      for j in range(4)], reads=["H", "identf"], writes=[pn])
                A("dve", lambda e, ps=ps, g=g: e.tensor_copy(out=ytmp[:, 512 * g:512 * g + 512], in_=ps[:, :]), r=[pn], w=["ytmp"])
                S.dma(ssm_dst[512 * g:512 * g + 512, :].rearrange("(a p) n -> p a n", p=128),
                      ytmp[:, 512 * g:512 * g + 512].rearrange("p (a n) -> p a n", n=128), reads=["ytmp"])

        def zero_state():
            A("pool", lambda e: e.memset(H[:, :], 0.0), r=["Hb"], w=["H"])
            A("pool", lambda e: e.memset(Hb[:, :], 0.0), w=["Hb"])
            A("pool", lambda e: e.memset(Atot[:, :], 0.0), w=["Atot"])

        zero_state()
        for t in range(16):
            A("pool", lambda e, t=t: e.memset(xbc_f[:, t, 0:HP], 0.0), w=["xbcf%d" % t])
        for t in range(8):
            A("pool", lambda e, t=t: e.memset(gl_f[:, t, 0:HP], 0.0), w=["glf%d" % t])
        for c in range(NCHA):
            l0_chunk(xp[c * 128:(c + 1) * 128, :], 128, "full", dst_x1=x1_d[c * 128:(c + 1) * 128, :])
        store_state_outputs(128, sc_p[:, :], cc_p[:, :], ssm_p)
        for b in range(DB):
            load_hist_from_state(b)
            l0_chunk(xsm[b * 4:(b + 1) * 4, :], 4, "full", dst_x1=x1_d[SEQ + b * 4:SEQ + (b + 1) * 4, :])
            store_state_outputs(4, sc_s[b, :, :], cc_s[b, :, :], ssm_s[b])


        S.barrier()
        st0.close()
        if debug_l0:
            st1 = ExitStack(); cur[0] = st1
            xt = T("xt_dbg", [128, D])
            for c in range(NCH):
                S.dma(xt[:, :], x1_d[c * 128:(c + 1) * 128, :], reads=["x1_d"], writes=["xt"])
                S.dma(y_p[c * 128:(c + 1) * 128, :], xt[:, :], reads=["xt"])
            S.dma(xt[:DB * 4, :], x1_d[SEQ:SEQ + DB * 4, :], reads=["x1_d"], writes=["xt"])
            S.dma(y_s[:, :], xt[:DB * 4, :], reads=["xt"])
            S.finish("sp")
            st1.close()
            return nc
        st1 = ExitStack(); cur[0] = st1
        NSLOT = TPC // 128
        NB = 64
        psn[0] = 6
        Wq = T("Wq", [128, 8, 1024], BF16)
        Wg = T("Wg", [128, 8, 1024], BF16)
        Wo = T("Wo", [64, 16, 1024], BF16)
        g1 = T("g1t", [128, 8]); S.dma(g1[:], g1_d[:, :], writes=["g1t"])
        qg = T("qgt", [128, 1]); S.dma(qg[:], qg_d[:, :], writes=["qgt"])
        kgb = T("kgbt", [128, 256]); S.dma(kgb[:], kgb_d[:, :], writes=["kgbt"])
        pm = T("pmt", [128, 64])
        pneg = T("pnegt", [128, 64])
        om = T("omt", [128, 64])
        maskM = T("maskMt", [128, 8, 128]); S.dma(maskM[:], maskM_d[:, :, :], writes=["maskMt"])
        maskS = T("maskSt", [4, 16]); S.dma(maskS[:], maskS_d[:, :], writes=["maskSt"])
        oidx = T("oidxt", [128, NSLOT], I32); S.dma(oidx[:], oidx_d[:, :], writes=["oidx"])
        pidx = T("pidxt", [128, 1]); S.dma(pidx[:], pidx_d[:, :], writes=["pidx"])
        Eoh = T("Eoh", [64, 64, 128], BF16)
        A("pool", lambda e: e.memset(Eoh[:], 1.0), w=["Eoh"])
        A("pool", lambda e: e.affine_select(out=Eoh[:], in_=Eoh[:], pattern=[[-1, 64], [0, 128]], compare_op=ALU.is_equal, fill=0.0,
                                            base=0, channel_multiplier=1), r=["Eoh"], w=["Eoh"])
        xt1 = T("xt1", [128, D])
        xn1 = T("xn1", [128, D])
        ss1 = T("ss1", [128, 8])
        xT1 = T("xT1", [128, 8, 128], BF16)
        kms = T("kms", [128, 2, max(NCHA, NPG)])
        kmT = T("kmT", [128, 2, 64])
        A("pool", lambda e: e.memset(kmT[:], 0.0), w=["kmT"])
        stA = ExitStack(); cur[0] = stA
        Wkv = T("Wkv", [128, 8, 512], BF16)
        stg1 = [T("stg1a", [128, 1024]), T("stg1b", [128, 1024])]
        si = 0
        for (wd, wt, ncol, wname) in ((wq_d, Wq, 1024, "Wq"), (wkv_d, Wkv, 512, "Wkv"), (wg_d, Wg, 1024, "Wg")):
            for kt in range(8):
                sb = stg1[si % 2]
                S.dma(sb[:, :ncol], wd[:, kt, :], writes=[sb.name])
                A("dve" if si % 2 == 0 else "pool", lambda e, sb=sb, wt=wt, kt=kt, ncol=ncol: e.tensor_scalar(
                    out=wt[:, kt, :], in0=sb[:, :ncol], scalar1=g1[:, kt:kt + 1], scalar2=None, op0=ALU.mult), r=[sb.name, "g1t"], w=[wname])
                si += 1
        for h in range(16):
            sb = stg1[si % 2]
            S.dma(sb[:64, :], wo_d[:, h, :], writes=[sb.name])
            A("dve" if si % 2 == 0 else "pool", lambda e, sb=sb, h=h: e.tensor_copy(out=Wo[:, h, :], in_=sb[:64, :]), r=[sb.name], w=["Wo"])
            si += 1

        KV = T("KV", [128, 512])
        KTst = T("KTst", [128, 2, 128], BF16)
        Vb = T("Vb", [128, 4, 65], BF16)
        A("pool", lambda e: e.memset(Vb[:], 1.0), w=["Vb"])

        def l1_norm_T(L):
            A("act", lambda e: e.activation(out=xn1[:L, :], in_=xt1[:L, :], func=AF.Square, scale=1.0 / 32, accum_out=ss1[:L, 0:1]), r=["xt1"], w=["xn1", "ss1"])
            A("act", lambda e: e.activation(out=ss1[:L, 1:2], in_=ss1[:L, 0:1], func=AF.Ln, bias=epsT[:L, 0:1]), r=["ss1", "epsT"], w=["ss1"])
            A("act", lambda e: e.activation(out=ss1[:L, 2:3], in_=ss1[:L, 1:2], func=AF.Exp, scale=-0.5), r=["ss1"], w=["ss1"])
            A("dve", lambda e: e.tensor_scalar(out=xn1[:L, :], in0=xt1[:L, :], scalar1=ss1[:L, 2:3], scalar2=None, op0=ALU.mult), r=["xt1", "ss1"], w=["xn1"])
            for half in range(2):
                ps, pn = nextps()
                S.mm([lambda e, j=j, half=half: e.transpose(out=ps[:, j * 128:j * 128 + L], in_=xn1[:L, (half * 4 + j) * 128:(half * 4 + j + 1) * 128],
                                                           identity=identf[:L, :L]) for j in range(4)], reads=["xn1", "identf"], writes=[pn])
                v = ps[:, :].rearrange("p (a b) -> p a b", b=128)[:, :, :L]
                A("act", lambda e, v=v, half=half: e.activation(out=xT1[:, half * 4:half * 4 + 4, :L], in_=v, func=AF.Copy), r=[pn], w=["xT1"])

        def l1_kv(src, L, kdst, vdst, col0, chunk_idx):
            S.dma(xt1[:L, :], src, reads=["x1_d"], writes=["xt1"])
            l1_norm_T(L)
            ps, pn = nextps()
            S.mm([lambda e, kt=kt: e.matmul(out=ps[:L, :], lhsT=xT1[:, kt, :L], rhs=Wkv[:, kt, :], start=(kt == 0), stop=(kt == 7)) for kt in range(8)],
                 reads=["xT1", "Wkv"], writes=[pn])
            for h in range(4):
                A("act", lambda e, h=h: e.activation(out=xn1[:L, h * 64:(h + 1) * 64], in_=ps[:L, h * 64:(h + 1) * 64], func=AF.Square, scale=0.125,
                                                     accum_out=ss1[:L, 4 + h:5 + h]), r=[pn], w=["xn1", "ss1"])
            A("act", lambda e: e.activation(out=ss1[:L, 4:8], in_=ss1[:L, 4:8], func=AF.Ln, bias=epsT[:L, 0:1]), r=["ss1", "epsT"], w=["ss1"])
            A("act", lambda e: e.activation(out=ss1[:L, 4:8], in_=ss1[:L, 4:8], func=AF.Exp, scale=-0.5), r=["ss1"], w=["ss1"])
            A("dve", lambda e: e.tensor_tensor(out=KV[:L, 0:256].rearrange("p (h d) -> p h d", d=64), in0=ps[:L, 0:256].rearrange("p (h d) -> p h d", d=64),
                                               in1=ss1[:L, 4:8].unsqueeze(2).broadcast_to([L, 4, 64]), op=ALU.mult), r=[pn, "ss1"], w=["KV"])
            A("pool", lambda e: e.tensor_tensor(out=KV[:L, 0:256], in0=KV[:L, 0:256], in1=kgb[:L, :], op=ALU.mult), r=["KV", "kgbt"], w=["KV"])
            A("act", lambda e: e.activation(out=KV[:L, 256:512], in_=ps[:L, 256:512], func=AF.Copy), r=[pn], w=["KV"])
            S.dma(kdst, KV[:L, 0:256], reads=["KV"])
            S.dma(vdst, KV[:L, 256:512], reads=["KV"])
            ps2, pn2 = nextps()
            S.mm([lambda e, pr=pr: e.transpose(out=ps2[:, pr * 128:pr * 128 + L], in_=KV[:L, pr * 128:(pr + 1) * 128], identity=identf[:L, :L]) for pr in range(2)],
                 reads=["KV", "identf"], writes=[pn2])
            for pr in range(2):
                A("act", lambda e, pr=pr: e.activation(out=KTst[:, pr, :L], in_=ps2[:, pr * 128:pr * 128 + L], func=AF.Copy,
                                                       accum_out=kms[:, pr, chunk_idx:chunk_idx + 1]), r=[pn2], w=["KTst", "kms"])
                S.dma(KT_d[pr, :, col0:col0 + L], KTst[:, pr, :L], reads=["KTst"], writes=["KT_d"])
            A("pool", lambda e: e.tensor_copy(out=Vb[:L, :, 0:64], in_=KV[:L, 256:512].rearrange("p (h d) -> p h d", d=64)), r=["KV"], w=["Vb"])
            S.dma(V_d[col0:col0 + L, :], Vb[:L, :, :].rearrange("p h c -> p (h c)"), reads=["Vb"], writes=["V_d"])

        for t in range(NCHA):
            l1_kv(x1_d[t * 128:(t + 1) * 128, :], 128, k_p[t * 128:(t + 1) * 128, :], v_p[t * 128:(t + 1) * 128, :], t * 128, t)
        NBP = NCHA // 2
        kv2 = kms[:, :, 0:NCHA].rearrange("p a (n two) -> p a n two", two=2)
        A("dve", lambda e: e.tensor_tensor(out=kmT[:, :, 0:NBP], in0=kv2[:, :, :, 0], in1=kv2[:, :, :, 1], op=ALU.add), r=["kms"], w=["kmT"])
        A("dve", lambda e: e.tensor_scalar(out=kmT[:, :, 0:NBP], in0=kmT[:, :, 0:NBP], scalar1=1.0 / 256, scalar2=None, op0=ALU.mult), r=["kmT"], w=["kmT"])
        for b in range(DB):
            l1_kv(x1_d[SEQ + b * 4:SEQ + (b + 1) * 4, :], 4, k_s[b * 4:(b + 1) * 4, :], v_s[b * 4:(b + 1) * 4, :], SEQ + b * 4, NCHA - 1 if False else 0)

        S.barrier()
        stA.close()
        cur[0] = st1
        QF = T("QF", [128, 8, 128])
        sq1 = T("sq1", [128, 4, 128])
        QPf = T("QPf", [128, 16, 128])
        QPb = T("QPb", [128, 16, 128], BF16)
        SG = T("SG", [64, 16, 128], BF16)
        Gm = T("Gm", [128, 16, 64])
        m8 = T("m8", [128, 16, 8])
        bsel = Gm
        biasT = T("biasT", [64, 16, 128], BF16)
        ATg = T("ATg", [64, 16, 128], BF16)
        rdt = xn1
        bcs = T("bcs", [64, 512])
        atmp = T("atmp", [64, 512])
        PT = [T("PT0", [128, 512], BF16), T("PT1", [128, 512], BF16)]
        KTst2s = [T("KTst2", [128, 2, 128], BF16), T("KTst2b", [128, 2, 128], BF16)]
        A("pool", lambda e: e.memset(QPf[:], 0.0), w=["QPf"])

        def l1_q(L):
            l1_norm_T(L)
            for half in range(2):
                ps, pn = nextps()
                S.mm([lambda e, r=r, kt=kt, half=half: e.matmul(out=ps[:, r * 128:r * 128 + L], lhsT=Wq[:, kt, (4 * half + r) * 128:(4 * half + r + 1) * 128],
                                                             rhs=xT1[:, kt, :L], start=(kt == 0), stop=(kt == 7)) for r in range(4) for kt in range(8)],
                     reads=["Wq", "xT1"], writes=[pn])
                v = ps[:, :].rearrange("p (a b) -> p a b", b=128)[:, :, :L]
                A("act", lambda e, v=v: e.activation(out=sq1[:, :, :L], in_=v, func=AF.Square, scale=0.125), r=[pn], w=["sq1"])
                pss, pns = nextps()
                S.mm([lambda e, r=r: e.matmul(out=pss[:, r * 128:r * 128 + L], lhsT=BD[:, :], rhs=sq1[:, r, :L], start=True, stop=True) for r in range(4)],
                     reads=["BD", "sq1"], writes=[pns])
                vs = pss[:, :].rearrange("p (a b) -> p a b", b=128)[:, :, :L]
                A("act", lambda e, vs=vs: e.activation(out=sq1[:, :, :L], in_=vs, func=AF.Ln, bias=epsT[:, 0:1]), r=[pns, "epsT"], w=["sq1"])
                A("act", lambda e: e.activation(out=sq1[:, :, :L], in_=sq1[:, :, :L], func=AF.Exp, scale=-0.5), r=["sq1"], w=["sq1"])
                A("dve", lambda e, v=v, half=half: e.tensor_tensor(out=QF[:, 4 * half:4 * half + 4, :L], in0=v, in1=sq1[:, :, :L], op=ALU.mult), r=[pn, "sq1", "QFk0", "QFk1", "QFv0", "QFv1"], w=["QF", "QFk0", "QFk1", "QFv0", "QFv1"])
            A("dve", lambda e: e.tensor_scalar(out=QF[:, :, :L], in0=QF[:, :, :L], scalar1=qg[:, 0:1], scalar2=None, op0=ALU.mult), r=["QF", "qgt"], w=["QF"])
            for hk in range(4):
                rows = slice(64 * (hk % 2), 64 * (hk % 2) + 64)
                t0 = (hk // 2) * 4
                A("pool", lambda e, hk=hk, rows=rows, t0=t0: e.tensor_copy(out=QPf[rows, 4 * hk:4 * hk + 4, :L], in_=QF[rows, t0:t0 + 4, :L]), r=["QF"], w=["QPf"])
            A("act", lambda e: e.activation(out=QPb[:, :, :L], in_=QPf[:, :, :L], func=AF.Copy), r=["QPf"], w=["QPb"])
            for bk in range(4):
                ps, pn = nextps()
                S.mm([lambda e, r=r, kt=kt, bk=bk: e.matmul(out=ps[0:64, r * 128:r * 128 + L], lhsT=Wg[:, kt, (4 * bk + r) * 64:(4 * bk + r + 1) * 64],
                                                           rhs=xT1[:, kt, :L], start=(kt == 0), stop=(kt == 7)) for r in range(4) for kt in range(8)],
                     reads=["Wg", "xT1"], writes=[pn])
                A("act", lambda e, ps=ps, bk=bk: e.activation(out=SG[:, 4 * bk:4 * bk + 4, :L], in_=ps[0:64, :].rearrange("p (a b) -> p a b", b=128)[:, :, :L],
                                                              func=AF.Silu), r=[pn], w=["SG"])

        def moba_select(L, kmt, kmname, slot):
            S.dma(pm[:, :], pm_d[:, slot, :], writes=["pmt"])
            S.dma(pneg[:, :], pneg_d[:, slot, :], writes=["pnegt"])
            S.dma(om[:, :], om_d[:, slot, :], writes=["omt"])
            for half in range(2):
                ps, pn = nextps()
                S.mm([lambda e, i=i, half=half: e.matmul(out=ps[:L, i * 64:(i + 1) * 64], lhsT=QPf[:, 8 * half + i, :L], rhs=kmt[:, (8 * half + i) // 8, :],
                                                        start=True, stop=True) for i in range(8)], reads=["QPf", kmname], writes=[pn])
                A("dve", lambda e, ps=ps, half=half: e.tensor_tensor(out=Gm[:L, 8 * half:8 * half + 8, :], in0=ps[:L, :].rearrange("p (a b) -> p a b", b=64),
                                                                     in1=pm[:L, :].unsqueeze(1).broadcast_to([L, 8, 64]), op=ALU.mult), r=[pn, "pmt"], w=["Gm"])
            A("dve", lambda e: e.tensor_tensor(out=Gm[:L, :, :], in0=Gm[:L, :, :], in1=pneg[:L, :].unsqueeze(1).broadcast_to([L, 16, 64]), op=ALU.add),
              r=["Gm", "pnegt"], w=["Gm"])
            for h in range(16):
                A("dve", lambda e, h=h: e.max(out=m8[:L, h, :], in_=Gm[:L, h, :]), r=["Gm"], w=["m8"])
            for h in range(16):
                A("dve", lambda e, h=h: e.tensor_scalar(out=bsel[:L, h, :], in0=Gm[:L, h, :], scalar1=m8[:L, h, 2:3], scalar2=None, op0=ALU.is_ge), r=["Gm", "m8"], w=["Gm"])
            A("dve", lambda e: e.tensor_tensor(out=bsel[:L, :, :], in0=bsel[:L, :, :], in1=pm[:L, :].unsqueeze(1).broadcast_to([L, 16, 64]), op=ALU.mult),
              r=["Gm", "pmt"], w=["Gm"])
            A("dve", lambda e: e.tensor_tensor(out=bsel[:L, :, :], in0=bsel[:L, :, :], in1=om[:L, :].unsqueeze(1).broadcast_to([L, 16, 64]), op=ALU.add),
              r=["Gm", "omt"], w=["Gm"])
            A("dve", lambda e: e.tensor_scalar(out=bsel[:L, :, :], in0=bsel[:L, :, :], scalar1=-NEG, scalar2=NEG, op0=ALU.mult, op1=ALU.add), r=["Gm"], w=["Gm"])
            for bk in range(4):
                ps, pn = nextps()
                S.mm([lambda e, r=r, bk=bk: e.transpose(out=ps[0:64, r * 128:r * 128 + L], in_=bsel[:L, 4 * bk + r, :], identity=identf[:L, :L]) for r in range(4)],
                     reads=["Gm", "identf"], writes=[pn])
                A("act", lambda e, ps=ps, bk=bk: e.activation(out=biasT[:, 4 * bk:4 * bk + 4, :L], in_=ps[0:64, :].rearrange("p (a b) -> p a b", b=128)[:, :, :L],
                                                              func=AF.Copy), r=[pn], w=["biasT"])

        def finish_group(L, hk, psO, pnO):
            W4 = 4 * L
            A("dve", lambda e: e.reciprocal(out=rdt[64:65, :W4], in_=psO[64:65, :W4]), r=[pnO], w=["xn1"])
            psb, pnb = nextps()
            S.mm([lambda e: e.matmul(out=psb[0:64, :W4], lhsT=onesf[64:65, 0:64], rhs=rdt[64:65, :W4], start=True, stop=True)], reads=["onesf", "xn1"], writes=[pnb])
            A("act", lambda e: e.activation(out=bcs[:, :W4], in_=psb[0:64, :W4], func=AF.Copy), r=[pnb], w=["bcs"])
            A("dve", lambda e: e.tensor_tensor(out=atmp[:, :W4], in0=psO[0:64, :W4], in1=bcs[:, :W4], op=ALU.mult), r=[pnO, "bcs"], w=["atmp"])
            A("pool", lambda e: e.tensor_tensor(out=ATg[:, 4 * hk:4 * hk + 4, :L], in0=atmp[:, :W4].rearrange("p (a b) -> p a b", b=L), in1=SG[:, 4 * hk:4 * hk + 4, :L],
                                                op=ALU.mult), r=["atmp", "SG"], w=["ATg"])

        def out_proj(L, dst):
            for half in range(2):
                ps, pn = nextps()
                S.mm([lambda e, h=h, half=half: e.matmul(out=ps[:L, :], lhsT=ATg[:, h, :L], rhs=Wo[:, h, 512 * half:512 * half + 512], start=(h == 0), stop=(h == 15))
                      for h in range(16)], reads=["ATg", "Wo"], writes=[pn])
                sl = slice(512 * half, 512 * half + 512)
                A("dve", lambda e, ps=ps, sl=sl: e.tensor_tensor(out=xn1[:L, sl], in0=xt1[:L, sl], in1=ps[:L, :], op=ALU.add), r=["xt1", pn], w=["xn1"])
            S.dma(dst, xn1[:L, :], reads=["xn1"])

        st2 = ExitStack(); cur[0] = st2
        NKT = NCHA
        NKB = max(NKT, NPG)
        KTp1 = T("KTp0", [128, NKB * 128], BF16)
        KTp = [KTp1, KTp1]
        Vp = [T("Vp%d" % i, [128, NKB, 65], BF16) for i in range(2)]
        bufi = 0
        for j in range(NSLOT):
            nkt = 8 * j + 8
            S.idma(xt1[:, :], x1_d[:, :], oidx[:, j:j + 1], reads=["x1_d", "oidx"], writes=["xt1"])
            l1_q(128)
            moba_select(128, kmT, "kmT", j)
            for hk in range(4):
                pr = hk // 2
                if hk % 2 == 0:
                    ktp = KTp[pr]
                    S.dma(ktp[:, :nkt * 128], KT_d[pr, :, 0:nkt * 128], reads=["KT_d"], writes=[ktp.name])
                vp = Vp[hk % 2]
                S.dma(vp[:, :nkt, :], V_d[0:nkt * 128, hk * 65:(hk + 1) * 65].rearrange("(k p) c -> p k c", p=128), reads=["V_d"], writes=[vp.name])
                psO, pnO = pst[7 - (hk % 2)], "ps%d" % (7 - (hk % 2))
                def emit_st(kt, ktp=ktp, hk=hk):
                    ps, pn = nextps(6)
                    S.mm([lambda e: e.matmul(out=ps[:, :], lhsT=ktp[:, kt * 128:(kt + 1) * 128],
                                             rhs=QPb[:, 4 * hk:4 * hk + 4, :].rearrange("p a b -> p (a b)"), start=True, stop=False),
                          lambda e: e.matmul(out=ps[:, :], lhsT=Eoh[:, kt // 2, :], rhs=biasT[:, 4 * hk:4 * hk + 4, :].rearrange("p a b -> p (a b)"),
                                             start=False, stop=True)], reads=[ktp.name, "QPb", "Eoh", "biasT"], writes=[pn])
                    return ps, pn
                pend = [emit_st(k_) for k_ in range(min(2, nkt))]
                for kt in range(nkt):
                    ps, pn = pend.pop(0)
                    if kt + 2 < nkt:
                        pend.append(emit_st(kt + 2))
                    pt = PT[kt % 2]
                    A("act", lambda e, ps=ps, pt=pt: e.activation(out=pt[:, :], in_=ps[:, :], func=AF.Exp, scale=0.125), r=[pn], w=[pt.name])
                    if kt >= 8 * j:
                        A("dve", lambda e, pt=pt, kt=kt, j=j: e.tensor_tensor(out=pt[:, :].rearrange("p (a b) -> p a b", b=128), in0=pt[:, :].rearrange("p (a b) -> p a b", b=128),
                                                                             in1=maskM[:, kt - 8 * j, :].unsqueeze(1).broadcast_to([128, 4, 128]), op=ALU.mult),
                          r=[pt.name, "maskMt"], w=[pt.name])
                    S.mm([lambda e, kt=kt, vp=vp, pt=pt, psO=psO, nkt=nkt: e.matmul(out=psO[0:65, :], lhsT=vp[:, kt, :], rhs=pt[:, :], start=(kt == 0), stop=(kt == nkt - 1))],
                         reads=[vp.name, pt.name], writes=[pnO])
                finish_group(128, hk, psO, pnO)
            out_proj(128, y_p[j * 128:(j + 1) * 128, :])

        PTs = Gm[:, :, :].rearrange("p a b -> p (a b)").bitcast(BF16)
        PTn = T("PTn", [4, 16], BF16)
        KTn = T("KTn", [128, 2, 4], BF16)
        Vn = T("Vn", [4, 4, 65], BF16)
        ptf = m8[:, :, :].rearrange("p a b -> p (a b)")[:, :NPG]
        pti = T("pti", [128, NPG], I32)
        kmTs = kmT
        VsAs = [T("VsA", [128, 4, 65], BF16)] * 2
        QFl = QF[:, :, :].rearrange("p a b -> p (a b)")
        Kpg = [QFl[:, 0:256], QFl[:, 256:512]]
        Vpg = [QFl[:, 512:768], QFl[:, 768:1024]]
        A("pool", lambda e: e.memset(VsAs[0][:], 1.0), w=["VsA"])
        A("pool", lambda e: e.memset(kmTs[:], 0.0), w=["kmT"])
        NBS = NPG // 2
        for b in range(DB):
            S.dma(pti[:, :], ptb_d[:, b, :], writes=["pti"])
            A("dve", lambda e: e.tensor_copy(out=ptf[:, :], in_=pti[:, :]), r=["pti"], w=["m8"])
            A("dve", lambda e: e.tensor_scalar(out=ptf[:, :], in0=ptf[:, :], scalar1=128.0, scalar2=pidx[:, 0:1], op0=ALU.mult, op1=ALU.add), r=["m8", "pidx"], w=["m8"])
            A("dve", lambda e: e.tensor_copy(out=pti[:, :], in_=ptf[:, :]), r=["m8"], w=["pti"])
            for pg in range(NPG):
                kp, vq = Kpg[pg % 2], Vpg[pg % 2]
                kn_, vn_ = "QFk%d" % (pg % 2), "QFv%d" % (pg % 2)
                KTst2, VsA = KTst2s[pg % 2], VsAs[pg % 2]
                S.idma(kp, ck_d[:, :], pti[:, pg:pg + 1], reads=["pti", "QF"], writes=[kn_])
                S.idma(vq, cv_d[:, :], pti[:, pg:pg + 1], reads=["pti", "QF"], writes=[vn_])
                ps, pn = nextps()
                S.mm([lambda e, pr=pr, kp=kp, ps=ps: e.transpose(out=ps[:, pr * 128:(pr + 1) * 128], in_=kp[:, pr * 128:(pr + 1) * 128], identity=identf[:, :]) for pr in range(2)],
                     reads=[kn_, "identf"], writes=[pn])
                for pr in range(2):
                    A("act", lambda e, pr=pr, pg=pg, ps=ps, KTst2=KTst2: e.activation(out=KTst2[:, pr, :], in_=ps[:, pr * 128:(pr + 1) * 128], func=AF.Copy,
                                                                                    accum_out=kms[:, pr, pg:pg + 1]), r=[pn], w=[KTst2.name + "_%d" % pr, "kms"])
                    S.dma(KTs_d[b, pr, :, pg * 128:(pg + 1) * 128], KTst2[:, pr, :], reads=[KTst2.name + "_%d" % pr], writes=["KTs_d"])
                A("pool", lambda e, vq=vq, VsA=VsA: e.tensor_copy(out=VsA[:, :, 0:64], in_=vq.rearrange("p (h d) -> p h d", d=64)), r=[vn_], w=[VsA.name])
                S.dma(Vs_d[b, pg * 128:(pg + 1) * 128, :], VsA[:, :, :].rearrange("p h c -> p (h c)"), reads=[VsA.name], writes=["Vs_d"])
            kv3 = kms[:, :, 0:NPG].rearrange("p a (n two) -> p a n two", two=2)
            A("dve", lambda e: e.tensor_tensor(out=kmTs[:, :, 0:NBS], in0=kv3[:, :, :, 0], in1=kv3[:, :, :, 1], op=ALU.add), r=["kms"], w=["kmT"])
            A("dve", lambda e: e.tensor_scalar(out=kmTs[:, :, 0:NBS], in0=kmTs[:, :, 0:NBS], scalar1=1.0 / 256, scalar2=None, op0=ALU.mult), r=["kmT"], w=["kmT"])
            S.dma(KTn[:, :, :], KT_d[:, :, SEQ + b * 4:SEQ + (b + 1) * 4].rearrange("a p c -> p a c"), reads=["KT_d"], writes=["KTn"])
            S.dma(Vn[:, :, :].rearrange("p h c -> p (h c)"), V_d[SEQ + b * 4:SEQ + (b + 1) * 4, :], reads=["V_d"], writes=["Vn"])
            S.dma(xt1[:4, :], x1_d[SEQ + b * 4:SEQ + (b + 1) * 4, :], reads=["x1_d"], writes=["xt1"])
            l1_q(4)
            moba_select(4, kmTs, "kmT", NSLOT)
            for hk in range(4):
                pr = hk // 2
                if hk % 2 == 0:
                    S.dma(KTp1[:, :NPG * 128], KTs_d[b, pr, :, :], reads=["KTs_d"], writes=["KTp0"])
                vp = Vp[hk % 2]
                S.dma(vp[:, :NPG, :], Vs_d[b, :, hk * 65:(hk + 1) * 65].rearrange("(k p) c -> p k c", p=128), reads=["Vs_d"], writes=[vp.name])
                qrhs = QPb[:, 4 * hk:4 * hk + 4, :4]
                brhs = biasT[:, 4 * hk:4 * hk + 4, :4]
                PPB = 32
                for bk in range((NPG + PPB - 1) // PPB):
                    ps, pn = nextps()
                    pgs = list(range(bk * PPB, min(NPG, (bk + 1) * PPB)))
                    fns = []
                    for pg in pgs:
                        o = ps[:, (pg - bk * PPB) * 16:(pg - bk * PPB + 1) * 16].rearrange("p (a b) -> p a b", b=4)
                        fns.append(lambda e, o=o, pg=pg, qrhs=qrhs: e.matmul(out=o, lhsT=KTp1[:, pg * 128:(pg + 1) * 128], rhs=qrhs, start=True, stop=False))
                        fns.append(lambda e, o=o, pg=pg, brhs=brhs: e.matmul(out=o, lhsT=Eoh[:, pg // 2, :], rhs=brhs, start=False, stop=True))
                    S.mm(fns, reads=["KTp0", "QPb", "Eoh", "biasT"], writes=[pn])
                    ncol = len(pgs) * 16
                    A("act", lambda e, ps=ps, bk=bk, ncol=ncol: e.activation(out=PTs[:, bk * PPB * 16:bk * PPB * 16 + ncol], in_=ps[:, :ncol], func=AF.Exp, scale=0.125),
                      r=[pn], w=["Gm"])
                ps, pn = nextps()
                S.mm([lambda e, ps=ps, pr=pr, qrhs=qrhs: e.matmul(out=ps[0:4, 0:16].rearrange("p (a b) -> p a b", b=4), lhsT=KTn[:, pr, :], rhs=qrhs, start=True, stop=True)],
                     reads=["KTn", "QPb"], writes=[pn])
                A("act", lambda e, ps=ps: e.activation(out=PTn[:, :], in_=ps[0:4, 0:16], func=AF.Exp, scale=0.125), r=[pn], w=["PTn"])
                A("dve", lambda e: e.tensor_tensor(out=PTn[:, :], in0=PTn[:, :], in1=maskS[:, :], op=ALU.mult), r=["PTn", "maskSt"], w=["PTn"])
                psO, pnO = pst[7 - (hk % 2)], "ps%d" % (7 - (hk % 2))
                fns = [lambda e, pg=pg, vp=vp, psO=psO: e.matmul(out=psO[0:65, 0:16], lhsT=vp[:, pg, :], rhs=PTs[:, pg * 16:(pg + 1) * 16], start=(pg == 0), stop=False)
                       for pg in range(NPG)]
                fns.append(lambda e, hk=hk, psO=psO: e.matmul(out=psO[0:65, 0:16], lhsT=Vn[:, hk, :], rhs=PTn[:, :], start=False, stop=True))
                S.mm(fns, reads=[vp.name, "Gm", "Vn", "PTn"], writes=[pnO])
                finish_group(4, hk, psO, pnO)
            out_proj(4, y_s[b * 4:(b + 1) * 4, :])
        S.barrier()
        st2.close()
        st1.close()
        S.finish("sp")
        print("instructions:", S.n_instr, "sem counts", S.cnt)
    return nc


def prep_inputs(cfg, inp):
    f = lambda a: np.ascontiguousarray(np.asarray(a, dtype=np.float32))
    TPC, DB = cfg.tpc, cfg.db
    xp_full = f(inp["x_prompt"])[0]
    common = {
        "w_in0": f(f(inp["w_in0"])[0].reshape(8, 128, 6160).transpose(1, 0, 2)),
        "w_out0": f(f(inp["w_out0"])[0].reshape(16, 128, 1024).transpose(1, 0, 2)),
        "g0": f(f(inp["norm0_g"])[0].reshape(8, 128).T),
        "cw": f(f(inp["ssd_conv_w"])[0].reshape(4, 16, 128).transpose(2, 1, 0)),
        "cb": f(f(inp["ssd_conv_b"])[0].reshape(16, 128).T),
        "ccw": f(f(inp["conf_conv_w"])[0].reshape(31, 8, 128).transpose(2, 1, 0)),
        "ccb": f(f(inp["conf_conv_b"])[0].reshape(8, 128).T),
        "lng": f(f(inp["conf_ln_g"])[0].reshape(8, 128).T),
        "lnb": f(f(inp["conf_ln_b"])[0].reshape(8, 128).T),
        "dtb": f(np.broadcast_to(f(inp["ssd_dt_bias"])[0][None, :], (128, 16))),
        "alog": f(np.broadcast_to(f(inp["ssd_a_log"])[0][None, :], (128, 16))),
        "dsk": f(np.broadcast_to(f(inp["ssd_d"])[0][None, :], (128, 16))),
        "sng": f(f(inp["ssd_norm_g"])[0].reshape(8, 128).T),
    }
    perm = [0, 4, 1, 5, 2, 6, 3, 7, 8, 12, 9, 13, 10, 14, 11, 15]
    w1 = f(inp["w_in1"])[0]
    wq = w1[:, :1024].reshape(1024, 16, 64)[:, perm, :].reshape(1024, 1024)
    wkv = w1[:, 1024:1536]
    wg = w1[:, 1536:2560]
    kt_layout = lambda w: f(w.reshape(8, 128, w.shape[1]).transpose(1, 0, 2))
    NSLOT = TPC // 128
    NPG = cfg.npg
    npool = cfg.npool
    common.update({
        "wq": kt_layout(wq), "wkv": kt_layout(wkv), "wg": kt_layout(wg),
        "wo": f(f(inp["w_out1"])[0].reshape(16, 64, 1024).transpose(1, 0, 2)),
        "g1": f(f(inp["norm1_g"])[0].reshape(8, 128).T),
        "qg": f(np.tile(f(inp["q_norm_g"])[0], 2).reshape(128, 1)),
        "kgb": f(np.broadcast_to(np.tile(f(inp["k_norm_g"])[0], 4)[None, :], (128, 256))),
        "maskS": f(np.tile((np.arange(4)[:, None] <= np.arange(4)[None, :]).astype(np.float32), (1, 4))),
        "pidx": f(np.arange(128, dtype=np.float32).reshape(128, 1)),
        "ck": f(inp["cache_k"]).reshape(npool * 128, 256),
        "cv": f(inp["cache_v"]).reshape(npool * 128, 256),
    })
    pt_all = np.ascontiguousarray(np.asarray(inp["page_table"], dtype=np.int32))
    tri = (np.arange(128)[:, None] <= np.arange(128)[None, :]).astype(np.float32)
    maps = []
    for c in range(NCORE):
        m = dict(common)
        pm = np.zeros((NSLOT + 1, 64), np.float32)
        om = np.zeros((NSLOT + 1, 64), np.float32)
        for j in range(NSLOT):
            own = (8 * j + c) // 2
            pm[j, :own] = 1.0
            om[j, own] = 1.0
        pm[NSLOT, :NPG // 2] = 1.0
        bc = lambda a: f(np.broadcast_to(a[None], (128,) + a.shape))
        m["pm"] = bc(pm)
        m["pneg"] = bc((pm - 1.0) * np.float32(1e30))
        m["om"] = bc(om)
        mm_ = np.ones((128, 8, 128), np.float32)
        for dl in range(8):
            if dl // 2 == c // 2:
                if dl == c:
                    mm_[:, dl, :] = tri
                elif dl > c:
                    mm_[:, dl, :] = 0.0
        m["maskM"] = mm_
        m["oidx"] = np.ascontiguousarray(((8 * np.arange(NSLOT)[None, :] + c) * 128 + np.arange(128)[:, None]).astype(np.int32))
        m["ptb"] = np.ascontiguousarray(np.broadcast_to(pt_all[c * DB:(c + 1) * DB][None], (128, DB, NPG)).astype(np.int32))
        m["xp"] = xp_full
        m["xsm"] = f(f(inp["x_sample"])[c * DB:(c + 1) * DB].reshape(DB * 4, D))
        m["st_ssm"] = f(f(inp["state_ssm"])[0, c * DB:(c + 1) * DB].reshape(DB, 1024, 128))
        m["st_sc"] = f(f(inp["state_ssd_conv"])[0, c * DB:(c + 1) * DB])
        m["st_cc"] = f(f(inp["state_conf_conv"])[0, c * DB:(c + 1) * DB])
        cm = np.zeros((128, 8), np.float32)
        cm[:, :c] = 1.0
        m["cmask"] = cm
        maps.append(m)
    return maps


_NC_CACHE = {}


def run(cfg, inp, debug_l0=False):
    key = (cfg.seq, cfg.dbt, cfg.npg, debug_l0)
    if key not in _NC_CACHE:
        _NC_CACHE[key] = build(cfg, debug_l0)
    nc = _NC_CACHE[key]
    maps = prep_inputs(cfg, inp)
    res = run_bass_kernel_spmd(nc, maps, core_ids=list(range(NCORE)))
    R = res.results
    cat = lambda k: np.concatenate([r[k] for r in R], axis=0)
    TPC, DB = cfg.tpc, cfg.db
    y_p = cat("y_p")[None]
    y_s = cat("y_s").reshape(cfg.dbt, 4, D)
    ssm_p = R[0]["ssm_p"].reshape(1, 1, 16, 64, 128)
    ssm_s = cat("ssm_s").reshape(1, cfg.dbt, 16, 64, 128)
    sc_p = R[0]["sc_p"].reshape(1, 1, 3, 2048)
    sc_s = cat("sc_s").reshape(1, cfg.dbt, 3, 2048)
    cc_p = R[0]["cc_p"].reshape(1, 1, 30, 1024)
    cc_s = cat("cc_s").reshape(1, cfg.dbt, 30, 1024)
    if debug_l0:
        return (y_p, y_s, ssm_p, ssm_s, sc_p, sc_s, cc_p, cc_s)
    NSLOT = TPC // 128
    yp = np.empty((cfg.seq, D), np.float32)
    for c in range(NCORE):
        for j in range(NSLOT):
            t = 8 * j + c
            yp[t * 128:(t + 1) * 128] = R[c]["y_p"][j * 128:(j + 1) * 128]
    y_p = yp[None]
    k_p = R[0]["k_p"].reshape(1, 1, cfg.seq, 4, 64)
    v_p = R[0]["v_p"].reshape(1, 1, cfg.seq, 4, 64)
    k_s = cat("k_s").reshape(1, cfg.dbt, 4, 4, 64)
    v_s = cat("v_s").reshape(1, cfg.dbt, 4, 4, 64)
    return (y_p, y_s, ssm_p, ssm_s, sc_p, sc_s, cc_p, cc_s, k_p, v_p, k_s, v_s)


def kernel(**inputs):
    cfg = Cfg(inputs["x_prompt"].shape[1], inputs["x_sample"].shape[0], inputs["page_table"].shape[1] * 128)
    return run(cfg, inputs)
```

```python
import numpy as np
from contextlib import ExitStack
import concourse.bass as bass
import concourse.mybir as mybir
from concourse.bass_utils import run_bass_kernel_spmd

F32 = mybir.dt.float32
BF16 = mybir.dt.bfloat16
I32 = mybir.dt.int32
ALU = mybir.AluOpType
AF = mybir.ActivationFunctionType
AX = mybir.AxisListType

NCORE = 8
D = 1024
HP = 32
EPS = 1e-6
NEG = -30000.0


class Sched:
    def __init__(self, nc, stack, n_dma_sems=24):
        self.nc = nc
        self.engs = {"pe": nc.tensor, "act": nc.scalar, "dve": nc.vector, "pool": nc.gpsimd, "sp": nc.sync}
        self.sem = {k: stack.enter_context(nc.semaphore("s_" + k)) for k in ("pe", "act", "dve", "pool")}
        self.cnt = {k: 0 for k in self.sem}
        self.dsem = [stack.enter_context(nc.semaphore("d%d" % i)) for i in range(n_dma_sems)]
        self.dval = [0] * n_dma_sems
        self.dnext = 0
        self.waited = {}
        self.lastw = {}
        self.reads = {}
        self.n_instr = 0

    def _semobj(self, key):
        return self.sem[key] if isinstance(key, str) else self.dsem[key[1]]

    def _wait(self, eng, key, val):
        if self.waited.get((eng, key), 0) >= val:
            return
        self.waited[(eng, key)] = val
        self.engs[eng].wait_ge(self._semobj(key), val)

    def _deps(self, eng, reads, writes):
        for b in reads:
            t = self.lastw.get(b)
            if t is not None:
                self._wait(eng, t[0], t[1])
        for b in writes:
            t = self.lastw.get(b)
            if t is not None:
                self._wait(eng, t[0], t[1])
            for k, v in self.reads.get(b, {}).items():
                if k != eng:
                    self._wait(eng, k, v)

    def _record(self, key, val, reads, writes):
        for b in reads:
            d = self.reads.setdefault(b, {})
            if d.get(key, 0) < val:
                d[key] = val
        for b in writes:
            self.lastw[b] = (key, val)
            self.reads[b] = {}

    def op(self, eng, fn, reads=(), writes=()):
        self._deps(eng, reads, writes)
        ins = fn(self.engs[eng])
        self.cnt[eng] += 1
        ins.then_inc(self.sem[eng], 1)
        self._record(eng, self.cnt[eng], reads, writes)
        self.n_instr += 1

    def mm(self, fns, reads=(), writes=()):
        self._deps("pe", reads, writes)
        ins = None
        for fn in fns:
            ins = fn(self.nc.tensor)
            self.n_instr += 1
        self.cnt["pe"] += 1
        ins.then_inc(self.sem["pe"], 1)
        self._record("pe", self.cnt["pe"], reads, writes)

    def dma(self, out, in_, reads=(), writes=(), q="sp", **kw):
        i = self.dnext
        self.dnext = (self.dnext + 1) % len(self.dsem)
        key = ("d", i)
        if self.dval[i]:
            self._wait(q, key, self.dval[i])
        self._deps(q, reads, writes)
        self.dval[i] += 16
        ins = self.engs[q].dma_start(out=out, in_=in_, **kw)
        ins.then_inc(self.dsem[i], 16)
        self._record(key, self.dval[i], reads, writes)
        self.n_instr += 1

    def idma(self, out, in_, idx_ap, reads=(), writes=()):
        q = "pool"
        i = self.dnext
        self.dnext = (self.dnext + 1) % len(self.dsem)
        key = ("d", i)
        if self.dval[i]:
            self._wait(q, key, self.dval[i])
        self._deps(q, reads, writes)
        self.dval[i] += 16
        ins = self.nc.gpsimd.indirect_dma_start(out=out, out_offset=None, in_=in_,
                                                in_offset=bass.IndirectOffsetOnAxis(ap=idx_ap, axis=0))
        ins.then_inc(self.dsem[i], 16)
        self._record(key, self.dval[i], reads, writes)
        self.n_instr += 1

    def barrier(self):
        for e in ("pe", "act", "dve", "pool", "sp"):
            for i, v in enumerate(self.dval):
                if v:
                    self._wait(e, ("d", i), v)
            for k, v in self.cnt.items():
                if v and k != e:
                    self._wait(e, k, v)

    def finish(self, eng="sp"):
        for i, v in enumerate(self.dval):
            if v:
                self._wait(eng, ("d", i), v)
        for k, v in self.cnt.items():
            if v:
                self._wait(eng, k, v)


class Cfg:
    def __init__(self, seq, dec_batch, past_len):
        self.seq = seq
        self.tpc = seq // NCORE
        self.nch = self.tpc // 128
        self.dbt = dec_batch
        self.db = dec_batch // NCORE
        self.npg = past_len // 128
        n_used = dec_batch * self.npg
        self.npool = n_used + (n_used + 3) // 4
        self.nblk_p = seq // 256
        self.bpc = self.tpc // 256


def build(cfg, debug_l0=False):
    nc = bass.Bass("TRN2", target_bir_lowering=False)
    TPC, NCH, DB, NPG = cfg.tpc, cfg.nch, cfg.db, cfg.npg
    SEQ = cfg.seq
    NCHA = SEQ // 128

    def din(name, shape, dt=F32):
        return nc.dram_tensor(name, list(shape), dt, kind="ExternalInput").ap()

    def dout(name, shape, dt=F32):
        return nc.dram_tensor(name, list(shape), dt, kind="ExternalOutput").ap()

    xp = din("xp", [SEQ, D])
    xsm = din("xsm", [DB * 4, D])
    st_ssm = din("st_ssm", [DB, 1024, 128])
    st_sc = din("st_sc", [DB, 3, 2048])
    st_cc = din("st_cc", [DB, 30, 1024])
    w_in0 = din("w_in0", [128, 8, 6160])
    w_out0 = din("w_out0", [128, 16, 1024])
    g0_d = din("g0", [128, 8])
    cw_d = din("cw", [128, 16, 4])
    cb_d = din("cb", [128, 16])
    ccw_d = din("ccw", [128, 8, 31])
    ccb_d = din("ccb", [128, 8])
    lng_d = din("lng", [128, 8])
    lnb_d = din("lnb", [128, 8])
    dtb_d = din("dtb", [128, 16])
    alog_d = din("alog", [128, 16])
    dsk_d = din("dsk", [128, 16])
    sng_d = din("sng", [128, 8])
    cmask_d = din("cmask", [128, 8])
    NSLOT_ = TPC // 128
    wq_d = din("wq", [128, 8, 1024])
    wkv_d = din("wkv", [128, 8, 512])
    wg_d = din("wg", [128, 8, 1024])
    wo_d = din("wo", [64, 16, 1024])
    g1_d = din("g1", [128, 8])
    qg_d = din("qg", [128, 1])
    kgb_d = din("kgb", [128, 256])
    pm_d = din("pm", [128, NSLOT_ + 1, 64])
    pneg_d = din("pneg", [128, NSLOT_ + 1, 64])
    om_d = din("om", [128, NSLOT_ + 1, 64])
    maskM_d = din("maskM", [128, 8, 128])
    maskS_d = din("maskS", [4, 16])
    oidx_d = din("oidx", [128, NSLOT_], I32)
    pidx_d = din("pidx", [128, 1])
    ptb_d = din("ptb", [128, DB, NPG], I32)
    ck_d = din("ck", [cfg.npool * 128, 256])
    cv_d = din("cv", [cfg.npool * 128, 256])
    k_p = dout("k_p", [SEQ, 256])
    v_p = dout("v_p", [SEQ, 256])
    k_s = dout("k_s", [DB * 4, 256])
    v_s = dout("v_s", [DB * 4, 256])

    y_p = dout("y_p", [TPC, D])
    y_s = dout("y_s", [DB * 4, D])
    ssm_p = dout("ssm_p", [1024, 128])
    ssm_s = dout("ssm_s", [DB, 1024, 128])
    sc_p = dout("sc_p", [3, 2048])
    sc_s = dout("sc_s", [DB, 3, 2048])
    cc_p = dout("cc_p", [30, 1024])
    cc_s = dout("cc_s", [DB, 30, 1024])

    x1_d = nc.dram_tensor("x1_d", [SEQ + DB * 4, D], F32, kind="Internal").ap()
    KT_d = nc.dram_tensor("KT_d", [2, 128, SEQ + DB * 4], BF16, kind="Internal").ap()
    V_d = nc.dram_tensor("V_d", [SEQ + DB * 4, 260], BF16, kind="Internal").ap()
    KTs_d = nc.dram_tensor("KTs_d", [DB, 2, 128, NPG * 128], BF16, kind="Internal").ap()
    Vs_d = nc.dram_tensor("Vs_d", [DB, NPG * 128, 260], BF16, kind="Internal").ap()

    st = ExitStack()
    with st:
        S = Sched(nc, st)
        A = lambda eng, fn, r=(), w=(): S.op(eng, fn, reads=r, writes=w)

        cur = [st]

        def T(name, shape, dt=F32):
            return cur[0].enter_context(nc.sbuf_tensor(name, list(shape), dt))

        pst = [st.enter_context(nc.psum_tensor("ps%d" % i, [128, 512], F32)) for i in range(8)]
        psi = [0]

        psn = [8]

        def nextps(n=None):
            n = n or psn[0]
            i = psi[0] % n
            psi[0] = (i + 1) % n
            return pst[i], "ps%d" % i

        identf = T("identf", [128, 128])
        triU = T("triU", [128, 128])
        SU = T("SU", [128, 128])
        onesf = T("onesf", [128, 128])
        epsT = T("epsT", [128, 1])
        oneT = T("oneT", [128, 1])
        for t_, cmp_, sgn in ((identf, ALU.is_equal, 1), (triU, ALU.is_ge, -1)):
            A("pool", lambda e, t_=t_: e.memset(t_[:], 1.0), w=[t_.name])
            A("pool", lambda e, t_=t_, cmp_=cmp_, sgn=sgn: e.affine_select(out=t_[:], in_=t_[:], pattern=[[-sgn, 128]], compare_op=cmp_,
                                                          fill=0.0, base=0, channel_multiplier=sgn), r=[t_.name], w=[t_.name])
        A("dve", lambda e: e.tensor_scalar(out=SU[:], in0=triU[:], scalar1=-1.0, scalar2=1.0, op0=ALU.mult, op1=ALU.add), r=["triU"], w=["SU"])
        A("pool", lambda e: e.memset(onesf[:], 1.0), w=["onesf"])
        A("pool", lambda e: e.memset(epsT[:], EPS), w=["epsT"])
        A("pool", lambda e: e.memset(oneT[:], 1.0), w=["oneT"])

        BD = T("BD", [128, 128])
        A("pool", lambda e: e.memset(BD[:], 0.0), w=["BD"])
        A("pool", lambda e: e.memset(BD[0:64, 0:64], 1.0), r=["BD"], w=["BD"])
        A("pool", lambda e: e.memset(BD[64:128, 64:128], 1.0), r=["BD"], w=["BD"])
        st0 = ExitStack()
        cur[0] = st0
        def ld(name, src, shape):
            t = T(name, shape)
            S.dma(t[:], src, writes=[name])
            return t
        g0 = ld("g0t", g0_d[:, :], [128, 8])
        cw = ld("cwt", cw_d[:, :, :], [128, 16, 4])
        cb = ld("cbt", cb_d[:, :], [128, 16])
        ccw = ld("ccwt", ccw_d[:, :, :], [128, 8, 31])
        ccb = ld("ccbt", ccb_d[:, :], [128, 8])
        lng = ld("lngt", lng_d[:, :], [128, 8])
        lnb = ld("lnbt", lnb_d[:, :], [128, 8])
        dtb = ld("dtbt", dtb_d[:, :], [128, 16])
        Ab = ld("Abt", alog_d[:, :], [128, 16])
        dsk = ld("dskt", dsk_d[:, :], [128, 16])
        sng = ld("sngt", sng_d[:, :], [128, 8])
        cmask = ld("cmaskt", cmask_d[:, :], [128, 8])
        A("act", lambda e: e.activation(out=Ab[:], in_=Ab[:], func=AF.Exp), r=["Abt"], w=["Abt"])
        A("dve", lambda e: e.tensor_scalar(out=Ab[:], in0=Ab[:], scalar1=-1.0, scalar2=None, op0=ALU.mult), r=["Abt"], w=["Abt"])

        Win = T("Win", [128, 8, 6160], BF16)
        Wout = T("Wout", [128, 16, 1024], BF16)
        xbc_c = T("xbc_c", [128, 16, 128])
        xbc_flat = xbc_c[:, :, :].rearrange("p a b -> p (a b)")
        XBCC = ["xbcc%d" % t for t in range(16)]
        stg = [xbc_flat[:, 0:770], xbc_flat[:, 1024:1024 + 770]]
        stgn = [XBCC[:8], XBCC[8:]]
        si = 0
        cast_engs = ["dve", "pool"]
        for kt in range(8):
            for q8 in range(8):
                sb = stg[si % 2]
                S.dma(sb, w_in0[:, kt, q8 * 770:(q8 + 1) * 770], writes=stgn[si % 2])
                A(cast_engs[si % 2], lambda e, sb=sb, kt=kt, q8=q8: e.tensor_scalar(
                    out=Win[:, kt, q8 * 770:(q8 + 1) * 770], in0=sb, scalar1=g0[:, kt:kt + 1], scalar2=None, op0=ALU.mult),
                    r=stgn[si % 2] + ["g0t"], w=["Win"])
                si += 1
        for t_ in range(16):
            for hf in range(2):
                sb = stg[si % 2]
                S.dma(sb[:, :512], w_out0[:, t_, hf * 512:(hf + 1) * 512], writes=stgn[si % 2])
                A(cast_engs[si % 2], lambda e, sb=sb, t_=t_, hf=hf: e.tensor_copy(out=Wout[:, t_, hf * 512:(hf + 1) * 512], in_=sb[:, :512]), r=stgn[si % 2], w=["Wout"])
                si += 1

        xt = T("xt", [128, D])
        xn = T("xn", [128, D])
        ss = T("ss", [128, 8])
        xnT = T("xnT", [128, 8, 128], BF16)
        xbc_f = T("xbc_f", [128, 16, HP + 128])
        gl_f = T("gl_f", [128, 8, HP + 128])
        scg = T("scg", [128, 8, 128], BF16)
        c_f = T("c_f", [128, 8, 128])
        cat_f = T("cat_f", [128, 16, 128], BF16)
        CTb = T("CTb", [128, 4, 128], BF16)
        BTb = T("BTb", [128, 4, 128], BF16)
        Btm = T("Btm", [128, 512], BF16)
        dtt = T("dtt", [128, 8, 16])
        aSU4 = [T("aSU0", [128, 4, 128])] * 2
        dec4 = [T("dec0", [128, 4, 128])] * 2
        cbm = T("cbm", [128, 4, 128])
        MT = T("MT", [128, 16, 128], BF16)
        xdt = T("xdt", [128, 1024], BF16)
        xdte = T("xdte", [128, 1024], BF16)
        yacc = T("yacc", [128, 1024])
        ytmp = T("ytmp", [128, 1024])
        H = T("H", [128, 1024])
        Hb = T("Hb", [128, 1024], BF16)
        ptmp2 = [T("ptmpa", [128, 128])] * 2
        cdb = T("cdb", [128, 16])
        Atot = T("Atot", [128, 16])
        hist_tm = xbc_flat[:32, :]
        hout = xbc_flat[:32, :]
        sz = xn
        csq = ytmp[:, :].rearrange("p (a b) -> p a b", b=128)
        sig = aSU4[0]
        lnst = dec4[0]

        def fm_inproj(L, col0s, evac):
            ps, pn = nextps()
            fns = []
            for j, c0 in enumerate(col0s):
                for kt in range(8):
                    fns.append(lambda e, j=j, c0=c0, kt=kt: e.matmul(out=ps[:, j * 128:j * 128 + L], lhsT=Win[:, kt, c0:c0 + 128],
                                                                   rhs=xnT[:, kt, :L], start=(kt == 0), stop=(kt == 7)))
            S.mm(fns, reads=["Win", "xnT"], writes=[pn])
            v = ps[:, :].rearrange("p (a b) -> p a b", b=128)[:, :len(col0s), :L]
            evac(v, pn)

        def transposes_to_tm(L, srcs, src_names, nm):
            ps, pn = nextps()
            S.mm([lambda e, j=j, s=s: e.transpose(out=ps[:L, j * 128:(j + 1) * 128], in_=s, identity=identf[:, :])
                  for j, s in enumerate(srcs)], reads=list(src_names) + ["identf"], writes=[pn])
            return ps[:L, :len(srcs) * 128], pn

        def conv_all(tiles, engs, out_tile, src_tile, ntap, w_t, b_t, L, src_pref, w_names, out_pref, taps=None):
            o0 = HP - (ntap - 1)
            for j in (taps if taps is not None else range(ntap)):
                for t in tiles:
                    eng = engs[t]
                    out_ap = out_tile[:, t, :L]
                    rn = [src_pref % t] + w_names
                    wn = out_pref % t
                    src = src_tile[:, t, o0 + j:o0 + j + L]
                    if j == 0:
                        A(eng, lambda e, out_ap=out_ap, src=src, t=t: e.tensor_scalar(out=out_ap, in0=src, scalar1=w_t[:, t, 0:1], scalar2=b_t[:, t:t + 1],
                                                                                    op0=ALU.mult, op1=ALU.add), r=rn, w=[wn])
                    elif eng == "dve":
                        A(eng, lambda e, out_ap=out_ap, src=src, t=t, j=j: e.scalar_tensor_tensor(out=out_ap, in0=src, scalar=w_t[:, t, j:j + 1], in1=out_ap,
                                                                                              op0=ALU.mult, op1=ALU.add), r=rn + [wn], w=[wn])
                    else:
                        pt_ = ptmp2[t % 2]
                        A(eng, lambda e, src=src, t=t, j=j, pt_=pt_: e.tensor_tensor(out=pt_[:, :L], in0=src, in1=w_t[:, t, j:j + 1].broadcast_to([128, L]), op=ALU.mult),
                          r=rn, w=[pt_.name])
                        A(eng, lambda e, out_ap=out_ap, pt_=pt_: e.tensor_tensor(out=out_ap, in0=out_ap, in1=pt_[:, :L], op=ALU.add), r=[pt_.name, wn], w=[wn])

        def l0_chunk(src_ap, L, mode, dst_x1=None):
            full = mode == "full"
            nx = 16 if (full or mode == "halo2") else 12
            S.dma(xt[:L, :], src_ap, writes=["xt"])
            A("act", lambda e: e.activation(out=xn[:L, :], in_=xt[:L, :], func=AF.Square, scale=1.0 / 32, accum_out=ss[:L, 0:1]),
              r=["xt"], w=["xn", "ss"])
            A("act", lambda e: e.activation(out=ss[:L, 1:2], in_=ss[:L, 0:1], func=AF.Ln, bias=epsT[:L, 0:1]), r=["ss", "epsT"], w=["ss"])
            A("act", lambda e: e.activation(out=ss[:L, 2:3], in_=ss[:L, 1:2], func=AF.Exp, scale=-0.5), r=["ss"], w=["ss"])
            A("dve", lambda e: e.tensor_scalar(out=xn[:L, :], in0=xt[:L, :], scalar1=ss[:L, 2:3], scalar2=None, op0=ALU.mult),
              r=["xt", "ss"], w=["xn"])
            for half in range(2):
                ps, pn = nextps()
                S.mm([lambda e, j=j: e.transpose(out=ps[:, j * 128:j * 128 + L], in_=xn[:L, (half * 4 + j) * 128:(half * 4 + j + 1) * 128],
                                                 identity=identf[:L, :L]) for j in range(4)], reads=["xn", "identf"], writes=[pn])
                v = ps[:, :].rearrange("p (a b) -> p a b", b=128)[:, :, :L]
                A("act", lambda e, v=v, half=half: e.activation(out=xnT[:, half * 4:half * 4 + 4, :L], in_=v, func=AF.Copy), r=[pn], w=["xnT"])
            if full or mode == "halo2":
                for g in range(2):
                    def evb(v, pn):
                        A("act", lambda e: e.activation(out=sig[:, :, :L], in_=v, func=AF.Sigmoid), r=[pn], w=["aSU0"])
                    fm_inproj(L, [4112 + 128 * t for t in range(4 * g, 4 * g + 4)], evb)

                    def eva(v, pn, g=g):
                        A("dve", lambda e: e.tensor_tensor(out=gl_f[:, 4 * g:4 * g + 4, HP:HP + L], in0=v, in1=sig[:, :, :L], op=ALU.mult),
                          r=[pn, "aSU0"], w=["glf%d" % t for t in range(4 * g, 4 * g + 4)])
                    fm_inproj(L, [3088 + 128 * t for t in range(4 * g, 4 * g + 4)], eva)
            if full:
                cconv = lambda taps: conv_all(list(range(8)), ["dve"] * 7 + ["pool"] * 1, c_f, gl_f, 31, ccw, ccb, L, "glf%d", ["ccwt", "ccbt"], "cf%d", taps=taps)
                cconv(range(0, 8))
            for g in range(nx // 4):
                def ev(v, pn, g=g):
                    A("act", lambda e: e.activation(out=xbc_f[:, 4 * g:4 * g + 4, HP:HP + L], in_=v, func=AF.Copy), r=[pn],
                      w=["xbcf%d" % t for t in range(4 * g, 4 * g + 4)])
                fm_inproj(L, [1024 + 128 * t for t in range(4 * g, 4 * g + 4)], ev)
            if mode in ("halo1", "halo2"):
                for t in range(nx):
                    A("pool", lambda e, t=t: e.tensor_copy(out=xbc_f[:, t, 0:HP], in_=xbc_f[:, t, HP:2 * HP]), r=["xbcf%d" % t], w=["xbcf%d" % t])
                if mode == "halo2":
                    for t in range(8):
                        A("pool", lambda e, t=t: e.tensor_copy(out=gl_f[:, t, 0:HP], in_=gl_f[:, t, HP:2 * HP]), r=["glf%d" % t], w=["glf%d" % t])
                return
            if full:
                for g in range(2):
                    def evc(v, pn, g=g):
                        A("act", lambda e: e.activation(out=scg[:, 4 * g:4 * g + 4, :L], in_=v, func=AF.Silu), r=[pn], w=["scg"])
                    fm_inproj(L, [5136 + 128 * t for t in range(4 * g, 4 * g + 4)], evc)
            if full:
                for half in range(2):
                    ps, pn = nextps()
                    S.mm([lambda e, kt=kt, half=half: e.matmul(out=ps[:L, :], lhsT=xnT[:, kt, :L], rhs=Win[:, kt, 512 * half:512 * half + 512],
                                                              start=(kt == 0), stop=(kt == 7)) for kt in range(8)], reads=["xnT", "Win"], writes=[pn])
                    sl = slice(512 * half, 512 * half + 512)
                    A("act", lambda e, ps=ps, sl=sl: e.activation(out=sz[:L, sl], in_=ps[:L, :], func=AF.Silu), r=[pn], w=["xn"])

            ps, pn = nextps()
            S.mm([lambda e, kt=kt: e.matmul(out=ps[:L, 0:16], lhsT=xnT[:, kt, :L], rhs=Win[:, kt, 3072:3088], start=(kt == 0), stop=(kt == 7))
                  for kt in range(8)], reads=["xnT", "Win"], writes=[pn])
            dtr, dta, dte, dtl, dtv, av, acs, eacs = [dtt[:L, i, :] for i in range(8)]
            A("dve", lambda e: e.tensor_tensor(out=dtr, in0=ps[:L, 0:16], in1=dtb[:L, :], op=ALU.add), r=[pn, "dtbt"], w=["dtt"])
            A("dve", lambda e: e.scalar_tensor_tensor(out=dta, in0=dtr, scalar=-1.0, in1=dtr, op0=ALU.mult, op1=ALU.min), r=["dtt"], w=["dtt"])
            A("act", lambda e: e.activation(out=dte, in_=dta, func=AF.Exp), r=["dtt"], w=["dtt"])
            A("act", lambda e: e.activation(out=dtl, in_=dte, func=AF.Ln, bias=oneT[:L, 0:1]), r=["dtt", "oneT"], w=["dtt"])
            A("dve", lambda e: e.scalar_tensor_tensor(out=dtv, in0=dtr, scalar=0.0, in1=dtl, op0=ALU.max, op1=ALU.add), r=["dtt"], w=["dtt"])
            A("dve", lambda e: e.tensor_tensor(out=av, in0=dtv, in1=Ab[:L, :], op=ALU.mult), r=["dtt", "Abt"], w=["dtt"])
            ps2, pn2 = nextps()
            S.mm([lambda e: e.matmul(out=ps2[:L, 0:16], lhsT=triU[:L, :L], rhs=av, start=True, stop=True),
                  lambda e: e.matmul(out=ps2[:, 16:32], lhsT=onesf[:L, :], rhs=av, start=True, stop=True)],
                 reads=["triU", "onesf", "dtt"], writes=[pn2])
            A("dve", lambda e: e.tensor_copy(out=acs, in_=ps2[:L, 0:16]), r=[pn2], w=["dtt"])
            A("act", lambda e: e.activation(out=cdb[:, :], in_=ps2[:, 16:32], func=AF.Exp), r=[pn2], w=["cdb"])
            A("dve", lambda e: e.tensor_tensor(out=Atot[:, :], in0=Atot[:, :], in1=ps2[:, 16:32], op=ALU.add), r=[pn2, "Atot"], w=["Atot"])
            A("dve", lambda e: e.tensor_tensor(out=dta, in0=ps2[:L, 16:32], in1=acs, op=ALU.subtract), r=[pn2, "dtt"], w=["dtt"])
            A("act", lambda e: e.activation(out=dta, in_=dta, func=AF.Exp), r=["dtt"], w=["dtt"])
            A("act", lambda e: e.activation(out=eacs, in_=acs, func=AF.Exp), r=["dtt"], w=["dtt"])
            conv_all(list(range(nx)), ["dve"] * 16, xbc_c, xbc_f, 4, cw, cb, L, "xbcf%d", ["cwt", "cbt"], "xbcc%d")
            for g in range(nx // 4):
                A("act", lambda e, g=g: e.activation(out=xbc_c[:, 4 * g:4 * g + 4, :L], in_=xbc_c[:, 4 * g:4 * g + 4, :L], func=AF.Silu),
                  r=["xbcc%d" % t for t in range(4 * g, 4 * g + 4)], w=["xbcc%d" % t for t in range(4 * g, 4 * g + 4)])
            if L >= HP:
                for t in range(nx):
                    A("pool", lambda e, t=t: e.tensor_copy(out=xbc_f[:, t, 0:HP], in_=xbc_f[:, t, L:L + HP]), r=["xbcf%d" % t], w=["xbcf%d" % t])
            if full:
                cconv(range(8, 16))
            for g in range(2):
                v, pn = transposes_to_tm(L, [xbc_c[:, 4 * g + j, :L] for j in range(4)], ["xbcc%d" % (4 * g + j) for j in range(4)], "xs")
                v3 = v.rearrange("p (h q) -> p h q", q=64)
                sl = slice(512 * g, 512 * (g + 1))
                A("dve", lambda e, v3=v3, sl=sl, g=g: e.tensor_tensor(out=xdt[:L, sl].rearrange("p (h q) -> p h q", q=64), in0=v3,
                                                                      in1=dtv[:, 8 * g:8 * g + 8].unsqueeze(2).broadcast_to([L, 8, 64]), op=ALU.mult),
                  r=[pn, "dtt"], w=["xdt"])
                if full:
                    A("dve", lambda e, v=v, sl=sl, g=g: e.tensor_tensor(out=yacc[:L, sl].rearrange("p (h q) -> p h q", q=64), in0=v.rearrange("p (h q) -> p h q", q=64), in1=dsk[:L, 8 * g:8 * g + 8].unsqueeze(2).broadcast_to([L, 8, 64]), op=ALU.mult),
                      r=[pn, "dskt"], w=["yacc"])
            v, pn = transposes_to_tm(L, [xbc_c[:, 8 + j, :L] for j in range(4)], ["xbcc%d" % (8 + j) for j in range(4)], "B")
            A("act", lambda e: e.activation(out=Btm[:L, :], in_=v, func=AF.Copy), r=[pn], w=["Btm"])
            A("dve", lambda e: e.tensor_tensor(out=xdte[:L, :].rearrange("p (h q) -> p h q", q=64), in0=xdt[:L, :].rearrange("p (h q) -> p h q", q=64),
                                               in1=dta.unsqueeze(2).broadcast_to([L, 16, 64]), op=ALU.mult), r=["xdt", "dtt"], w=["xdte"])
            if full:
                cconv(range(16, 24))
                A("pool", lambda e: e.tensor_copy(out=BTb[:, :, :L], in_=xbc_c[:, 8:12, :L]), r=["xbcc%d" % t for t in range(8, 12)], w=["BTb"])
                A("pool", lambda e: e.tensor_copy(out=CTb[:, :, :L], in_=xbc_c[:, 12:16, :L]), r=["xbcc%d" % t for t in range(12, 16)], w=["CTb"])
                psc, pnc = nextps()
                S.mm([lambda e, g=g: e.matmul(out=psc[:L, g * 128:g * 128 + L], lhsT=BTb[:, g, :L], rhs=CTb[:, g, :L], start=True, stop=True)
                      for g in range(4)], reads=["BTb", "CTb"], writes=[pnc])
                A("dve", lambda e: e.tensor_tensor(out=cbm[:L, :, :L], in0=psc[:L, :].rearrange("p (a b) -> p a b", b=128)[:, :, :L],
                                                   in1=triU[:L, :L].unsqueeze(1).broadcast_to([L, 4, L]), op=ALU.mult), r=[pnc, "triU"], w=["cbm"])
                for q4 in range(4):
                    aS, dc = aSU4[q4 % 2], dec4[q4 % 2]
                    A("pool", lambda e, aS=aS, q4=q4: e.tensor_tensor(out=aS[:L, :, :L], in0=SU[:L, :L].unsqueeze(1).broadcast_to([L, 4, L]),
                                                                     in1=av[:, 4 * q4:4 * q4 + 4].unsqueeze(2).broadcast_to([L, 4, L]), op=ALU.mult),
                      r=["SU", "dtt"], w=[aS.name])
                    ps, pn = nextps()
                    S.mm([lambda e, j=j, aS=aS: e.matmul(out=ps[:L, j * 128:j * 128 + L], lhsT=aS[:L, j, :L], rhs=triU[:L, :L], start=True, stop=True)
                          for j in range(4)], reads=[aS.name, "triU"], writes=[pn])
                    A("act", lambda e, ps=ps, dc=dc: e.activation(out=dc[:L, :, :L], in_=ps[:L, :].rearrange("p (a b) -> p a b", b=128)[:, :, :L],
                                                                  func=AF.Exp), r=[pn], w=[dc.name])
                    A("dve", lambda e, q4=q4, dc=dc: e.tensor_tensor(out=MT[:L, 4 * q4:4 * q4 + 4, :L], in0=dc[:L, :, :L],
                                                                     in1=cbm[:L, q4, :L].unsqueeze(1).broadcast_to([L, 4, L]), op=ALU.mult), r=[dc.name, "cbm"], w=["MT"])
                for half in range(2):
                    psd, pnd = nextps()
                    S.mm([lambda e, h=h: e.matmul(out=psd[:L, (h % 8) * 64:(h % 8) * 64 + 64], lhsT=MT[:L, h, :L], rhs=xdt[:L, h * 64:(h + 1) * 64],
                                                  start=True, stop=True) for h in range(8 * half, 8 * half + 8)], reads=["MT", "xdt"], writes=[pnd])
                    pso, pno = nextps()
                    S.mm([lambda e, g=g: e.matmul(out=pso[:L, (g % 2) * 256:(g % 2) * 256 + 256], lhsT=CTb[:, g, :L], rhs=Hb[:, g * 256:(g + 1) * 256],
                                                  start=True, stop=True) for g in range(2 * half, 2 * half + 2)], reads=["CTb", "Hb"], writes=[pno])
                    sl = slice(512 * half, 512 * half + 512)
                    A("dve", lambda e, pso=pso, sl=sl, half=half: e.tensor_tensor(
                        out=ytmp[:L, sl].rearrange("p (h q) -> p h q", q=64), in0=pso[:L, :].rearrange("p (h q) -> p h q", q=64),
                        in1=eacs[:, 8 * half:8 * half + 8].unsqueeze(2).broadcast_to([L, 8, 64]), op=ALU.mult), r=[pno, "dtt"], w=["ytmp"])
                    A("pool", lambda e, sl=sl: e.tensor_tensor(out=yacc[:L, sl], in0=yacc[:L, sl], in1=ytmp[:L, sl], op=ALU.add), r=["yacc", "ytmp"], w=["yacc"])
                    A("dve", lambda e, psd=psd, sl=sl: e.tensor_tensor(out=yacc[:L, sl], in0=yacc[:L, sl], in1=psd[:L, :], op=ALU.add), r=["yacc", pnd], w=["yacc"])
            if full:
                cconv(range(24, 31))
                if L >= HP:
                    for t in range(8):
                        A("pool", lambda e, t=t: e.tensor_copy(out=gl_f[:, t, 0:HP], in_=gl_f[:, t, L:L + HP]), r=["glf%d" % t], w=["glf%d" % t])
            for half in range(2):
                pss, pns = nextps()
                S.mm([lambda e, g=g: e.matmul(out=pss[:, (g % 2) * 256:(g % 2) * 256 + 256], lhsT=Btm[:L, g * 128:(g + 1) * 128],
                                              rhs=xdte[:L, g * 256:(g + 1) * 256], start=True, stop=True) for g in range(2 * half, 2 * half + 2)],
                     reads=["Btm", "xdte"], writes=[pns])
                sl = slice(512 * half, 512 * half + 512)
                A("dve", lambda e, sl=sl, half=half: e.tensor_tensor(out=H[:, sl].rearrange("p (h q) -> p h q", q=64), in0=H[:, sl].rearrange("p (h q) -> p h q", q=64),
                                                                     in1=cdb[:, 8 * half:8 * half + 8].unsqueeze(2).broadcast_to([128, 8, 64]), op=ALU.mult),
                  r=["H", "cdb", "Hb"], w=["H"])
                A("dve", lambda e, sl=sl, pss=pss: e.tensor_tensor(out=H[:, sl], in0=H[:, sl], in1=pss[:, :], op=ALU.add), r=["H", pns], w=["H"])
            if not full:
                return
            A("act", lambda e: e.activation(out=Hb[:, :], in_=H[:, :], func=AF.Copy), r=["H"], w=["Hb"])
            A("dve", lambda e: e.tensor_tensor(out=yacc[:L, :], in0=yacc[:L, :], in1=sz[:L, :], op=ALU.mult), r=["yacc", "xn"], w=["yacc"])
            for g in range(4):
                A("act", lambda e, g=g: e.activation(out=ytmp[:L, 256 * g:256 * g + 256], in_=yacc[:L, 256 * g:256 * g + 256], func=AF.Square, scale=1.0 / 16,
                                                     accum_out=ss[:L, 4 + g:5 + g]), r=["yacc"], w=["ytmp", "ss"])
            A("act", lambda e: e.activation(out=ss[:L, 4:8], in_=ss[:L, 4:8], func=AF.Ln, bias=epsT[:L, 0:1]), r=["ss", "epsT"], w=["ss"])
            A("act", lambda e: e.activation(out=ss[:L, 4:8], in_=ss[:L, 4:8], func=AF.Exp, scale=-0.5), r=["ss"], w=["ss"])
            A("dve", lambda e: e.tensor_tensor(out=yacc[:L, :].rearrange("p (g q) -> p g q", q=256), in0=yacc[:L, :].rearrange("p (g q) -> p g q", q=256),
                                               in1=ss[:L, 4:8].unsqueeze(2).broadcast_to([L, 4, 256]), op=ALU.mult), r=["yacc", "ss"], w=["yacc"])
            for half in range(2):
                ps, pn = nextps()
                S.mm([lambda e, j=j, half=half: e.transpose(out=ps[:, j * 128:j * 128 + L], in_=yacc[:L, (half * 4 + j) * 128:(half * 4 + j + 1) * 128],
                                                           identity=identf[:L, :L]) for j in range(4)], reads=["yacc", "identf"], writes=[pn])
                v = ps[:, :].rearrange("p (a b) -> p a b", b=128)[:, :, :L]
                for j in range(4):
                    A("act", lambda e, v=v, half=half, j=j: e.activation(out=cat_f[:, half * 4 + j, :L], in_=v[:, j, :], func=AF.Copy,
                                                                         scale=sng[:, half * 4 + j:half * 4 + j + 1]), r=[pn, "sngt"], w=["cat_f"])
            cfn = ["cf%d" % t for t in range(8)]
            A("act", lambda e: e.activation(out=csq[:, :, :L], in_=c_f[:, :, :L], func=AF.Square), r=cfn, w=["ytmp"])
            ps, pn = nextps()
            S.mm([lambda e, t=t: e.matmul(out=ps[:, 0:L], lhsT=onesf[:, :], rhs=c_f[:, t, :L], start=(t == 0), stop=(t == 7)) for t in range(8)] +
                 [lambda e, t=t: e.matmul(out=ps[:, 128:128 + L], lhsT=onesf[:, :], rhs=csq[:, t, :L], start=(t == 0), stop=(t == 7)) for t in range(8)],
                 reads=cfn + ["ytmp", "onesf"], writes=[pn])
            mean, ex2, var, rstd = [lnst[:, i, :L] for i in range(4)]
            A("dve", lambda e: e.tensor_scalar(out=mean, in0=ps[:, 0:L], scalar1=1.0 / 1024, scalar2=None, op0=ALU.mult), r=[pn], w=["dec0"])
            A("dve", lambda e: e.tensor_scalar(out=ex2, in0=ps[:, 128:128 + L], scalar1=1.0 / 1024, scalar2=None, op0=ALU.mult), r=[pn], w=["dec0"])
            A("dve", lambda e: e.tensor_tensor(out=var, in0=mean, in1=mean, op=ALU.mult), r=["dec0"], w=["dec0"])
            A("dve", lambda e: e.tensor_tensor(out=var, in0=ex2, in1=var, op=ALU.subtract), r=["dec0"], w=["dec0"])
            A("act", lambda e: e.activation(out=rstd, in_=var, func=AF.Ln, bias=epsT[:, 0:1]), r=["dec0", "epsT"], w=["dec0"])
            A("act", lambda e: e.activation(out=rstd, in_=rstd, func=AF.Exp, scale=-0.5), r=["dec0"], w=["dec0"])
            A("dve", lambda e: e.tensor_tensor(out=c_f[:, :, :L], in0=c_f[:, :, :L], in1=mean.unsqueeze(1).broadcast_to([128, 8, L]), op=ALU.subtract),
              r=cfn + ["dec0"], w=cfn)
            A("dve", lambda e: e.tensor_tensor(out=c_f[:, :, :L], in0=c_f[:, :, :L], in1=rstd.unsqueeze(1).broadcast_to([128, 8, L]), op=ALU.mult),
              r=cfn + ["dec0"], w=cfn)
            A("pool", lambda e: e.tensor_tensor(out=c_f[:, :, :L], in0=c_f[:, :, :L], in1=lng[:, :].unsqueeze(2).broadcast_to([128, 8, L]), op=ALU.mult),
              r=cfn + ["lngt"], w=cfn)
            A("pool", lambda e: e.tensor_tensor(out=c_f[:, :, :L], in0=c_f[:, :, :L], in1=lnb[:, :].unsqueeze(2).broadcast_to([128, 8, L]), op=ALU.add),
              r=cfn + ["lnbt"], w=cfn)
            A("act", lambda e: e.activation(out=c_f[:, :, :L], in_=c_f[:, :, :L], func=AF.Silu), r=cfn, w=cfn)
            A("dve", lambda e: e.tensor_tensor(out=cat_f[:, 8:16, :L], in0=c_f[:, :, :L], in1=scg[:, :, :L], op=ALU.mult), r=cfn + ["scg"], w=["cat_f"])
            for half in range(2):
                ps, pn = nextps()
                S.mm([lambda e, t=t, half=half: e.matmul(out=ps[:L, :], lhsT=cat_f[:, t, :L], rhs=Wout[:, t, 512 * half:512 * half + 512],
                                                        start=(t == 0), stop=(t == 15)) for t in range(16)], reads=["cat_f", "Wout"], writes=[pn])
                sl = slice(512 * half, 512 * half + 512)
                A("dve", lambda e, ps=ps, sl=sl: e.tensor_tensor(out=xn[:L, sl], in0=xt[:L, sl], in1=ps[:L, :], op=ALU.add), r=["xt", pn], w=["xn"])
            S.dma(dst_x1, xn[:L, :], reads=["xn"], writes=["x1_d"])

        def load_hist_from_state(b):
            S.dma(hist_tm[:3, :], st_sc[b, :, :], writes=XBCC)
            for g in range(4):
                ps, pn = nextps()
                S.mm([lambda e, j=j, g=g: e.transpose(out=ps[:, j * 128:j * 128 + 3], in_=hist_tm[:3, (4 * g + j) * 128:(4 * g + j + 1) * 128],
                                                      identity=identf[:3, :3]) for j in range(4)], reads=XBCC + ["identf"], writes=[pn])
                A("act", lambda e, ps=ps, g=g: e.activation(out=xbc_f[:, 4 * g:4 * g + 4, HP - 3:HP], in_=ps[:, :].rearrange("p (a b) -> p a b", b=128)[:, :, :3],
                                                            func=AF.Copy), r=[pn], w=["xbcf%d" % t for t in range(4 * g, 4 * g + 4)])
            S.dma(hist_tm[:30, :1024], st_cc[b, :, :], writes=XBCC)
            for g in range(2):
                ps, pn = nextps()
                S.mm([lambda e, j=j, g=g: e.transpose(out=ps[:, j * 128:j * 128 + 30], in_=hist_tm[:30, (4 * g + j) * 128:(4 * g + j + 1) * 128],
                                                      identity=identf[:30, :30]) for j in range(4)], reads=XBCC + ["identf"], writes=[pn])
                A("act", lambda e, ps=ps, g=g: e.activation(out=gl_f[:, 4 * g:4 * g + 4, HP - 30:HP], in_=ps[:, :].rearrange("p (a b) -> p a b", b=128)[:, :, :30],
                                                            func=AF.Copy), r=[pn], w=["glf%d" % t for t in range(4 * g, 4 * g + 4)])
            for g in range(2):
                S.dma(ytmp[:, 512 * g:512 * g + 512].rearrange("p (a n) -> p a n", n=128),
                      st_ssm[b, 512 * g:512 * g + 512, :].rearrange("(a p) n -> p a n", p=128), writes=["ytmp"])
                ps, pn = nextps()
                S.mm([lambda e, j=j, g=g: e.transpose(out=ps[:, j * 128:(j + 1) * 128], in_=ytmp[:, 512 * g + j * 128:512 * g + (j + 1) * 128],
                                                      identity=identf[:, :]) for j in range(4)], reads=["ytmp", "identf"], writes=[pn])
                A("dve", lambda e, ps=ps, g=g: e.tensor_copy(out=H[:, 512 * g:512 * g + 512], in_=ps[:, :]), r=[pn, "Hb"], w=["H"])
            A("act", lambda e: e.activation(out=Hb[:, :], in_=H[:, :], func=AF.Copy), r=["H"], w=["Hb"])

        def store_state_outputs(L, sc_dst, cc_dst, ssm_dst):
            for g in range(4):
                ps, pn = nextps()
                S.mm([lambda e, j=j, g=g: e.transpose(out=ps[:3, j * 128:(j + 1) * 128], in_=xbc_f[:, 4 * g + j, HP + L - 3:HP + L], identity=identf[:, :])
                      for j in range(4)], reads=["xbcf%d" % t for t in range(4 * g, 4 * g + 4)] + ["identf"], writes=[pn])
                A("act", lambda e, ps=ps, g=g: e.activation(out=hout[:3, 512 * g:512 * g + 512], in_=ps[:3, :], func=AF.Copy), r=[pn], w=XBCC)
            S.dma(sc_dst, hout[:3, :], reads=XBCC)
            for g in range(2):
                ps, pn = nextps()
                S.mm([lambda e, j=j, g=g: e.transpose(out=ps[:30, j * 128:(j + 1) * 128], in_=gl_f[:, 4 * g + j, HP + L - 30:HP + L], identity=identf[:, :])
                      for j in range(4)], reads=["glf%d" % t for t in range(4 * g, 4 * g + 4)] + ["identf"], writes=[pn])
                A("act", lambda e, ps=ps, g=g: e.activation(out=hout[:30, 512 * g:512 * g + 512], in_=ps[:30, :], func=AF.Copy), r=[pn], w=XBCC)
            S.dma(cc_dst, hout[:30, :1024], reads=XBCC)
            for g in range(2):
                ps, pn = nextps()
                S.mm([lambda e, j=j, g=g: e.transpose(out=ps[:, j * 128:(j + 1) * 128], in_=H[:, 512 * g + j * 128:512 * g + (j + 1) * 128], identity=identf[:, :])
                      for j in range(4)], reads=["H", "identf"], writes=[pn])
                A("dve", lambda e, ps=ps, g=g: e.tensor_copy(out=ytmp[:, 512 * g:512 * g + 512], in_=ps[:, :]), r=[pn], w=["ytmp"])
                S.dma(ssm_dst[512 * g:512 * g + 512, :].rearrange("(a p) n -> p a n", p=128),
                      ytmp[:, 512 * g:512 * g + 512].rearrange("p (a n) -> p a n", n=128), reads=["ytmp"])

        def zero_state():
            A("pool", lambda e: e.memset(H[:, :], 0.0), r=["Hb"], w=["H"])
            A("pool", lambda e: e.memset(Hb[:, :], 0.0), w=["Hb"])
            A("pool", lambda e: e.memset(Atot[:, :], 0.0), w=["Atot"])

        zero_state()
        for t in range(16):
            A("pool", lambda e, t=t: e.memset(xbc_f[:, t, 0:HP], 0.0), w=["xbcf%d" % t])
        for t in range(8):
            A("pool", lambda e, t=t: e.memset(gl_f[:, t, 0:HP], 0.0), w=["glf%d" % t])
        for c in range(NCHA):
            l0_chunk(xp[c * 128:(c + 1) * 128, :], 128, "full", dst_x1=x1_d[c * 128:(c + 1) * 128, :])
        store_state_outputs(128, sc_p[:, :], cc_p[:, :], ssm_p)
        for b in range(DB):
            load_hist_from_state(b)
            l0_chunk(xsm[b * 4:(b + 1) * 4, :], 4, "full", dst_x1=x1_d[SEQ + b * 4:SEQ + (b + 1) * 4, :])
            store_state_outputs(4, sc_s[b, :, :], cc_s[b, :, :], ssm_s[b])


        S.barrier()
        st0.close()
        if debug_l0:
            st1 = ExitStack(); cur[0] = st1
            xt = T("xt_dbg", [128, D])
            for c in range(NCH):
                S.dma(xt[:, :], x1_d[c * 128:(c + 1) * 128, :], reads=["x1_d"], writes=["xt"])
                S.dma(y_p[c * 128:(c + 1) * 128, :], xt[:, :], reads=["xt"])
            S.dma(xt[:DB * 4, :], x1_d[SEQ:SEQ + DB * 4, :], reads=["x1_d"], writes=["xt"])
            S.dma(y_s[:, :], xt[:DB * 4, :], reads=["xt"])
            S.finish("sp")
            st1.close()
            return nc
        st1 = ExitStack(); cur[0] = st1
        NSLOT = TPC // 128
        NB = 64
        psn[0] = 6
        Wq = T("Wq", [128, 8, 1024], BF16)
        Wg = T("Wg", [128, 8, 1024], BF16)
        Wo = T("Wo", [64, 16, 1024], BF16)
        g1 = T("g1t", [128, 8]); S.dma(g1[:], g1_d[:, :], writes=["g1t"])
        qg = T("qgt", [128, 1]); S.dma(qg[:], qg_d[:, :], writes=["qgt"])
        kgb = T("kgbt", [128, 256]); S.dma(kgb[:], kgb_d[:, :], writes=["kgbt"])
        pm = T("pmt", [128, 64])
        pneg = T("pnegt", [128, 64])
        om = T("omt", [128, 64])
        maskM = T("maskMt", [128, 8, 128]); S.dma(maskM[:], maskM_d[:, :, :], writes=["maskMt"])
        maskS = T("maskSt", [4, 16]); S.dma(maskS[:], maskS_d[:, :], writes=["maskSt"])
        oidx = T("oidxt", [128, NSLOT], I32); S.dma(oidx[:], oidx_d[:, :], writes=["oidx"])
        pidx = T("pidxt", [128, 1]); S.dma(pidx[:], pidx_d[:, :], writes=["pidx"])
        Eoh = T("Eoh", [64, 64, 128], BF16)
        A("pool", lambda e: e.memset(Eoh[:], 1.0), w=["Eoh"])
        A("pool", lambda e: e.affine_select(out=Eoh[:], in_=Eoh[:], pattern=[[-1, 64], [0, 128]], compare_op=ALU.is_equal, fill=0.0,
                                            base=0, channel_multiplier=1), r=["Eoh"], w=["Eoh"])
        xt1 = T("xt1", [128, D])
        xn1 = T("xn1", [128, D])
        ss1 = T("ss1", [128, 8])
        xT1 = T("xT1", [128, 8, 128], BF16)
        kms = T("kms", [128, 2, max(NCHA, NPG)])
        kmT = T("kmT", [128, 2, 64])
        A("pool", lambda e: e.memset(kmT[:], 0.0), w=["kmT"])
        stA = ExitStack(); cur[0] = stA
        Wkv = T("Wkv", [128, 8, 512], BF16)
        stg1 = [T("stg1a", [128, 1024]), T("stg1b", [128, 1024])]
        si = 0
        for (wd, wt, ncol, wname) in ((wq_d, Wq, 1024, "Wq"), (wkv_d, Wkv, 512, "Wkv"), (wg_d, Wg, 1024, "Wg")):
            for kt in range(8):
                sb = stg1[si % 2]
                S.dma(sb[:, :ncol], wd[:, kt, :], writes=[sb.name])
                A("dve" if si % 2 == 0 else "pool", lambda e, sb=sb, wt=wt, kt=kt, ncol=ncol: e.tensor_scalar(
                    out=wt[:, kt, :], in0=sb[:, :ncol], scalar1=g1[:, kt:kt + 1], scalar2=None, op0=ALU.mult), r=[sb.name, "g1t"], w=[wname])
                si += 1
        for h in range(16):
            sb = stg1[si % 2]
            S.dma(sb[:64, :], wo_d[:, h, :], writes=[sb.name])
            A("dve" if si % 2 == 0 else "pool", lambda e, sb=sb, h=h: e.tensor_copy(out=Wo[:, h, :], in_=sb[:64, :]), r=[sb.name], w=["Wo"])
            si += 1

        KV = T("KV", [128, 512])
        KTst = T("KTst", [128, 2, 128], BF16)
        Vb = T("Vb", [128, 4, 65], BF16)
        A("pool", lambda e: e.memset(Vb[:], 1.0), w=["Vb"])

        def l1_norm_T(L):
            A("act", lambda e: e.activation(out=xn1[:L, :], in_=xt1[:L, :], func=AF.Square, scale=1.0 / 32, accum_out=ss1[:L, 0:1]), r=["xt1"], w=["xn1", "ss1"])
            A("act", lambda e: e.activation(out=ss1[:L, 1:2], in_=ss1[:L, 0:1], func=AF.Ln, bias=epsT[:L, 0:1]), r=["ss1", "epsT"], w=["ss1"])
            A("act", lambda e: e.activation(out=ss1[:L, 2:3], in_=ss1[:L, 1:2], func=AF.Exp, scale=-0.5), r=["ss1"], w=["ss1"])
            A("dve", lambda e: e.tensor_scalar(out=xn1[:L, :], in0=xt1[:L, :], scalar1=ss1[:L, 2:3], scalar2=None, op0=ALU.mult), r=["xt1", "ss1"], w=["xn1"])
            for half in range(2):
                ps, pn = nextps()
                S.mm([lambda e, j=j, half=half: e.transpose(out=ps[:, j * 128:j * 128 + L], in_=xn1[:L, (half * 4 + j) * 128:(half * 4 + j + 1) * 128],
                                                           identity=identf[:L, :L]) for j in range(4)], reads=["xn1", "identf"], writes=[pn])
                v = ps[:, :].rearrange("p (a b) -> p a b", b=128)[:, :, :L]
                A("act", lambda e, v=v, half=half: e.activation(out=xT1[:, half * 4:half * 4 + 4, :L], in_=v, func=AF.Copy), r=[pn], w=["xT1"])

        def l1_kv(src, L, kdst, vdst, col0, chunk_idx):
            S.dma(xt1[:L, :], src, reads=["x1_d"], writes=["xt1"])
            l1_norm_T(L)
            ps, pn = nextps()
            S.mm([lambda e, kt=kt: e.matmul(out=ps[:L, :], lhsT=xT1[:, kt, :L], rhs=Wkv[:, kt, :], start=(kt == 0), stop=(kt == 7)) for kt in range(8)],
                 reads=["xT1", "Wkv"], writes=[pn])
            for h in range(4):
                A("act", lambda e, h=h: e.activation(out=xn1[:L, h * 64:(h + 1) * 64], in_=ps[:L, h * 64:(h + 1) * 64], func=AF.Square, scale=0.125,
                                                     accum_out=ss1[:L, 4 + h:5 + h]), r=[pn], w=["xn1", "ss1"])
            A("act", lambda e: e.activation(out=ss1[:L, 4:8], in_=ss1[:L, 4:8], func=AF.Ln, bias=epsT[:L, 0:1]), r=["ss1", "epsT"], w=["ss1"])
            A("act", lambda e: e.activation(out=ss1[:L, 4:8], in_=ss1[:L, 4:8], func=AF.Exp, scale=-0.5), r=["ss1"], w=["ss1"])
            A("dve", lambda e: e.tensor_tensor(out=KV[:L, 0:256].rearrange("p (h d) -> p h d", d=64), in0=ps[:L, 0:256].rearrange("p (h d) -> p h d", d=64),
                                               in1=ss1[:L, 4:8].unsqueeze(2).broadcast_to([L, 4, 64]), op=ALU.mult), r=[pn, "ss1"], w=["KV"])
            A("pool", lambda e: e.tensor_tensor(out=KV[:L, 0:256], in0=KV[:L, 0:256], in1=kgb[:L, :], op=ALU.mult), r=["KV", "kgbt"], w=["KV"])
            A("act", lambda e: e.activation(out=KV[:L, 256:512], in_=ps[:L, 256:512], func=AF.Copy), r=[pn], w=["KV"])
            S.dma(kdst, KV[:L, 0:256], reads=["KV"])
            S.dma(vdst, KV[:L, 256:512], reads=["KV"])
            ps2, pn2 = nextps()
            S.mm([lambda e, pr=pr: e.transpose(out=ps2[:, pr * 128:pr * 128 + L], in_=KV[:L, pr * 128:(pr + 1) * 128], identity=identf[:L, :L]) for pr in range(2)],
                 reads=["KV", "identf"], writes=[pn2])
            for pr in range(2):
                A("act", lambda e, pr=pr: e.activation(out=KTst[:, pr, :L], in_=ps2[:, pr * 128:pr * 128 + L], func=AF.Copy,
                                                       accum_out=kms[:, pr, chunk_idx:chunk_idx + 1]), r=[pn2], w=["KTst", "kms"])
                S.dma(KT_d[pr, :, col0:col0 + L], KTst[:, pr, :L], reads=["KTst"], writes=["KT_d"])
            A("pool", lambda e: e.tensor_copy(out=Vb[:L, :, 0:64], in_=KV[:L, 256:512].rearrange("p (h d) -> p h d", d=64)), r=["KV"], w=["Vb"])
            S.dma(V_d[col0:col0 + L, :], Vb[:L, :, :].rearrange("p h c -> p (h c)"), reads=["Vb"], writes=["V_d"])

        for t in range(NCHA):
            l1_kv(x1_d[t * 128:(t + 1) * 128, :], 128, k_p[t * 128:(t + 1) * 128, :], v_p[t * 128:(t + 1) * 128, :], t * 128, t)
        NBP = NCHA // 2
        kv2 = kms[:, :, 0:NCHA].rearrange("p a (n two) -> p a n two", two=2)
        A("dve", lambda e: e.tensor_tensor(out=kmT[:, :, 0:NBP], in0=kv2[:, :, :, 0], in1=kv2[:, :, :, 1], op=ALU.add), r=["kms"], w=["kmT"])
        A("dve", lambda e: e.tensor_scalar(out=kmT[:, :, 0:NBP], in0=kmT[:, :, 0:NBP], scalar1=1.0 / 256, scalar2=None, op0=ALU.mult), r=["kmT"], w=["kmT"])
        for b in range(DB):
            l1_kv(x1_d[SEQ + b * 4:SEQ + (b + 1) * 4, :], 4, k_s[b * 4:(b + 1) * 4, :], v_s[b * 4:(b + 1) * 4, :], SEQ + b * 4, NCHA - 1 if False else 0)

        S.barrier()
        stA.close()
        cur[0] = st1
        QF = T("QF", [128, 8, 128])
        sq1 = T("sq1", [128, 4, 128])
        QPf = T("QPf", [128, 16, 128])
        QPb = T("QPb", [128, 16, 128], BF16)
        SG = T("SG", [64, 16, 128], BF16)
        Gm = T("Gm", [128, 16, 64])
        m8 = T("m8", [128, 16, 8])
        bsel = Gm
        biasT = T("biasT", [64, 16, 128], BF16)
        ATg = T("ATg", [64, 16, 128], BF16)
        rdt = xn1
        bcs = T("bcs", [64, 512])
        atmp = T("atmp", [64, 512])
        PT = [T("PT0", [128, 512], BF16), T("PT1", [128, 512], BF16)]
        KTst2s = [T("KTst2", [128, 2, 128], BF16), T("KTst2b", [128, 2, 128], BF16)]
        A("pool", lambda e: e.memset(QPf[:], 0.0), w=["QPf"])

        def l1_q(L):
            l1_norm_T(L)
            for half in range(2):
                ps, pn = nextps()
                S.mm([lambda e, r=r, kt=kt, half=half: e.matmul(out=ps[:, r * 128:r * 128 + L], lhsT=Wq[:, kt, (4 * half + r) * 128:(4 * half + r + 1) * 128],
                                                             rhs=xT1[:, kt, :L], start=(kt == 0), stop=(kt == 7)) for r in range(4) for kt in range(8)],
                     reads=["Wq", "xT1"], writes=[pn])
                v = ps[:, :].rearrange("p (a b) -> p a b", b=128)[:, :, :L]
                A("act", lambda e, v=v: e.activation(out=sq1[:, :, :L], in_=v, func=AF.Square, scale=0.125), r=[pn], w=["sq1"])
                pss, pns = nextps()
                S.mm([lambda e, r=r: e.matmul(out=pss[:, r * 128:r * 128 + L], lhsT=BD[:, :], rhs=sq1[:, r, :L], start=True, stop=True) for r in range(4)],
                     reads=["BD", "sq1"], writes=[pns])
                vs = pss[:, :].rearrange("p (a b) -> p a b", b=128)[:, :, :L]
                A("act", lambda e, vs=vs: e.activation(out=sq1[:, :, :L], in_=vs, func=AF.Ln, bias=epsT[:, 0:1]), r=[pns, "epsT"], w=["sq1"])
                A("act", lambda e: e.activation(out=sq1[:, :, :L], in_=sq1[:, :, :L], func=AF.Exp, scale=-0.5), r=["sq1"], w=["sq1"])
                A("dve", lambda e, v=v, half=half: e.tensor_tensor(out=QF[:, 4 * half:4 * half + 4, :L], in0=v, in1=sq1[:, :, :L], op=ALU.mult), r=[pn, "sq1", "QFk0", "QFk1", "QFv0", "QFv1"], w=["QF", "QFk0", "QFk1", "QFv0", "QFv1"])
            A("dve", lambda e: e.tensor_scalar(out=QF[:, :, :L], in0=QF[:, :, :L], scalar1=qg[:, 0:1], scalar2=None, op0=ALU.mult), r=["QF", "qgt"], w=["QF"])
            for hk in range(4):
                rows = slice(64 * (hk % 2), 64 * (hk % 2) + 64)
                t0 = (hk // 2) * 4
                A("pool", lambda e, hk=hk, rows=rows, t0=t0: e.tensor_copy(out=QPf[rows, 4 * hk:4 * hk + 4, :L], in_=QF[rows, t0:t0 + 4, :L]), r=["QF"], w=["QPf"])
            A("act", lambda e: e.activation(out=QPb[:, :, :L], in_=QPf[:, :, :L], func=AF.Copy), r=["QPf"], w=["QPb"])
            for bk in range(4):
                ps, pn = nextps()
                S.mm([lambda e, r=r, kt=kt, bk=bk: e.matmul(out=ps[0:64, r * 128:r * 128 + L], lhsT=Wg[:, kt, (4 * bk + r) * 64:(4 * bk + r + 1) * 64],
                                                           rhs=xT1[:, kt, :L], start=(kt == 0), stop=(kt == 7)) for r in range(4) for kt in range(8)],
                     reads=["Wg", "xT1"], writes=[pn])
                A("act", lambda e, ps=ps, bk=bk: e.activation(out=SG[:, 4 * bk:4 * bk + 4, :L], in_=ps[0:64, :].rearrange("p (a b) -> p a b", b=128)[:, :, :L],
                                                              func=AF.Silu), r=[pn], w=["SG"])

        def moba_select(L, kmt, kmname, slot):
            S.dma(pm[:, :], pm_d[:, slot, :], writes=["pmt"])
            S.dma(pneg[:, :], pneg_d[:, slot, :], writes=["pnegt"])
            S.dma(om[:, :], om_d[:, slot, :], writes=["omt"])
            for half in range(2):
                ps, pn = nextps()
                S.mm([lambda e, i=i, half=half: e.matmul(out=ps[:L, i * 64:(i + 1) * 64], lhsT=QPf[:, 8 * half + i, :L], rhs=kmt[:, (8 * half + i) // 8, :],
                                                        start=True, stop=True) for i in range(8)], reads=["QPf", kmname], writes=[pn])
                A("dve", lambda e, ps=ps, half=half: e.tensor_tensor(out=Gm[:L, 8 * half:8 * half + 8, :], in0=ps[:L, :].rearrange("p (a b) -> p a b", b=64),
                                                                     in1=pm[:L, :].unsqueeze(1).broadcast_to([L, 8, 64]), op=ALU.mult), r=[pn, "pmt"], w=["Gm"])
            A("dve", lambda e: e.tensor_tensor(out=Gm[:L, :, :], in0=Gm[:L, :, :], in1=pneg[:L, :].unsqueeze(1).broadcast_to([L, 16, 64]), op=ALU.add),
              r=["Gm", "pnegt"], w=["Gm"])
            for h in range(16):
                A("dve", lambda e, h=h: e.max(out=m8[:L, h, :], in_=Gm[:L, h, :]), r=["Gm"], w=["m8"])
            for h in range(16):
                A("dve", lambda e, h=h: e.tensor_scalar(out=bsel[:L, h, :], in0=Gm[:L, h, :], scalar1=m8[:L, h, 2:3], scalar2=None, op0=ALU.is_ge), r=["Gm", "m8"], w=["Gm"])
            A("dve", lambda e: e.tensor_tensor(out=bsel[:L, :, :], in0=bsel[:L, :, :], in1=pm[:L, :].unsqueeze(1).broadcast_to([L, 16, 64]), op=ALU.mult),
              r=["Gm", "pmt"], w=["Gm"])
            A("dve", lambda e: e.tensor_tensor(out=bsel[:L, :, :], in0=bsel[:L, :, :], in1=om[:L, :].unsqueeze(1).broadcast_to([L, 16, 64]), op=ALU.add),
              r=["Gm", "omt"], w=["Gm"])
            A("dve", lambda e: e.tensor_scalar(out=bsel[:L, :, :], in0=bsel[:L, :, :], scalar1=-NEG, scalar2=NEG, op0=ALU.mult, op1=ALU.add), r=["Gm"], w=["Gm"])
            for bk in range(4):
                ps, pn = nextps()
                S.mm([lambda e, r=r, bk=bk: e.transpose(out=ps[0:64, r * 128:r * 128 + L], in_=bsel[:L, 4 * bk + r, :], identity=identf[:L, :L]) for r in range(4)],
                     reads=["Gm", "identf"], writes=[pn])
                A("act", lambda e, ps=ps, bk=bk: e.activation(out=biasT[:, 4 * bk:4 * bk + 4, :L], in_=ps[0:64, :].rearrange("p (a b) -> p a b", b=128)[:, :, :L],
                                                              func=AF.Copy), r=[pn], w=["biasT"])

        def finish_group(L, hk, psO, pnO):
            W4 = 4 * L
            A("dve", lambda e: e.reciprocal(out=rdt[64:65, :W4], in_=psO[64:65, :W4]), r=[pnO], w=["xn1"])
            psb, pnb = nextps()
            S.mm([lambda e: e.matmul(out=psb[0:64, :W4], lhsT=onesf[64:65, 0:64], rhs=rdt[64:65, :W4], start=True, stop=True)], reads=["onesf", "xn1"], writes=[pnb])
            A("act", lambda e: e.activation(out=bcs[:, :W4], in_=psb[0:64, :W4], func=AF.Copy), r=[pnb], w=["bcs"])
            A("dve", lambda e: e.tensor_tensor(out=atmp[:, :W4], in0=psO[0:64, :W4], in1=bcs[:, :W4], op=ALU.mult), r=[pnO, "bcs"], w=["atmp"])
            A("pool", lambda e: e.tensor_tensor(out=ATg[:, 4 * hk:4 * hk + 4, :L], in0=atmp[:, :W4].rearrange("p (a b) -> p a b", b=L), in1=SG[:, 4 * hk:4 * hk + 4, :L],
                                                op=ALU.mult), r=["atmp", "SG"], w=["ATg"])

        def out_proj(L, dst):
            for half in range(2):
                ps, pn = nextps()
                S.mm([lambda e, h=h, half=half: e.matmul(out=ps[:L, :], lhsT=ATg[:, h, :L], rhs=Wo[:, h, 512 * half:512 * half + 512], start=(h == 0), stop=(h == 15))
                      for h in range(16)], reads=["ATg", "Wo"], writes=[pn])
                sl = slice(512 * half, 512 * half + 512)
                A("dve", lambda e, ps=ps, sl=sl: e.tensor_tensor(out=xn1[:L, sl], in0=xt1[:L, sl], in1=ps[:L, :], op=ALU.add), r=["xt1", pn], w=["xn1"])
            S.dma(dst, xn1[:L, :], reads=["xn1"])

        st2 = ExitStack(); cur[0] = st2
        NKT = NCHA
        NKB = max(NKT, NPG)
        KTp1 = T("KTp0", [128, NKB * 128], BF16)
        KTp = [KTp1, KTp1]
        Vp = [T("Vp%d" % i, [128, NKB, 65], BF16) for i in range(2)]
        bufi = 0
        for j in range(NSLOT):
            nkt = 8 * j + 8
            S.idma(xt1[:, :], x1_d[:, :], oidx[:, j:j + 1], reads=["x1_d", "oidx"], writes=["xt1"])
            l1_q(128)
            moba_select(128, kmT, "kmT", j)
            for hk in range(4):
                pr = hk // 2
                if hk % 2 == 0:
                    ktp = KTp[pr]
                    S.dma(ktp[:, :nkt * 128], KT_d[pr, :, 0:nkt * 128], reads=["KT_d"], writes=[ktp.name])
                vp = Vp[hk % 2]
                S.dma(vp[:, :nkt, :], V_d[0:nkt * 128, hk * 65:(hk + 1) * 65].rearrange("(k p) c -> p k c", p=128), reads=["V_d"], writes=[vp.name])
                psO, pnO = pst[7 - (hk % 2)], "ps%d" % (7 - (hk % 2))
                def emit_st(kt, ktp=ktp, hk=hk):
                    ps, pn = nextps(6)
                    S.mm([lambda e: e.matmul(out=ps[:, :], lhsT=ktp[:, kt * 128:(kt + 1) * 128],
                                             rhs=QPb[:, 4 * hk:4 * hk + 4, :].rearrange("p a b -> p (a b)"), start=True, stop=False),
                          lambda e: e.matmul(out=ps[:, :], lhsT=Eoh[:, kt // 2, :], rhs=biasT[:, 4 * hk:4 * hk + 4, :].rearrange("p a b -> p (a b)"),
                                             start=False, stop=True)], reads=[ktp.name, "QPb", "Eoh", "biasT"], writes=[pn])
                    return ps, pn
                pend = [emit_st(k_) for k_ in range(min(2, nkt))]
                for kt in range(nkt):
                    ps, pn = pend.pop(0)
                    if kt + 2 < nkt:
                        pend.append(emit_st(kt + 2))
                    pt = PT[kt % 2]
                    A("act", lambda e, ps=ps, pt=pt: e.activation(out=pt[:, :], in_=ps[:, :], func=AF.Exp, scale=0.125), r=[pn], w=[pt.name])
                    if kt >= 8 * j:
                        A("dve", lambda e, pt=pt, kt=kt, j=j: e.tensor_tensor(out=pt[:, :].rearrange("p (a b) -> p a b", b=128), in0=pt[:, :].rearrange("p (a b) -> p a b", b=128),
                                                                             in1=maskM[:, kt - 8 * j, :].unsqueeze(1).broadcast_to([128, 4, 128]), op=ALU.mult),
                          r=[pt.name, "maskMt"], w=[pt.name])
                    S.mm([lambda e, kt=kt, vp=vp, pt=pt, psO=psO, nkt=nkt: e.matmul(out=psO[0:65, :], lhsT=vp[:, kt, :], rhs=pt[:, :], start=(kt == 0), stop=(kt == nkt - 1))],
                         reads=[vp.name, pt.name], writes=[pnO])
                finish_group(128, hk, psO, pnO)
            out_proj(128, y_p[j * 128:(j + 1) * 128, :])

        PTs = Gm[:, :, :].rearrange("p a b -> p (a b)").bitcast(BF16)
        PTn = T("PTn", [4, 16], BF16)
        KTn = T("KTn", [128, 2, 4], BF16)
        Vn = T("Vn", [4, 4, 65], BF16)
        ptf = m8[:, :, :].rearrange("p a b -> p (a b)")[:, :NPG]
        pti = T("pti", [128, NPG], I32)
        kmTs = kmT
        VsAs = [T("VsA", [128, 4, 65], BF16)] * 2
        QFl = QF[:, :, :].rearrange("p a b -> p (a b)")
        Kpg = [QFl[:, 0:256], QFl[:, 256:512]]
        Vpg = [QFl[:, 512:768], QFl[:, 768:1024]]
        A("pool", lambda e: e.memset(VsAs[0][:], 1.0), w=["VsA"])
        A("pool", lambda e: e.memset(kmTs[:], 0.0), w=["kmT"])
        NBS = NPG // 2
        for b in range(DB):
            S.dma(pti[:, :], ptb_d[:, b, :], writes=["pti"])
            A("dve", lambda e: e.tensor_copy(out=ptf[:, :], in_=pti[:, :]), r=["pti"], w=["m8"])
            A("dve", lambda e: e.tensor_scalar(out=ptf[:, :], in0=ptf[:, :], scalar1=128.0, scalar2=pidx[:, 0:1], op0=ALU.mult, op1=ALU.add), r=["m8", "pidx"], w=["m8"])
            A("dve", lambda e: e.tensor_copy(out=pti[:, :], in_=ptf[:, :]), r=["m8"], w=["pti"])
            for pg in range(NPG):
                kp, vq = Kpg[pg % 2], Vpg[pg % 2]
                kn_, vn_ = "QFk%d" % (pg % 2), "QFv%d" % (pg % 2)
                KTst2, VsA = KTst2s[pg % 2], VsAs[pg % 2]
                S.idma(kp, ck_d[:, :], pti[:, pg:pg + 1], reads=["pti", "QF"], writes=[kn_])
                S.idma(vq, cv_d[:, :], pti[:, pg:pg + 1], reads=["pti", "QF"], writes=[vn_])
                ps, pn = nextps()
                S.mm([lambda e, pr=pr, kp=kp, ps=ps: e.transpose(out=ps[:, pr * 128:(pr + 1) * 128], in_=kp[:, pr * 128:(pr + 1) * 128], identity=identf[:, :]) for pr in range(2)],
                     reads=[kn_, "identf"], writes=[pn])
                for pr in range(2):
                    A("act", lambda e, pr=pr, pg=pg, ps=ps, KTst2=KTst2: e.activation(out=KTst2[:, pr, :], in_=ps[:, pr * 128:(pr + 1) * 128], func=AF.Copy,
                                                                                    accum_out=kms[:, pr, pg:pg + 1]), r=[pn], w=[KTst2.name + "_%d" % pr, "kms"])
                    S.dma(KTs_d[b, pr, :, pg * 128:(pg + 1) * 128], KTst2[:, pr, :], reads=[KTst2.name + "_%d" % pr], writes=["KTs_d"])
                A("pool", lambda e, vq=vq, VsA=VsA: e.tensor_copy(out=VsA[:, :, 0:64], in_=vq.rearrange("p (h d) -> p h d", d=64)), r=[vn_], w=[VsA.name])
                S.dma(Vs_d[b, pg * 128:(pg + 1) * 128, :], VsA[:, :, :].rearrange("p h c -> p (h c)"), reads=[VsA.name], writes=["Vs_d"])
            kv3 = kms[:, :, 0:NPG].rearrange("p a (n two) -> p a n two", two=2)
            A("dve", lambda e: e.tensor_tensor(out=kmTs[:, :, 0:NBS], in0=kv3[:, :, :, 0], in1=kv3[:, :, :, 1], op=ALU.add), r=["kms"], w=["kmT"])
            A("dve", lambda e: e.tensor_scalar(out=kmTs[:, :, 0:NBS], in0=kmTs[:, :, 0:NBS], scalar1=1.0 / 256, scalar2=None, op0=ALU.mult), r=["kmT"], w=["kmT"])
            S.dma(KTn[:, :, :], KT_d[:, :, SEQ + b * 4:SEQ + (b + 1) * 4].rearrange("a p c -> p a c"), reads=["KT_d"], writes=["KTn"])
            S.dma(Vn[:, :, :].rearrange("p h c -> p (h c)"), V_d[SEQ + b * 4:SEQ + (b + 1) * 4, :], reads=["V_d"], writes=["Vn"])
            S.dma(xt1[:4, :], x1_d[SEQ + b * 4:SEQ + (b + 1) * 4, :], reads=["x1_d"], writes=["xt1"])
            l1_q(4)
            moba_select(4, kmTs, "kmT", NSLOT)
            for hk in range(4):
                pr = hk // 2
                if hk % 2 == 0:
                    S.dma(KTp1[:, :NPG * 128], KTs_d[b, pr, :, :], reads=["KTs_d"], writes=["KTp0"])
                vp = Vp[hk % 2]
                S.dma(vp[:, :NPG, :], Vs_d[b, :, hk * 65:(hk + 1) * 65].rearrange("(k p) c -> p k c", p=128), reads=["Vs_d"], writes=[vp.name])
                qrhs = QPb[:, 4 * hk:4 * hk + 4, :4]
                brhs = biasT[:, 4 * hk:4 * hk + 4, :4]
                PPB = 32
                for bk in range((NPG + PPB - 1) // PPB):
                    ps, pn = nextps()
                    pgs = list(range(bk * PPB, min(NPG, (bk + 1) * PPB)))
                    fns = []
                    for pg in pgs:
                        o = ps[:, (pg - bk * PPB) * 16:(pg - bk * PPB + 1) * 16].rearrange("p (a b) -> p a b", b=4)
                        fns.append(lambda e, o=o, pg=pg, qrhs=qrhs: e.matmul(out=o, lhsT=KTp1[:, pg * 128:(pg + 1) * 128], rhs=qrhs, start=True, stop=False))
                        fns.append(lambda e, o=o, pg=pg, brhs=brhs: e.matmul(out=o, lhsT=Eoh[:, pg // 2, :], rhs=brhs, start=False, stop=True))
                    S.mm(fns, reads=["KTp0", "QPb", "Eoh", "biasT"], writes=[pn])
                    ncol = len(pgs) * 16
                    A("act", lambda e, ps=ps, bk=bk, ncol=ncol: e.activation(out=PTs[:, bk * PPB * 16:bk * PPB * 16 + ncol], in_=ps[:, :ncol], func=AF.Exp, scale=0.125),
                      r=[pn], w=["Gm"])
                ps, pn = nextps()
                S.mm([lambda e, ps=ps, pr=pr, qrhs=qrhs: e.matmul(out=ps[0:4, 0:16].rearrange("p (a b) -> p a b", b=4), lhsT=KTn[:, pr, :], rhs=qrhs, start=True, stop=True)],
                     reads=["KTn", "QPb"], writes=[pn])
                A("act", lambda e, ps=ps: e.activation(out=PTn[:, :], in_=ps[0:4, 0:16], func=AF.Exp, scale=0.125), r=[pn], w=["PTn"])
                A("dve", lambda e: e.tensor_tensor(out=PTn[:, :], in0=PTn[:, :], in1=maskS[:, :], op=ALU.mult), r=["PTn", "maskSt"], w=["PTn"])
                psO, pnO = pst[7 - (hk % 2)], "ps%d" % (7 - (hk % 2))
                fns = [lambda e, pg=pg, vp=vp, psO=psO: e.matmul(out=psO[0:65, 0:16], lhsT=vp[:, pg, :], rhs=PTs[:, pg * 16:(pg + 1) * 16], start=(pg == 0), stop=False)
                       for pg in range(NPG)]
                fns.append(lambda e, hk=hk, psO=psO: e.matmul(out=psO[0:65, 0:16], lhsT=Vn[:, hk, :], rhs=PTn[:, :], start=False, stop=True))
                S.mm(fns, reads=[vp.name, "Gm", "Vn", "PTn"], writes=[pnO])
                finish_group(4, hk, psO, pnO)
            out_proj(4, y_s[b * 4:(b + 1) * 4, :])
        S.barrier()
        st2.close()
        st1.close()
        S.finish("sp")
        print("instructions:", S.n_instr, "sem counts", S.cnt)
    return nc


def prep_inputs(cfg, inp):
    f = lambda a: np.ascontiguousarray(np.asarray(a, dtype=np.float32))
    TPC, DB = cfg.tpc, cfg.db
    xp_full = f(inp["x_prompt"])[0]
    common = {
        "w_in0": f(f(inp["w_in0"])[0].reshape(8, 128, 6160).transpose(1, 0, 2)),
        "w_out0": f(f(inp["w_out0"])[0].reshape(16, 128, 1024).transpose(1, 0, 2)),
        "g0": f(f(inp["norm0_g"])[0].reshape(8, 128).T),
        "cw": f(f(inp["ssd_conv_w"])[0].reshape(4, 16, 128).transpose(2, 1, 0)),
        "cb": f(f(inp["ssd_conv_b"])[0].reshape(16, 128).T),
        "ccw": f(f(inp["conf_conv_w"])[0].reshape(31, 8, 128).transpose(2, 1, 0)),
        "ccb": f(f(inp["conf_conv_b"])[0].reshape(8, 128).T),
        "lng": f(f(inp["conf_ln_g"])[0].reshape(8, 128).T),
        "lnb": f(f(inp["conf_ln_b"])[0].reshape(8, 128).T),
        "dtb": f(np.broadcast_to(f(inp["ssd_dt_bias"])[0][None, :], (128, 16))),
        "alog": f(np.broadcast_to(f(inp["ssd_a_log"])[0][None, :], (128, 16))),
        "dsk": f(np.broadcast_to(f(inp["ssd_d"])[0][None, :], (128, 16))),
        "sng": f(f(inp["ssd_norm_g"])[0].reshape(8, 128).T),
    }
    perm = [0, 4, 1, 5, 2, 6, 3, 7, 8, 12, 9, 13, 10, 14, 11, 15]
    w1 = f(inp["w_in1"])[0]
    wq = w1[:, :1024].reshape(1024, 16, 64)[:, perm, :].reshape(1024, 1024)
    wkv = w1[:, 1024:1536]
    wg = w1[:, 1536:2560]
    kt_layout = lambda w: f(w.reshape(8, 128, w.shape[1]).transpose(1, 0, 2))
    NSLOT = TPC // 128
    NPG = cfg.npg
    npool = cfg.npool
    common.update({
        "wq": kt_layout(wq), "wkv": kt_layout(wkv), "wg": kt_layout(wg),
        "wo": f(f(inp["w_out1"])[0].reshape(16, 64, 1024).transpose(1, 0, 2)),
        "g1": f(f(inp["norm1_g"])[0].reshape(8, 128).T),
        "qg": f(np.tile(f(inp["q_norm_g"])[0], 2).reshape(128, 1)),
        "kgb": f(np.broadcast_to(np.tile(f(inp["k_norm_g"])[0], 4)[None, :], (128, 256))),
        "maskS": f(np.tile((np.arange(4)[:, None] <= np.arange(4)[None, :]).astype(np.float32), (1, 4))),
        "pidx": f(np.arange(128, dtype=np.float32).reshape(128, 1)),
        "ck": f(inp["cache_k"]).reshape(npool * 128, 256),
        "cv": f(inp["cache_v"]).reshape(npool * 128, 256),
    })
    pt_all = np.ascontiguousarray(np.asarray(inp["page_table"], dtype=np.int32))
    tri = (np.arange(128)[:, None] <= np.arange(128)[None, :]).astype(np.float32)
    maps = []
    for c in range(NCORE):
        m = dict(common)
        pm = np.zeros((NSLOT + 1, 64), np.float32)
        om = np.zeros((NSLOT + 1, 64), np.float32)
        for j in range(NSLOT):
            own = (8 * j + c) // 2
            pm[j, :own] = 1.0
            om[j, own] = 1.0
        pm[NSLOT, :NPG // 2] = 1.0
        bc = lambda a: f(np.broadcast_to(a[None], (128,) + a.shape))
        m["pm"] = bc(pm)
        m["pneg"] = bc((pm - 1.0) * np.float32(1e30))
        m["om"] = bc(om)
        mm_ = np.ones((128, 8, 128), np.float32)
        for dl in range(8):
            if dl // 2 == c // 2:
                if dl == c:
                    mm_[:, dl, :] = tri
                elif dl > c:
                    mm_[:, dl, :] = 0.0
        m["maskM"] = mm_
        m["oidx"] = np.ascontiguousarray(((8 * np.arange(NSLOT)[None, :] + c) * 128 + np.arange(128)[:, None]).astype(np.int32))
        m["ptb"] = np.ascontiguousarray(np.broadcast_to(pt_all[c * DB:(c + 1) * DB][None], (128, DB, NPG)).astype(np.int32))
        m["xp"] = xp_full
        m["xsm"] = f(f(inp["x_sample"])[c * DB:(c + 1) * DB].reshape(DB * 4, D))
        m["st_ssm"] = f(f(inp["state_ssm"])[0, c * DB:(c + 1) * DB].reshape(DB, 1024, 128))
        m["st_sc"] = f(f(inp["state_ssd_conv"])[0, c * DB:(c + 1) * DB])
        m["st_cc"] = f(f(inp["state_conf_conv"])[0, c * DB:(c + 1) * DB])
        cm = np.zeros((128, 8), np.float32)
        cm[:, :c] = 1.0
        m["cmask"] = cm
        maps.append(m)
    return maps


_NC_CACHE = {}


def run(cfg, inp, debug_l0=False):
    key = (cfg.seq, cfg.dbt, cfg.npg, debug_l0)
    if key not in _NC_CACHE:
        _NC_CACHE[key] = build(cfg, debug_l0)
    nc = _NC_CACHE[key]
    maps = prep_inputs(cfg, inp)
    res = run_bass_kernel_spmd(nc, maps, core_ids=list(range(NCORE)))
    R = res.results
    cat = lambda k: np.concatenate([r[k] for r in R], axis=0)
    TPC, DB = cfg.tpc, cfg.db
    y_p = cat("y_p")[None]
    y_s = cat("y_s").reshape(cfg.dbt, 4, D)
    ssm_p = R[0]["ssm_p"].reshape(1, 1, 16, 64, 128)
    ssm_s = cat("ssm_s").reshape(1, cfg.dbt, 16, 64, 128)
    sc_p = R[0]["sc_p"].reshape(1, 1, 3, 2048)
    sc_s = cat("sc_s").reshape(1, cfg.dbt, 3, 2048)
    cc_p = R[0]["cc_p"].reshape(1, 1, 30, 1024)
    cc_s = cat("cc_s").reshape(1, cfg.dbt, 30, 1024)
    if debug_l0:
        return (y_p, y_s, ssm_p, ssm_s, sc_p, sc_s, cc_p, cc_s)
    NSLOT = TPC // 128
    yp = np.empty((cfg.seq, D), np.float32)
    for c in range(NCORE):
        for j in range(NSLOT):
            t = 8 * j + c
            yp[t * 128:(t + 1) * 128] = R[c]["y_p"][j * 128:(j + 1) * 128]
    y_p = yp[None]
    k_p = R[0]["k_p"].reshape(1, 1, cfg.seq, 4, 64)
    v_p = R[0]["v_p"].reshape(1, 1, cfg.seq, 4, 64)
    k_s = cat("k_s").reshape(1, cfg.dbt, 4, 4, 64)
    v_s = cat("v_s").reshape(1, cfg.dbt, 4, 4, 64)
    return (y_p, y_s, ssm_p, ssm_s, sc_p, sc_s, cc_p, cc_s, k_p, v_p, k_s, v_s)


def kernel(**inputs):
    cfg = Cfg(inputs["x_prompt"].shape[1], inputs["x_sample"].shape[0], inputs["page_table"].shape[1] * 128)
    return run(cfg, inputs)
```

```python
import numpy as np
from contextlib import ExitStack
import concourse.bass as bass
import concourse.mybir as mybir
from concourse.bass_utils import run_bass_kernel_spmd

F32 = mybir.dt.float32
BF16 = mybir.dt.bfloat16
I32 = mybir.dt.int32
ALU = mybir.AluOpType
AF = mybir.ActivationFunctionType
AX = mybir.AxisListType

NCORE = 8
D = 1024
HP = 32
EPS = 1e-6
NEG = -30000.0


class Sched:
    def __init__(self, nc, stack, n_dma_sems=24):
        self.nc = nc
        self.engs = {"pe": nc.tensor, "act": nc.scalar, "dve": nc.vector, "pool": nc.gpsimd, "sp": nc.sync}
        self.sem = {k: stack.enter_context(nc.semaphore("s_" + k)) for k in ("pe", "act", "dve", "pool")}
        self.cnt = {k: 0 for k in self.sem}
        self.dsem = [stack.enter_context(nc.semaphore("d%d" % i)) for i in range(n_dma_sems)]
        self.dval = [0] * n_dma_sems
        self.dnext = 0
        self.waited = {}
        self.lastw = {}
        self.reads = {}
        self.n_instr = 0

    def _semobj(self, key):
        return self.sem[key] if isinstance(key, str) else self.dsem[key[1]]

    def _wait(self, eng, key, val):
        if self.waited.get((eng, key), 0) >= val:
            return
        self.waited[(eng, key)] = val
        self.engs[eng].wait_ge(self._semobj(key), val)

    def _deps(self, eng, reads, writes):
        for b in reads:
            t = self.lastw.get(b)
            if t is not None:
                self._wait(eng, t[0], t[1])
        for b in writes:
            t = self.lastw.get(b)
            if t is not None:
                self._wait(eng, t[0], t[1])
            for k, v in self.reads.get(b, {}).items():
                if k != eng:
                    self._wait(eng, k, v)

    def _record(self, key, val, reads, writes):
        for b in reads:
            d = self.reads.setdefault(b, {})
            if d.get(key, 0) < val:
                d[key] = val
        for b in writes:
            self.lastw[b] = (key, val)
            self.reads[b] = {}

    def op(self, eng, fn, reads=(), writes=()):
        self._deps(eng, reads, writes)
        ins = fn(self.engs[eng])
        self.cnt[eng] += 1
        ins.then_inc(self.sem[eng], 1)
        self._record(eng, self.cnt[eng], reads, writes)
        self.n_instr += 1

    def mm(self, fns, reads=(), writes=()):
        self._deps("pe", reads, writes)
        ins = None
        for fn in fns:
            ins = fn(self.nc.tensor)
            self.n_instr += 1
        self.cnt["pe"] += 1
        ins.then_inc(self.sem["pe"], 1)
        self._record("pe", self.cnt["pe"], reads, writes)

    def dma(self, out, in_, reads=(), writes=(), q="sp", **kw):
        i = self.dnext
        self.dnext = (self.dnext + 1) % len(self.dsem)
        key = ("d", i)
        if self.dval[i]:
            self._wait(q, key, self.dval[i])
        self._deps(q, reads, writes)
        self.dval[i] += 16
        ins = self.engs[q].dma_start(out=out, in_=in_, **kw)
        ins.then_inc(self.dsem[i], 16)
        self._record(key, self.dval[i], reads, writes)
        self.n_instr += 1

    def idma(self, out, in_, idx_ap, reads=(), writes=()):
        q = "pool"
        i = self.dnext
        self.dnext = (self.dnext + 1) % len(self.dsem)
        key = ("d", i)
        if self.dval[i]:
            self._wait(q, key, self.dval[i])
        self._deps(q, reads, writes)
        self.dval[i] += 16
        ins = self.nc.gpsimd.indirect_dma_start(out=out, out_offset=None, in_=in_,
                                                in_offset=bass.IndirectOffsetOnAxis(ap=idx_ap, axis=0))
        ins.then_inc(self.dsem[i], 16)
        self._record(key, self.dval[i], reads, writes)
        self.n_instr += 1

    def barrier(self):
        for e in ("pe", "act", "dve", "pool", "sp"):
            for i, v in enumerate(self.dval):
                if v:
                    self._wait(e, ("d", i), v)
            for k, v in self.cnt.items():
                if v and k != e:
                    self._wait(e, k, v)

    def finish(self, eng="sp"):
        for i, v in enumerate(self.dval):
            if v:
                self._wait(eng, ("d", i), v)
        for k, v in self.cnt.items():
            if v:
                self._wait(eng, k, v)


class Cfg:
    def __init__(self, seq, dec_batch, past_len):
        self.seq = seq
        self.tpc = seq // NCORE
        self.nch = self.tpc // 128
        self.dbt = dec_batch
        self.db = dec_batch // NCORE
        self.npg = past_len // 128
        n_used = dec_batch * self.npg
        self.npool = n_used + (n_used + 3) // 4
        self.nblk_p = seq // 256
        self.bpc = self.tpc // 256


def build(cfg, debug_l0=False):
    nc = bass.Bass("TRN2", target_bir_lowering=False)
    TPC, NCH, DB, NPG = cfg.tpc, cfg.nch, cfg.db, cfg.npg
    SEQ = cfg.seq
    NCHA = SEQ // 128

    def din(name, shape, dt=F32):
        return nc.dram_tensor(name, list(shape), dt, kind="ExternalInput").ap()

    def dout(name, shape, dt=F32):
        return nc.dram_tensor(name, list(shape), dt, kind="ExternalOutput").ap()

    xp = din("xp", [SEQ, D])
    xsm = din("xsm", [DB * 4, D])
    st_ssm = din("st_ssm", [DB, 1024, 128])
    st_sc = din("st_sc", [DB, 3, 2048])
    st_cc = din("st_cc", [DB, 30, 1024])
    w_in0 = din("w_in0", [128, 8, 6160])
    w_out0 = din("w_out0", [128, 16, 1024])
    g0_d = din("g0", [128, 8])
    cw_d = din("cw", [128, 16, 4])
    cb_d = din("cb", [128, 16])
    ccw_d = din("ccw", [128, 8, 31])
    ccb_d = din("ccb", [128, 8])
    lng_d = din("lng", [128, 8])
    lnb_d = din("lnb", [128, 8])
    dtb_d = din("dtb", [128, 16])
    alog_d = din("alog", [128, 16])
    dsk_d = din("dsk", [128, 16])
    sng_d = din("sng", [128, 8])
    cmask_d = din("cmask", [128, 8])
    NSLOT_ = TPC // 128
    wq_d = din("wq", [128, 8, 1024])
    wkv_d = din("wkv", [128, 8, 512])
    wg_d = din("wg", [128, 8, 1024])
    wo_d = din("wo", [64, 16, 1024])
    g1_d = din("g1", [128, 8])
    qg_d = din("qg", [128, 1])
    kgb_d = din("kgb", [128, 256])
    pm_d = din("pm", [128, NSLOT_ + 1, 64])
    pneg_d = din("pneg", [128, NSLOT_ + 1, 64])
    om_d = din("om", [128, NSLOT_ + 1, 64])
    maskM_d = din("maskM", [128, 8, 128])
    maskS_d = din("maskS", [4, 16])
    oidx_d = din("oidx", [128, NSLOT_], I32)
    pidx_d = din("pidx", [128, 1])
    ptb_d = din("ptb", [128, DB, NPG], I32)
    ck_d = din("ck", [cfg.npool * 128, 256])
    cv_d = din("cv", [cfg.npool * 128, 256])
    k_p = dout("k_p", [SEQ, 256])
    v_p = dout("v_p", [SEQ, 256])
    k_s = dout("k_s", [DB * 4, 256])
    v_s = dout("v_s", [DB * 4, 256])

    y_p = dout("y_p", [TPC, D])
    y_s = dout("y_s", [DB * 4, D])
    ssm_p = dout("ssm_p", [1024, 128])
    ssm_s = dout("ssm_s", [DB, 1024, 128])
    sc_p = dout("sc_p", [3, 2048])
    sc_s = dout("sc_s", [DB, 3, 2048])
    cc_p = dout("cc_p", [30, 1024])
    cc_s = dout("cc_s", [DB, 30, 1024])

    x1_d = nc.dram_tensor("x1_d", [SEQ + DB * 4, D], F32, kind="Internal").ap()
    KT_d = nc.dram_tensor("KT_d", [2, 128, SEQ + DB * 4], BF16, kind="Internal").ap()
    V_d = nc.dram_tensor("V_d", [SEQ + DB * 4, 260], BF16, kind="Internal").ap()
    KTs_d = nc.dram_tensor("KTs_d", [DB, 2, 128, NPG * 128], BF16, kind="Internal").ap()
    Vs_d = nc.dram_tensor("Vs_d", [DB, NPG * 128, 260], BF16, kind="Internal").ap()

    st = ExitStack()
    with st:
        S = Sched(nc, st)
        A = lambda eng, fn, r=(), w=(): S.op(eng, fn, reads=r, writes=w)

        cur = [st]

        def T(name, shape, dt=F32):
            return cur[0].enter_context(nc.sbuf_tensor(name, list(shape), dt))

        pst = [st.enter_context(nc.psum_tensor("ps%d" % i, [128, 512], F32)) for i in range(8)]
        psi = [0]

        psn = [8]

        def nextps(n=None):
            n = n or psn[0]
            i = psi[0] % n
            psi[0] = (i + 1) % n
            return pst[i], "ps%d" % i

        identf = T("identf", [128, 128])
        triU = T("triU", [128, 128])
        SU = T("SU", [128, 128])
        onesf = T("onesf", [128, 128])
        epsT = T("epsT", [128, 1])
        oneT = T("oneT", [128, 1])
        for t_, cmp_, sgn in ((identf, ALU.is_equal, 1), (triU, ALU.is_ge, -1)):
            A("pool", lambda e, t_=t_: e.memset(t_[:], 1.0), w=[t_.name])
            A("pool", lambda e, t_=t_, cmp_=cmp_, sgn=sgn: e.affine_select(out=t_[:], in_=t_[:], pattern=[[-sgn, 128]], compare_op=cmp_,
                                                          fill=0.0, base=0, channel_multiplier=sgn), r=[t_.name], w=[t_.name])
        A("dve", lambda e: e.tensor_scalar(out=SU[:], in0=triU[:], scalar1=-1.0, scalar2=1.0, op0=ALU.mult, op1=ALU.add), r=["triU"], w=["SU"])
        A("pool", lambda e: e.memset(onesf[:], 1.0), w=["onesf"])
        A("pool", lambda e: e.memset(epsT[:], EPS), w=["epsT"])
        A("pool", lambda e: e.memset(oneT[:], 1.0), w=["oneT"])

        BD = T("BD", [128, 128])
        A("pool", lambda e: e.memset(BD[:], 0.0), w=["BD"])
        A("pool", lambda e: e.memset(BD[0:64, 0:64], 1.0), r=["BD"], w=["BD"])
        A("pool", lambda e: e.memset(BD[64:128, 64:128], 1.0), r=["BD"], w=["BD"])
        st0 = ExitStack()
        cur[0] = st0
        def ld(name, src, shape):
            t = T(name, shape)
            S.dma(t[:], src, writes=[name])
            return t
        g0 = ld("g0t", g0_d[:, :], [128, 8])
        cw = ld("cwt", cw_d[:, :, :], [128, 16, 4])
        cb = ld("cbt", cb_d[:, :], [128, 16])
        ccw = ld("ccwt", ccw_d[:, :, :], [128, 8, 31])
        ccb = ld("ccbt", ccb_d[:, :], [128, 8])
        lng = ld("lngt", lng_d[:, :], [128, 8])
        lnb = ld("lnbt", lnb_d[:, :], [128, 8])
        dtb = ld("dtbt", dtb_d[:, :], [128, 16])
        Ab = ld("Abt", alog_d[:, :], [128, 16])
        dsk = ld("dskt", dsk_d[:, :], [128, 16])
        sng = ld("sngt", sng_d[:, :], [128, 8])
        cmask = ld("cmaskt", cmask_d[:, :], [128, 8])
        A("act", lambda e: e.activation(out=Ab[:], in_=Ab[:], func=AF.Exp), r=["Abt"], w=["Abt"])
        A("dve", lambda e: e.tensor_scalar(out=Ab[:], in0=Ab[:], scalar1=-1.0, scalar2=None, op0=ALU.mult), r=["Abt"], w=["Abt"])

        Win = T("Win", [128, 8, 6160], BF16)
        Wout = T("Wout", [128, 16, 1024], BF16)
        xbc_c = T("xbc_c", [128, 16, 128])
        xbc_flat = xbc_c[:, :, :].rearrange("p a b -> p (a b)")
        XBCC = ["xbcc%d" % t for t in range(16)]
        stg = [xbc_flat[:, 0:770], xbc_flat[:, 1024:1024 + 770]]
        stgn = [XBCC[:8], XBCC[8:]]
        si = 0
        cast_engs = ["dve", "pool"]
        for kt in range(8):
            for q8 in range(8):
                sb = stg[si % 2]
                S.dma(sb, w_in0[:, kt, q8 * 770:(q8 + 1) * 770], writes=stgn[si % 2])
                A(cast_engs[si % 2], lambda e, sb=sb, kt=kt, q8=q8: e.tensor_scalar(
                    out=Win[:, kt, q8 * 770:(q8 + 1) * 770], in0=sb, scalar1=g0[:, kt:kt + 1], scalar2=None, op0=ALU.mult),
                    r=stgn[si % 2] + ["g0t"], w=["Win"])
                si += 1
        for t_ in range(16):
            for hf in range(2):
                sb = stg[si % 2]
                S.dma(sb[:, :512], w_out0[:, t_, hf * 512:(hf + 1) * 512], writes=stgn[si % 2])
                A(cast_engs[si % 2], lambda e, sb=sb, t_=t_, hf=hf: e.tensor_copy(out=Wout[:, t_, hf * 512:(hf + 1) * 512], in_=sb[:, :512]), r=stgn[si % 2], w=["Wout"])
                si += 1

        xt = T("xt", [128, D])
        xn = T("xn", [128, D])
        ss = T("ss", [128, 8])
        xnT = T("xnT", [128, 8, 128], BF16)
        xbc_f = T("xbc_f", [128, 16, HP + 128])
        gl_f = T("gl_f", [128, 8, HP + 128])
        scg = T("scg", [128, 8, 128], BF16)
        c_f = T("c_f", [128, 8, 128])
        cat_f = T("cat_f", [128, 16, 128], BF16)
        CTb = T("CTb", [128, 4, 128], BF16)
        BTb = T("BTb", [128, 4, 128], BF16)
        Btm = T("Btm", [128, 512], BF16)
        dtt = T("dtt", [128, 8, 16])
        aSU4 = [T("aSU0", [128, 4, 128])] * 2
        dec4 = [T("dec0", [128, 4, 128])] * 2
        cbm = T("cbm", [128, 4, 128])
        MT = T("MT", [128, 16, 128], BF16)
        xdt = T("xdt", [128, 1024], BF16)
        xdte = T("xdte", [128, 1024], BF16)
        yacc = T("yacc", [128, 1024])
        ytmp = T("ytmp", [128, 1024])
        H = T("H", [128, 1024])
        Hb = T("Hb", [128, 1024], BF16)
        ptmp2 = [T("ptmpa", [128, 128])] * 2
        cdb = T("cdb", [128, 16])
        Atot = T("Atot", [128, 16])
        hist_tm = xbc_flat[:32, :]
        hout = xbc_flat[:32, :]
        sz = xn
        csq = ytmp[:, :].rearrange("p (a b) -> p a b", b=128)
        sig = aSU4[0]
        lnst = dec4[0]

        def fm_inproj(L, col0s, evac):
            ps, pn = nextps()
            fns = []
            for j, c0 in enumerate(col0s):
                for kt in range(8):
                    fns.append(lambda e, j=j, c0=c0, kt=kt: e.matmul(out=ps[:, j * 128:j * 128 + L], lhsT=Win[:, kt, c0:c0 + 128],
                                                                   rhs=xnT[:, kt, :L], start=(kt == 0), stop=(kt == 7)))
            S.mm(fns, reads=["Win", "xnT"], writes=[pn])
            v = ps[:, :].rearrange("p (a b) -> p a b", b=128)[:, :len(col0s), :L]
            evac(v, pn)

        def transposes_to_tm(L, srcs, src_names, nm):
            ps, pn = nextps()
            S.mm([lambda e, j=j, s=s: e.transpose(out=ps[:L, j * 128:(j + 1) * 128], in_=s, identity=identf[:, :])
                  for j, s in enumerate(srcs)], reads=list(src_names) + ["identf"], writes=[pn])
            return ps[:L, :len(srcs) * 128], pn

        def conv_all(tiles, engs, out_tile, src_tile, ntap, w_t, b_t, L, src_pref, w_names, out_pref, taps=None):
            o0 = HP - (ntap - 1)
            for j in (taps if taps is not None else range(ntap)):
                for t in tiles:
                    eng = engs[t]
                    out_ap = out_tile[:, t, :L]
                    rn = [src_pref % t] + w_names
                    wn = out_pref % t
                    src = src_tile[:, t, o0 + j:o0 + j + L]
                    if j == 0:
                        A(eng, lambda e, out_ap=out_ap, src=src, t=t: e.tensor_scalar(out=out_ap, in0=src, scalar1=w_t[:, t, 0:1], scalar2=b_t[:, t:t + 1],
                                                                                    op0=ALU.mult, op1=ALU.add), r=rn, w=[wn])
                    elif eng == "dve":
                        A(eng, lambda e, out_ap=out_ap, src=src, t=t, j=j: e.scalar_tensor_tensor(out=out_ap, in0=src, scalar=w_t[:, t, j:j + 1], in1=out_ap,
                                                                                              op0=ALU.mult, op1=ALU.add), r=rn + [wn], w=[wn])
                    else:
                        pt_ = ptmp2[t % 2]
                        A(eng, lambda e, src=src, t=t, j=j, pt_=pt_: e.tensor_tensor(out=pt_[:, :L], in0=src, in1=w_t[:, t, j:j + 1].broadcast_to([128, L]), op=ALU.mult),
                          r=rn, w=[pt_.name])
                        A(eng, lambda e, out_ap=out_ap, pt_=pt_: e.tensor_tensor(out=out_ap, in0=out_ap, in1=pt_[:, :L], op=ALU.add), r=[pt_.name, wn], w=[wn])

        def l0_chunk(src_ap, L, mode, dst_x1=None):
            full = mode == "full"
            nx = 16 if (full or mode == "halo2") else 12
            S.dma(xt[:L, :], src_ap, writes=["xt"])
            A("act", lambda e: e.activation(out=xn[:L, :], in_=xt[:L, :], func=AF.Square, scale=1.0 / 32, accum_out=ss[:L, 0:1]),
              r=["xt"], w=["xn", "ss"])
            A("act", lambda e: e.activation(out=ss[:L, 1:2], in_=ss[:L, 0:1], func=AF.Ln, bias=epsT[:L, 0:1]), r=["ss", "epsT"], w=["ss"])
            A("act", lambda e: e.activation(out=ss[:L, 2:3], in_=ss[:L, 1:2], func=AF.Exp, scale=-0.5), r=["ss"], w=["ss"])
            A("dve", lambda e: e.tensor_scalar(out=xn[:L, :], in0=xt[:L, :], scalar1=ss[:L, 2:3], scalar2=None, op0=ALU.mult),
              r=["xt", "ss"], w=["xn"])
            for half in range(2):
                ps, pn = nextps()
                S.mm([lambda e, j=j: e.transpose(out=ps[:, j * 128:j * 128 + L], in_=xn[:L, (half * 4 + j) * 128:(half * 4 + j + 1) * 128],
                                                 identity=identf[:L, :L]) for j in range(4)], reads=["xn", "identf"], writes=[pn])
                v = ps[:, :].rearrange("p (a b) -> p a b", b=128)[:, :, :L]
                A("act", lambda e, v=v, half=half: e.activation(out=xnT[:, half * 4:half * 4 + 4, :L], in_=v, func=AF.Copy), r=[pn], w=["xnT"])
            if full or mode == "halo2":
                for g in range(2):
                    def evb(v, pn):
                        A("act", lambda e: e.activation(out=sig[:, :, :L], in_=v, func=AF.Sigmoid), r=[pn], w=["aSU0"])
                    fm_inproj(L, [4112 + 128 * t for t in range(4 * g, 4 * g + 4)], evb)

                    def eva(v, pn, g=g):
                        A("dve", lambda e: e.tensor_tensor(out=gl_f[:, 4 * g:4 * g + 4, HP:HP + L], in0=v, in1=sig[:, :, :L], op=ALU.mult),
                          r=[pn, "aSU0"], w=["glf%d" % t for t in range(4 * g, 4 * g + 4)])
                    fm_inproj(L, [3088 + 128 * t for t in range(4 * g, 4 * g + 4)], eva)
            if full:
                cconv = lambda taps: conv_all(list(range(8)), ["dve"] * 7 + ["pool"] * 1, c_f, gl_f, 31, ccw, ccb, L, "glf%d", ["ccwt", "ccbt"], "cf%d", taps=taps)
                cconv(range(0, 8))
            for g in range(nx // 4):
                def ev(v, pn, g=g):
                    A("act", lambda e: e.activation(out=xbc_f[:, 4 * g:4 * g + 4, HP:HP + L], in_=v, func=AF.Copy), r=[pn],
                      w=["xbcf%d" % t for t in range(4 * g, 4 * g + 4)])
                fm_inproj(L, [1024 + 128 * t for t in range(4 * g, 4 * g + 4)], ev)
            if mode in ("halo1", "halo2"):
                for t in range(nx):
                    A("pool", lambda e, t=t: e.tensor_copy(out=xbc_f[:, t, 0:HP], in_=xbc_f[:, t, HP:2 * HP]), r=["xbcf%d" % t], w=["xbcf%d" % t])
                if mode == "halo2":
                    for t in range(8):
                        A("pool", lambda e, t=t: e.tensor_copy(out=gl_f[:, t, 0:HP], in_=gl_f[:, t, HP:2 * HP]), r=["glf%d" % t], w=["glf%d" % t])
                return
            if full:
                for g in range(2):
                    def evc(v, pn, g=g):
                        A("act", lambda e: e.activation(out=scg[:, 4 * g:4 * g + 4, :L], in_=v, func=AF.Silu), r=[pn], w=["scg"])
                    fm_inproj(L, [5136 + 128 * t for t in range(4 * g, 4 * g + 4)], evc)
            if full:
                for half in range(2):
                    ps, pn = nextps()
                    S.mm([lambda e, kt=kt, half=half: e.matmul(out=ps[:L, :], lhsT=xnT[:, kt, :L], rhs=Win[:, kt, 512 * half:512 * half + 512],
                                                              start=(kt == 0), stop=(kt == 7)) for kt in range(8)], reads=["xnT", "Win"], writes=[pn])
                    sl = slice(512 * half, 512 * half + 512)
                    A("act", lambda e, ps=ps, sl=sl: e.activation(out=sz[:L, sl], in_=ps[:L, :], func=AF.Silu), r=[pn], w=["xn"])

            ps, pn = nextps()
            S.mm([lambda e, kt=kt: e.matmul(out=ps[:L, 0:16], lhsT=xnT[:, kt, :L], rhs=Win[:, kt, 3072:3088], start=(kt == 0), stop=(kt == 7))
                  for kt in range(8)], reads=["xnT", "Win"], writes=[pn])
            dtr, dta, dte, dtl, dtv, av, acs, eacs = [dtt[:L, i, :] for i in range(8)]
            A("dve", lambda e: e.tensor_tensor(out=dtr, in0=ps[:L, 0:16], in1=dtb[:L, :], op=ALU.add), r=[pn, "dtbt"], w=["dtt"])
            A("dve", lambda e: e.scalar_tensor_tensor(out=dta, in0=dtr, scalar=-1.0, in1=dtr, op0=ALU.mult, op1=ALU.min), r=["dtt"], w=["dtt"])
            A("act", lambda e: e.activation(out=dte, in_=dta, func=AF.Exp), r=["dtt"], w=["dtt"])
            A("act", lambda e: e.activation(out=dtl, in_=dte, func=AF.Ln, bias=oneT[:L, 0:1]), r=["dtt", "oneT"], w=["dtt"])
            A("dve", lambda e: e.scalar_tensor_tensor(out=dtv, in0=dtr, scalar=0.0, in1=dtl, op0=ALU.max, op1=ALU.add), r=["dtt"], w=["dtt"])
            A("dve", lambda e: e.tensor_tensor(out=av, in0=dtv, in1=Ab[:L, :], op=ALU.mult), r=["dtt", "Abt"], w=["dtt"])
            ps2, pn2 = nextps()
            S.mm([lambda e: e.matmul(out=ps2[:L, 0:16], lhsT=triU[:L, :L], rhs=av, start=True, stop=True),
                  lambda e: e.matmul(out=ps2[:, 16:32], lhsT=onesf[:L, :], rhs=av, start=True, stop=True)],
                 reads=["triU", "onesf", "dtt"], writes=[pn2])
            A("dve", lambda e: e.tensor_copy(out=acs, in_=ps2[:L, 0:16]), r=[pn2], w=["dtt"])
            A("act", lambda e: e.activation(out=cdb[:, :], in_=ps2[:, 16:32], func=AF.Exp), r=[pn2], w=["cdb"])
            A("dve", lambda e: e.tensor_tensor(out=Atot[:, :], in0=Atot[:, :], in1=ps2[:, 16:32], op=ALU.add), r=[pn2, "Atot"], w=["Atot"])
            A("dve", lambda e: e.tensor_tensor(out=dta, in0=ps2[:L, 16:32], in1=acs, op=ALU.subtract), r=[pn2, "dtt"], w=["dtt"])
            A("act", lambda e: e.activation(out=dta, in_=dta, func=AF.Exp), r=["dtt"], w=["dtt"])
            A("act", lambda e: e.activation(out=eacs, in_=acs, func=AF.Exp), r=["dtt"], w=["dtt"])
            conv_all(list(range(nx)), ["dve"] * 16, xbc_c, xbc_f, 4, cw, cb, L, "xbcf%d", ["cwt", "cbt"], "xbcc%d")
            for g in range(nx // 4):
                A("act", lambda e, g=g: e.activation(out=xbc_c[:, 4 * g:4 * g + 4, :L], in_=xbc_c[:, 4 * g:4 * g + 4, :L], func=AF.Silu),
                  r=["xbcc%d" % t for t in range(4 * g, 4 * g + 4)], w=["xbcc%d" % t for t in range(4 * g, 4 * g + 4)])
            if L >= HP:
                for t in range(nx):
                    A("pool", lambda e, t=t: e.tensor_copy(out=xbc_f[:, t, 0:HP], in_=xbc_f[:, t, L:L + HP]), r=["xbcf%d" % t], w=["xbcf%d" % t])
            if full:
                cconv(range(8, 16))
            for g in range(2):
                v, pn = transposes_to_tm(L, [xbc_c[:, 4 * g + j, :L] for j in range(4)], ["xbcc%d" % (4 * g + j) for j in range(4)], "xs")
                v3 = v.rearrange("p (h q) -> p h q", q=64)
                sl = slice(512 * g, 512 * (g + 1))
                A("dve", lambda e, v3=v3, sl=sl, g=g: e.tensor_tensor(out=xdt[:L, sl].rearrange("p (h q) -> p h q", q=64), in0=v3,
                                                                      in1=dtv[:, 8 * g:8 * g + 8].unsqueeze(2).broadcast_to([L, 8, 64]), op=ALU.mult),
                  r=[pn, "dtt"], w=["xdt"])
                if full:
                    A("dve", lambda e, v=v, sl=sl, g=g: e.tensor_tensor(out=yacc[:L, sl].rearrange("p (h q) -> p h q", q=64), in0=v.rearrange("p (h q) -> p h q", q=64), in1=dsk[:L, 8 * g:8 * g + 8].unsqueeze(2).broadcast_to([L, 8, 64]), op=ALU.mult),
                      r=[pn, "dskt"], w=["yacc"])
            v, pn = transposes_to_tm(L, [xbc_c[:, 8 + j, :L] for j in range(4)], ["xbcc%d" % (8 + j) for j in range(4)], "B")
            A("act", lambda e: e.activation(out=Btm[:L, :], in_=v, func=AF.Copy), r=[pn], w=["Btm"])
            A("dve", lambda e: e.tensor_tensor(out=xdte[:L, :].rearrange("p (h q) -> p h q", q=64), in0=xdt[:L, :].rearrange("p (h q) -> p h q", q=64),
                                               in1=dta.unsqueeze(2).broadcast_to([L, 16, 64]), op=ALU.mult), r=["xdt", "dtt"], w=["xdte"])
            if full:
                cconv(range(16, 24))
                A("pool", lambda e: e.tensor_copy(out=BTb[:, :, :L], in_=xbc_c[:, 8:12, :L]), r=["xbcc%d" % t for t in range(8, 12)], w=["BTb"])
                A("pool", lambda e: e.tensor_copy(out=CTb[:, :, :L], in_=xbc_c[:, 12:16, :L]), r=["xbcc%d" % t for t in range(12, 16)], w=["CTb"])
                psc, pnc = nextps()
                S.mm([lambda e, g=g: e.matmul(out=psc[:L, g * 128:g * 128 + L], lhsT=BTb[:, g, :L], rhs=CTb[:, g, :L], start=True, stop=True)
                      for g in range(4)], reads=["BTb", "CTb"], writes=[pnc])
                A("dve", lambda e: e.tensor_tensor(out=cbm[:L, :, :L], in0=psc[:L, :].rearrange("p (a b) -> p a b", b=128)[:, :, :L],
                                                   in1=triU[:L, :L].unsqueeze(1).broadcast_to([L, 4, L]), op=ALU.mult), r=[pnc, "triU"], w=["cbm"])
                for q4 in range(4):
                    aS, dc = aSU4[q4 % 2], dec4[q4 % 2]
                    A("pool", lambda e, aS=aS, q4=q4: e.tensor_tensor(out=aS[:L, :, :L], in0=SU[:L, :L].unsqueeze(1).broadcast_to([L, 4, L]),
                                                                     in1=av[:, 4 * q4:4 * q4 + 4].unsqueeze(2).broadcast_to([L, 4, L]), op=ALU.mult),
                      r=["SU", "dtt"], w=[aS.name])
                    ps, pn = nextps()
                    S.mm([lambda e, j=j, aS=aS: e.matmul(out=ps[:L, j * 128:j * 128 + L], lhsT=aS[:L, j, :L], rhs=triU[:L, :L], start=True, stop=True)
                          for j in range(4)], reads=[aS.name, "triU"], writes=[pn])
                    A("act", lambda e, ps=ps, dc=dc: e.activation(out=dc[:L, :, :L], in_=ps[:L, :].rearrange("p (a b) -> p a b", b=128)[:, :, :L],
                                                                  func=AF.Exp), r=[pn], w=[dc.name])
                    A("dve", lambda e, q4=q4, dc=dc: e.tensor_tensor(out=MT[:L, 4 * q4:4 * q4 + 4, :L], in0=dc[:L, :, :L],
                                                                     in1=cbm[:L, q4, :L].unsqueeze(1).broadcast_to([L, 4, L]), op=ALU.mult), r=[dc.name, "cbm"], w=["MT"])
                for half in range(2):
                    psd, pnd = nextps()
                    S.mm([lambda e, h=h: e.matmul(out=psd[:L, (h % 8) * 64:(h % 8) * 64 + 64], lhsT=MT[:L, h, :L], rhs=xdt[:L, h * 64:(h + 1) * 64],
                                                  start=True, stop=True) for h in range(8 * half, 8 * half + 8)], reads=["MT", "xdt"], writes=[pnd])
                    pso, pno = nextps()
                    S.mm([lambda e, g=g: e.matmul(out=pso[:L, (g % 2) * 256:(g % 2) * 256 + 256], lhsT=CTb[:, g, :L], rhs=Hb[:, g * 256:(g + 1) * 256],
                                                  start=True, stop=True) for g in range(2 * half, 2 * half + 2)], reads=["CTb", "Hb"], writes=[pno])
                    sl = slice(512 * half, 512 * half + 512)
                    A("dve", lambda e, pso=pso, sl=sl, half=half: e.tensor_tensor(
                        out=ytmp[:L, sl].rearrange("p (h q) -> p h q", q=64), in0=pso[:L, :].rearrange("p (h q) -> p h q", q=64),
                        in1=eacs[:, 8 * half:8 * half + 8].unsqueeze(2).broadcast_to([L, 8, 64]), op=ALU.mult), r=[pno, "dtt"], w=["ytmp"])
                    A("pool", lambda e, sl=sl: e.tensor_tensor(out=yacc[:L, sl], in0=yacc[:L, sl], in1=ytmp[:L, sl], op=ALU.add), r=["yacc", "ytmp"], w=["yacc"])
                    A("dve", lambda e, psd=psd, sl=sl: e.tensor_tensor(out=yacc[:L, sl], in0=yacc[:L, sl], in1=psd[:L, :], op=ALU.add), r=["yacc", pnd], w=["yacc"])
            if full:
                cconv(range(24, 31))
                if L >= HP:
                    for t in range(8):
                        A("pool", lambda e, t=t: e.tensor_copy(out=gl_f[:, t, 0:HP], in_=gl_f[:, t, L:L + HP]), r=["glf%d" % t], w=["glf%d" % t])
            for half in range(2):
                pss, pns = nextps()
                S.mm([lambda e, g=g: e.matmul(out=pss[:, (g % 2) * 256:(g % 2) * 256 + 256], lhsT=Btm[:L, g * 128:(g + 1) * 128],
                                              rhs=xdte[:L, g * 256:(g + 1) * 256], start=True, stop=True) for g in range(2 * half, 2 * half + 2)],
                     reads=["Btm", "xdte"], writes=[pns])
                sl = slice(512 * half, 512 * half + 512)
                A("dve", lambda e, sl=sl, half=half: e.tensor_tensor(out=H[:, sl].rearrange("p (h q) -> p h q", q=64), in0=H[:, sl].rearrange("p (h q) -> p h q", q=64),
                                                                     in1=cdb[:, 8 * half:8 * half + 8].unsqueeze(2).broadcast_to([128, 8, 64]), op=ALU.mult),
                  r=["H", "cdb", "Hb"], w=["H"])
                A("dve", lambda e, sl=sl, pss=pss: e.tensor_tensor(out=H[:, sl], in0=H[:, sl], in1=pss[:, :], op=ALU.add), r=["H", pns], w=["H"])
            if not full:
                return
            A("act", lambda e: e.activation(out=Hb[:, :], in_=H[:, :], func=AF.Copy), r=["H"], w=["Hb"])
            A("dve", lambda e: e.tensor_tensor(out=yacc[:L, :], in0=yacc[:L, :], in1=sz[:L, :], op=ALU.mult), r=["yacc", "xn"], w=["yacc"])
            for g in range(4):
                A("act", lambda e, g=g: e.activation(out=ytmp[:L, 256 * g:256 * g + 256], in_=yacc[:L, 256 * g:256 * g + 256], func=AF.Square, scale=1.0 / 16,
                                                     accum_out=ss[:L, 4 + g:5 + g]), r=["yacc"], w=["ytmp", "ss"])
            A("act", lambda e: e.activation(out=ss[:L, 4:8], in_=ss[:L, 4:8], func=AF.Ln, bias=epsT[:L, 0:1]), r=["ss", "epsT"], w=["ss"])
            A("act", lambda e: e.activation(out=ss[:L, 4:8], in_=ss[:L, 4:8], func=AF.Exp, scale=-0.5), r=["ss"], w=["ss"])
            A("dve", lambda e: e.tensor_tensor(out=yacc[:L, :].rearrange("p (g q) -> p g q", q=256), in0=yacc[:L, :].rearrange("p (g q) -> p g q", q=256),
                                               in1=ss[:L, 4:8].unsqueeze(2).broadcast_to([L, 4, 256]), op=ALU.mult), r=["yacc", "ss"], w=["yacc"])
            for half in range(2):
                ps, pn = nextps()
                S.mm([lambda e, j=j, half=half: e.transpose(out=ps[:, j * 128:j * 128 + L], in_=yacc[:L, (half * 4 + j) * 128:(half * 4 + j + 1) * 128],
                                                           identity=identf[:L, :L]) for j in range(4)], reads=["yacc", "identf"], writes=[pn])
                v = ps[:, :].rearrange("p (a b) -> p a b", b=128)[:, :, :L]
                for j in range(4):
                    A("act", lambda e, v=v, half=half, j=j: e.activation(out=cat_f[:, half * 4 + j, :L], in_=v[:, j, :], func=AF.Copy,
                                                                         scale=sng[:, half * 4 + j:half * 4 + j + 1]), r=[pn, "sngt"], w=["cat_f"])
            cfn = ["cf%d" % t for t in range(8)]
            A("act", lambda e: e.activation(out=csq[:, :, :L], in_=c_f[:, :, :L], func=AF.Square), r=cfn, w=["ytmp"])
            ps, pn = nextps()
            S.mm([lambda e, t=t: e.matmul(out=ps[:, 0:L], lhsT=onesf[:, :], rhs=c_f[:, t, :L], start=(t == 0), stop=(t == 7)) for t in range(8)] +
                 [lambda e, t=t: e.matmul(out=ps[:, 128:128 + L], lhsT=onesf[:, :], rhs=csq[:, t, :L], start=(t == 0), stop=(t == 7)) for t in range(8)],
                 reads=cfn + ["ytmp", "onesf"], writes=[pn])
            mean, ex2, var, rstd = [lnst[:, i, :L] for i in range(4)]
            A("dve", lambda e: e.tensor_scalar(out=mean, in0=ps[:, 0:L], scalar1=1.0 / 1024, scalar2=None, op0=ALU.mult), r=[pn], w=["dec0"])
            A("dve", lambda e: e.tensor_scalar(out=ex2, in0=ps[:, 128:128 + L], scalar1=1.0 / 1024, scalar2=None, op0=ALU.mult), r=[pn], w=["dec0"])
            A("dve", lambda e: e.tensor_tensor(out=var, in0=mean, in1=mean, op=ALU.mult), r=["dec0"], w=["dec0"])
            A("dve", lambda e: e.tensor_tensor(out=var, in0=ex2, in1=var, op=ALU.subtract), r=["dec0"], w=["dec0"])
            A("act", lambda e: e.activation(out=rstd, in_=var, func=AF.Ln, bias=epsT[:, 0:1]), r=["dec0", "epsT"], w=["dec0"])
            A("act", lambda e: e.activation(out=rstd, in_=rstd, func=AF.Exp, scale=-0.5), r=["dec0"], w=["dec0"])
            A("dve", lambda e: e.tensor_tensor(out=c_f[:, :, :L], in0=c_f[:, :, :L], in1=mean.unsqueeze(1).broadcast_to([128, 8, L]), op=ALU.subtract),
              r=cfn + ["dec0"], w=cfn)
            A("dve", lambda e: e.tensor_tensor(out=c_f[:, :, :L], in0=c_f[:, :, :L], in1=rstd.unsqueeze(1).broadcast_to([128, 8, L]), op=ALU.mult),
              r=cfn + ["dec0"], w=cfn)
            A("pool", lambda e: e.tensor_tensor(out=c_f[:, :, :L], in0=c_f[:, :, :L], in1=lng[:, :].unsqueeze(2).broadcast_to([128, 8, L]), op=ALU.mult),
              r=cfn + ["lngt"], w=cfn)
            A("pool", lambda e: e.tensor_tensor(out=c_f[:, :, :L], in0=c_f[:, :, :L], in1=lnb[:, :].unsqueeze(2).broadcast_to([128, 8, L]), op=ALU.add),
              r=cfn + ["lnbt"], w=cfn)
            A("act", lambda e: e.activation(out=c_f[:, :, :L], in_=c_f[:, :, :L], func=AF.Silu), r=cfn, w=cfn)
            A("dve", lambda e: e.tensor_tensor(out=cat_f[:, 8:16, :L], in0=c_f[:, :, :L], in1=scg[:, :, :L], op=ALU.mult), r=cfn + ["scg"], w=["cat_f"])
            for half in range(2):
                ps, pn = nextps()
                S.mm([lambda e, t=t, half=half: e.matmul(out=ps[:L, :], lhsT=cat_f[:, t, :L], rhs=Wout[:, t, 512 * half:512 * half + 512],
                                                        start=(t == 0), stop=(t == 15)) for t in range(16)], reads=["cat_f", "Wout"], writes=[pn])
                sl = slice(512 * half, 512 * half + 512)
                A("dve", lambda e, ps=ps, sl=sl: e.tensor_tensor(out=xn[:L, sl], in0=xt[:L, sl], in1=ps[:L, :], op=ALU.add), r=["xt", pn], w=["xn"])
            S.dma(dst_x1, xn[:L, :], reads=["xn"], writes=["x1_d"])

        def load_hist_from_state(b):
            S.dma(hist_tm[:3, :], st_sc[b, :, :], writes=XBCC)
            for g in range(4):
                ps, pn = nextps()
                S.mm([lambda e, j=j, g=g: e.transpose(out=ps[:, j * 128:j * 128 + 3], in_=hist_tm[:3, (4 * g + j) * 128:(4 * g + j + 1) * 128],
                                                      identity=identf[:3, :3]) for j in range(4)], reads=XBCC + ["identf"], writes=[pn])
                A("act", lambda e, ps=ps, g=g: e.activation(out=xbc_f[:, 4 * g:4 * g + 4, HP - 3:HP], in_=ps[:, :].rearrange("p (a b) -> p a b", b=128)[:, :, :3],
                                                            func=AF.Copy), r=[pn], w=["xbcf%d" % t for t in range(4 * g, 4 * g + 4)])
            S.dma(hist_tm[:30, :1024], st_cc[b, :, :], writes=XBCC)
            for g in range(2):
                ps, pn = nextps()
                S.mm([lambda e, j=j, g=g: e.transpose(out=ps[:, j * 128:j * 128 + 30], in_=hist_tm[:30, (4 * g + j) * 128:(4 * g + j + 1) * 128],
                                                      identity=identf[:30, :30]) for j in range(4)], reads=XBCC + ["identf"], writes=[pn])
                A("act", lambda e, ps=ps, g=g: e.activation(out=gl_f[:, 4 * g:4 * g + 4, HP - 30:HP], in_=ps[:, :].rearrange("p (a b) -> p a b", b=128)[:, :, :30],
                                                            func=AF.Copy), r=[pn], w=["glf%d" % t for t in range(4 * g, 4 * g + 4)])
            for g in range(2):
                S.dma(ytmp[:, 512 * g:512 * g + 512].rearrange("p (a n) -> p a n", n=128),
                      st_ssm[b, 512 * g:512 * g + 512, :].rearrange("(a p) n -> p a n", p=128), writes=["ytmp"])
                ps, pn = nextps()
                S.mm([lambda e, j=j, g=g: e.transpose(out=ps[:, j * 128:(j + 1) * 128], in_=ytmp[:, 512 * g + j * 128:512 * g + (j + 1) * 128],
                                                      identity=identf[:, :]) for j in range(4)], reads=["ytmp", "identf"], writes=[pn])
                A("dve", lambda e, ps=ps, g=g: e.tensor_copy(out=H[:, 512 * g:512 * g + 512], in_=ps[:, :]), r=[pn, "Hb"], w=["H"])
            A("act", lambda e: e.activation(out=Hb[:, :], in_=H[:, :], func=AF.Copy), r=["H"], w=["Hb"])

        def store_state_outputs(L, sc_dst, cc_dst, ssm_dst):
            for g in range(4):
                ps, pn = nextps()
                S.mm([lambda e, j=j, g=g: e.transpose(out=ps[:3, j * 128:(j + 1) * 128], in_=xbc_f[:, 4 * g + j, HP + L - 3:HP + L], identity=identf[:, :])
                      for j in range(4)], reads=["xbcf%d" % t for t in range(4 * g, 4 * g + 4)] + ["identf"], writes=[pn])
                A("act", lambda e, ps=ps, g=g: e.activation(out=hout[:3, 512 * g:512 * g + 512], in_=ps[:3, :], func=AF.Copy), r=[pn], w=XBCC)
            S.dma(sc_dst, hout[:3, :], reads=XBCC)
            for g in range(2):
                ps, pn = nextps()
                S.mm([lambda e, j=j, g=g: e.transpose(out=ps[:30, j * 128:(j + 1) * 128], in_=gl_f[:, 4 * g + j, HP + L - 30:HP + L], identity=identf[:, :])
                      for j in range(4)], reads=["glf%d" % t for t in range(4 * g, 4 * g + 4)] + ["identf"], writes=[pn])
                A("act", lambda e, ps=ps, g=g: e.activation(out=hout[:30, 512 * g:512 * g + 512], in_=ps[:30, :], func=AF.Copy), r=[pn], w=XBCC)
            S.dma(cc_dst, hout[:30, :1024], reads=XBCC)
            for g in range(2):
                ps, pn = nextps()
                S.mm([lambda e, j=j, g=g: e.transpose(out=ps[:, j * 128:(j + 1) * 128], in_=H[:, 512 * g + j * 128:512 * g + (j + 1) * 128], identity=identf[:, :])
                      for j in range(4)], reads=["H", "identf"], writes=[pn])
                A("dve", lambda e, ps=ps, g=g: e.tensor_copy(out=ytmp[:, 512 * g:512 * g + 512], in_=ps[:, :]), r=[pn], w=["ytmp"])
                S.dma(ssm_dst[512 * g:512 * g + 512, :].rearrange("(a p) n -> p a n", p=128),
                      ytmp[:, 512 * g:512 * g + 512].rearrange("p (a n) -> p a n", n=128), reads=["ytmp"])

        def zero_state():
            A("pool", lambda e: e.memset(H[:, :], 0.0), r=["Hb"], w=["H"])
            A("pool", lambda e: e.memset(Hb[:, :], 0.0), w=["Hb"])
            A("pool", lambda e: e.memset(Atot[:, :], 0.0), w=["Atot"])

        zero_state()
        for t in range(16):
            A("pool", lambda e, t=t: e.memset(xbc_f[:, t, 0:HP], 0.0), w=["xbcf%d" % t])
        for t in range(8):
            A("pool", lambda e, t=t: e.memset(gl_f[:, t, 0:HP], 0.0), w=["glf%d" % t])
        for c in range(NCHA):
            l0_chunk(xp[c * 128:(c + 1) * 128, :], 128, "full", dst_x1=x1_d[c * 128:(c + 1) * 128, :])
        store_state_outputs(128, sc_p[:, :], cc_p[:, :], ssm_p)
        for b in range(DB):
            load_hist_from_state(b)
            l0_chunk(xsm[b * 4:(b + 1) * 4, :], 4, "full", dst_x1=x1_d[SEQ + b * 4:SEQ + (b + 1) * 4, :])
            store_state_outputs(4, sc_s[b, :, :], cc_s[b, :, :], ssm_s[b])


        S.barrier()
        st0.close()
        if debug_l0:
            st1 = ExitStack(); cur[0] = st1
            xt = T("xt_dbg", [128, D])
            for c in range(NCH):
                S.dma(xt[:, :], x1_d[c * 128:(c + 1) * 128, :], reads=["x1_d"], writes=["xt"])
                S.dma(y_p[c * 128:(c + 1) * 128, :], xt[:, :], reads=["xt"])
            S.dma(xt[:DB * 4, :], x1_d[SEQ:SEQ + DB * 4, :], reads=["x1_d"], writes=["xt"])
            S.dma(y_s[:, :], xt[:DB * 4, :], reads=["xt"])
            S.finish("sp")
            st1.close()
            return nc
        st1 = ExitStack(); cur[0] = st1
        NSLOT = TPC // 128
        NB = 64
        psn[0] = 6
        Wq = T("Wq", [128, 8, 1024], BF16)
        Wg = T("Wg", [128, 8, 1024], BF16)
        Wo = T("Wo", [64, 16, 1024], BF16)
        g1 = T("g1t", [128, 8]); S.dma(g1[:], g1_d[:, :], writes=["g1t"])
        qg = T("qgt", [128, 1]); S.dma(qg[:], qg_d[:, :], writes=["qgt"])
        kgb = T("kgbt", [128, 256]); S.dma(kgb[:], kgb_d[:, :], writes=["kgbt"])
        pm = T("pmt", [128, 64])
        pneg = T("pnegt", [128, 64])
        om = T("omt", [128, 64])
        maskM = T("maskMt", [128, 8, 128]); S.dma(maskM[:], maskM_d[:, :, :], writes=["maskMt"])
        maskS = T("maskSt", [4, 16]); S.dma(maskS[:], maskS_d[:, :], writes=["maskSt"])
        oidx = T("oidxt", [128, NSLOT], I32); S.dma(oidx[:], oidx_d[:, :], writes=["oidx"])
        pidx = T("pidxt", [128, 1]); S.dma(pidx[:], pidx_d[:, :], writes=["pidx"])
        Eoh = T("Eoh", [64, 64, 128], BF16)
        A("pool", lambda e: e.memset(Eoh[:], 1.0), w=["Eoh"])
        A("pool", lambda e: e.affine_select(out=Eoh[:], in_=Eoh[:], pattern=[[-1, 64], [0, 128]], compare_op=ALU.is_equal, fill=0.0,
                                            base=0, channel_multiplier=1), r=["Eoh"], w=["Eoh"])
        xt1 = T("xt1", [128, D])
        xn1 = T("xn1", [128, D])
        ss1 = T("ss1", [128, 8])
        xT1 = T("xT1", [128, 8, 128], BF16)
        kms = T("kms", [128, 2, max(NCHA, NPG)])
        kmT = T("kmT", [128, 2, 64])
        A("pool", lambda e: e.memset(kmT[:], 0.0), w=["kmT"])
        stA = ExitStack(); cur[0] = stA
        Wkv = T("Wkv", [128, 8, 512], BF16)
        stg1 = [T("stg1a", [128, 1024]), T("stg1b", [128, 1024])]
        si = 0
        for (wd, wt, ncol, wname) in ((wq_d, Wq, 1024, "Wq"), (wkv_d, Wkv, 512, "Wkv"), (wg_d, Wg, 1024, "Wg")):
            for kt in range(8):
                sb = stg1[si % 2]
                S.dma(sb[:, :ncol], wd[:, kt, :], writes=[sb.name])
                A("dve" if si % 2 == 0 else "pool", lambda e, sb=sb, wt=wt, kt=kt, ncol=ncol: e.tensor_scalar(
                    out=wt[:, kt, :], in0=sb[:, :ncol], scalar1=g1[:, kt:kt + 1], scalar2=None, op0=ALU.mult), r=[sb.name, "g1t"], w=[wname])
                si += 1
        for h in range(16):
            sb = stg1[si % 2]
            S.dma(sb[:64, :], wo_d[:, h, :], writes=[sb.name])
            A("dve" if si % 2 == 0 else "pool", lambda e, sb=sb, h=h: e.tensor_copy(out=Wo[:, h, :], in_=sb[:64, :]), r=[sb.name], w=["Wo"])
            si += 1

        KV = T("KV", [128, 512])
        KTst = T("KTst", [128, 2, 128], BF16)
        Vb = T("Vb", [128, 4, 65], BF16)
        A("pool", lambda e: e.memset(Vb[:], 1.0), w=["Vb"])

        def l1_norm_T(L):
            A("act", lambda e: e.activation(out=xn1[:L, :], in_=xt1[:L, :], func=AF.Square, scale=1.0 / 32, accum_out=ss1[:L, 0:1]), r=["xt1"], w=["xn1", "ss1"])
            A("act", lambda e: e.activation(out=ss1[:L, 1:2], in_=ss1[:L, 0:1], func=AF.Ln, bias=epsT[:L, 0:1]), r=["ss1", "epsT"], w=["ss1"])
            A("act", lambda e: e.activation(out=ss1[:L, 2:3], in_=ss1[:L, 1:2], func=AF.Exp, scale=-0.5), r=["ss1"], w=["ss1"])
            A("dve", lambda e: e.tensor_scalar(out=xn1[:L, :], in0=xt1[:L, :], scalar1=ss1[:L, 2:3], scalar2=None, op0=ALU.mult), r=["xt1", "ss1"], w=["xn1"])
            for half in range(2):
                ps, pn = nextps()
                S.mm([lambda e, j=j, half=half: e.transpose(out=ps[:, j * 128:j * 128 + L], in_=xn1[:L, (half * 4 + j) * 128:(half * 4 + j + 1) * 128],
                                                           identity=identf[:L, :L]) for j in range(4)], reads=["xn1", "identf"], writes=[pn])
                v = ps[:, :].rearrange("p (a b) -> p a b", b=128)[:, :, :L]
                A("act", lambda e, v=v, half=half: e.activation(out=xT1[:, half * 4:half * 4 + 4, :L], in_=v, func=AF.Copy), r=[pn], w=["xT1"])

        def l1_kv(src, L, kdst, vdst, col0, chunk_idx):
            S.dma(xt1[:L, :], src, reads=["x1_d"], writes=["xt1"])
            l1_norm_T(L)
            ps, pn = nextps()
            S.mm([lambda e, kt=kt: e.matmul(out=ps[:L, :], lhsT=xT1[:, kt, :L], rhs=Wkv[:, kt, :], start=(kt == 0), stop=(kt == 7)) for kt in range(8)],
                 reads=["xT1", "Wkv"], writes=[pn])
            for h in range(4):
                A("act", lambda e, h=h: e.activation(out=xn1[:L, h * 64:(h + 1) * 64], in_=ps[:L, h * 64:(h + 1) * 64], func=AF.Square, scale=0.125,
                                                     accum_out=ss1[:L, 4 + h:5 + h]), r=[pn], w=["xn1", "ss1"])
            A("act", lambda e: e.activation(out=ss1[:L, 4:8], in_=ss1[:L, 4:8], func=AF.Ln, bias=epsT[:L, 0:1]), r=["ss1", "epsT"], w=["ss1"])
            A("act", lambda e: e.activation(out=ss1[:L, 4:8], in_=ss1[:L, 4:8], func=AF.Exp, scale=-0.5), r=["ss1"], w=["ss1"])
            A("dve", lambda e: e.tensor_tensor(out=KV[:L, 0:256].rearrange("p (h d) -> p h d", d=64), in0=ps[:L, 0:256].rearrange("p (h d) -> p h d", d=64),
                                               in1=ss1[:L, 4:8].unsqueeze(2).broadcast_to([L, 4, 64]), op=ALU.mult), r=[pn, "ss1"], w=["KV"])
            A("dve", lambda e: e.tensor_tensor(out=KV[:L, 0:256], in0=KV[:L, 0:256], in1=kgb[:L, :], op=ALU.mult), r=["KV", "kgbt"], w=["KV"])
            A("act", lambda e: e.activation(out=KV[:L, 256:512], in_=ps[:L, 256:512], func=AF.Copy), r=[pn], w=["KV"])
            S.dma(kdst, KV[:L, 0:256], reads=["KV"])
            S.dma(vdst, KV[:L, 256:512], reads=["KV"])
            ps2, pn2 = nextps()
            S.mm([lambda e, pr=pr: e.transpose(out=ps2[:, pr * 128:pr * 128 + L], in_=KV[:L, pr * 128:(pr + 1) * 128], identity=identf[:L, :L]) for pr in range(2)],
                 reads=["KV", "identf"], writes=[pn2])
            for pr in range(2):
                A("act", lambda e, pr=pr: e.activation(out=KTst[:, pr, :L], in_=ps2[:, pr * 128:pr * 128 + L], func=AF.Copy,
                                                       accum_out=kms[:, pr, chunk_idx:chunk_idx + 1]), r=[pn2], w=["KTst", "kms"])
                S.dma(KT_d[pr, :, col0:col0 + L], KTst[:, pr, :L], reads=["KTst"], writes=["KT_d"])
            A("dve", lambda e: e.tensor_copy(out=Vb[:L, :, 0:64], in_=KV[:L, 256:512].rearrange("p (h d) -> p h d", d=64)), r=["KV"], w=["Vb"])
            S.dma(V_d[col0:col0 + L, :], Vb[:L, :, :].rearrange("p h c -> p (h c)"), reads=["Vb"], writes=["V_d"])

        for t in range(NCHA):
            l1_kv(x1_d[t * 128:(t + 1) * 128, :], 128, k_p[t * 128:(t + 1) * 128, :], v_p[t * 128:(t + 1) * 128, :], t * 128, t)
        NBP = NCHA // 2
        kv2 = kms[:, :, 0:NCHA].rearrange("p a (n two) -> p a n two", two=2)
        A("dve", lambda e: e.tensor_tensor(out=kmT[:, :, 0:NBP], in0=kv2[:, :, :, 0], in1=kv2[:, :, :, 1], op=ALU.add), r=["kms"], w=["kmT"])
        A("dve", lambda e: e.tensor_scalar(out=kmT[:, :, 0:NBP], in0=kmT[:, :, 0:NBP], scalar1=1.0 / 256, scalar2=None, op0=ALU.mult), r=["kmT"], w=["kmT"])
        for b in range(DB):
            l1_kv(x1_d[SEQ + b * 4:SEQ + (b + 1) * 4, :], 4, k_s[b * 4:(b + 1) * 4, :], v_s[b * 4:(b + 1) * 4, :], SEQ + b * 4, NCHA - 1 if False else 0)

        S.barrier()
        stA.close()
        cur[0] = st1
        QF = T("QF", [128, 8, 128])
        sq1 = T("sq1", [128, 4, 128])
        QPf = T("QPf", [128, 16, 128])
        QPb = T("QPb", [128, 16, 128], BF16)
        SG = T("SG", [64, 16, 128], BF16)
        Gm = T("Gm", [128, 16, 64])
        m8 = T("m8", [128, 16, 8])
        bsel = Gm
        biasT = T("biasT", [64, 16, 128], BF16)
        ATg = T("ATg", [64, 16, 128], BF16)
        rdt = xn1
        bcs = T("bcs", [64, 512])
        atmp = T("atmp", [64, 512])
        PT = [T("PT0", [128, 512], BF16), T("PT1", [128, 512], BF16)]
        KTst2s = [T("KTst2", [128, 2, 128], BF16), T("KTst2b", [128, 2, 128], BF16)]
        A("pool", lambda e: e.memset(QPf[:], 0.0), w=["QPf"])

        def l1_q(L):
            l1_norm_T(L)
            for half in range(2):
                ps, pn = nextps()
                S.mm([lambda e, r=r, kt=kt, half=half: e.matmul(out=ps[:, r * 128:r * 128 + L], lhsT=Wq[:, kt, (4 * half + r) * 128:(4 * half + r + 1) * 128],
                                                             rhs=xT1[:, kt, :L], start=(kt == 0), stop=(kt == 7)) for r in range(4) for kt in range(8)],
                     reads=["Wq", "xT1"], writes=[pn])
                v = ps[:, :].rearrange("p (a b) -> p a b", b=128)[:, :, :L]
                A("act", lambda e, v=v: e.activation(out=sq1[:, :, :L], in_=v, func=AF.Square, scale=0.125), r=[pn], w=["sq1"])
                pss, pns = nextps()
                S.mm([lambda e, r=r: e.matmul(out=pss[:, r * 128:r * 128 + L], lhsT=BD[:, :], rhs=sq1[:, r, :L], start=True, stop=True) for r in range(4)],
                     reads=["BD", "sq1"], writes=[pns])
                vs = pss[:, :].rearrange("p (a b) -> p a b", b=128)[:, :, :L]
                A("act", lambda e, vs=vs: e.activation(out=sq1[:, :, :L], in_=vs, func=AF.Ln, bias=epsT[:, 0:1]), r=[pns, "epsT"], w=["sq1"])
                A("act", lambda e: e.activation(out=sq1[:, :, :L], in_=sq1[:, :, :L], func=AF.Exp, scale=-0.5), r=["sq1"], w=["sq1"])
                A("dve", lambda e, v=v, half=half: e.tensor_tensor(out=QF[:, 4 * half:4 * half + 4, :L], in0=v, in1=sq1[:, :, :L], op=ALU.mult), r=[pn, "sq1", "QFk0", "QFk1", "QFv0", "QFv1"], w=["QF", "QFk0", "QFk1", "QFv0", "QFv1"])
            A("dve", lambda e: e.tensor_scalar(out=QF[:, :, :L], in0=QF[:, :, :L], scalar1=qg[:, 0:1], scalar2=None, op0=ALU.mult), r=["QF", "qgt"], w=["QF"])
            for hk in range(4):
                rows = slice(64 * (hk % 2), 64 * (hk % 2) + 64)
                t0 = (hk // 2) * 4
                A("dve", lambda e, hk=hk, rows=rows, t0=t0: e.tensor_copy(out=QPf[rows, 4 * hk:4 * hk + 4, :L], in_=QF[rows, t0:t0 + 4, :L]), r=["QF"], w=["QPf"])
            A("act", lambda e: e.activation(out=QPb[:, :, :L], in_=QPf[:, :, :L], func=AF.Copy), r=["QPf"], w=["QPb"])
            for bk in range(4):
                ps, pn = nextps()
                S.mm([lambda e, r=r, kt=kt, bk=bk: e.matmul(out=ps[0:64, r * 128:r * 128 + L], lhsT=Wg[:, kt, (4 * bk + r) * 64:(4 * bk + r + 1) * 64],
                                                           rhs=xT1[:, kt, :L], start=(kt == 0), stop=(kt == 7)) for r in range(4) for kt in range(8)],
                     reads=["Wg", "xT1"], writes=[pn])
                A("act", lambda e, ps=ps, bk=bk: e.activation(out=SG[:, 4 * bk:4 * bk + 4, :L], in_=ps[0:64, :].rearrange("p (a b) -> p a b", b=128)[:, :, :L],
                                                              func=AF.Silu), r=[pn], w=["SG"])

        def moba_select(L, kmt, kmname, slot):
            S.dma(pm[:, :], pm_d[:, slot, :], writes=["pmt"])
            S.dma(pneg[:, :], pneg_d[:, slot, :], writes=["pnegt"])
            S.dma(om[:, :], om_d[:, slot, :], writes=["omt"])
            for half in range(2):
                ps, pn = nextps()
                S.mm([lambda e, i=i, half=half: e.matmul(out=ps[:L, i * 64:(i + 1) * 64], lhsT=QPf[:, 8 * half + i, :L], rhs=kmt[:, (8 * half + i) // 8, :],
                                                        start=True, stop=True) for i in range(8)], reads=["QPf", kmname], writes=[pn])
                A("dve", lambda e, ps=ps, half=half: e.tensor_tensor(out=Gm[:L, 8 * half:8 * half + 8, :], in0=ps[:L, :].rearrange("p (a b) -> p a b", b=64),
                                                                     in1=pm[:L, :].unsqueeze(1).broadcast_to([L, 8, 64]), op=ALU.mult), r=[pn, "pmt"], w=["Gm"])
            A("dve", lambda e: e.tensor_tensor(out=Gm[:L, :, :], in0=Gm[:L, :, :], in1=pneg[:L, :].unsqueeze(1).broadcast_to([L, 16, 64]), op=ALU.add),
              r=["Gm", "pnegt"], w=["Gm"])
            for h in range(16):
                A("dve", lambda e, h=h: e.max(out=m8[:L, h, :], in_=Gm[:L, h, :]), r=["Gm"], w=["m8"])
            for h in range(16):
                A("dve", lambda e, h=h: e.tensor_scalar(out=bsel[:L, h, :], in0=Gm[:L, h, :], scalar1=m8[:L, h, 2:3], scalar2=None, op0=ALU.is_ge), r=["Gm", "m8"], w=["Gm"])
            A("dve", lambda e: e.tensor_tensor(out=bsel[:L, :, :], in0=bsel[:L, :, :], in1=pm[:L, :].unsqueeze(1).broadcast_to([L, 16, 64]), op=ALU.mult),
              r=["Gm", "pmt"], w=["Gm"])
            A("dve", lambda e: e.tensor_tensor(out=bsel[:L, :, :], in0=bsel[:L, :, :], in1=om[:L, :].unsqueeze(1).broadcast_to([L, 16, 64]), op=ALU.add),
              r=["Gm", "omt"], w=["Gm"])
            A("dve", lambda e: e.tensor_scalar(out=bsel[:L, :, :], in0=bsel[:L, :, :], scalar1=-NEG, scalar2=NEG, op0=ALU.mult, op1=ALU.add), r=["Gm"], w=["Gm"])
            for bk in range(4):
                ps, pn = nextps()
                S.mm([lambda e, r=r, bk=bk: e.transpose(out=ps[0:64, r * 128:r * 128 + L], in_=bsel[:L, 4 * bk + r, :], identity=identf[:L, :L]) for r in range(4)],
                     reads=["Gm", "identf"], writes=[pn])
                A("act", lambda e, ps=ps, bk=bk: e.activation(out=biasT[:, 4 * bk:4 * bk + 4, :L], in_=ps[0:64, :].rearrange("p (a b) -> p a b", b=128)[:, :, :L],
                                                              func=AF.Copy), r=[pn], w=["biasT"])

        def finish_group(L, hk, psO, pnO):
            W4 = 4 * L
            A("dve", lambda e: e.reciprocal(out=rdt[64:65, :W4], in_=psO[64:65, :W4]), r=[pnO], w=["xn1"])
            psb, pnb = nextps()
            S.mm([lambda e: e.matmul(out=psb[0:64, :W4], lhsT=onesf[64:65, 0:64], rhs=rdt[64:65, :W4], start=True, stop=True)], reads=["onesf", "xn1"], writes=[pnb])
            A("act", lambda e: e.activation(out=bcs[:, :W4], in_=psb[0:64, :W4], func=AF.Copy), r=[pnb], w=["bcs"])
            A("dve", lambda e: e.tensor_tensor(out=atmp[:, :W4], in0=psO[0:64, :W4], in1=bcs[:, :W4], op=ALU.mult), r=[pnO, "bcs"], w=["atmp"])
            A("dve", lambda e: e.tensor_tensor(out=ATg[:, 4 * hk:4 * hk + 4, :L], in0=atmp[:, :W4].rearrange("p (a b) -> p a b", b=L), in1=SG[:, 4 * hk:4 * hk + 4, :L],
                                                op=ALU.mult), r=["atmp", "SG"], w=["ATg"])

        def out_proj(L, dst):
            for half in range(2):
                ps, pn = nextps()
                S.mm([lambda e, h=h, half=half: e.matmul(out=ps[:L, :], lhsT=ATg[:, h, :L], rhs=Wo[:, h, 512 * half:512 * half + 512], start=(h == 0), stop=(h == 15))
                      for h in range(16)], reads=["ATg", "Wo"], writes=[pn])
                sl = slice(512 * half, 512 * half + 512)
                A("dve", lambda e, ps=ps, sl=sl: e.tensor_tensor(out=xn1[:L, sl], in0=xt1[:L, sl], in1=ps[:L, :], op=ALU.add), r=["xt1", pn], w=["xn1"])
            S.dma(dst, xn1[:L, :], reads=["xn1"])

        st2 = ExitStack(); cur[0] = st2
        NKT = NCHA
        NKB = max(NKT, NPG)
        KTp1 = T("KTp0", [128, NKB * 128], BF16)
        KTp = [KTp1, KTp1]
        Vp = [T("Vp%d" % i, [128, NKB, 65], BF16) for i in range(2)]
        bufi = 0
        for j in range(NSLOT):
            nkt = 8 * j + 8
            S.idma(xt1[:, :], x1_d[:, :], oidx[:, j:j + 1], reads=["x1_d", "oidx"], writes=["xt1"])
            l1_q(128)
            moba_select(128, kmT, "kmT", j)
            for hk in range(4):
                pr = hk // 2
                if hk % 2 == 0:
                    ktp = KTp[pr]
                    S.dma(ktp[:, :nkt * 128], KT_d[pr, :, 0:nkt * 128], reads=["KT_d"], writes=[ktp.name])
                vp = Vp[hk % 2]
                S.dma(vp[:, :nkt, :], V_d[0:nkt * 128, hk * 65:(hk + 1) * 65].rearrange("(k p) c -> p k c", p=128), reads=["V_d"], writes=[vp.name])
                psO, pnO = pst[7 - (hk % 2)], "ps%d" % (7 - (hk % 2))
                def emit_st(kt, ktp=ktp, hk=hk):
                    ps, pn = nextps(6)
                    S.mm([lambda e: e.matmul(out=ps[:, :], lhsT=ktp[:, kt * 128:(kt + 1) * 128],
                                             rhs=QPb[:, 4 * hk:4 * hk + 4, :].rearrange("p a b -> p (a b)"), start=True, stop=False),
                          lambda e: e.matmul(out=ps[:, :], lhsT=Eoh[:, kt // 2, :], rhs=biasT[:, 4 * hk:4 * hk + 4, :].rearrange("p a b -> p (a b)"),
                                             start=False, stop=True)], reads=[ktp.name, "QPb", "Eoh", "biasT"], writes=[pn])
                    return ps, pn
                pend = [emit_st(k_) for k_ in range(min(2, nkt))]
                for kt in range(nkt):
                    ps, pn = pend.pop(0)
                    if kt + 2 < nkt:
                        pend.append(emit_st(kt + 2))
                    pt = PT[kt % 2]
                    A("act", lambda e, ps=ps, pt=pt: e.activation(out=pt[:, :], in_=ps[:, :], func=AF.Exp, scale=0.125), r=[pn], w=[pt.name])
                    if kt >= 8 * j:
                        A("dve", lambda e, pt=pt, kt=kt, j=j: e.tensor_tensor(out=pt[:, :].rearrange("p (a b) -> p a b", b=128), in0=pt[:, :].rearrange("p (a b) -> p a b", b=128),
                                                                             in1=maskM[:, kt - 8 * j, :].unsqueeze(1).broadcast_to([128, 4, 128]), op=ALU.mult),
                          r=[pt.name, "maskMt"], w=[pt.name])
                    S.mm([lambda e, kt=kt, vp=vp, pt=pt, psO=psO, nkt=nkt: e.matmul(out=psO[0:65, :], lhsT=vp[:, kt, :], rhs=pt[:, :], start=(kt == 0), stop=(kt == nkt - 1))],
                         reads=[vp.name, pt.name], writes=[pnO])
                finish_group(128, hk, psO, pnO)
            out_proj(128, y_p[j * 128:(j + 1) * 128, :])

        PTs = Gm[:, :, :].rearrange("p a b -> p (a b)").bitcast(BF16)
        PTn = T("PTn", [4, 16], BF16)
        KTn = T("KTn", [128, 2, 4], BF16)
        Vn = T("Vn", [4, 4, 65], BF16)
        ptf = m8[:, :, :].rearrange("p a b -> p (a b)")[:, :NPG]
        pti = T("pti", [128, NPG], I32)
        kmTs = kmT
        VsAs = [T("VsA", [128, 4, 65], BF16)] * 2
        QFl = QF[:, :, :].rearrange("p a b -> p (a b)")
        Kpg = [QFl[:, 0:256], QFl[:, 256:512]]
        Vpg = [QFl[:, 512:768], QFl[:, 768:1024]]
        A("pool", lambda e: e.memset(VsAs[0][:], 1.0), w=["VsA"])
        A("pool", lambda e: e.memset(kmTs[:], 0.0), w=["kmT"])
        NBS = NPG // 2
        for b in range(DB):
            S.dma(pti[:, :], ptb_d[:, b, :], writes=["pti"])
            A("dve", lambda e: e.tensor_copy(out=ptf[:, :], in_=pti[:, :]), r=["pti"], w=["m8"])
            A("dve", lambda e: e.tensor_scalar(out=ptf[:, :], in0=ptf[:, :], scalar1=128.0, scalar2=pidx[:, 0:1], op0=ALU.mult, op1=ALU.add), r=["m8", "pidx"], w=["m8"])
            A("dve", lambda e: e.tensor_copy(out=pti[:, :], in_=ptf[:, :]), r=["m8"], w=["pti"])
            for pg in range(NPG):
                kp, vq = Kpg[pg % 2], Vpg[pg % 2]
                kn_, vn_ = "QFk%d" % (pg % 2), "QFv%d" % (pg % 2)
                KTst2, VsA = KTst2s[pg % 2], VsAs[pg % 2]
                S.idma(kp, ck_d[:, :], pti[:, pg:pg + 1], reads=["pti", "QF"], writes=[kn_])
                S.idma(vq, cv_d[:, :], pti[:, pg:pg + 1], reads=["pti", "QF"], writes=[vn_])
                ps, pn = nextps()
                S.mm([lambda e, pr=pr, kp=kp, ps=ps: e.transpose(out=ps[:, pr * 128:(pr + 1) * 128], in_=kp[:, pr * 128:(pr + 1) * 128], identity=identf[:, :]) for pr in range(2)],
                     reads=[kn_, "identf"], writes=[pn])
                for pr in range(2):
                    A("act", lambda e, pr=pr, pg=pg, ps=ps, KTst2=KTst2: e.activation(out=KTst2[:, pr, :], in_=ps[:, pr * 128:(pr + 1) * 128], func=AF.Copy,
                                                                                    accum_out=kms[:, pr, pg:pg + 1]), r=[pn], w=[KTst2.name + "_%d" % pr, "kms"])
                    S.dma(KTs_d[b, pr, :, pg * 128:(pg + 1) * 128], KTst2[:, pr, :], reads=[KTst2.name + "_%d" % pr], writes=["KTs_d"])
                A("dve", lambda e, vq=vq, VsA=VsA: e.tensor_copy(out=VsA[:, :, 0:64], in_=vq.rearrange("p (h d) -> p h d", d=64)), r=[vn_], w=[VsA.name])
                S.dma(Vs_d[b, pg * 128:(pg + 1) * 128, :], VsA[:, :, :].rearrange("p h c -> p (h c)"), reads=[VsA.name], writes=["Vs_d"])
            kv3 = kms[:, :, 0:NPG].rearrange("p a (n two) -> p a n two", two=2)
            A("dve", lambda e: e.tensor_tensor(out=kmTs[:, :, 0:NBS], in0=kv3[:, :, :, 0], in1=kv3[:, :, :, 1], op=ALU.add), r=["kms"], w=["kmT"])
            A("dve", lambda e: e.tensor_scalar(out=kmTs[:, :, 0:NBS], in0=kmTs[:, :, 0:NBS], scalar1=1.0 / 256, scalar2=None, op0=ALU.mult), r=["kmT"], w=["kmT"])
            S.dma(KTn[:, :, :], KT_d[:, :, SEQ + b * 4:SEQ + (b + 1) * 4].rearrange("a p c -> p a c"), reads=["KT_d"], writes=["KTn"])
            S.dma(Vn[:, :, :].rearrange("p h c -> p (h c)"), V_d[SEQ + b * 4:SEQ + (b + 1) * 4, :], reads=["V_d"], writes=["Vn"])
            S.dma(xt1[:4, :], x1_d[SEQ + b * 4:SEQ + (b + 1) * 4, :], reads=["x1_d"], writes=["xt1"])
            l1_q(4)
            moba_select(4, kmTs, "kmT", NSLOT)
            for hk in range(4):
                pr = hk // 2
                if hk % 2 == 0:
                    S.dma(KTp1[:, :NPG * 128], KTs_d[b, pr, :, :], reads=["KTs_d"], writes=["KTp0"])
                vp = Vp[hk % 2]
                S.dma(vp[:, :NPG, :], Vs_d[b, :, hk * 65:(hk + 1) * 65].rearrange("(k p) c -> p k c", p=128), reads=["Vs_d"], writes=[vp.name])
                qrhs = QPb[:, 4 * hk:4 * hk + 4, :4]
                brhs = biasT[:, 4 * hk:4 * hk + 4, :4]
                PPB = 32
                for bk in range((NPG + PPB - 1) // PPB):
                    ps, pn = nextps()
                    pgs = list(range(bk * PPB, min(NPG, (bk + 1) * PPB)))
                    fns = []
                    for pg in pgs:
                        o = ps[:, (pg - bk * PPB) * 16:(pg - bk * PPB + 1) * 16].rearrange("p (a b) -> p a b", b=4)
                        fns.append(lambda e, o=o, pg=pg, qrhs=qrhs: e.matmul(out=o, lhsT=KTp1[:, pg * 128:(pg + 1) * 128], rhs=qrhs, start=True, stop=False))
                        fns.append(lambda e, o=o, pg=pg, brhs=brhs: e.matmul(out=o, lhsT=Eoh[:, pg // 2, :], rhs=brhs, start=False, stop=True))
                    S.mm(fns, reads=["KTp0", "QPb", "Eoh", "biasT"], writes=[pn])
                    ncol = len(pgs) * 16
                    A("act", lambda e, ps=ps, bk=bk, ncol=ncol: e.activation(out=PTs[:, bk * PPB * 16:bk * PPB * 16 + ncol], in_=ps[:, :ncol], func=AF.Exp, scale=0.125),
                      r=[pn], w=["Gm"])
                ps, pn = nextps()
                S.mm([lambda e, ps=ps, pr=pr, qrhs=qrhs: e.matmul(out=ps[0:4, 0:16].rearrange("p (a b) -> p a b", b=4), lhsT=KTn[:, pr, :], rhs=qrhs, start=True, stop=True)],
                     reads=["KTn", "QPb"], writes=[pn])
                A("act", lambda e, ps=ps: e.activation(out=PTn[:, :], in_=ps[0:4, 0:16], func=AF.Exp, scale=0.125), r=[pn], w=["PTn"])
                A("dve", lambda e: e.tensor_tensor(out=PTn[:, :], in0=PTn[:, :], in1=maskS[:, :], op=ALU.mult), r=["PTn", "maskSt"], w=["PTn"])
                psO, pnO = pst[7 - (hk % 2)], "ps%d" % (7 - (hk % 2))
                fns = [lambda e, pg=pg, vp=vp, psO=psO: e.matmul(out=psO[0:65, 0:16], lhsT=vp[:, pg, :], rhs=PTs[:, pg * 16:(pg + 1) * 16], start=(pg == 0), stop=False)
                       for pg in range(NPG)]
                fns.append(lambda e, hk=hk, psO=psO: e.matmul(out=psO[0:65, 0:16], lhsT=Vn[:, hk, :], rhs=PTn[:, :], start=False, stop=True))
                S.mm(fns, reads=[vp.name, "Gm", "Vn", "PTn"], writes=[pnO])
                finish_group(4, hk, psO, pnO)
            out_proj(4, y_s[b * 4:(b + 1) * 4, :])
        S.barrier()
        st2.close()
        st1.close()
        S.finish("sp")
        print("instructions:", S.n_instr, "sem counts", S.cnt)
    return nc


def prep_inputs(cfg, inp):
    f = lambda a: np.ascontiguousarray(np.asarray(a, dtype=np.float32))
    TPC, DB = cfg.tpc, cfg.db
    xp_full = f(inp["x_prompt"])[0]
    common = {
        "w_in0": f(f(inp["w_in0"])[0].reshape(8, 128, 6160).transpose(1, 0, 2)),
        "w_out0": f(f(inp["w_out0"])[0].reshape(16, 128, 1024).transpose(1, 0, 2)),
        "g0": f(f(inp["norm0_g"])[0].reshape(8, 128).T),
        "cw": f(f(inp["ssd_conv_w"])[0].reshape(4, 16, 128).transpose(2, 1, 0)),
        "cb": f(f(inp["ssd_conv_b"])[0].reshape(16, 128).T),
        "ccw": f(f(inp["conf_conv_w"])[0].reshape(31, 8, 128).transpose(2, 1, 0)),
        "ccb": f(f(inp["conf_conv_b"])[0].reshape(8, 128).T),
        "lng": f(f(inp["conf_ln_g"])[0].reshape(8, 128).T),
        "lnb": f(f(inp["conf_ln_b"])[0].reshape(8, 128).T),
        "dtb": f(np.broadcast_to(f(inp["ssd_dt_bias"])[0][None, :], (128, 16))),
        "alog": f(np.broadcast_to(f(inp["ssd_a_log"])[0][None, :], (128, 16))),
        "dsk": f(np.broadcast_to(f(inp["ssd_d"])[0][None, :], (128, 16))),
        "sng": f(f(inp["ssd_norm_g"])[0].reshape(8, 128).T),
    }
    perm = [0, 4, 1, 5, 2, 6, 3, 7, 8, 12, 9, 13, 10, 14, 11, 15]
    w1 = f(inp["w_in1"])[0]
    wq = w1[:, :1024].reshape(1024, 16, 64)[:, perm, :].reshape(1024, 1024)
    wkv = w1[:, 1024:1536]
    wg = w1[:, 1536:2560]
    kt_layout = lambda w: f(w.reshape(8, 128, w.shape[1]).transpose(1, 0, 2))
    NSLOT = TPC // 128
    NPG = cfg.npg
    npool = cfg.npool
    common.update({
        "wq": kt_layout(wq), "wkv": kt_layout(wkv), "wg": kt_layout(wg),
        "wo": f(f(inp["w_out1"])[0].reshape(16, 64, 1024).transpose(1, 0, 2)),
        "g1": f(f(inp["norm1_g"])[0].reshape(8, 128).T),
        "qg": f(np.tile(f(inp["q_norm_g"])[0], 2).reshape(128, 1)),
        "kgb": f(np.broadcast_to(np.tile(f(inp["k_norm_g"])[0], 4)[None, :], (128, 256))),
        "maskS": f(np.tile((np.arange(4)[:, None] <= np.arange(4)[None, :]).astype(np.float32), (1, 4))),
        "pidx": f(np.arange(128, dtype=np.float32).reshape(128, 1)),
        "ck": f(inp["cache_k"]).reshape(npool * 128, 256),
        "cv": f(inp["cache_v"]).reshape(npool * 128, 256),
    })
    pt_all = np.ascontiguousarray(np.asarray(inp["page_table"], dtype=np.int32))
    tri = (np.arange(128)[:, None] <= np.arange(128)[None, :]).astype(np.float32)
    maps = []
    for c in range(NCORE):
        m = dict(common)
        pm = np.zeros((NSLOT + 1, 64), np.float32)
        om = np.zeros((NSLOT + 1, 64), np.float32)
        for j in range(NSLOT):
            own = (8 * j + c) // 2
            pm[j, :own] = 1.0
            om[j, own] = 1.0
        pm[NSLOT, :NPG // 2] = 1.0
        bc = lambda a: f(np.broadcast_to(a[None], (128,) + a.shape))
        m["pm"] = bc(pm)
        m["pneg"] = bc((pm - 1.0) * np.float32(1e30))
        m["om"] = bc(om)
        mm_ = np.ones((128, 8, 128), np.float32)
        for dl in range(8):
            if dl // 2 == c // 2:
                if dl == c:
                    mm_[:, dl, :] = tri
                elif dl > c:
                    mm_[:, dl, :] = 0.0
        m["maskM"] = mm_
        m["oidx"] = np.ascontiguousarray(((8 * np.arange(NSLOT)[None, :] + c) * 128 + np.arange(128)[:, None]).astype(np.int32))
        m["ptb"] = np.ascontiguousarray(np.broadcast_to(pt_all[c * DB:(c + 1) * DB][None], (128, DB, NPG)).astype(np.int32))
        m["xp"] = xp_full
        m["xsm"] = f(f(inp["x_sample"])[c * DB:(c + 1) * DB].reshape(DB * 4, D))
        m["st_ssm"] = f(f(inp["state_ssm"])[0, c * DB:(c + 1) * DB].reshape(DB, 1024, 128))
        m["st_sc"] = f(f(inp["state_ssd_conv"])[0, c * DB:(c + 1) * DB])
        m["st_cc"] = f(f(inp["state_conf_conv"])[0, c * DB:(c + 1) * DB])
        cm = np.zeros((128, 8), np.float32)
        cm[:, :c] = 1.0
        m["cmask"] = cm
        maps.append(m)
    return maps


_NC_CACHE = {}


def run(cfg, inp, debug_l0=False):
    key = (cfg.seq, cfg.dbt, cfg.npg, debug_l0)
    if key not in _NC_CACHE:
        _NC_CACHE[key] = build(cfg, debug_l0)
    nc = _NC_CACHE[key]
    maps = prep_inputs(cfg, inp)
    res = run_bass_kernel_spmd(nc, maps, core_ids=list(range(NCORE)))
    R = res.results
    cat = lambda k: np.concatenate([r[k] for r in R], axis=0)
    TPC, DB = cfg.tpc, cfg.db
    y_p = cat("y_p")[None]
    y_s = cat("y_s").reshape(cfg.dbt, 4, D)
    ssm_p = R[0]["ssm_p"].reshape(1, 1, 16, 64, 128)
    ssm_s = cat("ssm_s").reshape(1, cfg.dbt, 16, 64, 128)
    sc_p = R[0]["sc_p"].reshape(1, 1, 3, 2048)
    sc_s = cat("sc_s").reshape(1, cfg.dbt, 3, 2048)
    cc_p = R[0]["cc_p"].reshape(1, 1, 30, 1024)
    cc_s = cat("cc_s").reshape(1, cfg.dbt, 30, 1024)
    if debug_l0:
        return (y_p, y_s, ssm_p, ssm_s, sc_p, sc_s, cc_p, cc_s)
    NSLOT = TPC // 128
    yp = np.empty((cfg.seq, D), np.float32)
    for c in range(NCORE):
        for j in range(NSLOT):
            t = 8 * j + c
            yp[t * 128:(t + 1) * 128] = R[c]["y_p"][j * 128:(j + 1) * 128]
    y_p = yp[None]
    k_p = R[0]["k_p"].reshape(1, 1, cfg.seq, 4, 64)
    v_p = R[0]["v_p"].reshape(1, 1, cfg.seq, 4, 64)
    k_s = cat("k_s").reshape(1, cfg.dbt, 4, 4, 64)
    v_s = cat("v_s").reshape(1, cfg.dbt, 4, 4, 64)
    return (y_p, y_s, ssm_p, ssm_s, sc_p, sc_s, cc_p, cc_s, k_p, v_p, k_s, v_s)


def kernel(**inputs):
    cfg = Cfg(inputs["x_prompt"].shape[1], inputs["x_sample"].shape[0], inputs["page_table"].shape[1] * 128)
    return run(cfg, inputs)
```
